# Optimizing a Trainium2 kernel written in Bass

```python
import math
import jax, jax.numpy as jnp
from jax import lax
import numpy as np

D_MODEL = 1024
BATCH = 32
SEQ = 256
DEPTH = 2
DEC_BATCH = 8
DEC_SEQ = 1024
PAST_LEN = 256

GRID_W = 64
CHUNK = 128
Q_BLOCK = 128
EPS = 1e-6
ROPE_BASE = 10000.0
W_A = 256
G_A = 4
DG_A = W_A // G_A
H_B = 4
DK_B = 64
DV_B = 64
W_B = H_B * DV_B
H_C = 4
HD_C = 64
DV_C = 2 * HD_C
W_C = H_C * DV_C
ROPE_PAIRS = HD_C // 4
MIX_W = W_A + W_B + W_C
IN_A = 2 * W_A
IN_B = 2 * H_B * DK_B + 2 * W_B
IN_C = 2 * H_C * 2 * HD_C + W_C
IN_W = IN_A + IN_B + IN_C
D_FF = 2816
CONV_W = 3

kernel_name = 'hybrid_dit_prefix_step'


def rms_norm(x, g):
    xf = x.astype(jnp.float32)
    y = xf * lax.rsqrt(jnp.mean(xf * xf, axis=-1, keepdims=True) + EPS)
    return (y * g.astype(jnp.float32)).astype(x.dtype)


def adaln(cond2d, w_mod_l, b_mod_l):
    m = jax.nn.silu(cond2d) @ w_mod_l + b_mod_l
    return jnp.split(m[:, None, :], 6, axis=-1)


def axial_rope_angles(L):
    rows = L // GRID_W
    t_row = jnp.repeat(jnp.arange(rows, dtype=jnp.float32), GRID_W)
    t_col = jnp.tile(jnp.arange(GRID_W, dtype=jnp.float32), rows)
    inv = ROPE_BASE ** (-jnp.arange(ROPE_PAIRS, dtype=jnp.float32) / ROPE_PAIRS)
    return t_row[:, None] * inv, t_col[:, None] * inv


def _rotate(x, ang):
    x1, x2 = x[..., :ROPE_PAIRS], x[..., ROPE_PAIRS:]
    cos = jnp.cos(ang)[:, None, None, :]
    sin = jnp.sin(ang)[:, None, None, :]
    return jnp.concatenate([x1 * cos - x2 * sin, x1 * sin + x2 * cos], axis=-1)


def apply_axial_rope(x, ang_row, ang_col):
    xf = x.astype(jnp.float32)
    half = HD_C // 2
    out = jnp.concatenate([_rotate(xf[..., :half], ang_row), _rotate(xf[..., half:], ang_col)], axis=-1)
    return out.astype(x.dtype)


def chunk_sgu(z, norm_g, w_s, b_s):
    B, L, _ = z.shape
    u, v = jnp.split(jax.nn.gelu(z), 2, axis=-1)
    v = rms_norm(v, norm_g)
    vg = v.reshape(B, L // CHUNK, CHUNK, G_A, DG_A)
    s = jnp.einsum('gpq,bnqgc->bnpgc', w_s, vg) + b_s.T[None, None, :, :, None]
    return u * s.reshape(B, L, W_A)


def retention_chunked(q, k, v, log_gamma, r0):
    B, L, H, _ = q.shape
    dv = v.shape[-1]
    n = L // CHUNK

    def chunks(t):
        return t.astype(jnp.float32).reshape(B, n, CHUNK, H, t.shape[-1]).transpose(1, 0, 3, 2, 4)

    pos = jnp.arange(CHUNK, dtype=jnp.float32)
    dist = pos[:, None] - pos[None, :]
    lg = log_gamma[:, None, None]
    decay_mat = jnp.where(dist >= 0, jnp.exp(lg * jnp.maximum(dist, 0.0)), 0.0)
    q_decay = jnp.exp(log_gamma[:, None] * (pos + 1.0))[:, :, None]
    k_decay = jnp.exp(log_gamma[:, None] * (CHUNK - 1.0 - pos))[:, :, None]
    chunk_decay = jnp.exp(log_gamma * CHUNK)[:, None, None]

    def step(r, inp):
        qc, kc, vc = inp
        inner = jnp.einsum('bhqd,bhkd->bhqk', qc, kc) * decay_mat
        o = jnp.einsum('bhqk,bhke->bhqe', inner, vc) + jnp.einsum('bhqd,bhde->bhqe', qc, r) * q_decay
        r_new = r * chunk_decay + jnp.einsum('bhkd,bhke->bhde', kc * k_decay, vc)
        return r_new, o

    r_fin, o = lax.scan(step, r0.astype(jnp.float32), (chunks(q), chunks(k), chunks(v)))
    o = o.transpose(1, 0, 3, 2, 4).reshape(B, L, H, dv)
    return o, r_fin


def bidirectional_retention(q, k, v, logit_f, logit_b, r0_f, r0_b):
    lg_f = jax.nn.log_sigmoid(logit_f.astype(jnp.float32))
    lg_b = jax.nn.log_sigmoid(logit_b.astype(jnp.float32))
    o_f, r_f = retention_chunked(q, k, v, lg_f, r0_f)
    o_b, r_b = retention_chunked(q[:, ::-1], k[:, ::-1], v[:, ::-1], lg_b, r0_b)
    return o_f + o_b[:, ::-1], r_f, r_b


def diff_attention(q, k, v, lam, lam_init, norm_g):
    B, Lq = q.shape[:2]
    nb = Lq // Q_BLOCK
    qb = q.astype(jnp.float32).reshape(B, nb, Q_BLOCK, H_C, 2, HD_C).transpose(1, 0, 2, 3, 4, 5)
    kf = k.astype(jnp.float32)
    vf = v.astype(jnp.float32)
    scale = HD_C ** -0.5

    def block(qq):
        s = jnp.einsum('bqhid,bkhid->bhiqk', qq, kf) * scale
        p = jax.nn.softmax(s, axis=-1)
        a = p[:, :, 0] - lam * p[:, :, 1]
        return jnp.einsum('bhqk,bkhe->bqhe', a, vf)

    o = lax.map(block, qb)
    o = o.transpose(1, 0, 2, 3, 4).reshape(B, Lq, H_C, DV_C)
    o = rms_norm(o, norm_g) * (1.0 - lam_init)
    return o.astype(q.dtype).reshape(B, Lq, W_C)


def conv_ffn(h, lp):
    up = h @ lp['ffn_up']
    pad = jnp.pad(up, ((0, 0), (1, 1), (0, 0)))
    w = lp['ffn_conv']
    y = pad[:, :-2] * w[0] + pad[:, 1:-1] * w[1] + pad[:, 2:] * w[2] + lp['ffn_conv_b']
    a, b = jnp.split(y, 2, axis=-1)
    return (jax.nn.silu(a) * b) @ lp['ffn_down']


def mixing(h, lp, lam_init, cache):
    B, L, _ = h.shape
    z = h @ lp['w_in']
    za, zb, zc = jnp.split(z, [IN_A, IN_A + IN_B], axis=-1)
    y_a = chunk_sgu(za, lp['sgu_norm'], lp['sgu_w'], lp['sgu_b'])
    qb, kb, vb, gb = jnp.split(zb, 4, axis=-1)
    qb = qb.reshape(B, L, H_B, DK_B)
    kb = kb.reshape(B, L, H_B, DK_B) * (DK_B ** -0.5)
    vb = vb.reshape(B, L, H_B, DV_B)
    if cache is None:
        r0_f = jnp.zeros((B, H_B, DK_B, DV_B), jnp.float32)
        r0_b = r0_f
    else:
        r0_f, r0_b = cache[2], cache[3]
    o_ret, r_f, r_b = bidirectional_retention(qb, kb, vb, lp['ret_logit_fwd'], lp['ret_logit_bwd'], r0_f, r0_b)
    o_ret = rms_norm(o_ret.astype(h.dtype), lp['ret_norm'])
    y_b = jax.nn.silu(gb) * o_ret.reshape(B, L, W_B)
    qc, kc, vc = jnp.split(zc, [H_C * 2 * HD_C, 2 * H_C * 2 * HD_C], axis=-1)
    qc = rms_norm(qc.reshape(B, L, H_C, 2, HD_C), lp['q_norm'])
    kc = rms_norm(kc.reshape(B, L, H_C, 2, HD_C), lp['k_norm'])
    vc = vc.reshape(B, L, H_C, DV_C)
    dl = lp['diff_lam'].astype(jnp.float32)
    lam = jnp.exp(jnp.sum(dl[0] * dl[1])) - jnp.exp(jnp.sum(dl[2] * dl[3])) + lam_init
    if cache is None:
        keys, vals = kc, vc
        new_ctx = (kc, vc, r_f.astype(h.dtype), r_b.astype(h.dtype))
    else:
        ang_row, ang_col = axial_rope_angles(L)
        qc = apply_axial_rope(qc, ang_row, ang_col)
        keys = jnp.concatenate([cache[0], apply_axial_rope(kc, ang_row, ang_col)], axis=1)
        vals = jnp.concatenate([cache[1], vc], axis=1)
        new_ctx = None
    y_c = diff_attention(qc, keys, vals, lam, lam_init, lp['diff_norm'])
    y = jnp.concatenate([y_a, y_b, y_c], axis=-1) @ lp['w_out']
    return y, new_ctx


def trunk_layer(x, mods, lp, lam_init, cache):
    sh1, sc1, g1, sh2, sc2, g2 = mods
    h = rms_norm(x, lp['norm1']) * (1.0 + sc1) + sh1
    y, new_ctx = mixing(h, lp, lam_init, cache)
    x = x + g1 * y
    h = rms_norm(x, lp['norm2']) * (1.0 + sc2) + sh2
    x = x + g2 * conv_ffn(h, lp)
    return x, new_ctx


def setup_inputs(seed: int = 0) -> dict:
    key = jax.random.key(seed)
    ks = jax.random.split(key, 32)
    nrm = lambda i, shape, s: jax.random.normal(ks[i], shape, jnp.float32) * s
    base_gamma = 1.0 - 2.0 ** (-5.0 - np.arange(H_B, dtype=np.float32))
    base_logit = jnp.asarray(np.log(base_gamma / (1.0 - base_gamma)), jnp.float32)
    return {
        'x_prompt': nrm(0, (BATCH, SEQ, D_MODEL), 1.0),
        'x_sample': nrm(1, (DEC_BATCH, DEC_SEQ, D_MODEL), 1.0),
        'c': nrm(2, (DEC_BATCH, D_MODEL), 1.0),
        'cache_k': nrm(3, (DEC_BATCH, DEPTH, PAST_LEN, H_C, 2, HD_C), 1.0),
        'cache_v': nrm(4, (DEC_BATCH, DEPTH, PAST_LEN, H_C, DV_C), 1.0),
        'state_ret_fwd': nrm(5, (DEC_BATCH, DEPTH, H_B, DK_B, DV_B), 1.0),
        'state_ret_bwd': nrm(6, (DEC_BATCH, DEPTH, H_B, DK_B, DV_B), 1.0),
        'c_ctx': nrm(7, (D_MODEL,), 1.0),
        'norm1': 1.0 + nrm(8, (DEPTH, D_MODEL), 0.02),
        'w_mod': nrm(9, (DEPTH, D_MODEL, 6 * D_MODEL), 0.5 * D_MODEL ** -0.5),
        'b_mod': nrm(10, (DEPTH, 6 * D_MODEL), 0.02),
        'w_in': nrm(11, (DEPTH, D_MODEL, IN_W), D_MODEL ** -0.5),
        'sgu_norm': 1.0 + nrm(12, (DEPTH, W_A), 0.02),
        'sgu_w': nrm(13, (DEPTH, G_A, CHUNK, CHUNK), CHUNK ** -0.5),
        'sgu_b': 1.0 + nrm(14, (DEPTH, G_A, CHUNK), 0.01),
        'ret_logit_fwd': base_logit + nrm(15, (DEPTH, H_B), 0.1),
        'ret_logit_bwd': base_logit + nrm(16, (DEPTH, H_B), 0.1),
        'ret_norm': 1.0 + nrm(17, (DEPTH, H_B, DV_B), 0.02),
        'q_norm': 1.0 + nrm(18, (DEPTH, HD_C), 0.02),
        'k_norm': 1.0 + nrm(19, (DEPTH, HD_C), 0.02),
        'diff_lam': nrm(20, (DEPTH, 4, HD_C), 0.1),
        'diff_norm': 1.0 + nrm(21, (DEPTH, DV_C), 0.02),
        'w_out': nrm(22, (DEPTH, MIX_W, D_MODEL), MIX_W ** -0.5),
        'norm2': 1.0 + nrm(23, (DEPTH, D_MODEL), 0.02),
        'ffn_up': nrm(24, (DEPTH, D_MODEL, 2 * D_FF), D_MODEL ** -0.5),
        'ffn_conv': nrm(25, (DEPTH, CONV_W, 2 * D_FF), CONV_W ** -0.5),
        'ffn_conv_b': nrm(26, (DEPTH, 2 * D_FF), 0.01),
        'ffn_down': nrm(27, (DEPTH, D_FF, D_MODEL), D_FF ** -0.5),
    }


def reference(x_prompt, x_sample, c, cache_k, cache_v, state_ret_fwd, state_ret_bwd, c_ctx,
              norm1, w_mod, b_mod, w_in, sgu_norm, sgu_w, sgu_b, ret_logit_fwd, ret_logit_bwd,
              ret_norm, q_norm, k_norm, diff_lam, diff_norm, w_out, norm2, ffn_up, ffn_conv,
              ffn_conv_b, ffn_down):
    def layer_params(l):
        return dict(norm1=norm1[l], w_in=w_in[l], sgu_norm=sgu_norm[l], sgu_w=sgu_w[l], sgu_b=sgu_b[l],
                    ret_logit_fwd=ret_logit_fwd[l], ret_logit_bwd=ret_logit_bwd[l], ret_norm=ret_norm[l],
                    q_norm=q_norm[l], k_norm=k_norm[l], diff_lam=diff_lam[l], diff_norm=diff_norm[l],
                    w_out=w_out[l], norm2=norm2[l], ffn_up=ffn_up[l], ffn_conv=ffn_conv[l],
                    ffn_conv_b=ffn_conv_b[l], ffn_down=ffn_down[l])

    y_prompt = x_prompt
    ks_out, vs_out, rf_out, rb_out = [], [], [], []
    for l in range(DEPTH):
        lam_init = 0.8 - 0.6 * math.exp(-0.3 * l)
        mods = adaln(c_ctx[None, :], w_mod[l], b_mod[l])
        y_prompt, (k_l, v_l, rf_l, rb_l) = trunk_layer(y_prompt, mods, layer_params(l), lam_init, None)
        ks_out.append(k_l)
        vs_out.append(v_l)
        rf_out.append(rf_l)
        rb_out.append(rb_l)
    new_cache_k = jnp.stack(ks_out, axis=1)
    new_cache_v = jnp.stack(vs_out, axis=1)
    new_state_ret_fwd = jnp.stack(rf_out, axis=1)
    new_state_ret_bwd = jnp.stack(rb_out, axis=1)

    y_sample = x_sample
    for l in range(DEPTH):
        lam_init = 0.8 - 0.6 * math.exp(-0.3 * l)
        mods = adaln(c, w_mod[l], b_mod[l])
        cache_l = (cache_k[:, l], cache_v[:, l], state_ret_fwd[:, l], state_ret_bwd[:, l])
        y_sample, _ = trunk_layer(y_sample, mods, layer_params(l), lam_init, cache_l)

    return (y_prompt, y_sample, new_cache_k, new_cache_v, new_state_ret_fwd, new_state_ret_bwd)
```

```python
import math
from contextlib import ExitStack
import numpy as np
import concourse.bass as bass
import concourse.mybir as mybir
from concourse.bass_utils import run_bass_kernel_spmd

F32 = mybir.dt.float32
BF16 = mybir.dt.bfloat16
AF = mybir.ActivationFunctionType
ALU = mybir.AluOpType
AX = mybir.AxisListType

D = 1024
T = 1024
DFF = 2816
EPS = 1e-6
NSLOT = 4


class Res:
    __slots__ = ("w", "r", "sem", "persist", "excl")

    def __init__(self, persist=False, excl=False):
        self.excl = excl
        self.w = None
        self.r = {}
        self.sem = None
        self.persist = persist


class TT:
    def __init__(self, t, res=None):
        self.t = t
        self.res = res if res is not None else Res()


class Sch:
    def __init__(self, nc, es):
        self.nc = nc
        self.es = es
        self.E = {"pe": nc.tensor, "act": nc.scalar, "dve": nc.vector, "pool": nc.gpsimd, "sp": nc.sync}
        self.sem = {}
        self.cnt = {}
        self.seen = {k: {} for k in self.E}
        for k in ("pe", "act", "dve", "pool"):
            self.sem[k] = es.enter_context(nc.semaphore("s_" + k))
            self.cnt[k] = 0
        self.nsem = 0
        self.dsems = []
        self.free = []
        self.live = []

    def newsem(self):
        if self.free:
            return self.free.pop()
        name = "d%d" % self.nsem
        self.nsem += 1
        self.sem[name] = self.es.enter_context(self.nc.semaphore(name))
        self.cnt[name] = 0
        self.dsems.append(name)
        return name

    def recycle(self):
        keep = []
        for r in self.live:
            if r.persist:
                keep.append(r)
            else:
                self.free.append(r.sem)
                r.sem = None
        self.live = keep

    def _wait(self, eng, raw, other):
        best = {}
        for t in raw:
            if t is None:
                continue
            k, v = t
            if k == eng and eng in ("pe", "sp"):
                continue
            if v > best.get(k, 0):
                best[k] = v
        for t in other:
            if t is None:
                continue
            k, v = t
            if k == eng and eng in ("pe", "sp"):
                continue
            if v > best.get(k, 0):
                best[k] = v
        sn = self.seen[eng]
        for k, v in best.items():
            if sn.get(k, 0) >= v:
                continue
            self.E[eng].wait_ge(self.sem[k], v)
            sn[k] = v

    def _deps(self, reads, writes):
        raw = [r.w for r in reads]
        other = []
        for w in writes:
            other.append(w.w)
            other.extend(w.r.items())
        return raw, other

    def _commit(self, tok, reads, writes):
        k, v = tok
        for r in reads:
            if r.r.get(k, 0) < v:
                r.r[k] = v
        for w in writes:
            w.w = tok
            w.r = {}

    def op(self, eng, fn, reads=(), writes=()):
        reads = [getattr(x, 'res', x) for x in reads] + [x.extra for x in reads if hasattr(x, 'extra')]
        writes = [getattr(x, 'res', x) for x in writes]
        writes = writes + [r for r in reads if r.excl and r not in writes]
        raw, other = self._deps(reads, writes)
        self._wait(eng, raw, other)
        ins = fn()
        ins.then_inc(self.sem[eng], 1)
        self.cnt[eng] += 1
        tok = (eng, self.cnt[eng])
        self._commit(tok, reads, writes)
        return tok

    def dma(self, q, out, in_, reads=(), writes=(), key=None):
        reads = [getattr(x, 'res', x) for x in reads]
        writes = [getattr(x, 'res', x) for x in writes]
        kres = key if key is not None else (writes[0] if writes else reads[0])
        kres = getattr(kres, 'res', kres)
        if kres.sem is None:
            kres.sem = self.newsem()
            self.live.append(kres)
        raw, other = self._deps(reads, writes)
        other = list(other) + [(kres.sem, self.cnt[kres.sem])]
        self._wait(q, raw, other)
        ins = self.E[q].dma_start(out=out, in_=in_)
        ins.then_inc(self.sem[kres.sem], 16)
        self.cnt[kres.sem] += 16
        tok = (kres.sem, self.cnt[kres.sem])
        self._commit(tok, reads, writes)
        return tok

    def barrier(self):
        engs = ("pe", "act", "dve", "pool")
        for e in engs + ("sp",):
            for f in engs:
                if e == f and e == "pe":
                    continue
                v = self.cnt[f]
                if v > 0 and self.seen[e].get(f, 0) < v:
                    self.E[e].wait_ge(self.sem[f], v)
                    self.seen[e][f] = v
            for r in self.live:
                if r.persist:
                    continue
                v = self.cnt[r.sem]
                if v > 0 and self.seen[e].get(r.sem, 0) < v:
                    self.E[e].wait_ge(self.sem[r.sem], v)
                    self.seen[e][r.sem] = v
        self.recycle()

    def drain(self, q="sp"):
        for k in self.dsems:
            v = self.cnt[k]
            if v > 0 and self.seen[q].get(k, 0) < v:
                self.E[q].wait_ge(self.sem[k], v)
                self.seen[q][k] = v
        for f in ("pe", "act", "dve", "pool"):
            v = self.cnt[f]
            if v > 0 and self.seen[q].get(f, 0) < v:
                self.E[q].wait_ge(self.sem[f], v)
                self.seen[q][f] = v


def host_consts():
    ROPE_PAIRS = 16
    t = np.arange(1024)
    inv = (10000.0 ** (-np.arange(ROPE_PAIRS, dtype=np.float32) / ROPE_PAIRS)).astype(np.float32)
    ar = (t // 64).astype(np.float32)[:, None] * inv
    ac = (t % 64).astype(np.float32)[:, None] * inv
    cr, sr, cc, sc_ = np.cos(ar), np.sin(ar), np.cos(ac), np.sin(ac)
    cos64 = np.concatenate([cr, cr, cc, cc], axis=1).astype(np.float32)
    sin64 = np.concatenate([-sr, sr, -sc_, sc_], axis=1).astype(np.float32)
    k = np.arange(128)[:, None]
    q = np.arange(128)[None, :]
    dm = np.zeros((128, 4, 128), np.float32)
    dm[:, 0] = np.maximum(q - k, 0)
    dm[:, 1] = np.maximum(k - q, 0)
    dm[:, 2] = (q >= k)
    dm[:, 3] = (k >= q)
    pr = np.zeros((128, 2, 128), np.float32)
    pr[:, 0, :] = np.arange(128) + 1.0
    pr[:, 1, :] = 128.0 - np.arange(128)
    kp = np.zeros((128, 2), np.float32)
    kp[:, 0] = 127.0 - np.arange(128)
    kp[:, 1] = np.arange(128)
    return dict(cos64=cos64, sin64=sin64, dmat=dm, posrow=pr, kpos=kp)


IN_SPECS = [
    ("xp", (1024, 1024)), ("xs", (1024, 1024)), ("cvec", (2, 1024)),
    ("ck", (2, 256, 512)), ("cv", (2, 256, 512)), ("srf", (2, 4, 64, 64)), ("srb", (2, 4, 64, 64)),
    ("norm1", (2, 1024)), ("w_mod", (2, 1024, 6144)), ("b_mod", (2, 6144)), ("w_in", (2, 1024, 3072)),
    ("sgu_norm", (2, 256)), ("sgu_w", (2, 4, 128, 128)), ("sgu_b", (2, 4, 128)),
    ("rlf", (2, 4)), ("rlb", (2, 4)), ("ret_norm", (2, 256)), ("q_norm", (2, 64)), ("k_norm", (2, 64)),
    ("diff_lam", (2, 256)), ("diff_norm", (2, 128)), ("w_out", (2, 1024, 1024)), ("norm2", (2, 1024)),
    ("ffn_up", (2, 1024, 5632)), ("ffn_conv", (2, 3, 5632)), ("ffn_conv_b", (2, 5632)),
    ("ffn_down", (2, 2816, 1024)),
    ("cos64", (1024, 64)), ("sin64", (1024, 64)), ("dmat", (128, 4, 128)), ("posrow", (128, 2, 128)),
    ("kpos", (128, 2)),
]
OUT_SPECS = [
    ("yp", (1024, 1024)), ("ys", (1024, 1024)), ("nk", (4, 2, 256, 512)), ("nv", (4, 2, 256, 512)),
    ("nrf", (4, 2, 4, 64, 64)), ("nrb", (4, 2, 4, 64, 64)),
]


def build(cfg=None):
    cfg = cfg or {}
    NL = cfg.get("n_layers", 2)
    HALVES = cfg.get("halves", (0, 1))
    taps = cfg.get("taps", None)
    nc = bass.Bass("TRN2", target_bir_lowering=False)
    try:
        nc.allow_low_precision("bf16 matmul operands with fp32 accumulation")
    except Exception:
        pass
    din = {n: nc.dram_tensor(n, list(s), F32, kind="ExternalInput").ap() for n, s in IN_SPECS}
    dout = {n: nc.dram_tensor(n, list(s), F32, kind="ExternalOutput").ap() for n, s in OUT_SPECS}
    es = ExitStack()
    STOP = cfg.get("stop", None)

    class _Stop(Exception):
        pass
    try:
      with es:
        S = Sch(nc, es)

        def ckpt(k):
            if STOP == k:
                S.drain("sp")
                raise _Stop()

        def sb(name, shape, dt=F32):
            return TT(es.enter_context(nc.sbuf_tensor(name, list(shape), dt)), Res(persist=True))

        def tap(name, ap, shape, reads):
            if taps is None or name not in taps:
                return
            d = nc.dram_tensor("tap_" + name, list(shape), ap.dtype, kind="ExternalOutput").ap()
            taps[name] = (list(shape), ap.dtype)
            S.dma("sp", d, ap, reads=reads)

        PSt = [es.enter_context(nc.psum_tensor("ps%d" % i, [128, 1024], F32)) for i in range(4)]
        PB = []
        for i in range(8):
            PB.append(TT(PSt[i // 2], Res(excl=True)))

        def pbank(i):
            return PSt[i // 2][:, (i % 2) * 512:(i % 2) * 512 + 512]

        rot = {"n": 0}

        def nextbank(pool=(0, 1, 2, 3), key="n"):
            i = pool[rot.setdefault(key, 0) % len(pool)]
            rot[key] += 1
            return i

        xT = sb("xT", [128, 8, T])
        hT = sb("hT", [128, 8, T], BF16)
        yT = sb("yT", [128, 8, T], BF16)
        ring = [sb("ring%d" % i, [128, 4096], BF16) for i in range(NSLOT)]
        for r_ in ring:
            r_.extra = Res(persist=True)
            r_.extra.sem = S.newsem()
            r_.res.sem = S.newsem()
        ARENA_BYTES = 74 * 1024
        arena = sb("arena", [128, ARENA_BYTES // 4], F32)

        identF = sb("identF", [128, 128])
        identB = sb("identB", [128, 128], BF16)
        onesB = sb("onesB", [128, 128], BF16)
        sTb = sb("sTb", [128, 8, 2], BF16)
        MOD = [sb("MOD%d" % l, [128, 48, 2]) for l in range(2)]
        GS = [[sb("GS%d%d" % (l, n), [128, 8, 2]) for n in range(2)] for l in range(2)]
        NT = sb("NT", [128, 32])
        CW = sb("CW", [128, 2, 4, 44])
        WST = sb("WST", [128, 8, 128], BF16)
        BS = sb("BS", [128, 8])
        SGN = sb("SGN", [128, 2, 256])
        RN = sb("RN", [128, 2, 256])
        QN = sb("QN", [128, 2, 64])
        KN = sb("KN", [128, 2, 64])
        DN = sb("DN", [128, 2, 128])
        LG = sb("LG", [128, 16])
        KP = sb("KP", [128, 2])
        C128 = sb("C128", [128, 64])
        DTm = sb("DTm", [128, 2, 4, 128])
        QD = sb("QD", [128, 2, 2, 2, 128])
        KDE = sb("KDE", [128, 2, 2, 256])
        CD = sb("CD", [128, 2, 2, 2, 64])
        NLAM = sb("NLAM", [128, 2])
        COS = sb("COS", [128, 8, 64])
        SIN = sb("SIN", [128, 8, 64])
        small = [sb("small%d" % i, [128, 16]) for i in range(8)]
        srot = {"i": 0}

        def nsmall():
            s = small[srot["i"] % len(small)]
            srot["i"] += 1
            return s

        wq = []

        def slotv(slot, a, b):
            return slot.t[:, 0:a * b].rearrange("p (a b) -> p a b", b=b)

        def wsrc(name, l, c0, c1):
            return din[name][l, :, c0:c1].rearrange("(kc p) n -> p kc n", p=128)

        def plan_weights():
            def modt(l, which):
                for t2 in range(2):
                    c0 = which * 1024 + t2 * 512
                    wq.append([(lambda s: slotv(s, 8, 512), wsrc("w_mod", l, c0, c0 + 512))])
            first = True
            for half in HALVES:
                for l in range(NL):
                    if first:
                        modt(l, 0)
                        modt(l, 1)
                    for g in range(6):
                        wq.append([(lambda s: slotv(s, 8, 512), wsrc("w_in", l, g * 512, (g + 1) * 512))])
                    if first:
                        modt(l, 2)
                    for g in range(2):
                        wq.append([(lambda s: slotv(s, 8, 512), wsrc("w_out", l, g * 512, (g + 1) * 512))])
                    if first:
                        modt(l, 3)
                        modt(l, 4)
                    for t in range(11):
                        wq.append([
                            (lambda s: slotv(s, 8, 512)[:, :, 0:256], wsrc("ffn_up", l, t * 256, (t + 1) * 256)),
                            (lambda s: slotv(s, 8, 512)[:, :, 256:512],
                             wsrc("ffn_up", l, DFF + t * 256, DFF + (t + 1) * 256)),
                        ])
                    if first:
                        modt(l, 5)
                    for m in range(8):
                        wq.append([(lambda s: slotv(s, 22, 128), wsrc("ffn_down", l, m * 128, (m + 1) * 128))])
                first = False

        plan_weights()
        wstate = {"next_load": 0, "next_use": 0}

        def w_issue():
            i = wstate["next_load"]
            if i >= len(wq):
                return
            slot = ring[i % NSLOT]
            for n_, (dstf, src) in enumerate(wq[i]):
                S.dma("pool", dstf(slot), src, writes=[slot.res if n_ == 0 else slot.extra])
            wstate["next_load"] = i + 1

        def w_get():
            i = wstate["next_use"]
            wstate["next_use"] = i + 1
            assert i < wstate["next_load"]
            return ring[i % NSLOT]

        def w_done():
            w_issue()

        class Arena:
            def __init__(self):
                self.off = 0

            def reset(self):
                self.off = 0

            def get(self, shape, dt):
                n = 1
                for s_ in shape[1:]:
                    n *= s_
                nbytes = n * (4 if dt == F32 else 2)
                nbytes = (nbytes + 63) // 64 * 64
                w0 = self.off // 4
                w1 = (self.off + nbytes) // 4
                assert self.off + nbytes <= ARENA_BYTES, ("arena overflow", self.off, nbytes)
                self.off += nbytes
                ap = arena.t[0:shape[0], w0:w1]
                if dt == BF16:
                    ap = ap.bitcast(BF16)
                nfree = n
                ap = ap[:, 0:nfree]
                if len(shape) > 2:
                    names = " ".join("d%d" % i for i in range(len(shape) - 1))
                    kw = {"d%d" % i: shape[i + 1] for i in range(len(shape) - 1)}
                    ap = ap.rearrange("p (%s) -> p %s" % (names, names), **kw)
                return ap

        AR = Arena()

        class AV:
            def __init__(self, shape, dt=F32):
                self.ap = AR.get(shape, dt)
                self.res = Res()

            @property
            def t(self):
                return self.ap

        S.op("pool", lambda: nc.gpsimd.memset(identF.t[:], 0.0), writes=[identF])
        S.op("pool", lambda: nc.gpsimd.affine_select(out=identF.t[:], in_=identF.t[:], pattern=[[-1, 128]],
                                                      compare_op=ALU.not_equal, fill=1.0, base=0,
                                                      channel_multiplier=1), reads=[identF], writes=[identF])
        S.op("pool", lambda: nc.gpsimd.memset(onesB.t[:], 1.0), writes=[onesB])
        S.op("pool", lambda: nc.gpsimd.memset(C128.t[:], 128.0), writes=[C128])
        S.op("dve", lambda: nc.vector.tensor_copy(identB.t[:], identF.t[:]), reads=[identF], writes=[identB])

        ckpt(1)
        for _ in range(NSLOT):
            w_issue()

        cres = Res()
        crow = AV([2, 1024])
        bmrow = AV([96, 128])
        nrow = AV([32, 128])
        cvrow = AV([44, 2, 4, 128])
        wsraw = AV([128, 8, 128])
        bsrow = AV([8, 128])
        BMT = sb("BMT", [128, 96])
        DL = AV([128, 2, 256])
        DM = AV([128, 4, 128])
        PR = AV([128, 2, 128])

        def cload(dst_tt, dst_ap, src_ap):
            S.dma("sp", dst_ap, src_ap, writes=[dst_tt])

        cload(crow, crow.t[:], din["cvec"])
        cload(bmrow, bmrow.t[:], din["b_mod"].rearrange("l (j p) -> (l j) p", p=128))
        cload(nrow, nrow.t[0:16, :], din["norm1"].rearrange("l (kc p) -> (l kc) p", p=128))
        cload(nrow, nrow.t[16:32, :], din["norm2"].rearrange("l (kc p) -> (l kc) p", p=128))
        for l in range(2):
            cload(cvrow, cvrow.t[:, l, 0:3, :], din["ffn_conv"][l].rearrange("j (c p) -> c j p", p=128))
            cload(cvrow, cvrow.t[:, l, 3, :], din["ffn_conv_b"][l].rearrange("(c p) -> c p", p=128))
        cload(wsraw, wsraw.t[:], din["sgu_w"].rearrange("l g p q -> p (l g) q"))
        cload(bsrow, bsrow.t[:], din["sgu_b"].rearrange("l g p -> (l g) p"))
        cload(SGN, SGN.t[:], din["sgu_norm"].partition_broadcast(128))
        cload(RN, RN.t[:], din["ret_norm"].partition_broadcast(128))
        cload(QN, QN.t[:], din["q_norm"].partition_broadcast(128))
        cload(KN, KN.t[:], din["k_norm"].partition_broadcast(128))
        cload(DN, DN.t[:], din["diff_norm"].partition_broadcast(128))
        cload(DL, DL.t[:], din["diff_lam"].partition_broadcast(128))
        cload(LG, LG.t[:, 0:8], din["rlf"].rearrange("l h -> (l h)").partition_broadcast(128))
        cload(LG, LG.t[:, 8:16], din["rlb"].rearrange("l h -> (l h)").partition_broadcast(128))
        cload(DM, DM.t[:], din["dmat"])
        cload(PR, PR.t[:], din["posrow"])
        cload(KP, KP.t[:], din["kpos"])
        cload(COS, COS.t[:], din["cos64"].rearrange("(t p) c -> p t c", p=128))
        cload(SIN, SIN.t[:], din["sin64"].rearrange("(t p) c -> p t c", p=128))

        ckpt(2)
        csil = AV([2, 1024])
        S.op("act", lambda: nc.scalar.activation(out=csil.t[:], in_=crow.t[:], func=AF.Silu),
             reads=[crow], writes=[csil])
        b = nextbank()

        def f():
            ins = None
            for kc in range(8):
                ins = nc.tensor.transpose(pbank(b)[:, kc * 2:kc * 2 + 2], csil.t[0:2, kc * 128:(kc + 1) * 128],
                                          identF.t[0:2, 0:2])
            return ins
        S.op("pe", f, reads=[csil, identF], writes=[PB[b]])
        S.op("dve", lambda: nc.vector.tensor_copy(sTb.t[:].rearrange("p a b -> p (a b)"), pbank(b)[:, 0:16]),
             reads=[PB[b]], writes=[sTb])

        ckpt(3)
        b = nextbank()
        S.op("pe", lambda: nc.tensor.transpose(pbank(b)[:, 0:32], nrow.t[0:32, :], identF.t[0:32, 0:32]),
             reads=[nrow, identF], writes=[PB[b]])
        S.op("dve", lambda: nc.vector.tensor_copy(NT.t[:], pbank(b)[:, 0:32]), reads=[PB[b]], writes=[NT])
        b = nextbank()
        S.op("pe", lambda: nc.tensor.transpose(pbank(b)[:, 0:96], bmrow.t[0:96, :], identF.t[0:96, 0:96]),
             reads=[bmrow, identF], writes=[PB[b]])
        S.op("dve", lambda: nc.vector.tensor_copy(BMT.t[:], pbank(b)[:, 0:96]), reads=[PB[b]], writes=[BMT])
        ckpt(4)
        b = nextbank()

        def f():
            ins = None
            for l in range(2):
                for j in range(4):
                    c0 = (l * 4 + j) * 44
                    ins = nc.tensor.transpose(pbank(b)[:, c0:c0 + 44], cvrow.t[0:44, l, j, :], identF.t[0:44, 0:44])
            return ins
        S.op("pe", f, reads=[cvrow, identF], writes=[PB[b]])
        S.op("dve", lambda: nc.vector.tensor_copy(CW.t[:].rearrange("p a b c -> p (a b c)"), pbank(b)[:, 0:352]),
             reads=[PB[b]], writes=[CW])
        ckpt(5)
        for hb in range(2):
            b = nextbank()

            def f():
                ins = None
                for i in range(4):
                    ins = nc.tensor.transpose(pbank(b)[:, i * 128:(i + 1) * 128], wsraw.t[:, hb * 4 + i, :], identF.t[:])
                return ins
            S.op("pe", f, reads=[wsraw, identF], writes=[PB[b]])
            S.op("dve", lambda: nc.vector.tensor_copy(
                WST.t[:, hb * 4:(hb + 1) * 4, :].rearrange("p a b -> p (a b)"), pbank(b)[:, 0:512]),
                reads=[PB[b]], writes=[WST])
        b = nextbank()
        S.op("pe", lambda: nc.tensor.transpose(pbank(b)[:, 0:8], bsrow.t[0:8, :], identF.t[0:8, 0:8]),
             reads=[bsrow, identF], writes=[PB[b]])
        S.op("dve", lambda: nc.vector.tensor_copy(BS.t[:], pbank(b)[:, 0:8]), reads=[PB[b]], writes=[BS])

        ckpt(6)
        modrow = sb("modrow", [2, 1024])

        def mods_group(l, which):
            for t2 in range(2):
                slot = w_get()
                b = nextbank()
                wv = slotv(slot, 8, 512)

                def f():
                    ins = None
                    for kc in range(8):
                        ins = nc.tensor.matmul(pbank(b)[0:2, :], sTb.t[:, kc, :], wv[:, kc, :],
                                               start=(kc == 0), stop=(kc == 7))
                    return ins
                S.op("pe", f, reads=[sTb, slot], writes=[PB[b]])
                w_done()
                S.op("dve", lambda: nc.vector.tensor_copy(modrow.t[:, t2 * 512:(t2 + 1) * 512], pbank(b)[0:2, :]),
                     reads=[PB[b]], writes=[modrow])
            b = nextbank()

            def f():
                ins = None
                for jb in range(8):
                    ins = nc.tensor.transpose(pbank(b)[:, jb * 2:jb * 2 + 2], modrow.t[0:2, jb * 128:(jb + 1) * 128],
                                              identF.t[0:2, 0:2])
                return ins
            S.op("pe", f, reads=[modrow, identF], writes=[PB[b]])
            c0 = l * 48 + which * 8
            S.op("dve", lambda: nc.vector.tensor_tensor(
                out=MOD[l].t[:, which * 8:(which + 1) * 8, :], in0=pbank(b)[:, 0:16].rearrange("p (a b) -> p a b", b=2),
                in1=BMT.t[:, c0:c0 + 8].unsqueeze(2).broadcast_to([128, 8, 2]), op=ALU.add),
                 reads=[PB[b], BMT], writes=[MOD[l]])
            if which in (1, 4):
                n = 0 if which == 1 else 1
                ntv = NT.t[:, (n * 2 + l) * 8:(n * 2 + l) * 8 + 8]
                S.op("dve", lambda: nc.vector.scalar_tensor_tensor(
                    out=GS[l][n].t[:], in0=MOD[l].t[:, which * 8:(which + 1) * 8, :], scalar=1.0,
                    in1=ntv.unsqueeze(2).broadcast_to([128, 8, 2]), op0=ALU.add, op1=ALU.mult),
                    reads=[MOD[l], NT], writes=[GS[l][n]])

        ckpt(7)
        S.op("act", lambda: nc.scalar.activation(out=LG.t[:], in_=LG.t[:], func=AF.Exp, scale=-1.0),
             reads=[LG], writes=[LG])
        S.op("act", lambda: nc.scalar.activation(out=LG.t[:], in_=LG.t[:], func=AF.Ln, bias=1.0, scale=1.0),
             reads=[LG], writes=[LG])
        S.op("dve", lambda: nc.vector.tensor_scalar(LG.t[:], LG.t[:], -1.0, None, ALU.mult), reads=[LG], writes=[LG])

        ckpt(8)

        def lgi(d, l, h):
            return d * 8 + l * 4 + h

        dtmp = [AV([128, 128]) for i in range(4)]
        for l in range(NL):
            for h in range(4):
                tf, tb = dtmp[(h % 2) * 2], dtmp[(h % 2) * 2 + 1]
                i_f, i_b = lgi(0, l, h), lgi(1, l, h)
                S.op("act", lambda: nc.scalar.activation(out=tf.t[:], in_=DM.t[:, 0, :], func=AF.Exp,
                                                         scale=LG.t[:, i_f:i_f + 1]), reads=[DM, LG], writes=[tf])
                S.op("act", lambda: nc.scalar.activation(out=tb.t[:], in_=DM.t[:, 1, :], func=AF.Exp,
                                                         scale=LG.t[:, i_b:i_b + 1]), reads=[DM, LG], writes=[tb])
                S.op("dve", lambda: nc.vector.scalar_tensor_tensor(out=tf.t[:], in0=tf.t[:], scalar=0.125,
                                                                   in1=DM.t[:, 2, :], op0=ALU.mult, op1=ALU.mult),
                     reads=[tf, DM], writes=[tf])
                S.op("dve", lambda: nc.vector.scalar_tensor_tensor(out=tb.t[:], in0=tb.t[:], scalar=0.125,
                                                                   in1=DM.t[:, 3, :], op0=ALU.mult, op1=ALU.mult),
                     reads=[tb, DM], writes=[tb])
                S.op("dve", lambda: nc.vector.tensor_tensor(out=DTm.t[:, l, h, :], in0=tf.t[:], in1=tb.t[:],
                                                            op=ALU.add), reads=[tf, tb], writes=[DTm])
            for hp in range(2):
                for d in range(2):
                    for j in range(2):
                        ii = lgi(d, l, 2 * hp + j)
                        ps_ = slice(j * 64, (j + 1) * 64)
                        S.op("act", lambda: nc.scalar.activation(out=QD.t[ps_, l, hp, d, :], in_=PR.t[ps_, d, :],
                                                                 func=AF.Exp, scale=LG.t[ps_, ii:ii + 1]),
                             reads=[PR, LG], writes=[QD])
                        S.op("act", lambda: nc.scalar.activation(out=CD.t[ps_, l, d, hp, :], in_=C128.t[ps_, :],
                                                                 func=AF.Exp, scale=LG.t[ps_, ii:ii + 1]),
                             reads=[C128, LG], writes=[CD])
            for d in range(2):
                for h in range(4):
                    ii = lgi(d, l, h)
                    S.op("act", lambda: nc.scalar.activation(
                        out=KDE.t[:, l, d, h * 64:(h + 1) * 64], in_=KP.t[:, d:d + 1].broadcast_to([128, 64]),
                        func=AF.Exp, scale=LG.t[:, ii:ii + 1], bias=math.log(0.125)),
                        reads=[KP, LG], writes=[KDE])
            lam_init = 0.8 - 0.6 * math.exp(-0.3 * l)
            pr_ = nsmall()
            dlv = DL.t[:, l, :].rearrange("p (a b c) -> p a b c", a=2, b=2)
            lt = dtmp[0]
            S.op("dve", lambda: nc.vector.tensor_tensor(out=lt.t[:].rearrange("p (a c) -> p a c", a=2),
                                                        in0=dlv[:, :, 0, :], in1=dlv[:, :, 1, :], op=ALU.mult),
                 reads=[DL], writes=[lt])
            S.op("dve", lambda: nc.vector.tensor_reduce(out=pr_.t[:, 0:2],
                                                        in_=lt.t[:].rearrange("p (a c) -> p a c", a=2),
                                                        axis=AX.X, op=ALU.add), reads=[lt], writes=[pr_])
            S.op("act", lambda: nc.scalar.activation(out=pr_.t[:, 2:4], in_=pr_.t[:, 0:2], func=AF.Exp),
                 reads=[pr_], writes=[pr_])
            S.op("dve", lambda: nc.vector.tensor_tensor(out=pr_.t[:, 4:5], in0=pr_.t[:, 3:4], in1=pr_.t[:, 2:3],
                                                        op=ALU.subtract), reads=[pr_], writes=[pr_])
            S.op("dve", lambda: nc.vector.tensor_scalar(NLAM.t[:, l:l + 1], pr_.t[:, 4:5], -lam_init, None, ALU.add),
                 reads=[pr_], writes=[NLAM])
            S.op("dve", lambda: nc.vector.tensor_scalar(DN.t[:, l, :], DN.t[:, l, :], 1.0 - lam_init, None, ALU.mult),
                 reads=[DN], writes=[DN])

        S.barrier()
        ckpt(10)

        def rstd_from_ss(ss_ap, out_ap, scale, bias, R, W):
            S.op("act", lambda: nc.scalar.activation(out=out_ap, in_=ss_ap, func=AF.Sqrt, bias=bias, scale=scale),
                 reads=R, writes=W)
            S.op("dve", lambda: nc.vector.reciprocal(out_ap, out_ap), reads=W, writes=W)

        def load_x(src):
            xin = [AV([128, 1024]) for _ in range(2)]
            for tt in range(8):
                xi = xin[tt % 2]
                S.dma("sp", xi.t[:], src[tt * 128:(tt + 1) * 128, :], writes=[xi])
                for hb in range(2):
                    b = nextbank()

                    def f():
                        ins = None
                        for i in range(4):
                            c = hb * 4 + i
                            ins = nc.tensor.transpose(pbank(b)[:, i * 128:(i + 1) * 128],
                                                      xi.t[:, c * 128:(c + 1) * 128], identF.t[:])
                        return ins
                    S.op("pe", f, reads=[xi, identF], writes=[PB[b]])
                    eng = "act" if hb == 0 else "dve"
                    dst = xT.t[:, hb * 4:(hb + 1) * 4, tt * 128:(tt + 1) * 128]
                    srcp = pbank(b).rearrange("p (a b) -> p a b", b=128)
                    if eng == "act":
                        S.op("act", lambda: nc.scalar.copy(out=dst, in_=srcp), reads=[PB[b]], writes=[xT])
                    else:
                        S.op("dve", lambda: nc.vector.tensor_copy(dst, srcp), reads=[PB[b]], writes=[xT])

        def store_x(dst):
            xo = [AV([128, 1024]) for _ in range(2)]
            for tt in range(8):
                xi = xo[tt % 2]
                for hb in range(2):
                    b = nextbank()

                    def f():
                        ins = None
                        for i in range(4):
                            c = hb * 4 + i
                            ins = nc.tensor.transpose(pbank(b)[:, i * 128:(i + 1) * 128],
                                                      xT.t[:, c, tt * 128:(tt + 1) * 128], identF.t[:])
                        return ins
                    S.op("pe", f, reads=[xT, identF], writes=[PB[b]])
                    dstp = xi.t[:, hb * 512:(hb + 1) * 512]
                    if hb == 0:
                        S.op("act", lambda: nc.scalar.copy(out=dstp, in_=pbank(b)), reads=[PB[b]], writes=[xi])
                    else:
                        S.op("dve", lambda: nc.vector.tensor_copy(dstp, pbank(b)), reads=[PB[b]], writes=[xi])
                S.dma("sp", dst[tt * 128:(tt + 1) * 128, :], xi.t[:], reads=[xi])

        def norm_mod(l, n, cond):
            sq = yT
            RB = AV([128, T])
            S.op("act", lambda: nc.scalar.activation(out=sq.t[:], in_=xT.t[:], func=AF.Square),
                 reads=[xT], writes=[sq])
            for tg in range(2):
                b = nextbank()

                def f():
                    ins = None
                    for kc in range(8):
                        ins = nc.tensor.matmul(pbank(b), onesB.t[:], sq.t[:, kc, tg * 512:(tg + 1) * 512],
                                               start=(kc == 0), stop=(kc == 7))
                    return ins
                S.op("pe", f, reads=[sq, onesB], writes=[PB[b]])
                rstd_from_ss(pbank(b), RB.t[:, tg * 512:(tg + 1) * 512], 1.0 / D, EPS, [PB[b]], [RB])
            shi = 0 if n == 0 else 3
            tmp = [AV([128, 1024]) for _ in range(2)]
            for kc in range(8):
                tm = tmp[kc % 2]
                S.op("dve", lambda: nc.vector.scalar_tensor_tensor(
                    out=tm.t[:], in0=xT.t[:, kc, :], scalar=GS[l][n].t[:, kc, cond:cond + 1], in1=RB.t[:],
                    op0=ALU.mult, op1=ALU.mult), reads=[xT, GS[l][n], RB], writes=[tm])
                S.op("act", lambda: nc.scalar.activation(out=hT.t[:, kc, :], in_=tm.t[:], func=AF.Identity,
                                                         bias=MOD[l].t[:, shi * 8 + kc, cond:cond + 1], scale=1.0),
                     reads=[tm, MOD[l]], writes=[hT])

        def zmm(slot, tt, b):
            wv = slotv(slot, 8, 512)

            def f():
                ins = None
                for kc in range(8):
                    ins = nc.tensor.matmul(pbank(b), hT.t[:, kc, tt * 128:(tt + 1) * 128], wv[:, kc, :],
                                           start=(kc == 0), stop=(kc == 7))
                return ins
            S.op("pe", f, reads=[hT, slot], writes=[PB[b]])

        def group_rstd(src_ap, ngrp, gsz, scale, bias, sqt, ss):
            src_tt, sq_tt = sqt
            S.op("dve", lambda: nc.vector.tensor_tensor(out=sq_tt.t[:, 0:ngrp * gsz], in0=src_ap, in1=src_ap,
                                                        op=ALU.mult), reads=[src_tt], writes=[sq_tt])
            S.op("dve", lambda: nc.vector.tensor_reduce(
                out=ss.t[:, 0:ngrp], in_=sq_tt.t[:, 0:ngrp * gsz].rearrange("p (a b) -> p a b", b=gsz),
                axis=AX.X, op=ALU.add), reads=[sq_tt], writes=[ss])
            rstd_from_ss(ss.t[:, 0:ngrp], ss.t[:, 0:ngrp], scale, bias, [ss], [ss])

        def transposes_to(src_tt, src_ap_fn, nblk, dst_tt, dst_ap, pool=(0, 1, 2, 3)):
            b = nextbank(pool)
            pv = pbank(b).bitcast(BF16)

            def f():
                ins = None
                for i in range(nblk):
                    ins = nc.tensor.transpose(pv[:, i * 128:(i + 1) * 128], src_ap_fn(i), identB.t[:])
                return ins
            S.op("pe", f, reads=[src_tt, identB], writes=[PB[b]])
            S.op("dve", lambda: nc.vector.tensor_copy(dst_ap, pv[:, 0:nblk * 128].rearrange("p (a b) -> p a b", b=128)),
                 reads=[PB[b]], writes=[dst_tt])

        def mixer(l, half):
            cond = half
            nseq, L = (4, 256) if half == 0 else (1, 1024)
            cpl = L // 128
            AR.reset()
            ytok = AV([128, 8, 1024], BF16)
            ar_mark = AR.off

            class _V:
                pass
            SCR = _V()
            SCR.ap = yT.t[:].bitcast(F32)
            SCR.t = SCR.ap
            SCR.res = yT.res
            slot = w_get()
            GE = AV([128, 8, 512])
            VN = AV([128, 8, 256], BF16)
            ssG = AV([128, 8])
            for tt in range(8):
                b = nextbank()
                zmm(slot, tt, b)
                S.op("act", lambda: nc.scalar.activation(out=GE.t[:, tt, :], in_=pbank(b), func=AF.Gelu_apprx_tanh),
                     reads=[PB[b]], writes=[GE])
            gv = GE.t[:, :, 256:512]
            sv = SCR.t[:, :, 0:256]
            S.op("dve", lambda: nc.vector.tensor_tensor(out=sv, in0=gv, in1=gv, op=ALU.mult), reads=[GE], writes=[SCR])
            S.op("dve", lambda: nc.vector.tensor_reduce(out=ssG.t[:], in_=sv, axis=AX.X, op=ALU.add),
                 reads=[SCR], writes=[ssG])
            rstd_from_ss(ssG.t[:], ssG.t[:], 1.0 / 256, EPS, [ssG], [ssG])
            S.op("dve", lambda: nc.vector.tensor_tensor(out=gv, in0=gv,
                                                        in1=ssG.t[:].unsqueeze(2).broadcast_to([128, 8, 256]),
                                                        op=ALU.mult), reads=[GE, ssG], writes=[GE])
            S.op("dve", lambda: nc.vector.tensor_tensor(
                out=VN.t[:], in0=gv, in1=SGN.t[:, l, :].unsqueeze(1).broadcast_to([128, 8, 256]), op=ALU.mult),
                reads=[GE, SGN], writes=[VN])
            for tt in range(8):
                b2 = nextbank()

                def f():
                    ins = None
                    for g in range(4):
                        ins = nc.tensor.matmul(pbank(b2)[:, g * 64:(g + 1) * 64], WST.t[:, l * 4 + g, :],
                                               VN.t[:, tt, g * 64:(g + 1) * 64], start=True, stop=True)
                    return ins
                S.op("pe", f, reads=[WST, VN], writes=[PB[b2]])
                for g in range(4):
                    S.op("dve", lambda: nc.vector.scalar_tensor_tensor(
                        out=ytok.t[:, tt, g * 64:(g + 1) * 64], in0=pbank(b2)[:, g * 64:(g + 1) * 64],
                        scalar=BS.t[:, l * 4 + g:l * 4 + g + 1], in1=GE.t[:, tt, g * 64:(g + 1) * 64],
                        op0=ALU.add, op1=ALU.mult), reads=[PB[b2], BS, GE], writes=[ytok])
            w_done()
            S.barrier()
            ckpt(13)
            AR.off = ar_mark

            QT = AV([128, 2, 1024], BF16)
            KT = AV([128, 2, 1024], BF16)
            VB = AV([128, 8, 256], BF16)
            SG = AV([128, 8, 256], BF16)
            KVS = AV([128, 8, 2, 2, 64])
            RS = AV([128, 2, 2, 64])
            RSb = AV([128, 8, 2, 2, 64], BF16)

            class _W:
                pass
            RSd = []
            RSbd = []
            for d_ in range(2):
                v_ = _W()
                v_.ap = RS.t[:, d_, :, :]
                v_.t = v_.ap
                v_.res = Res()
                RSd.append(v_)
                w_ = _W()
                w_.res = Res()
                RSbd.append(w_)
            qkb = [AV([128, 512], BF16) for _ in range(2)]
            KF = [AV([128, 2, 256], BF16) for _ in range(2)]
            gtmp = [AV([128, 256]) for _ in range(2)]
            slot1 = w_get()
            slot2 = w_get()
            for tt in range(8):
                b1 = nextbank()
                zmm(slot1, tt, b1)
                if tt == 0:
                    ckpt(1301)
                b2 = nextbank()
                zmm(slot2, tt, b2)
                if tt == 0:
                    ckpt(1302)
                qk = qkb[tt % 2]
                S.op("act", lambda: nc.scalar.copy(out=qk.t[:], in_=pbank(b1)), reads=[PB[b1]], writes=[qk])
                if tt == 0:
                    ckpt(1303)
                kf = KF[tt % 2]
                for d in range(2):
                    S.op("dve", lambda: nc.vector.tensor_tensor(
                        out=kf.t[:, d, :], in0=pbank(b1)[:, 256:512], in1=KDE.t[:, l, d, :], op=ALU.mult),
                        reads=[PB[b1], KDE], writes=[kf])
                    if tt == 0 and d == 0:
                        ckpt(1304)
                if tt == 0:
                    ckpt(131)
                bt = nextbank((6, 7), "t67")
                pv = pbank(bt).bitcast(BF16)

                def f():
                    ins = None
                    for i in range(4):
                        ins = nc.tensor.transpose(pv[:, i * 128:(i + 1) * 128], qk.t[:, i * 128:(i + 1) * 128],
                                                  identB.t[:])
                    return ins
                S.op("pe", f, reads=[qk, identB], writes=[PB[bt]])
                if tt == 0:
                    ckpt(132)
                S.op("dve", lambda: nc.vector.tensor_copy(
                    QT.t[:, :, tt * 128:(tt + 1) * 128], pv[:, 0:256].rearrange("p (a b) -> p a b", b=128)),
                    reads=[PB[bt]], writes=[QT])
                S.op("dve", lambda: nc.vector.tensor_copy(
                    KT.t[:, :, tt * 128:(tt + 1) * 128], pv[:, 256:512].rearrange("p (a b) -> p a b", b=128)),
                    reads=[PB[bt]], writes=[KT])
                if tt == 0:
                    ckpt(133)
                S.op("act", lambda: nc.scalar.copy(out=VB.t[:, tt, :], in_=pbank(b2)[:, 0:256]),
                     reads=[PB[b2]], writes=[VB])
                gt_ = gtmp[tt % 2]
                S.op("act", lambda: nc.scalar.activation(out=gt_.t[:], in_=pbank(b2)[:, 256:512], func=AF.Silu),
                     reads=[PB[b2]], writes=[gt_])
                S.op("dve", lambda: nc.vector.tensor_tensor(out=SG.t[:, tt, :], in0=gt_.t[:], in1=RN.t[:, l, :],
                                                            op=ALU.mult), reads=[gt_, RN], writes=[SG])
                if tt == 0:
                    ckpt(134)
                bk = nextbank((4, 5), "t45")

                def f():
                    ins = None
                    for d in range(2):
                        for hp in range(2):
                            ins = nc.tensor.matmul(pbank(bk)[:, (d * 2 + hp) * 128:(d * 2 + hp + 1) * 128],
                                                   kf.t[:, d, hp * 128:(hp + 1) * 128],
                                                   VB.t[:, tt, hp * 128:(hp + 1) * 128], start=True, stop=True)
                    return ins
                S.op("pe", f, reads=[kf, VB], writes=[PB[bk]])
                if tt == 0:
                    ckpt(135)
                pk = pbank(bk).rearrange("p (a b) -> p a b", b=128)
                for j in range(2):
                    ps_ = slice(j * 64, (j + 1) * 64)
                    S.op("dve", lambda: nc.vector.tensor_copy(
                        KVS.t[ps_, tt, :, :, :].rearrange("p a b c -> p (a b) c"), pk[ps_, :, j * 64:(j + 1) * 64]),
                        reads=[PB[bk]], writes=[KVS])
            w_done()
            w_done()

            ckpt(14)
            st_stage = [AV([128, 2, 2, 64]) for _ in range(2)]
            for s_ in range(nseq):
                if half == 0:
                    S.op("dve", lambda: nc.vector.memset(RS.t[:], 0.0), writes=[RSd[0], RSd[1]])
                else:
                    for d, nm in ((0, "srf"), (1, "srb")):
                        for j in range(2):
                            srcs = din[nm][l].rearrange("(hp j) d e -> j d hp e", j=2)[j]
                            S.dma("sp", RS.t[j * 64:(j + 1) * 64, d, :, :], srcs, writes=[RSd[d]])
                def step(d, tt):
                    S.op("dve", lambda: nc.vector.tensor_copy(RSb.t[:, tt, d, :, :], RSd[d].t[:]),
                         reads=[RSd[d]], writes=[RSbd[d]])
                    S.op("dve", lambda: nc.vector.tensor_tensor(out=RSd[d].t[:], in0=RSd[d].t[:],
                                                                in1=CD.t[:, l, d, :, :], op=ALU.mult),
                         reads=[RSd[d], CD], writes=[RSd[d]])
                    S.op("dve", lambda: nc.vector.tensor_tensor(out=RSd[d].t[:], in0=RSd[d].t[:],
                                                                in1=KVS.t[:, tt, d, :, :], op=ALU.add),
                         reads=[RSd[d], KVS], writes=[RSd[d]])
                for c in range(cpl):
                    step(0, s_ * cpl + c)
                    step(1, s_ * cpl + (cpl - 1 - c))
                if half == 0:
                    stg = st_stage[s_ % 2]
                    S.op("dve", lambda: nc.vector.tensor_copy(stg.t[:], RS.t[:]), reads=[RSd[0], RSd[1]], writes=[stg])
                    for d, nm in ((0, "nrf"), (1, "nrb")):
                        for j in range(2):
                            dsts = dout[nm][s_, l].rearrange("(hp j) d e -> j d hp e", j=2)[j]
                            S.dma("sp", dsts, stg.t[j * 64:(j + 1) * 64, d, :, :], reads=[stg])

            ckpt(15)
            MT = [AV([128, 512], BF16) for _ in range(2)]
            QF = [AV([128, 2, 2, 128], BF16) for _ in range(2)]
            OR = AV([128, 8, 256])
            for tt in range(8):
                c = tt % cpl
                tsl = slice(tt * 128, (tt + 1) * 128)
                bia = nextbank((0, 1), "t01")
                bib = nextbank((0, 1), "t01")

                def f():
                    ins = None
                    for j, bnk in ((0, bia), (1, bib)):
                        ps_ = slice(j * 64, (j + 1) * 64)
                        for hp in range(2):
                            ins = nc.tensor.matmul(pbank(bnk)[:, hp * 128:(hp + 1) * 128], KT.t[ps_, hp, tsl],
                                                   QT.t[ps_, hp, tsl], start=True, stop=True)
                    return ins
                S.op("pe", f, reads=[KT, QT], writes=[PB[bia], PB[bib]])
                mt = MT[tt % 2]
                mt4 = mt.t[:].rearrange("p (hp j q) -> p hp j q", hp=2, j=2)
                for j, bnk in ((0, bia), (1, bib)):
                    S.op("dve", lambda: nc.vector.tensor_tensor(
                        out=mt4[:, :, j, :], in0=pbank(bnk)[:, 0:256].rearrange("p (a b) -> p a b", b=128),
                        in1=DTm.t[:, l, :, :].rearrange("p (hp j) q -> p hp j q", j=2)[:, :, j, :], op=ALU.mult),
                        reads=[PB[bnk], DTm], writes=[mt])
                qf = QF[tt % 2]
                S.op("dve", lambda: nc.vector.tensor_tensor(
                    out=qf.t[:], in0=QD.t[:, l, :, :, :],
                    in1=QT.t[:, :, tsl].unsqueeze(2).broadcast_to([128, 2, 2, 128]), op=ALU.mult),
                    reads=[QT, QD], writes=[qf])
                use_f = not (half == 0 and c == 0)
                use_b = not (half == 0 and c == cpl - 1)
                bo = nextbank((2, 3), "t23")

                def f():
                    ins = None
                    for h in range(4):
                        hp, j = h // 2, h % 2
                        ps_ = slice(j * 64, (j + 1) * 64)
                        o_ap = pbank(bo)[:, h * 64:(h + 1) * 64]
                        last = not (use_f or use_b)
                        ins = nc.tensor.matmul(o_ap, mt.t[:, h * 128:(h + 1) * 128], VB.t[:, tt, h * 64:(h + 1) * 64],
                                               start=True, stop=last)
                        if use_f:
                            ins = nc.tensor.matmul(o_ap, qf.t[ps_, hp, 0, :], RSb.t[ps_, tt, 0, hp, :],
                                                   start=False, stop=not use_b)
                        if use_b:
                            ins = nc.tensor.matmul(o_ap, qf.t[ps_, hp, 1, :], RSb.t[ps_, tt, 1, hp, :],
                                                   start=False, stop=True)
                    return ins
                S.op("pe", f, reads=[mt, VB, qf, RSbd[0], RSbd[1]], writes=[PB[bo]])
                S.op("act", lambda: nc.scalar.copy(out=OR.t[:, tt, :], in_=pbank(bo)[:, 0:256]),
                     reads=[PB[bo]], writes=[OR])
            sv = SCR.t[:, :, 0:256]
            S.op("act", lambda: nc.scalar.activation(out=sv, in_=OR.t[:], func=AF.Square), reads=[OR], writes=[SCR])
            ssR = AV([128, 8, 4])
            S.op("dve", lambda: nc.vector.tensor_reduce(out=ssR.t[:], in_=sv.rearrange("p t (h c) -> p t h c", c=64),
                                                        axis=AX.X, op=ALU.add), reads=[SCR], writes=[ssR])
            rstd_from_ss(ssR.t[:], ssR.t[:], 1.0 / 64, EPS, [ssR], [ssR])
            o4 = OR.t[:].rearrange("p t (h c) -> p t h c", c=64)
            S.op("dve", lambda: nc.vector.tensor_tensor(out=o4, in0=o4,
                                                        in1=ssR.t[:].unsqueeze(3).broadcast_to([128, 8, 4, 64]),
                                                        op=ALU.mult), reads=[OR, ssR], writes=[OR])
            S.op("dve", lambda: nc.vector.tensor_tensor(out=ytok.t[:, :, 256:512], in0=OR.t[:], in1=SG.t[:],
                                                        op=ALU.mult), reads=[OR, SG], writes=[ytok])
            S.barrier()
            ckpt(16)
            AR.off = ar_mark

            NKC = 2 if half == 0 else 10
            KOFF = 0 if half == 0 else 256
            NKEY = 1024 + KOFF
            QTa = AV([128, 4, 1024], BF16)
            KTa = AV([128, 4, NKEY], BF16)
            VA = AV([128, NKEY // 128, 4, 132], BF16)
            S.op("dve", lambda: nc.vector.memset(VA.t[:, :, :, 128:129], 1.0), writes=[VA])
            ar_att = AR.off
            stg = [AV([128, 512]) for _ in range(2)]

            class _G:
                pass
            GR = []
            for g_ in range(2):
                o_ = _G()
                o_.QA = AV([128, 4, 512])
                o_.QB = AV([128, 4, 512], BF16)
                o_.ss = AV([128, 32])
                o_.SQ = _G()
                o_.SQ.ap = SCR.t[:, g_ * 4:(g_ + 1) * 4, :]
                o_.SQ.t = o_.SQ.ap
                o_.SQ.res = Res()
                GR.append(o_)
            S.op("dve", lambda: nc.vector.memset(GR[0].ss.t[:], 0.0), reads=[yT], writes=[GR[0].ss, GR[0].SQ, GR[1].SQ])
            if half == 1:
                for kc in range(2):
                    st = stg[kc % 2]
                    S.dma("sp", st.t[:], din["ck"][l, kc * 128:(kc + 1) * 128, :], writes=[st])
                    S.op("act", lambda: nc.scalar.copy(out=GR[0].QB.t[:, kc, :], in_=st.t[:]), reads=[st],
                         writes=[GR[0].QB])
                    transposes_to(GR[0].QB, lambda i: GR[0].QB.t[:, kc, i * 128:(i + 1) * 128], 4, KTa,
                                  KTa.t[:, :, kc * 128:(kc + 1) * 128])
                for kc in range(2):
                    st = stg[kc % 2]
                    S.dma("sp", st.t[:], din["cv"][l, kc * 128:(kc + 1) * 128, :], writes=[st])
                    S.op("act", lambda: nc.scalar.copy(out=VA.t[:, kc, :, 0:128],
                                                       in_=st.t[:].rearrange("p (a b) -> p a b", b=128)),
                         reads=[st], writes=[VA])

            def stageA(slot, g):
                G = GR[g]
                for t in range(4):
                    tt = g * 4 + t
                    b = nextbank()
                    zmm(slot, tt, b)
                    S.op("act", lambda: nc.scalar.copy(out=G.QA.t[:, t, :], in_=pbank(b)), reads=[PB[b]], writes=[G.QA])
                    S.op("act", lambda: nc.scalar.activation(out=G.SQ.t[:, t, :], in_=pbank(b), func=AF.Square),
                         reads=[PB[b]], writes=[G.SQ])
                    S.op("dve", lambda: nc.vector.tensor_reduce(
                        out=G.ss.t[:, t * 8:(t + 1) * 8], in_=G.SQ.t[:, t, :].rearrange("p (a b) -> p a b", b=64),
                        axis=AX.X, op=ALU.add), reads=[G.SQ], writes=[G.ss])

            def stageB(g, gain_tt, which):
                G = GR[g]
                if which == "q":
                    rstd_from_ss(G.ss.t[:], G.ss.t[:], 1.0, 64 * EPS, [G.ss], [G.ss])
                else:
                    rstd_from_ss(G.ss.t[:], G.ss.t[:], 1.0 / 64, EPS, [G.ss], [G.ss])
                qf_ = G.QA.t[:].rearrange("p a b -> p (a b)")
                sf_ = G.SQ.t[:].rearrange("p a b -> p (a b)")
                q3 = qf_.rearrange("p (a b) -> p a b", b=64)
                S.op("dve", lambda: nc.vector.tensor_tensor(out=q3, in0=q3,
                                                            in1=G.ss.t[:].unsqueeze(2).broadcast_to([128, 32, 64]),
                                                            op=ALU.mult), reads=[G.QA, G.ss], writes=[G.QA])
                gbc = gain_tt.t[:, l, :].unsqueeze(1).broadcast_to([128, 32, 64])
                if half == 0 and which == "q":
                    S.op("dve", lambda: nc.vector.tensor_tensor(
                        out=G.QB.t[:].rearrange("p a (g c) -> p (a g) c", c=64), in0=q3, in1=gbc, op=ALU.mult),
                        reads=[G.QA, gain_tt], writes=[G.QB])
                else:
                    S.op("dve", lambda: nc.vector.tensor_tensor(out=q3, in0=q3, in1=gbc, op=ALU.mult),
                         reads=[G.QA, gain_tt], writes=[G.QA])
                    if half == 0:
                        for s2 in range(2):
                            s_ = g * 2 + s2
                            S.dma("sp", dout["nk"][s_, l].rearrange("(t p) c -> p t c", p=128),
                                  G.QA.t[:, 2 * s2:2 * s2 + 2, :], reads=[G.QA])
                        S.op("act", lambda: nc.scalar.copy(out=G.QB.t[:].rearrange("p a b -> p (a b)"), in_=qf_),
                             reads=[G.QA], writes=[G.QB])
                    else:
                        x5 = G.QA.t[:].rearrange("p t (g a s c) -> p t g a s c", a=2, s=2, c=16)
                        r5 = G.SQ.t[:].rearrange("p t (g a s c) -> p t g a s c", a=2, s=2, c=16)
                        s5 = SIN.t[:, g * 4:(g + 1) * 4, :].rearrange("p t (a s c) -> p t a s c", s=2, c=16)
                        for sidx in range(2):
                            for ax_ in range(2):
                                S.op("dve", lambda: nc.vector.tensor_tensor(
                                    out=r5[:, :, :, ax_, sidx, :], in0=x5[:, :, :, ax_, 1 - sidx, :],
                                    in1=s5[:, :, ax_, sidx, :].unsqueeze(2).broadcast_to([128, 4, 8, 16]), op=ALU.mult),
                                    reads=[G.QA, SIN], writes=[G.SQ])
                        q4 = G.QA.t[:].rearrange("p t (g c) -> p t g c", c=64)
                        S.op("dve", lambda: nc.vector.tensor_tensor(
                            out=q4, in0=q4,
                            in1=COS.t[:, g * 4:(g + 1) * 4, :].unsqueeze(2).broadcast_to([128, 4, 8, 64]), op=ALU.mult),
                            reads=[G.QA, COS], writes=[G.QA])
                        S.op("dve", lambda: nc.vector.tensor_tensor(out=G.QB.t[:].rearrange("p a b -> p (a b)"),
                                                                    in0=qf_, in1=sf_, op=ALU.add),
                             reads=[G.QA, G.SQ], writes=[G.QB])

            def stageC(g, dstT, doff):
                G = GR[g]
                for t in range(4):
                    tt = g * 4 + t
                    b = nextbank()
                    pv = pbank(b).bitcast(BF16)

                    def f():
                        ins = None
                        for i in range(4):
                            ins = nc.tensor.transpose(pv[:, i * 128:(i + 1) * 128], G.QB.t[:, t, i * 128:(i + 1) * 128],
                                                      identB.t[:])
                        return ins
                    S.op("pe", f, reads=[G.QB, identB], writes=[PB[b]])
                    S.op("act", lambda: nc.scalar.copy(out=dstT.t[:, :, doff + tt * 128:doff + (tt + 1) * 128],
                                                       in_=pv[:, 0:512].rearrange("p (a b) -> p a b", b=128)),
                         reads=[PB[b]], writes=[dstT])

            def stageV(slot, g):
                for t in range(4):
                    tt = g * 4 + t
                    b = nextbank()
                    zmm(slot, tt, b)
                    kci = KOFF // 128 + tt
                    if half == 0:
                        st = stg[tt % 2]
                        S.op("act", lambda: nc.scalar.copy(out=st.t[:], in_=pbank(b)), reads=[PB[b]], writes=[st])
                        S.dma("sp", dout["nv"][tt // 2, l, (tt % 2) * 128:(tt % 2) * 128 + 128, :], st.t[:], reads=[st])
                        S.op("act", lambda: nc.scalar.copy(out=VA.t[:, kci, :, 0:128],
                                                           in_=st.t[:].rearrange("p (a b) -> p a b", b=128)),
                             reads=[st], writes=[VA])
                    else:
                        S.op("act", lambda: nc.scalar.copy(out=VA.t[:, kci, :, 0:128],
                                                           in_=pbank(b).rearrange("p (a b) -> p a b", b=128)),
                             reads=[PB[b]], writes=[VA])

            slot3 = w_get()
            stageA(slot3, 0)
            stageA(slot3, 1)
            w_done()
            slot4 = w_get()
            stageB(0, QN, "q")
            stageA(slot4, 0)
            stageC(0, QTa, 0)
            stageB(1, QN, "q")
            stageA(slot4, 1)
            w_done()
            slot5 = w_get()
            stageC(1, QTa, 0)
            stageB(0, KN, "k")
            stageV(slot5, 0)
            stageC(0, KTa, KOFF)
            stageB(1, KN, "k")
            stageV(slot5, 1)
            stageC(1, KTa, KOFF)
            w_done()
            S.barrier()
            ckpt(17)
            AR.off = ar_att
            OALL = AV([128, 8, 4, 128])
            ETP = [AV([128, 2, 2, 256], BF16) for _ in range(2)]
            ot2 = [AV([128, 128]) for _ in range(2)]
            work = []
            for qg in range(4):
                kcs = [qg * 2, qg * 2 + 1] if half == 0 else list(range(10))
                for h in range(4):
                    for pi in range(len(kcs) // 2):
                        work.append((qg, h, pi, len(kcs) // 2, (kcs[2 * pi], kcs[2 * pi + 1])))
            st_banks = {}

            def emit_st(widx):
                qg, h, pi, npair, kc2 = work[widx]
                q0 = qg * 256
                bsA, bsB = ((0, 1), (2, 3))[widx % 2]
                st_banks[widx] = (bsA, bsB)

                def f():
                    ins = None
                    for i, bnk in ((0, bsA), (1, bsB)):
                        ps_ = slice(i * 64, (i + 1) * 64)
                        for kk in range(2):
                            kc = kc2[kk]
                            ins = nc.tensor.matmul(pbank(bnk)[:, kk * 256:(kk + 1) * 256],
                                                   KTa.t[ps_, h, kc * 128:(kc + 1) * 128],
                                                   QTa.t[ps_, h, q0:q0 + 256], start=True, stop=True)
                    return ins
                S.op("pe", f, reads=[KTa, QTa], writes=[PB[bsA], PB[bsB]])

            emit_st(0)
            itn = 0
            for widx in range(len(work)):
                qg, h, pi, npair, kc2 = work[widx]
                q0 = qg * 256
                if pi == 0:
                    accs = ((4, 5), (6, 7))[itn % 2]
                    itn += 1
                if widx + 1 < len(work):
                    emit_st(widx + 1)
                bsA, bsB = st_banks.pop(widx)
                etp = ETP[widx % 2]
                S.op("act", lambda: nc.scalar.activation(out=etp.t[:].rearrange("p i a b -> p (i a b)"),
                                                         in_=PSt[bsA // 2][:, 0:1024], func=AF.Exp),
                     reads=[PB[bsA], PB[bsB]], writes=[etp])

                def f():
                    ins = None
                    for kk in range(2):
                        kc = kc2[kk]
                        for qb_ in range(2):
                            for i in range(2):
                                first = (pi == 0 and kk == 0 and i == 0)
                                last = (pi == npair - 1 and kk == 1 and i == 1)
                                ins = nc.tensor.matmul(pbank(accs[qb_])[:, i * 129:(i + 1) * 129],
                                                       etp.t[:, i, kk, qb_ * 128:(qb_ + 1) * 128],
                                                       VA.t[:, kc, h, 0:129], start=first, stop=last)
                    return ins
                S.op("pe", f, reads=[etp, VA], writes=[PB[accs[0]], PB[accs[1]]])
                if pi == npair - 1:
                    for qb_ in range(2):
                        tt = (q0 + qb_ * 128) // 128
                        ab = accs[qb_]
                        acc = pbank(ab)
                        rc = nsmall()
                        S.op("dve", lambda: nc.vector.reciprocal(
                            rc.t[:, 0:2], acc[:, 0:258].rearrange("p (a b) -> p a b", b=129)[:, :, 128]),
                            reads=[PB[ab]], writes=[rc])
                        S.op("dve", lambda: nc.vector.tensor_tensor(out=rc.t[:, 2:3], in0=rc.t[:, 1:2],
                                                                    in1=NLAM.t[:, l:l + 1], op=ALU.mult),
                             reads=[rc, NLAM], writes=[rc])
                        o1 = ot2[qb_]
                        S.op("dve", lambda: nc.vector.tensor_scalar(o1.t[:], acc[:, 129:257], rc.t[:, 2:3], None,
                                                                    ALU.mult), reads=[PB[ab], rc], writes=[o1])
                        S.op("dve", lambda: nc.vector.scalar_tensor_tensor(
                            out=OALL.t[:, tt, h, :], in0=acc[:, 0:128], scalar=rc.t[:, 0:1], in1=o1.t[:],
                            op0=ALU.mult, op1=ALU.add), reads=[PB[ab], rc, o1], writes=[OALL])
            of_ = OALL.t[:].rearrange("p a b c -> p (a b c)")
            OSQ = _V()
            OSQ.ap = SCR.t[:].rearrange("p a b -> p (a b)")
            OSQ.t = OSQ.ap
            OSQ.res = yT.res
            ss2 = AV([128, 32])
            S.op("dve", lambda: nc.vector.tensor_tensor(out=OSQ.t[:], in0=of_, in1=of_, op=ALU.mult),
                 reads=[OALL], writes=[OSQ])
            S.op("dve", lambda: nc.vector.tensor_reduce(out=ss2.t[:], in_=OSQ.t[:].rearrange("p (a b) -> p a b", b=128),
                                                        axis=AX.X, op=ALU.add), reads=[OSQ], writes=[ss2])
            rstd_from_ss(ss2.t[:], ss2.t[:], 1.0 / 128, EPS, [ss2], [ss2])
            o3 = of_.rearrange("p (a b) -> p a b", b=128)
            S.op("dve", lambda: nc.vector.tensor_tensor(out=o3, in0=o3,
                                                        in1=ss2.t[:].unsqueeze(2).broadcast_to([128, 32, 128]),
                                                        op=ALU.mult), reads=[OALL, ss2], writes=[OALL])
            S.op("dve", lambda: nc.vector.tensor_tensor(
                out=ytok.t[:, :, 512:1024].rearrange("p t (h c) -> p t h c", c=128), in0=OALL.t[:],
                in1=DN.t[:, l, :].unsqueeze(1).unsqueeze(1).broadcast_to([128, 8, 4, 128]), op=ALU.mult),
                reads=[OALL, DN], writes=[ytok])

            ckpt(18)
            if half == HALVES[0]:
                mods_group(l, 2)
            for tt in range(8):
                transposes_to(ytok, lambda i: ytok.t[:, tt, i * 128:(i + 1) * 128], 8, yT,
                              yT.t[:, :, tt * 128:(tt + 1) * 128])
            for cg in range(2):
                slot = w_get()
                wv = slotv(slot, 8, 512)
                for m in range(4):
                    mm = cg * 4 + m
                    for tg in range(2):
                        b = nextbank()

                        def f():
                            ins = None
                            for kc in range(8):
                                ins = nc.tensor.matmul(pbank(b), wv[:, kc, m * 128:(m + 1) * 128],
                                                       yT.t[:, kc, tg * 512:(tg + 1) * 512],
                                                       start=(kc == 0), stop=(kc == 7))
                            return ins
                        S.op("pe", f, reads=[slot, yT], writes=[PB[b]])
                        xs_ = xT.t[:, mm, tg * 512:(tg + 1) * 512]
                        S.op("dve", lambda: nc.vector.scalar_tensor_tensor(
                            out=xs_, in0=pbank(b), scalar=MOD[l].t[:, 2 * 8 + mm, cond:cond + 1], in1=xs_,
                            op0=ALU.mult, op1=ALU.add), reads=[PB[b], MOD[l], xT], writes=[xT])
                w_done()
            S.barrier()

        def ffn(l, half):
            cond = half
            nseq, L = (4, 256) if half == 0 else (1, 1024)
            AR.reset()
            gT = AV([128, 22, 1024], BF16)
            y0 = [AV([128, 1024]) for _ in range(3)]
            sa = [AV([128, 1024]) for _ in range(2)]
            upb = ((0, 1), (2, 3), (4, 5))
            ui = 0
            for t in range(11):
                slot = w_get()
                wv = slotv(slot, 8, 512)
                for sub in range(2):
                    j = 2 * t + sub
                    for ab in range(2):
                        cols = ab * 256 + sub * 128
                        jj = ab * 22 + j
                        bb = upb[ui % 3]
                        yy = y0[ui % 3]
                        ui += 1
                        for tg in range(2):
                            bnk = bb[tg]

                            def f():
                                ins = None
                                for kc in range(8):
                                    ins = nc.tensor.matmul(pbank(bnk), wv[:, kc, cols:cols + 128],
                                                           hT.t[:, kc, tg * 512:(tg + 1) * 512],
                                                           start=(kc == 0), stop=(kc == 7))
                                return ins
                            S.op("pe", f, reads=[slot, hT], writes=[PB[bnk]])
                        pfull = PSt[bb[0] // 2][:, 0:1024]
                        S.op("act", lambda: nc.scalar.activation(out=yy.t[:], in_=pfull, func=AF.Identity,
                                                                 bias=CW.t[:, l, 3, jj:jj + 1],
                                                                 scale=CW.t[:, l, 1, jj:jj + 1]),
                             reads=[PB[bb[0]], PB[bb[1]], CW], writes=[yy])
                        pv3 = pfull.rearrange("p (s t) -> p s t", t=L)
                        yv3 = yy.t[:].rearrange("p (s t) -> p s t", t=L)
                        S.op("dve", lambda: nc.vector.scalar_tensor_tensor(
                            out=yv3[:, :, 1:L], in0=pv3[:, :, 0:L - 1], scalar=CW.t[:, l, 0, jj:jj + 1],
                            in1=yv3[:, :, 1:L], op0=ALU.mult, op1=ALU.add),
                            reads=[PB[bb[0]], PB[bb[1]], CW, yy], writes=[yy])
                        S.op("dve", lambda: nc.vector.scalar_tensor_tensor(
                            out=yv3[:, :, 0:L - 1], in0=pv3[:, :, 1:L], scalar=CW.t[:, l, 2, jj:jj + 1],
                            in1=yv3[:, :, 0:L - 1], op0=ALU.mult, op1=ALU.add),
                            reads=[PB[bb[0]], PB[bb[1]], CW, yy], writes=[yy])
                        if ab == 0:
                            sa_ = sa[j % 2]
                            S.op("act", lambda: nc.scalar.activation(out=sa_.t[:], in_=yy.t[:], func=AF.Silu),
                                 reads=[yy], writes=[sa_])
                        else:
                            sa_ = sa[j % 2]
                            S.op("pool", lambda: nc.gpsimd.tensor_tensor(out=gT.t[:, j, :], in0=sa_.t[:], in1=yy.t[:],
                                                                         op=ALU.mult), reads=[sa_, yy], writes=[gT])
                w_done()
            if half == HALVES[0]:
                mods_group(l, 5)
            for m in range(8):
                slot = w_get()
                wv = slotv(slot, 22, 128)
                for tg in range(2):
                    b = nextbank((6, 7), "t67")

                    def f():
                        ins = None
                        for kc in range(22):
                            ins = nc.tensor.matmul(pbank(b), wv[:, kc, :], gT.t[:, kc, tg * 512:(tg + 1) * 512],
                                                   start=(kc == 0), stop=(kc == 21))
                        return ins
                    S.op("pe", f, reads=[slot, gT], writes=[PB[b]])
                    xs_ = xT.t[:, m, tg * 512:(tg + 1) * 512]
                    S.op("dve", lambda: nc.vector.scalar_tensor_tensor(
                        out=xs_, in0=pbank(b), scalar=MOD[l].t[:, 5 * 8 + m, cond:cond + 1], in1=xs_,
                        op0=ALU.mult, op1=ALU.add), reads=[PB[b], MOD[l], xT], writes=[xT])
                w_done()
            S.barrier()

        for half in HALVES:
            AR.reset()
            load_x(din["xp"] if half == 0 else din["xs"])
            S.barrier()
            ckpt(11)
            for l in range(NL):
                AR.reset()
                if half == HALVES[0]:
                    mods_group(l, 0)
                    mods_group(l, 1)
                norm_mod(l, 0, half)
                S.barrier()
                ckpt(12)
                mixer(l, half)
                ckpt(19)
                AR.reset()
                if half == HALVES[0]:
                    mods_group(l, 3)
                    mods_group(l, 4)
                norm_mod(l, 1, half)
                S.barrier()
                ckpt(20)
                ffn(l, half)
                ckpt(21)
            AR.reset()
            store_x(dout["yp"] if half == 0 else dout["ys"])
            S.barrier()
        S.drain("sp")
    except _Stop:
        pass
    return nc


_CACHE = {}


def _prep_inputs(inputs):
    f32 = lambda a: np.ascontiguousarray(np.asarray(a, dtype=np.float32))
    I = {k: f32(v) for k, v in inputs.items()}
    hc = host_consts()
    shared = {
        "norm1": I["norm1"], "w_mod": I["w_mod"], "b_mod": I["b_mod"], "w_in": I["w_in"],
        "sgu_norm": I["sgu_norm"], "sgu_w": I["sgu_w"], "sgu_b": I["sgu_b"],
        "rlf": I["ret_logit_fwd"], "rlb": I["ret_logit_bwd"], "ret_norm": I["ret_norm"].reshape(2, 256),
        "q_norm": I["q_norm"], "k_norm": I["k_norm"], "diff_lam": I["diff_lam"].reshape(2, 256),
        "diff_norm": I["diff_norm"], "w_out": I["w_out"], "norm2": I["norm2"], "ffn_up": I["ffn_up"],
        "ffn_conv": I["ffn_conv"], "ffn_conv_b": I["ffn_conv_b"], "ffn_down": I["ffn_down"],
    }
    shared.update(hc)
    maps = []
    for c in range(8):
        m = dict(shared)
        m["xp"] = np.ascontiguousarray(I["x_prompt"][4 * c:4 * c + 4].reshape(1024, 1024))
        m["xs"] = np.ascontiguousarray(I["x_sample"][c])
        m["cvec"] = np.ascontiguousarray(np.stack([I["c_ctx"], I["c"][c]], axis=0))
        m["ck"] = np.ascontiguousarray(I["cache_k"][c].reshape(2, 256, 512))
        m["cv"] = np.ascontiguousarray(I["cache_v"][c].reshape(2, 256, 512))
        m["srf"] = np.ascontiguousarray(I["state_ret_fwd"][c])
        m["srb"] = np.ascontiguousarray(I["state_ret_bwd"][c])
        maps.append(m)
    return maps


def kernel(**inputs):
    maps = _prep_inputs(inputs)
    if "nc" not in _CACHE:
        _CACHE["nc"] = build()
    nc = _CACHE["nc"]
    res = run_bass_kernel_spmd(nc, maps, core_ids=list(range(8)))
    R = res.results
    yp = np.concatenate([np.asarray(R[c]["yp"]).reshape(4, 256, 1024) for c in range(8)], axis=0)
    ys = np.stack([np.asarray(R[c]["ys"]) for c in range(8)], axis=0)
    nk = np.concatenate([np.asarray(R[c]["nk"]).reshape(4, 2, 256, 4, 2, 64) for c in range(8)], axis=0)
    nv = np.concatenate([np.asarray(R[c]["nv"]).reshape(4, 2, 256, 4, 128) for c in range(8)], axis=0)
    nrf = np.concatenate([np.asarray(R[c]["nrf"]) for c in range(8)], axis=0)
    nrb = np.concatenate([np.asarray(R[c]["nrb"]) for c in range(8)], axis=0)
    return (yp.astype(np.float32), ys.astype(np.float32), nk.astype(np.float32), nv.astype(np.float32),
            nrf.astype(np.float32), nrb.astype(np.float32))
```

```python
import math
from contextlib import ExitStack
import numpy as np
import concourse.bass as bass
import concourse.mybir as mybir
from concourse.bass_utils import run_bass_kernel_spmd

F32 = mybir.dt.float32
BF16 = mybir.dt.bfloat16
AF = mybir.ActivationFunctionType
ALU = mybir.AluOpType
AX = mybir.AxisListType

D = 1024
T = 1024
DFF = 2816
EPS = 1e-6
NSLOT = 4


class Res:
    __slots__ = ("w", "r", "sem", "persist", "excl")

    def __init__(self, persist=False, excl=False):
        self.excl = excl
        self.w = None
        self.r = {}
        self.sem = None
        self.persist = persist


class TT:
    def __init__(self, t, res=None):
        self.t = t
        self.res = res if res is not None else Res()


class Sch:
    def __init__(self, nc, es):
        self.nc = nc
        self.es = es
        self.E = {"pe": nc.tensor, "act": nc.scalar, "dve": nc.vector, "pool": nc.gpsimd, "sp": nc.sync}
        self.sem = {}
        self.cnt = {}
        self.seen = {k: {} for k in self.E}
        for k in ("pe", "act", "dve", "pool"):
            self.sem[k] = es.enter_context(nc.semaphore("s_" + k))
            self.cnt[k] = 0
        self.nsem = 0
        self.dsems = []
        self.free = []
        self.live = []

    def newsem(self):
        if self.free:
            return self.free.pop()
        name = "d%d" % self.nsem
        self.nsem += 1
        self.sem[name] = self.es.enter_context(self.nc.semaphore(name))
        self.cnt[name] = 0
        self.dsems.append(name)
        return name

    def recycle(self):
        keep = []
        for r in self.live:
            if r.persist:
                keep.append(r)
            else:
                self.free.append(r.sem)
                r.sem = None
        self.live = keep

    def _wait(self, eng, raw, other):
        best = {}
        for t in raw:
            if t is None:
                continue
            k, v = t
            if k == eng and eng in ("pe", "sp"):
                continue
            if v > best.get(k, 0):
                best[k] = v
        for t in other:
            if t is None:
                continue
            k, v = t
            if k == eng and eng in ("pe", "sp"):
                continue
            if v > best.get(k, 0):
                best[k] = v
        sn = self.seen[eng]
        for k, v in best.items():
            if sn.get(k, 0) >= v:
                continue
            self.E[eng].wait_ge(self.sem[k], v)
            sn[k] = v

    def _deps(self, reads, writes):
        raw = [r.w for r in reads]
        other = []
        for w in writes:
            other.append(w.w)
            other.extend(w.r.items())
        return raw, other

    def _commit(self, tok, reads, writes):
        k, v = tok
        for r in reads:
            if r.r.get(k, 0) < v:
                r.r[k] = v
        for w in writes:
            w.w = tok
            w.r = {}

    def op(self, eng, fn, reads=(), writes=()):
        reads = [getattr(x, 'res', x) for x in reads] + [x.extra for x in reads if hasattr(x, 'extra')]
        writes = [getattr(x, 'res', x) for x in writes]
        writes = writes + [r for r in reads if r.excl and r not in writes]
        raw, other = self._deps(reads, writes)
        self._wait(eng, raw, other)
        ins = fn()
        ins.then_inc(self.sem[eng], 1)
        self.cnt[eng] += 1
        tok = (eng, self.cnt[eng])
        self._commit(tok, reads, writes)
        return tok

    def dma(self, q, out, in_, reads=(), writes=(), key=None):
        reads = [getattr(x, 'res', x) for x in reads]
        writes = [getattr(x, 'res', x) for x in writes]
        kres = key if key is not None else (writes[0] if writes else reads[0])
        kres = getattr(kres, 'res', kres)
        if kres.sem is None:
            kres.sem = self.newsem()
            self.live.append(kres)
        raw, other = self._deps(reads, writes)
        other = list(other) + [(kres.sem, self.cnt[kres.sem])]
        self._wait(q, raw, other)
        ins = self.E[q].dma_start(out=out, in_=in_)
        ins.then_inc(self.sem[kres.sem], 16)
        self.cnt[kres.sem] += 16
        tok = (kres.sem, self.cnt[kres.sem])
        self._commit(tok, reads, writes)
        return tok

    def barrier(self):
        engs = ("pe", "act", "dve", "pool")
        for e in engs + ("sp",):
            for f in engs:
                if e == f and e == "pe":
                    continue
                v = self.cnt[f]
                if v > 0 and self.seen[e].get(f, 0) < v:
                    self.E[e].wait_ge(self.sem[f], v)
                    self.seen[e][f] = v
            for r in self.live:
                if r.persist:
                    continue
                v = self.cnt[r.sem]
                if v > 0 and self.seen[e].get(r.sem, 0) < v:
                    self.E[e].wait_ge(self.sem[r.sem], v)
                    self.seen[e][r.sem] = v
        self.recycle()

    def drain(self, q="sp"):
        for k in self.dsems:
            v = self.cnt[k]
            if v > 0 and self.seen[q].get(k, 0) < v:
                self.E[q].wait_ge(self.sem[k], v)
                self.seen[q][k] = v
        for f in ("pe", "act", "dve", "pool"):
            v = self.cnt[f]
            if v > 0 and self.seen[q].get(f, 0) < v:
                self.E[q].wait_ge(self.sem[f], v)
                self.seen[q][f] = v


def host_consts():
    ROPE_PAIRS = 16
    t = np.arange(1024)
    inv = (10000.0 ** (-np.arange(ROPE_PAIRS, dtype=np.float32) / ROPE_PAIRS)).astype(np.float32)
    ar = (t // 64).astype(np.float32)[:, None] * inv
    ac = (t % 64).astype(np.float32)[:, None] * inv
    cr, sr, cc, sc_ = np.cos(ar), np.sin(ar), np.cos(ac), np.sin(ac)
    cos64 = np.concatenate([cr, cr, cc, cc], axis=1).astype(np.float32)
    sin64 = np.concatenate([-sr, sr, -sc_, sc_], axis=1).astype(np.float32)
    k = np.arange(128)[:, None]
    q = np.arange(128)[None, :]
    dm = np.zeros((128, 4, 128), np.float32)
    dm[:, 0] = np.maximum(q - k, 0)
    dm[:, 1] = np.maximum(k - q, 0)
    dm[:, 2] = (q >= k)
    dm[:, 3] = (k >= q)
    pr = np.zeros((128, 2, 128), np.float32)
    pr[:, 0, :] = np.arange(128) + 1.0
    pr[:, 1, :] = 128.0 - np.arange(128)
    kp = np.zeros((128, 2), np.float32)
    kp[:, 0] = 127.0 - np.arange(128)
    kp[:, 1] = np.arange(128)
    return dict(cos64=cos64, sin64=sin64, dmat=dm, posrow=pr, kpos=kp)


IN_SPECS = [
    ("xp", (1024, 1024)), ("xs", (1024, 1024)), ("cvec", (2, 1024)),
    ("ck", (2, 256, 512)), ("cv", (2, 256, 512)), ("srf", (2, 4, 64, 64)), ("srb", (2, 4, 64, 64)),
    ("norm1", (2, 1024)), ("w_mod", (2, 1024, 6144)), ("b_mod", (2, 6144)), ("w_in", (2, 1024, 3072)),
    ("sgu_norm", (2, 256)), ("sgu_w", (2, 4, 128, 128)), ("sgu_b", (2, 4, 128)),
    ("rlf", (2, 4)), ("rlb", (2, 4)), ("ret_norm", (2, 256)), ("q_norm", (2, 64)), ("k_norm", (2, 64)),
    ("diff_lam", (2, 256)), ("diff_norm", (2, 128)), ("w_out", (2, 1024, 1024)), ("norm2", (2, 1024)),
    ("ffn_up", (2, 1024, 5632)), ("ffn_conv", (2, 3, 5632)), ("ffn_conv_b", (2, 5632)),
    ("ffn_down", (2, 2816, 1024)),
    ("cos64", (1024, 64)), ("sin64", (1024, 64)), ("dmat", (128, 4, 128)), ("posrow", (128, 2, 128)),
    ("kpos", (128, 2)),
]
OUT_SPECS = [
    ("yp", (1024, 1024)), ("ys", (1024, 1024)), ("nk", (4, 2, 256, 512)), ("nv", (4, 2, 256, 512)),
    ("nrf", (4, 2, 4, 64, 64)), ("nrb", (4, 2, 4, 64, 64)),
]


def build(cfg=None):
    cfg = cfg or {}
    NL = cfg.get("n_layers", 2)
    HALVES = cfg.get("halves", (0, 1))
    taps = cfg.get("taps", None)
    nc = bass.Bass("TRN2", target_bir_lowering=False)
    try:
        nc.allow_low_precision("bf16 matmul operands with fp32 accumulation")
    except Exception:
        pass
    din = {n: nc.dram_tensor(n, list(s), F32, kind="ExternalInput").ap() for n, s in IN_SPECS}
    dout = {n: nc.dram_tensor(n, list(s), F32, kind="ExternalOutput").ap() for n, s in OUT_SPECS}
    es = ExitStack()
    STOP = cfg.get("stop", None)

    class _Stop(Exception):
        pass
    try:
      with es:
        S = Sch(nc, es)

        def ckpt(k):
            if STOP == k:
                S.drain("sp")
                raise _Stop()

        def sb(name, shape, dt=F32):
            return TT(es.enter_context(nc.sbuf_tensor(name, list(shape), dt)), Res(persist=True))

        def tap(name, ap, shape, reads):
            if taps is None or name not in taps:
                return
            d = nc.dram_tensor("tap_" + name, list(shape), ap.dtype, kind="ExternalOutput").ap()
            taps[name] = (list(shape), ap.dtype)
            S.dma("sp", d, ap, reads=reads)

        PSt = [es.enter_context(nc.psum_tensor("ps%d" % i, [128, 1024], F32)) for i in range(4)]
        PB = []
        for i in range(8):
            PB.append(TT(PSt[i // 2], Res(excl=True)))

        def pbank(i):
            return PSt[i // 2][:, (i % 2) * 512:(i % 2) * 512 + 512]

        rot = {"n": 0}

        def nextbank(pool=(0, 1, 2, 3), key="n"):
            i = pool[rot.setdefault(key, 0) % len(pool)]
            rot[key] += 1
            return i

        xT = sb("xT", [128, 8, T])
        hT = sb("hT", [128, 8, T], BF16)
        yT = sb("yT", [128, 8, T], BF16)
        yT.halves = [Res(persist=True), Res(persist=True)]
        ring = [sb("ring%d" % i, [128, 4096], BF16) for i in range(NSLOT)]
        for r_ in ring:
            r_.extra = Res(persist=True)
            r_.extra.sem = S.newsem()
            r_.res.sem = S.newsem()
        ARENA_BYTES = 74 * 1024
        arena = sb("arena", [128, ARENA_BYTES // 4], F32)

        identF = sb("identF", [128, 128])
        identB = sb("identB", [128, 128], BF16)
        onesB = sb("onesB", [128, 128], BF16)
        sTb = sb("sTb", [128, 8, 2], BF16)
        MOD = [sb("MOD%d" % l, [128, 48, 2]) for l in range(2)]
        GS = [[sb("GS%d%d" % (l, n), [128, 8, 2]) for n in range(2)] for l in range(2)]
        NT = sb("NT", [128, 32])
        CW = sb("CW", [128, 2, 4, 44])
        WST = sb("WST", [128, 8, 128], BF16)
        BS = sb("BS", [128, 8])
        SGN = sb("SGN", [128, 2, 256])
        RN = sb("RN", [128, 2, 256])
        QN = sb("QN", [128, 2, 64])
        KN = sb("KN", [128, 2, 64])
        DN = sb("DN", [128, 2, 128])
        LG = sb("LG", [128, 16])
        KP = sb("KP", [128, 2])
        C128 = sb("C128", [128, 64])
        DTm = sb("DTm", [128, 2, 4, 128])
        QD = sb("QD", [128, 2, 2, 2, 128])
        KDE = sb("KDE", [128, 2, 2, 256])
        CD = sb("CD", [128, 2, 2, 2, 64])
        NLAM = sb("NLAM", [128, 2])
        COS = sb("COS", [128, 8, 64])
        SIN = sb("SIN", [128, 8, 64])
        small = [sb("small%d" % i, [128, 16]) for i in range(8)]
        srot = {"i": 0}

        def nsmall():
            s = small[srot["i"] % len(small)]
            srot["i"] += 1
            return s

        wq = []

        def slotv(slot, a, b):
            return slot.t[:, 0:a * b].rearrange("p (a b) -> p a b", b=b)

        def wsrc(name, l, c0, c1):
            return din[name][l, :, c0:c1].rearrange("(kc p) n -> p kc n", p=128)

        def plan_weights():
            def modt(l, which):
                for t2 in range(2):
                    c0 = which * 1024 + t2 * 512
                    wq.append([(lambda s: slotv(s, 8, 512), wsrc("w_mod", l, c0, c0 + 512))])
            first = True
            for half in HALVES:
                for l in range(NL):
                    if first:
                        modt(l, 0)
                        modt(l, 1)
                    for g in range(6):
                        wq.append([(lambda s: slotv(s, 8, 512), wsrc("w_in", l, g * 512, (g + 1) * 512))])
                    if first:
                        modt(l, 2)
                    for g in range(2):
                        wq.append([(lambda s: slotv(s, 8, 512), wsrc("w_out", l, g * 512, (g + 1) * 512))])
                    if first:
                        modt(l, 3)
                        modt(l, 4)
                    for t in range(11):
                        wq.append([
                            (lambda s: slotv(s, 8, 512)[:, :, 0:256], wsrc("ffn_up", l, t * 256, (t + 1) * 256)),
                            (lambda s: slotv(s, 8, 512)[:, :, 256:512],
                             wsrc("ffn_up", l, DFF + t * 256, DFF + (t + 1) * 256)),
                        ])
                    if first:
                        modt(l, 5)
                    for m in range(8):
                        wq.append([(lambda s: slotv(s, 22, 128), wsrc("ffn_down", l, m * 128, (m + 1) * 128))])
                first = False

        plan_weights()
        wstate = {"next_load": 0, "next_use": 0}

        def w_issue():
            i = wstate["next_load"]
            if i >= len(wq):
                return
            slot = ring[i % NSLOT]
            for n_, (dstf, src) in enumerate(wq[i]):
                S.dma("pool", dstf(slot), src, writes=[slot.res if n_ == 0 else slot.extra])
            wstate["next_load"] = i + 1

        def w_get():
            i = wstate["next_use"]
            wstate["next_use"] = i + 1
            assert i < wstate["next_load"]
            return ring[i % NSLOT]

        def w_done():
            w_issue()

        class Arena:
            def __init__(self):
                self.off = 0

            def reset(self):
                self.off = 0

            def get(self, shape, dt):
                n = 1
                for s_ in shape[1:]:
                    n *= s_
                nbytes = n * (4 if dt == F32 else 2)
                nbytes = (nbytes + 63) // 64 * 64
                w0 = self.off // 4
                w1 = (self.off + nbytes) // 4
                assert self.off + nbytes <= ARENA_BYTES, ("arena overflow", self.off, nbytes)
                self.off += nbytes
                ap = arena.t[0:shape[0], w0:w1]
                if dt == BF16:
                    ap = ap.bitcast(BF16)
                nfree = n
                ap = ap[:, 0:nfree]
                if len(shape) > 2:
                    names = " ".join("d%d" % i for i in range(len(shape) - 1))
                    kw = {"d%d" % i: shape[i + 1] for i in range(len(shape) - 1)}
                    ap = ap.rearrange("p (%s) -> p %s" % (names, names), **kw)
                return ap

        AR = Arena()

        class AV:
            def __init__(self, shape, dt=F32):
                self.ap = AR.get(shape, dt)
                self.res = Res()

            @property
            def t(self):
                return self.ap

        S.op("pool", lambda: nc.gpsimd.memset(identF.t[:], 0.0), writes=[identF])
        S.op("pool", lambda: nc.gpsimd.affine_select(out=identF.t[:], in_=identF.t[:], pattern=[[-1, 128]],
                                                      compare_op=ALU.not_equal, fill=1.0, base=0,
                                                      channel_multiplier=1), reads=[identF], writes=[identF])
        S.op("pool", lambda: nc.gpsimd.memset(onesB.t[:], 1.0), writes=[onesB])
        S.op("pool", lambda: nc.gpsimd.memset(C128.t[:], 128.0), writes=[C128])
        S.op("dve", lambda: nc.vector.tensor_copy(identB.t[:], identF.t[:]), reads=[identF], writes=[identB])

        ckpt(1)
        for _ in range(NSLOT):
            w_issue()

        cres = Res()
        crow = AV([2, 1024])
        bmrow = AV([96, 128])
        nrow = AV([32, 128])
        cvrow = AV([44, 2, 4, 128])
        wsraw = AV([128, 8, 128])
        bsrow = AV([8, 128])
        BMT = sb("BMT", [128, 96])
        DL = AV([128, 2, 256])
        DM = AV([128, 4, 128])
        PR = AV([128, 2, 128])

        def cload(dst_tt, dst_ap, src_ap):
            S.dma("sp", dst_ap, src_ap, writes=[dst_tt])

        cload(crow, crow.t[:], din["cvec"])
        cload(bmrow, bmrow.t[:], din["b_mod"].rearrange("l (j p) -> (l j) p", p=128))
        cload(nrow, nrow.t[0:16, :], din["norm1"].rearrange("l (kc p) -> (l kc) p", p=128))
        cload(nrow, nrow.t[16:32, :], din["norm2"].rearrange("l (kc p) -> (l kc) p", p=128))
        for l in range(2):
            cload(cvrow, cvrow.t[:, l, 0:3, :], din["ffn_conv"][l].rearrange("j (c p) -> c j p", p=128))
            cload(cvrow, cvrow.t[:, l, 3, :], din["ffn_conv_b"][l].rearrange("(c p) -> c p", p=128))
        cload(wsraw, wsraw.t[:], din["sgu_w"].rearrange("l g p q -> p (l g) q"))
        cload(bsrow, bsrow.t[:], din["sgu_b"].rearrange("l g p -> (l g) p"))
        cload(SGN, SGN.t[:], din["sgu_norm"].partition_broadcast(128))
        cload(RN, RN.t[:], din["ret_norm"].partition_broadcast(128))
        cload(QN, QN.t[:], din["q_norm"].partition_broadcast(128))
        cload(KN, KN.t[:], din["k_norm"].partition_broadcast(128))
        cload(DN, DN.t[:], din["diff_norm"].partition_broadcast(128))
        cload(DL, DL.t[:], din["diff_lam"].partition_broadcast(128))
        cload(LG, LG.t[:, 0:8], din["rlf"].rearrange("l h -> (l h)").partition_broadcast(128))
        cload(LG, LG.t[:, 8:16], din["rlb"].rearrange("l h -> (l h)").partition_broadcast(128))
        cload(DM, DM.t[:], din["dmat"])
        cload(PR, PR.t[:], din["posrow"])
        cload(KP, KP.t[:], din["kpos"])
        cload(COS, COS.t[:], din["cos64"].rearrange("(t p) c -> p t c", p=128))
        cload(SIN, SIN.t[:], din["sin64"].rearrange("(t p) c -> p t c", p=128))

        ckpt(2)
        csil = AV([2, 1024])
        S.op("act", lambda: nc.scalar.activation(out=csil.t[:], in_=crow.t[:], func=AF.Silu),
             reads=[crow], writes=[csil])
        b = nextbank()

        def f():
            ins = None
            for kc in range(8):
                ins = nc.tensor.transpose(pbank(b)[:, kc * 2:kc * 2 + 2], csil.t[0:2, kc * 128:(kc + 1) * 128],
                                          identF.t[0:2, 0:2])
            return ins
        S.op("pe", f, reads=[csil, identF], writes=[PB[b]])
        S.op("dve", lambda: nc.vector.tensor_copy(sTb.t[:].rearrange("p a b -> p (a b)"), pbank(b)[:, 0:16]),
             reads=[PB[b]], writes=[sTb])

        ckpt(3)
        b = nextbank()
        S.op("pe", lambda: nc.tensor.transpose(pbank(b)[:, 0:32], nrow.t[0:32, :], identF.t[0:32, 0:32]),
             reads=[nrow, identF], writes=[PB[b]])
        S.op("dve", lambda: nc.vector.tensor_copy(NT.t[:], pbank(b)[:, 0:32]), reads=[PB[b]], writes=[NT])
        b = nextbank()
        S.op("pe", lambda: nc.tensor.transpose(pbank(b)[:, 0:96], bmrow.t[0:96, :], identF.t[0:96, 0:96]),
             reads=[bmrow, identF], writes=[PB[b]])
        S.op("dve", lambda: nc.vector.tensor_copy(BMT.t[:], pbank(b)[:, 0:96]), reads=[PB[b]], writes=[BMT])
        ckpt(4)
        b = nextbank()

        def f():
            ins = None
            for l in range(2):
                for j in range(4):
                    c0 = (l * 4 + j) * 44
                    ins = nc.tensor.transpose(pbank(b)[:, c0:c0 + 44], cvrow.t[0:44, l, j, :], identF.t[0:44, 0:44])
            return ins
        S.op("pe", f, reads=[cvrow, identF], writes=[PB[b]])
        S.op("dve", lambda: nc.vector.tensor_copy(CW.t[:].rearrange("p a b c -> p (a b c)"), pbank(b)[:, 0:352]),
             reads=[PB[b]], writes=[CW])
        ckpt(5)
        for hb in range(2):
            b = nextbank()

            def f():
                ins = None
                for i in range(4):
                    ins = nc.tensor.transpose(pbank(b)[:, i * 128:(i + 1) * 128], wsraw.t[:, hb * 4 + i, :], identF.t[:])
                return ins
            S.op("pe", f, reads=[wsraw, identF], writes=[PB[b]])
            S.op("dve", lambda: nc.vector.tensor_copy(
                WST.t[:, hb * 4:(hb + 1) * 4, :].rearrange("p a b -> p (a b)"), pbank(b)[:, 0:512]),
                reads=[PB[b]], writes=[WST])
        b = nextbank()
        S.op("pe", lambda: nc.tensor.transpose(pbank(b)[:, 0:8], bsrow.t[0:8, :], identF.t[0:8, 0:8]),
             reads=[bsrow, identF], writes=[PB[b]])
        S.op("dve", lambda: nc.vector.tensor_copy(BS.t[:], pbank(b)[:, 0:8]), reads=[PB[b]], writes=[BS])

        ckpt(6)
        modrow = sb("modrow", [2, 1024])

        def mods_group(l, which):
            for t2 in range(2):
                slot = w_get()
                b = nextbank()
                wv = slotv(slot, 8, 512)

                def f():
                    ins = None
                    for kc in range(8):
                        ins = nc.tensor.matmul(pbank(b)[0:2, :], sTb.t[:, kc, :], wv[:, kc, :],
                                               start=(kc == 0), stop=(kc == 7))
                    return ins
                S.op("pe", f, reads=[sTb, slot], writes=[PB[b]])
                w_done()
                S.op("dve", lambda: nc.vector.tensor_copy(modrow.t[:, t2 * 512:(t2 + 1) * 512], pbank(b)[0:2, :]),
                     reads=[PB[b]], writes=[modrow])
            b = nextbank()

            def f():
                ins = None
                for jb in range(8):
                    ins = nc.tensor.transpose(pbank(b)[:, jb * 2:jb * 2 + 2], modrow.t[0:2, jb * 128:(jb + 1) * 128],
                                              identF.t[0:2, 0:2])
                return ins
            S.op("pe", f, reads=[modrow, identF], writes=[PB[b]])
            c0 = l * 48 + which * 8
            S.op("dve", lambda: nc.vector.tensor_tensor(
                out=MOD[l].t[:, which * 8:(which + 1) * 8, :], in0=pbank(b)[:, 0:16].rearrange("p (a b) -> p a b", b=2),
                in1=BMT.t[:, c0:c0 + 8].unsqueeze(2).broadcast_to([128, 8, 2]), op=ALU.add),
                 reads=[PB[b], BMT], writes=[MOD[l]])
            if which in (1, 4):
                n = 0 if which == 1 else 1
                ntv = NT.t[:, (n * 2 + l) * 8:(n * 2 + l) * 8 + 8]
                S.op("dve", lambda: nc.vector.scalar_tensor_tensor(
                    out=GS[l][n].t[:], in0=MOD[l].t[:, which * 8:(which + 1) * 8, :], scalar=1.0,
                    in1=ntv.unsqueeze(2).broadcast_to([128, 8, 2]), op0=ALU.add, op1=ALU.mult),
                    reads=[MOD[l], NT], writes=[GS[l][n]])

        ckpt(7)
        S.op("act", lambda: nc.scalar.activation(out=LG.t[:], in_=LG.t[:], func=AF.Exp, scale=-1.0),
             reads=[LG], writes=[LG])
        S.op("act", lambda: nc.scalar.activation(out=LG.t[:], in_=LG.t[:], func=AF.Ln, bias=1.0, scale=1.0),
             reads=[LG], writes=[LG])
        S.op("dve", lambda: nc.vector.tensor_scalar(LG.t[:], LG.t[:], -1.0, None, ALU.mult), reads=[LG], writes=[LG])

        ckpt(8)

        def lgi(d, l, h):
            return d * 8 + l * 4 + h

        dtmp = [AV([128, 128]) for i in range(4)]
        for l in range(NL):
            for h in range(4):
                tf, tb = dtmp[(h % 2) * 2], dtmp[(h % 2) * 2 + 1]
                i_f, i_b = lgi(0, l, h), lgi(1, l, h)
                S.op("act", lambda: nc.scalar.activation(out=tf.t[:], in_=DM.t[:, 0, :], func=AF.Exp,
                                                         scale=LG.t[:, i_f:i_f + 1]), reads=[DM, LG], writes=[tf])
                S.op("act", lambda: nc.scalar.activation(out=tb.t[:], in_=DM.t[:, 1, :], func=AF.Exp,
                                                         scale=LG.t[:, i_b:i_b + 1]), reads=[DM, LG], writes=[tb])
                S.op("dve", lambda: nc.vector.scalar_tensor_tensor(out=tf.t[:], in0=tf.t[:], scalar=0.125,
                                                                   in1=DM.t[:, 2, :], op0=ALU.mult, op1=ALU.mult),
                     reads=[tf, DM], writes=[tf])
                S.op("dve", lambda: nc.vector.scalar_tensor_tensor(out=tb.t[:], in0=tb.t[:], scalar=0.125,
                                                                   in1=DM.t[:, 3, :], op0=ALU.mult, op1=ALU.mult),
                     reads=[tb, DM], writes=[tb])
                S.op("dve", lambda: nc.vector.tensor_tensor(out=DTm.t[:, l, h, :], in0=tf.t[:], in1=tb.t[:],
                                                            op=ALU.add), reads=[tf, tb], writes=[DTm])
            for hp in range(2):
                for d in range(2):
                    for j in range(2):
                        ii = lgi(d, l, 2 * hp + j)
                        ps_ = slice(j * 64, (j + 1) * 64)
                        S.op("act", lambda: nc.scalar.activation(out=QD.t[ps_, l, hp, d, :], in_=PR.t[ps_, d, :],
                                                                 func=AF.Exp, scale=LG.t[ps_, ii:ii + 1]),
                             reads=[PR, LG], writes=[QD])
                        S.op("act", lambda: nc.scalar.activation(out=CD.t[ps_, l, d, hp, :], in_=C128.t[ps_, :],
                                                                 func=AF.Exp, scale=LG.t[ps_, ii:ii + 1]),
                             reads=[C128, LG], writes=[CD])
            for d in range(2):
                for h in range(4):
                    ii = lgi(d, l, h)
                    S.op("act", lambda: nc.scalar.activation(
                        out=KDE.t[:, l, d, h * 64:(h + 1) * 64], in_=KP.t[:, d:d + 1].broadcast_to([128, 64]),
                        func=AF.Exp, scale=LG.t[:, ii:ii + 1], bias=math.log(0.125)),
                        reads=[KP, LG], writes=[KDE])
            lam_init = 0.8 - 0.6 * math.exp(-0.3 * l)
            pr_ = nsmall()
            dlv = DL.t[:, l, :].rearrange("p (a b c) -> p a b c", a=2, b=2)
            lt = dtmp[0]
            S.op("dve", lambda: nc.vector.tensor_tensor(out=lt.t[:].rearrange("p (a c) -> p a c", a=2),
                                                        in0=dlv[:, :, 0, :], in1=dlv[:, :, 1, :], op=ALU.mult),
                 reads=[DL], writes=[lt])
            S.op("dve", lambda: nc.vector.tensor_reduce(out=pr_.t[:, 0:2],
                                                        in_=lt.t[:].rearrange("p (a c) -> p a c", a=2),
                                                        axis=AX.X, op=ALU.add), reads=[lt], writes=[pr_])
            S.op("act", lambda: nc.scalar.activation(out=pr_.t[:, 2:4], in_=pr_.t[:, 0:2], func=AF.Exp),
                 reads=[pr_], writes=[pr_])
            S.op("dve", lambda: nc.vector.tensor_tensor(out=pr_.t[:, 4:5], in0=pr_.t[:, 3:4], in1=pr_.t[:, 2:3],
                                                        op=ALU.subtract), reads=[pr_], writes=[pr_])
            S.op("dve", lambda: nc.vector.tensor_scalar(NLAM.t[:, l:l + 1], pr_.t[:, 4:5], -lam_init, None, ALU.add),
                 reads=[pr_], writes=[NLAM])
            S.op("dve", lambda: nc.vector.tensor_scalar(DN.t[:, l, :], DN.t[:, l, :], 1.0 - lam_init, None, ALU.mult),
                 reads=[DN], writes=[DN])

        S.barrier()
        ckpt(10)

        def rstd_from_ss(ss_ap, out_ap, scale, bias, R, W):
            S.op("act", lambda: nc.scalar.activation(out=out_ap, in_=ss_ap, func=AF.Sqrt, bias=bias, scale=scale),
                 reads=R, writes=W)
            S.op("dve", lambda: nc.vector.reciprocal(out_ap, out_ap), reads=W, writes=W)

        def load_x(src):
            xin = [AV([128, 1024]) for _ in range(2)]
            for tt in range(8):
                xi = xin[tt % 2]
                S.dma("sp", xi.t[:], src[tt * 128:(tt + 1) * 128, :], writes=[xi])
                for hb in range(2):
                    b = nextbank()

                    def f():
                        ins = None
                        for i in range(4):
                            c = hb * 4 + i
                            ins = nc.tensor.transpose(pbank(b)[:, i * 128:(i + 1) * 128],
                                                      xi.t[:, c * 128:(c + 1) * 128], identF.t[:])
                        return ins
                    S.op("pe", f, reads=[xi, identF], writes=[PB[b]])
                    eng = "act" if hb == 0 else "dve"
                    dst = xT.t[:, hb * 4:(hb + 1) * 4, tt * 128:(tt + 1) * 128]
                    srcp = pbank(b).rearrange("p (a b) -> p a b", b=128)
                    if eng == "act":
                        S.op("act", lambda: nc.scalar.copy(out=dst, in_=srcp), reads=[PB[b]], writes=[xT])
                    else:
                        S.op("dve", lambda: nc.vector.tensor_copy(dst, srcp), reads=[PB[b]], writes=[xT])

        def store_x(dst):
            xo = [AV([128, 1024]) for _ in range(2)]
            for tt in range(8):
                xi = xo[tt % 2]
                for hb in range(2):
                    b = nextbank()

                    def f():
                        ins = None
                        for i in range(4):
                            c = hb * 4 + i
                            ins = nc.tensor.transpose(pbank(b)[:, i * 128:(i + 1) * 128],
                                                      xT.t[:, c, tt * 128:(tt + 1) * 128], identF.t[:])
                        return ins
                    S.op("pe", f, reads=[xT, identF], writes=[PB[b]])
                    dstp = xi.t[:, hb * 512:(hb + 1) * 512]
                    if hb == 0:
                        S.op("act", lambda: nc.scalar.copy(out=dstp, in_=pbank(b)), reads=[PB[b]], writes=[xi])
                    else:
                        S.op("dve", lambda: nc.vector.tensor_copy(dstp, pbank(b)), reads=[PB[b]], writes=[xi])
                S.dma("sp", dst[tt * 128:(tt + 1) * 128, :], xi.t[:], reads=[xi])

        def norm_mod(l, n, cond):
            sq = yT
            RB = AV([128, T])
            S.op("act", lambda: nc.scalar.activation(out=sq.t[:], in_=xT.t[:], func=AF.Square),
                 reads=[xT], writes=[sq] + yT.halves)
            for tg in range(2):
                b = nextbank()

                def f():
                    ins = None
                    for kc in range(8):
                        ins = nc.tensor.matmul(pbank(b), onesB.t[:], sq.t[:, kc, tg * 512:(tg + 1) * 512],
                                               start=(kc == 0), stop=(kc == 7))
                    return ins
                S.op("pe", f, reads=[sq, onesB], writes=[PB[b]])
                rstd_from_ss(pbank(b), RB.t[:, tg * 512:(tg + 1) * 512], 1.0 / D, EPS, [PB[b]], [RB])
            shi = 0 if n == 0 else 3
            tmp = [AV([128, 1024]) for _ in range(2)]
            for kc in range(8):
                tm = tmp[kc % 2]
                S.op("dve", lambda: nc.vector.scalar_tensor_tensor(
                    out=tm.t[:], in0=xT.t[:, kc, :], scalar=GS[l][n].t[:, kc, cond:cond + 1], in1=RB.t[:],
                    op0=ALU.mult, op1=ALU.mult), reads=[xT, GS[l][n], RB], writes=[tm])
                S.op("act", lambda: nc.scalar.activation(out=hT.t[:, kc, :], in_=tm.t[:], func=AF.Identity,
                                                         bias=MOD[l].t[:, shi * 8 + kc, cond:cond + 1], scale=1.0),
                     reads=[tm, MOD[l]], writes=[hT])

        def zmm(slot, tt, b):
            wv = slotv(slot, 8, 512)

            def f():
                ins = None
                for kc in range(8):
                    ins = nc.tensor.matmul(pbank(b), hT.t[:, kc, tt * 128:(tt + 1) * 128], wv[:, kc, :],
                                           start=(kc == 0), stop=(kc == 7))
                return ins
            S.op("pe", f, reads=[hT, slot], writes=[PB[b]])

        def group_rstd(src_ap, ngrp, gsz, scale, bias, sqt, ss):
            src_tt, sq_tt = sqt
            S.op("dve", lambda: nc.vector.tensor_tensor(out=sq_tt.t[:, 0:ngrp * gsz], in0=src_ap, in1=src_ap,
                                                        op=ALU.mult), reads=[src_tt], writes=[sq_tt])
            S.op("dve", lambda: nc.vector.tensor_reduce(
                out=ss.t[:, 0:ngrp], in_=sq_tt.t[:, 0:ngrp * gsz].rearrange("p (a b) -> p a b", b=gsz),
                axis=AX.X, op=ALU.add), reads=[sq_tt], writes=[ss])
            rstd_from_ss(ss.t[:, 0:ngrp], ss.t[:, 0:ngrp], scale, bias, [ss], [ss])

        def transposes_to(src_tt, src_ap_fn, nblk, dst_tt, dst_ap, pool=(0, 1, 2, 3)):
            b = nextbank(pool)
            pv = pbank(b).bitcast(BF16)

            def f():
                ins = None
                for i in range(nblk):
                    ins = nc.tensor.transpose(pv[:, i * 128:(i + 1) * 128], src_ap_fn(i), identB.t[:])
                return ins
            S.op("pe", f, reads=[src_tt, identB], writes=[PB[b]])
            S.op("dve", lambda: nc.vector.tensor_copy(dst_ap, pv[:, 0:nblk * 128].rearrange("p (a b) -> p a b", b=128)),
                 reads=[PB[b]], writes=[dst_tt])

        def mixer(l, half):
            cond = half
            nseq, L = (4, 256) if half == 0 else (1, 1024)
            cpl = L // 128
            AR.reset()
            ytok = AV([128, 8, 1024], BF16)
            ar_mark = AR.off

            class _V:
                pass
            SCR = _V()
            SCR.ap = yT.t[:].bitcast(F32)
            SCR.t = SCR.ap
            SCR.res = yT.res
            slot = w_get()
            GE = AV([128, 8, 512])
            VN = AV([128, 8, 256], BF16)
            ssG = AV([128, 8])
            for tt in range(8):
                b = nextbank()
                zmm(slot, tt, b)
                S.op("act", lambda: nc.scalar.activation(out=GE.t[:, tt, :], in_=pbank(b), func=AF.Gelu_apprx_tanh),
                     reads=[PB[b]], writes=[GE])
            gv = GE.t[:, :, 256:512]
            sv = SCR.t[:, :, 0:256]
            S.op("dve", lambda: nc.vector.tensor_tensor(out=sv, in0=gv, in1=gv, op=ALU.mult), reads=[GE], writes=[SCR])
            S.op("dve", lambda: nc.vector.tensor_reduce(out=ssG.t[:], in_=sv, axis=AX.X, op=ALU.add),
                 reads=[SCR], writes=[ssG])
            rstd_from_ss(ssG.t[:], ssG.t[:], 1.0 / 256, EPS, [ssG], [ssG])
            S.op("dve", lambda: nc.vector.tensor_tensor(out=gv, in0=gv,
                                                        in1=ssG.t[:].unsqueeze(2).broadcast_to([128, 8, 256]),
                                                        op=ALU.mult), reads=[GE, ssG], writes=[GE])
            S.op("dve", lambda: nc.vector.tensor_tensor(
                out=VN.t[:], in0=gv, in1=SGN.t[:, l, :].unsqueeze(1).broadcast_to([128, 8, 256]), op=ALU.mult),
                reads=[GE, SGN], writes=[VN])
            for tt in range(8):
                b2 = nextbank()

                def f():
                    ins = None
                    for g in range(4):
                        ins = nc.tensor.matmul(pbank(b2)[:, g * 64:(g + 1) * 64], WST.t[:, l * 4 + g, :],
                                               VN.t[:, tt, g * 64:(g + 1) * 64], start=True, stop=True)
                    return ins
                S.op("pe", f, reads=[WST, VN], writes=[PB[b2]])
                for g in range(4):
                    S.op("dve", lambda: nc.vector.scalar_tensor_tensor(
                        out=ytok.t[:, tt, g * 64:(g + 1) * 64], in0=pbank(b2)[:, g * 64:(g + 1) * 64],
                        scalar=BS.t[:, l * 4 + g:l * 4 + g + 1], in1=GE.t[:, tt, g * 64:(g + 1) * 64],
                        op0=ALU.add, op1=ALU.mult), reads=[PB[b2], BS, GE], writes=[ytok])
            w_done()
            S.barrier()
            ckpt(13)
            AR.off = ar_mark

            QT = AV([128, 2, 1024], BF16)
            KT = AV([128, 2, 1024], BF16)
            VB = AV([128, 8, 256], BF16)
            SG = AV([128, 8, 256], BF16)
            KVS = AV([128, 8, 2, 2, 64])
            RS = AV([128, 2, 2, 64])
            RSb = AV([128, 8, 2, 2, 64], BF16)

            class _W:
                pass
            RSd = []
            RSbd = []
            for d_ in range(2):
                v_ = _W()
                v_.ap = RS.t[:, d_, :, :]
                v_.t = v_.ap
                v_.res = Res()
                RSd.append(v_)
                w_ = _W()
                w_.res = Res()
                RSbd.append(w_)
            qkb = [AV([128, 512], BF16) for _ in range(2)]
            KF = [AV([128, 2, 256], BF16) for _ in range(2)]
            gtmp = [AV([128, 256]) for _ in range(2)]
            slot1 = w_get()
            slot2 = w_get()
            for tt in range(8):
                b1 = nextbank()
                zmm(slot1, tt, b1)
                if tt == 0:
                    ckpt(1301)
                b2 = nextbank()
                zmm(slot2, tt, b2)
                if tt == 0:
                    ckpt(1302)
                qk = qkb[tt % 2]
                S.op("act", lambda: nc.scalar.copy(out=qk.t[:], in_=pbank(b1)), reads=[PB[b1]], writes=[qk])
                if tt == 0:
                    ckpt(1303)
                kf = KF[tt % 2]
                for d in range(2):
                    S.op("dve", lambda: nc.vector.tensor_tensor(
                        out=kf.t[:, d, :], in0=pbank(b1)[:, 256:512], in1=KDE.t[:, l, d, :], op=ALU.mult),
                        reads=[PB[b1], KDE], writes=[kf])
                    if tt == 0 and d == 0:
                        ckpt(1304)
                if tt == 0:
                    ckpt(131)
                bt = nextbank((6, 7), "t67")
                pv = pbank(bt).bitcast(BF16)

                def f():
                    ins = None
                    for i in range(4):
                        ins = nc.tensor.transpose(pv[:, i * 128:(i + 1) * 128], qk.t[:, i * 128:(i + 1) * 128],
                                                  identB.t[:])
                    return ins
                S.op("pe", f, reads=[qk, identB], writes=[PB[bt]])
                if tt == 0:
                    ckpt(132)
                S.op("dve", lambda: nc.vector.tensor_copy(
                    QT.t[:, :, tt * 128:(tt + 1) * 128], pv[:, 0:256].rearrange("p (a b) -> p a b", b=128)),
                    reads=[PB[bt]], writes=[QT])
                S.op("dve", lambda: nc.vector.tensor_copy(
                    KT.t[:, :, tt * 128:(tt + 1) * 128], pv[:, 256:512].rearrange("p (a b) -> p a b", b=128)),
                    reads=[PB[bt]], writes=[KT])
                if tt == 0:
                    ckpt(133)
                S.op("act", lambda: nc.scalar.copy(out=VB.t[:, tt, :], in_=pbank(b2)[:, 0:256]),
                     reads=[PB[b2]], writes=[VB])
                gt_ = gtmp[tt % 2]
                S.op("act", lambda: nc.scalar.activation(out=gt_.t[:], in_=pbank(b2)[:, 256:512], func=AF.Silu),
                     reads=[PB[b2]], writes=[gt_])
                S.op("dve", lambda: nc.vector.tensor_tensor(out=SG.t[:, tt, :], in0=gt_.t[:], in1=RN.t[:, l, :],
                                                            op=ALU.mult), reads=[gt_, RN], writes=[SG])
                if tt == 0:
                    ckpt(134)
                bk = nextbank((4, 5), "t45")

                def f():
                    ins = None
                    for d in range(2):
                        for hp in range(2):
                            ins = nc.tensor.matmul(pbank(bk)[:, (d * 2 + hp) * 128:(d * 2 + hp + 1) * 128],
                                                   kf.t[:, d, hp * 128:(hp + 1) * 128],
                                                   VB.t[:, tt, hp * 128:(hp + 1) * 128], start=True, stop=True)
                    return ins
                S.op("pe", f, reads=[kf, VB], writes=[PB[bk]])
                if tt == 0:
                    ckpt(135)
                pk = pbank(bk).rearrange("p (a b) -> p a b", b=128)
                for j in range(2):
                    ps_ = slice(j * 64, (j + 1) * 64)
                    S.op("dve", lambda: nc.vector.tensor_copy(
                        KVS.t[ps_, tt, :, :, :].rearrange("p a b c -> p (a b) c"), pk[ps_, :, j * 64:(j + 1) * 64]),
                        reads=[PB[bk]], writes=[KVS])
            w_done()
            w_done()

            ckpt(14)
            st_stage = [AV([128, 2, 2, 64]) for _ in range(2)]
            for s_ in range(nseq):
                if half == 0:
                    S.op("dve", lambda: nc.vector.memset(RS.t[:], 0.0), writes=[RSd[0], RSd[1]])
                else:
                    for d, nm in ((0, "srf"), (1, "srb")):
                        for j in range(2):
                            srcs = din[nm][l].rearrange("(hp j) d e -> j d hp e", j=2)[j]
                            S.dma("sp", RS.t[j * 64:(j + 1) * 64, d, :, :], srcs, writes=[RSd[d]])
                def step(d, tt):
                    S.op("dve", lambda: nc.vector.tensor_copy(RSb.t[:, tt, d, :, :], RSd[d].t[:]),
                         reads=[RSd[d]], writes=[RSbd[d]])
                    S.op("dve", lambda: nc.vector.tensor_tensor(out=RSd[d].t[:], in0=RSd[d].t[:],
                                                                in1=CD.t[:, l, d, :, :], op=ALU.mult),
                         reads=[RSd[d], CD], writes=[RSd[d]])
                    S.op("dve", lambda: nc.vector.tensor_tensor(out=RSd[d].t[:], in0=RSd[d].t[:],
                                                                in1=KVS.t[:, tt, d, :, :], op=ALU.add),
                         reads=[RSd[d], KVS], writes=[RSd[d]])
                for c in range(cpl):
                    step(0, s_ * cpl + c)
                    step(1, s_ * cpl + (cpl - 1 - c))
                if half == 0:
                    stg = st_stage[s_ % 2]
                    S.op("dve", lambda: nc.vector.tensor_copy(stg.t[:], RS.t[:]), reads=[RSd[0], RSd[1]], writes=[stg])
                    for d, nm in ((0, "nrf"), (1, "nrb")):
                        for j in range(2):
                            dsts = dout[nm][s_, l].rearrange("(hp j) d e -> j d hp e", j=2)[j]
                            S.dma("sp", dsts, stg.t[j * 64:(j + 1) * 64, d, :, :], reads=[stg])

            ckpt(15)
            MT = [AV([128, 512], BF16) for _ in range(2)]
            QF = [AV([128, 2, 2, 128], BF16) for _ in range(2)]
            OR = AV([128, 8, 256])
            for tt in range(8):
                c = tt % cpl
                tsl = slice(tt * 128, (tt + 1) * 128)
                bia = nextbank((0, 1), "t01")
                bib = nextbank((0, 1), "t01")

                def f():
                    ins = None
                    for j, bnk in ((0, bia), (1, bib)):
                        ps_ = slice(j * 64, (j + 1) * 64)
                        for hp in range(2):
                            ins = nc.tensor.matmul(pbank(bnk)[:, hp * 128:(hp + 1) * 128], KT.t[ps_, hp, tsl],
                                                   QT.t[ps_, hp, tsl], start=True, stop=True)
                    return ins
                S.op("pe", f, reads=[KT, QT], writes=[PB[bia], PB[bib]])
                mt = MT[tt % 2]
                mt4 = mt.t[:].rearrange("p (hp j q) -> p hp j q", hp=2, j=2)
                for j, bnk in ((0, bia), (1, bib)):
                    S.op("dve", lambda: nc.vector.tensor_tensor(
                        out=mt4[:, :, j, :], in0=pbank(bnk)[:, 0:256].rearrange("p (a b) -> p a b", b=128),
                        in1=DTm.t[:, l, :, :].rearrange("p (hp j) q -> p hp j q", j=2)[:, :, j, :], op=ALU.mult),
                        reads=[PB[bnk], DTm], writes=[mt])
                qf = QF[tt % 2]
                S.op("dve", lambda: nc.vector.tensor_tensor(
                    out=qf.t[:], in0=QD.t[:, l, :, :, :],
                    in1=QT.t[:, :, tsl].unsqueeze(2).broadcast_to([128, 2, 2, 128]), op=ALU.mult),
                    reads=[QT, QD], writes=[qf])
                use_f = not (half == 0 and c == 0)
                use_b = not (half == 0 and c == cpl - 1)
                bo = nextbank((2, 3), "t23")

                def f():
                    ins = None
                    for h in range(4):
                        hp, j = h // 2, h % 2
                        ps_ = slice(j * 64, (j + 1) * 64)
                        o_ap = pbank(bo)[:, h * 64:(h + 1) * 64]
                        last = not (use_f or use_b)
                        ins = nc.tensor.matmul(o_ap, mt.t[:, h * 128:(h + 1) * 128], VB.t[:, tt, h * 64:(h + 1) * 64],
                                               start=True, stop=last)
                        if use_f:
                            ins = nc.tensor.matmul(o_ap, qf.t[ps_, hp, 0, :], RSb.t[ps_, tt, 0, hp, :],
                                                   start=False, stop=not use_b)
                        if use_b:
                            ins = nc.tensor.matmul(o_ap, qf.t[ps_, hp, 1, :], RSb.t[ps_, tt, 1, hp, :],
                                                   start=False, stop=True)
                    return ins
                S.op("pe", f, reads=[mt, VB, qf, RSbd[0], RSbd[1]], writes=[PB[bo]])
                S.op("act", lambda: nc.scalar.copy(out=OR.t[:, tt, :], in_=pbank(bo)[:, 0:256]),
                     reads=[PB[bo]], writes=[OR])
            sv = SCR.t[:, :, 0:256]
            S.op("act", lambda: nc.scalar.activation(out=sv, in_=OR.t[:], func=AF.Square), reads=[OR], writes=[SCR])
            ssR = AV([128, 8, 4])
            S.op("dve", lambda: nc.vector.tensor_reduce(out=ssR.t[:], in_=sv.rearrange("p t (h c) -> p t h c", c=64),
                                                        axis=AX.X, op=ALU.add), reads=[SCR], writes=[ssR])
            rstd_from_ss(ssR.t[:], ssR.t[:], 1.0 / 64, EPS, [ssR], [ssR])
            o4 = OR.t[:].rearrange("p t (h c) -> p t h c", c=64)
            S.op("dve", lambda: nc.vector.tensor_tensor(out=o4, in0=o4,
                                                        in1=ssR.t[:].unsqueeze(3).broadcast_to([128, 8, 4, 64]),
                                                        op=ALU.mult), reads=[OR, ssR], writes=[OR])
            S.op("dve", lambda: nc.vector.tensor_tensor(out=ytok.t[:, :, 256:512], in0=OR.t[:], in1=SG.t[:],
                                                        op=ALU.mult), reads=[OR, SG], writes=[ytok])
            S.barrier()
            ckpt(16)
            AR.off = ar_mark

            NKC = 2 if half == 0 else 10
            KOFF = 0 if half == 0 else 256
            NKEY = 1024 + KOFF
            QTa = AV([128, 4, 1024], BF16)
            KTa = AV([128, 4, NKEY], BF16)
            VA = AV([128, NKEY // 128, 4, 132], BF16)
            S.op("dve", lambda: nc.vector.memset(VA.t[:, :, :, 128:129], 1.0), writes=[VA])
            ar_att = AR.off
            stg = [AV([128, 512]) for _ in range(2)]

            class _G:
                pass
            GR = []
            for g_ in range(2):
                o_ = _G()
                o_.QA = AV([128, 4, 512])
                o_.QB = AV([128, 4, 512], BF16)
                o_.ss = AV([128, 32])
                o_.SQ = _G()
                o_.SQ.ap = SCR.t[:, g_ * 4:(g_ + 1) * 4, :]
                o_.SQ.t = o_.SQ.ap
                o_.SQ.res = Res()
                GR.append(o_)
            S.op("dve", lambda: nc.vector.memset(GR[0].ss.t[:], 0.0), reads=[yT], writes=[GR[0].ss, GR[0].SQ, GR[1].SQ])
            if half == 1:
                for kc in range(2):
                    st = stg[kc % 2]
                    S.dma("sp", st.t[:], din["ck"][l, kc * 128:(kc + 1) * 128, :], writes=[st])
                    S.op("act", lambda: nc.scalar.copy(out=GR[0].QB.t[:, kc, :], in_=st.t[:]), reads=[st],
                         writes=[GR[0].QB])
                    transposes_to(GR[0].QB, lambda i: GR[0].QB.t[:, kc, i * 128:(i + 1) * 128], 4, KTa,
                                  KTa.t[:, :, kc * 128:(kc + 1) * 128])
                for kc in range(2):
                    st = stg[kc % 2]
                    S.dma("sp", st.t[:], din["cv"][l, kc * 128:(kc + 1) * 128, :], writes=[st])
                    S.op("act", lambda: nc.scalar.copy(out=VA.t[:, kc, :, 0:128],
                                                       in_=st.t[:].rearrange("p (a b) -> p a b", b=128)),
                         reads=[st], writes=[VA])

            def stageA1(slot, g):
                G = GR[g]
                for t in range(4):
                    tt = g * 4 + t
                    b = nextbank()
                    zmm(slot, tt, b)
                    S.op("act", lambda: nc.scalar.copy(out=G.QA.t[:, t, :], in_=pbank(b)), reads=[PB[b]], writes=[G.QA])
                    S.op("act", lambda: nc.scalar.activation(out=G.SQ.t[:, t, :], in_=pbank(b), func=AF.Square),
                         reads=[PB[b]], writes=[G.SQ])

            def stageA2(g):
                G = GR[g]
                S.op("dve", lambda: nc.vector.tensor_reduce(
                    out=G.ss.t[:], in_=G.SQ.t[:].rearrange("p t (a b) -> p (t a) b", b=64),
                    axis=AX.X, op=ALU.add), reads=[G.SQ], writes=[G.ss])

            def stageB(g, gain_tt, which):
                G = GR[g]
                if which == "q":
                    rstd_from_ss(G.ss.t[:], G.ss.t[:], 1.0, 64 * EPS, [G.ss], [G.ss])
                else:
                    rstd_from_ss(G.ss.t[:], G.ss.t[:], 1.0 / 64, EPS, [G.ss], [G.ss])
                qf_ = G.QA.t[:].rearrange("p a b -> p (a b)")
                sf_ = G.SQ.t[:].rearrange("p a b -> p (a b)")
                q3 = qf_.rearrange("p (a b) -> p a b", b=64)
                S.op("dve", lambda: nc.vector.tensor_tensor(out=q3, in0=q3,
                                                            in1=G.ss.t[:].unsqueeze(2).broadcast_to([128, 32, 64]),
                                                            op=ALU.mult), reads=[G.QA, G.ss], writes=[G.QA])
                gbc = gain_tt.t[:, l, :].unsqueeze(1).broadcast_to([128, 32, 64])
                if half == 0 and which == "q":
                    S.op("dve", lambda: nc.vector.tensor_tensor(
                        out=G.QB.t[:].rearrange("p a (g c) -> p (a g) c", c=64), in0=q3, in1=gbc, op=ALU.mult),
                        reads=[G.QA, gain_tt], writes=[G.QB])
                else:
                    S.op("dve", lambda: nc.vector.tensor_tensor(out=q3, in0=q3, in1=gbc, op=ALU.mult),
                         reads=[G.QA, gain_tt], writes=[G.QA])
                    if half == 0:
                        for s2 in range(2):
                            s_ = g * 2 + s2
                            S.dma("sp", dout["nk"][s_, l].rearrange("(t p) c -> p t c", p=128),
                                  G.QA.t[:, 2 * s2:2 * s2 + 2, :], reads=[G.QA])
                        S.op("act", lambda: nc.scalar.copy(out=G.QB.t[:].rearrange("p a b -> p (a b)"), in_=qf_),
                             reads=[G.QA], writes=[G.QB])
                    else:
                        x5 = G.QA.t[:].rearrange("p t (g a s c) -> p t g a s c", a=2, s=2, c=16)
                        r5 = G.SQ.t[:].rearrange("p t (g a s c) -> p t g a s c", a=2, s=2, c=16)
                        s5 = SIN.t[:, g * 4:(g + 1) * 4, :].rearrange("p t (a s c) -> p t a s c", s=2, c=16)
                        for sidx in range(2):
                            for ax_ in range(2):
                                S.op("dve", lambda: nc.vector.tensor_tensor(
                                    out=r5[:, :, :, ax_, sidx, :], in0=x5[:, :, :, ax_, 1 - sidx, :],
                                    in1=s5[:, :, ax_, sidx, :].unsqueeze(2).broadcast_to([128, 4, 8, 16]), op=ALU.mult),
                                    reads=[G.QA, SIN], writes=[G.SQ])
                        q4 = G.QA.t[:].rearrange("p t (g c) -> p t g c", c=64)
                        S.op("dve", lambda: nc.vector.tensor_tensor(
                            out=q4, in0=q4,
                            in1=COS.t[:, g * 4:(g + 1) * 4, :].unsqueeze(2).broadcast_to([128, 4, 8, 64]), op=ALU.mult),
                            reads=[G.QA, COS], writes=[G.QA])
                        S.op("dve", lambda: nc.vector.tensor_tensor(out=G.QB.t[:].rearrange("p a b -> p (a b)"),
                                                                    in0=qf_, in1=sf_, op=ALU.add),
                             reads=[G.QA, G.SQ], writes=[G.QB])

            def stageC(g, dstT, doff):
                G = GR[g]
                for t in range(4):
                    tt = g * 4 + t
                    b = nextbank()
                    pv = pbank(b).bitcast(BF16)

                    def f():
                        ins = None
                        for i in range(4):
                            ins = nc.tensor.transpose(pv[:, i * 128:(i + 1) * 128], G.QB.t[:, t, i * 128:(i + 1) * 128],
                                                      identB.t[:])
                        return ins
                    S.op("pe", f, reads=[G.QB, identB], writes=[PB[b]])
                    S.op("act", lambda: nc.scalar.copy(out=dstT.t[:, :, doff + tt * 128:doff + (tt + 1) * 128],
                                                       in_=pv[:, 0:512].rearrange("p (a b) -> p a b", b=128)),
                         reads=[PB[b]], writes=[dstT])

            def stageV(slot, g):
                for t in range(4):
                    tt = g * 4 + t
                    b = nextbank()
                    zmm(slot, tt, b)
                    kci = KOFF // 128 + tt
                    if half == 0:
                        st = stg[tt % 2]
                        S.op("act", lambda: nc.scalar.copy(out=st.t[:], in_=pbank(b)), reads=[PB[b]], writes=[st])
                        S.dma("sp", dout["nv"][tt // 2, l, (tt % 2) * 128:(tt % 2) * 128 + 128, :], st.t[:], reads=[st])
                        S.op("act", lambda: nc.scalar.copy(out=VA.t[:, kci, :, 0:128],
                                                           in_=st.t[:].rearrange("p (a b) -> p a b", b=128)),
                             reads=[st], writes=[VA])
                    else:
                        S.op("act", lambda: nc.scalar.copy(out=VA.t[:, kci, :, 0:128],
                                                           in_=pbank(b).rearrange("p (a b) -> p a b", b=128)),
                             reads=[PB[b]], writes=[VA])

            slot3 = w_get()
            stageA1(slot3, 0)
            stageA2(0)
            stageB(0, QN, "q")
            stageA1(slot3, 1)
            stageA2(1)
            w_done()
            slot4 = w_get()
            stageC(0, QTa, 0)
            stageB(1, QN, "q")
            stageA1(slot4, 0)
            stageA2(0)
            stageC(1, QTa, 0)
            stageB(0, KN, "k")
            stageA1(slot4, 1)
            stageA2(1)
            w_done()
            slot5 = w_get()
            stageC(0, KTa, KOFF)
            stageV(slot5, 0)
            stageB(1, KN, "k")
            stageV(slot5, 1)
            stageC(1, KTa, KOFF)
            w_done()
            S.barrier()
            ckpt(17)
            AR.off = ar_att
            OALL = AV([128, 8, 4, 128])
            ETP = [AV([128, 2, 2, 256], BF16) for _ in range(2)]
            ot2 = [AV([128, 128]) for _ in range(2)]
            work = []
            for qg in range(4):
                kcs = [qg * 2, qg * 2 + 1] if half == 0 else list(range(10))
                for h in range(4):
                    for pi in range(len(kcs) // 2):
                        work.append((qg, h, pi, len(kcs) // 2, (kcs[2 * pi], kcs[2 * pi + 1])))
            st_banks = {}

            def emit_st(widx):
                qg, h, pi, npair, kc2 = work[widx]
                q0 = qg * 256
                bsA, bsB = ((0, 1), (2, 3))[widx % 2]
                st_banks[widx] = (bsA, bsB)

                def f():
                    ins = None
                    for i, bnk in ((0, bsA), (1, bsB)):
                        ps_ = slice(i * 64, (i + 1) * 64)
                        for kk in range(2):
                            kc = kc2[kk]
                            ins = nc.tensor.matmul(pbank(bnk)[:, kk * 256:(kk + 1) * 256],
                                                   KTa.t[ps_, h, kc * 128:(kc + 1) * 128],
                                                   QTa.t[ps_, h, q0:q0 + 256], start=True, stop=True)
                    return ins
                S.op("pe", f, reads=[KTa, QTa], writes=[PB[bsA], PB[bsB]])

            OSQh = AV([128, 2048])

            def norm_half(hh):
                ov = OALL.t[:, hh * 4:(hh + 1) * 4, :, :]
                of_ = ov.rearrange("p a b c -> p (a b c)")
                ss2 = AV([128, 16])
                S.op("dve", lambda: nc.vector.tensor_tensor(out=OSQh.t[:], in0=of_, in1=of_, op=ALU.mult),
                     reads=[OALLh[hh]], writes=[OSQh])
                S.op("dve", lambda: nc.vector.tensor_reduce(out=ss2.t[:], in_=OSQh.t[:].rearrange("p (a b) -> p a b", b=128),
                                                            axis=AX.X, op=ALU.add), reads=[OSQh], writes=[ss2])
                rstd_from_ss(ss2.t[:], ss2.t[:], 1.0 / 128, EPS, [ss2], [ss2])
                o3 = of_.rearrange("p (a b) -> p a b", b=128)
                S.op("dve", lambda: nc.vector.tensor_tensor(out=o3, in0=o3,
                                                            in1=ss2.t[:].unsqueeze(2).broadcast_to([128, 16, 128]),
                                                            op=ALU.mult), reads=[OALLh[hh], ss2], writes=[OALLh[hh]])
                S.op("dve", lambda: nc.vector.tensor_tensor(
                    out=ytok.t[:, hh * 4:(hh + 1) * 4, 512:1024].rearrange("p t (h c) -> p t h c", c=128), in0=ov,
                    in1=DN.t[:, l, :].unsqueeze(1).unsqueeze(1).broadcast_to([128, 4, 4, 128]), op=ALU.mult),
                    reads=[OALLh[hh], DN], writes=[ytok])

            OALLh = [Res(), Res()]
            emit_st(0)
            itn = 0
            for widx in range(len(work)):
                qg, h, pi, npair, kc2 = work[widx]
                q0 = qg * 256
                if pi == 0:
                    accs = ((4, 5), (6, 7))[itn % 2]
                    itn += 1
                if widx + 1 < len(work):
                    emit_st(widx + 1)
                bsA, bsB = st_banks.pop(widx)
                etp = ETP[widx % 2]
                S.op("act", lambda: nc.scalar.activation(out=etp.t[:].rearrange("p i a b -> p (i a b)"),
                                                         in_=PSt[bsA // 2][:, 0:1024], func=AF.Exp),
                     reads=[PB[bsA], PB[bsB]], writes=[etp])

                def f():
                    ins = None
                    for kk in range(2):
                        kc = kc2[kk]
                        for qb_ in range(2):
                            for i in range(2):
                                first = (pi == 0 and kk == 0 and i == 0)
                                last = (pi == npair - 1 and kk == 1 and i == 1)
                                ins = nc.tensor.matmul(pbank(accs[qb_])[:, i * 129:(i + 1) * 129],
                                                       etp.t[:, i, kk, qb_ * 128:(qb_ + 1) * 128],
                                                       VA.t[:, kc, h, 0:129], start=first, stop=last)
                    return ins
                S.op("pe", f, reads=[etp, VA], writes=[PB[accs[0]], PB[accs[1]]])
                if pi == npair - 1:
                    for qb_ in range(2):
                        tt = (q0 + qb_ * 128) // 128
                        ab = accs[qb_]
                        acc = pbank(ab)
                        rc = nsmall()
                        S.op("dve", lambda: nc.vector.reciprocal(
                            rc.t[:, 0:2], acc[:, 0:258].rearrange("p (a b) -> p a b", b=129)[:, :, 128]),
                            reads=[PB[ab]], writes=[rc])
                        S.op("dve", lambda: nc.vector.tensor_tensor(out=rc.t[:, 2:3], in0=rc.t[:, 1:2],
                                                                    in1=NLAM.t[:, l:l + 1], op=ALU.mult),
                             reads=[rc, NLAM], writes=[rc])
                        o1 = ot2[qb_]
                        S.op("dve", lambda: nc.vector.tensor_scalar(o1.t[:], acc[:, 129:257], rc.t[:, 2:3], None,
                                                                    ALU.mult), reads=[PB[ab], rc], writes=[o1])
                        S.op("dve", lambda: nc.vector.scalar_tensor_tensor(
                            out=OALL.t[:, tt, h, :], in0=acc[:, 0:128], scalar=rc.t[:, 0:1], in1=o1.t[:],
                            op0=ALU.mult, op1=ALU.add), reads=[PB[ab], rc, o1], writes=[OALLh[qg // 2]])
                        if qb_ == 1 and h == 3 and qg in (1, 3):
                            norm_half(qg // 2)
            ckpt(18)
            if half == HALVES[0]:
                mods_group(l, 2)
            slots_o = [w_get(), w_get()]
            for tg in range(2):
                for tt in range(tg * 4, tg * 4 + 4):
                    b = nextbank()
                    pv = pbank(b).bitcast(BF16)

                    def f():
                        ins = None
                        for i in range(8):
                            ins = nc.tensor.transpose(pv[:, i * 128:(i + 1) * 128], ytok.t[:, tt, i * 128:(i + 1) * 128],
                                                      identB.t[:])
                        return ins
                    S.op("pe", f, reads=[ytok, identB], writes=[PB[b]])
                    S.op("dve", lambda: nc.vector.tensor_copy(
                        yT.t[:, :, tt * 128:(tt + 1) * 128], pv[:, 0:1024].rearrange("p (a b) -> p a b", b=128)),
                        reads=[PB[b]], writes=[yT.halves[tg], yT])
                for cg in range(2):
                    slot = slots_o[cg]
                    wv = slotv(slot, 8, 512)
                    for m in range(4):
                        mm = cg * 4 + m
                        b = nextbank()

                        def f():
                            ins = None
                            for kc in range(8):
                                ins = nc.tensor.matmul(pbank(b), wv[:, kc, m * 128:(m + 1) * 128],
                                                       yT.t[:, kc, tg * 512:(tg + 1) * 512],
                                                       start=(kc == 0), stop=(kc == 7))
                            return ins
                        S.op("pe", f, reads=[slot, yT.halves[tg]], writes=[PB[b]])
                        xs_ = xT.t[:, mm, tg * 512:(tg + 1) * 512]
                        S.op("dve", lambda: nc.vector.scalar_tensor_tensor(
                            out=xs_, in0=pbank(b), scalar=MOD[l].t[:, 2 * 8 + mm, cond:cond + 1], in1=xs_,
                            op0=ALU.mult, op1=ALU.add), reads=[PB[b], MOD[l], xT], writes=[xT])
            w_done()
            w_done()
            S.barrier()

        def ffn(l, half):
            cond = half
            nseq, L = (4, 256) if half == 0 else (1, 1024)
            AR.reset()
            gT = AV([128, 22, 1024], BF16)
            y0 = [AV([128, 1024]) for _ in range(3)]
            sa = [AV([128, 1024]) for _ in range(2)]
            upb = ((0, 1), (2, 3), (4, 5))
            ui = 0
            for t in range(11):
                slot = w_get()
                wv = slotv(slot, 8, 512)
                for sub in range(2):
                    j = 2 * t + sub
                    for ab in range(2):
                        cols = ab * 256 + sub * 128
                        jj = ab * 22 + j
                        bb = upb[ui % 3]
                        yy = y0[ui % 3]
                        ui += 1
                        for tg in range(2):
                            bnk = bb[tg]

                            def f():
                                ins = None
                                for kc in range(8):
                                    ins = nc.tensor.matmul(pbank(bnk), wv[:, kc, cols:cols + 128],
                                                           hT.t[:, kc, tg * 512:(tg + 1) * 512],
                                                           start=(kc == 0), stop=(kc == 7))
                                return ins
                            S.op("pe", f, reads=[slot, hT], writes=[PB[bnk]])
                        pfull = PSt[bb[0] // 2][:, 0:1024]
                        S.op("act", lambda: nc.scalar.activation(out=yy.t[:], in_=pfull, func=AF.Identity,
                                                                 bias=CW.t[:, l, 3, jj:jj + 1],
                                                                 scale=CW.t[:, l, 1, jj:jj + 1]),
                             reads=[PB[bb[0]], PB[bb[1]], CW], writes=[yy])
                        pv3 = pfull.rearrange("p (s t) -> p s t", t=L)
                        yv3 = yy.t[:].rearrange("p (s t) -> p s t", t=L)
                        S.op("dve", lambda: nc.vector.scalar_tensor_tensor(
                            out=yv3[:, :, 1:L], in0=pv3[:, :, 0:L - 1], scalar=CW.t[:, l, 0, jj:jj + 1],
                            in1=yv3[:, :, 1:L], op0=ALU.mult, op1=ALU.add),
                            reads=[PB[bb[0]], PB[bb[1]], CW, yy], writes=[yy])
                        S.op("dve", lambda: nc.vector.scalar_tensor_tensor(
                            out=yv3[:, :, 0:L - 1], in0=pv3[:, :, 1:L], scalar=CW.t[:, l, 2, jj:jj + 1],
                            in1=yv3[:, :, 0:L - 1], op0=ALU.mult, op1=ALU.add),
                            reads=[PB[bb[0]], PB[bb[1]], CW, yy], writes=[yy])
                        if ab == 0:
                            sa_ = sa[j % 2]
                            S.op("act", lambda: nc.scalar.activation(out=sa_.t[:], in_=yy.t[:], func=AF.Silu),
                                 reads=[yy], writes=[sa_])
                        else:
                            sa_ = sa[j % 2]
                            S.op("pool", lambda: nc.gpsimd.tensor_tensor(out=gT.t[:, j, :], in0=sa_.t[:], in1=yy.t[:],
                                                                         op=ALU.mult), reads=[sa_, yy], writes=[gT])
                w_done()
            if half == HALVES[0]:
                mods_group(l, 5)
            for m in range(8):
                slot = w_get()
                wv = slotv(slot, 22, 128)
                for tg in range(2):
                    b = nextbank((6, 7), "t67")

                    def f():
                        ins = None
                        for kc in range(22):
                            ins = nc.tensor.matmul(pbank(b), wv[:, kc, :], gT.t[:, kc, tg * 512:(tg + 1) * 512],
                                                   start=(kc == 0), stop=(kc == 21))
                        return ins
                    S.op("pe", f, reads=[slot, gT], writes=[PB[b]])
                    xs_ = xT.t[:, m, tg * 512:(tg + 1) * 512]
                    S.op("dve", lambda: nc.vector.scalar_tensor_tensor(
                        out=xs_, in0=pbank(b), scalar=MOD[l].t[:, 5 * 8 + m, cond:cond + 1], in1=xs_,
                        op0=ALU.mult, op1=ALU.add), reads=[PB[b], MOD[l], xT], writes=[xT])
                w_done()
            S.barrier()

        for half in HALVES:
            AR.reset()
            load_x(din["xp"] if half == 0 else din["xs"])
            S.barrier()
            ckpt(11)
            for l in range(NL):
                AR.reset()
                if half == HALVES[0]:
                    mods_group(l, 0)
                    mods_group(l, 1)
                norm_mod(l, 0, half)
                S.barrier()
                ckpt(12)
                mixer(l, half)
                ckpt(19)
                AR.reset()
                if half == HALVES[0]:
                    mods_group(l, 3)
                    mods_group(l, 4)
                norm_mod(l, 1, half)
                S.barrier()
                ckpt(20)
                ffn(l, half)
                ckpt(21)
            AR.reset()
            store_x(dout["yp"] if half == 0 else dout["ys"])
            S.barrier()
        S.drain("sp")
    except _Stop:
        pass
    return nc


_CACHE = {}


def _prep_inputs(inputs):
    f32 = lambda a: np.ascontiguousarray(np.asarray(a, dtype=np.float32))
    I = {k: f32(v) for k, v in inputs.items()}
    hc = host_consts()
    shared = {
        "norm1": I["norm1"], "w_mod": I["w_mod"], "b_mod": I["b_mod"], "w_in": I["w_in"],
        "sgu_norm": I["sgu_norm"], "sgu_w": I["sgu_w"], "sgu_b": I["sgu_b"],
        "rlf": I["ret_logit_fwd"], "rlb": I["ret_logit_bwd"], "ret_norm": I["ret_norm"].reshape(2, 256),
        "q_norm": I["q_norm"], "k_norm": I["k_norm"], "diff_lam": I["diff_lam"].reshape(2, 256),
        "diff_norm": I["diff_norm"], "w_out": I["w_out"], "norm2": I["norm2"], "ffn_up": I["ffn_up"],
        "ffn_conv": I["ffn_conv"], "ffn_conv_b": I["ffn_conv_b"], "ffn_down": I["ffn_down"],
    }
    shared.update(hc)
    maps = []
    for c in range(8):
        m = dict(shared)
        m["xp"] = np.ascontiguousarray(I["x_prompt"][4 * c:4 * c + 4].reshape(1024, 1024))
        m["xs"] = np.ascontiguousarray(I["x_sample"][c])
        m["cvec"] = np.ascontiguousarray(np.stack([I["c_ctx"], I["c"][c]], axis=0))
        m["ck"] = np.ascontiguousarray(I["cache_k"][c].reshape(2, 256, 512))
        m["cv"] = np.ascontiguousarray(I["cache_v"][c].reshape(2, 256, 512))
        m["srf"] = np.ascontiguousarray(I["state_ret_fwd"][c])
        m["srb"] = np.ascontiguousarray(I["state_ret_bwd"][c])
        maps.append(m)
    return maps


def kernel(**inputs):
    maps = _prep_inputs(inputs)
    if "nc" not in _CACHE:
        _CACHE["nc"] = build()
    nc = _CACHE["nc"]
    res = run_bass_kernel_spmd(nc, maps, core_ids=list(range(8)))
    R = res.results
    yp = np.concatenate([np.asarray(R[c]["yp"]).reshape(4, 256, 1024) for c in range(8)], axis=0)
    ys = np.stack([np.asarray(R[c]["ys"]) for c in range(8)], axis=0)
    nk = np.concatenate([np.asarray(R[c]["nk"]).reshape(4, 2, 256, 4, 2, 64) for c in range(8)], axis=0)
    nv = np.concatenate([np.asarray(R[c]["nv"]).reshape(4, 2, 256, 4, 128) for c in range(8)], axis=0)
    nrf = np.concatenate([np.asarray(R[c]["nrf"]) for c in range(8)], axis=0)
    nrb = np.concatenate([np.asarray(R[c]["nrb"]) for c in range(8)], axis=0)
    return (yp.astype(np.float32), ys.astype(np.float32), nk.astype(np.float32), nv.astype(np.float32),
            nrf.astype(np.float32), nrb.astype(np.float32))
```

```python
import math
from contextlib import ExitStack
import numpy as np
import concourse.bass as bass
import concourse.mybir as mybir
from concourse.bass_utils import run_bass_kernel_spmd

F32 = mybir.dt.float32
BF16 = mybir.dt.bfloat16
AF = mybir.ActivationFunctionType
ALU = mybir.AluOpType
AX = mybir.AxisListType

D = 1024
T = 1024
DFF = 2816
EPS = 1e-6
NSLOT = 4


class Res:
    __slots__ = ("w", "r", "sem", "persist", "excl")

    def __init__(self, persist=False, excl=False):
        self.excl = excl
        self.w = None
        self.r = {}
        self.sem = None
        self.persist = persist


class TT:
    def __init__(self, t, res=None):
        self.t = t
        self.res = res if res is not None else Res()


class Sch:
    def __init__(self, nc, es):
        self.nc = nc
        self.es = es
        self.E = {"pe": nc.tensor, "act": nc.scalar, "dve": nc.vector, "pool": nc.gpsimd, "sp": nc.sync}
        self.sem = {}
        self.cnt = {}
        self.seen = {k: {} for k in self.E}
        for k in ("pe", "act", "dve", "pool"):
            self.sem[k] = es.enter_context(nc.semaphore("s_" + k))
            self.cnt[k] = 0
        self.nsem = 0
        self.dsems = []
        self.free = []
        self.live = []

    def newsem(self):
        if self.free:
            return self.free.pop()
        name = "d%d" % self.nsem
        self.nsem += 1
        self.sem[name] = self.es.enter_context(self.nc.semaphore(name))
        self.cnt[name] = 0
        self.dsems.append(name)
        return name

    def recycle(self):
        keep = []
        for r in self.live:
            if r.persist:
                keep.append(r)
            else:
                self.free.append(r.sem)
                r.sem = None
        self.live = keep

    def _wait(self, eng, raw, other):
        best = {}
        for t in raw:
            if t is None:
                continue
            k, v = t
            if k == eng and eng in ("pe", "sp"):
                continue
            if v > best.get(k, 0):
                best[k] = v
        for t in other:
            if t is None:
                continue
            k, v = t
            if k == eng and eng in ("pe", "sp"):
                continue
            if v > best.get(k, 0):
                best[k] = v
        sn = self.seen[eng]
        for k, v in best.items():
            if sn.get(k, 0) >= v:
                continue
            self.E[eng].wait_ge(self.sem[k], v)
            sn[k] = v

    def _deps(self, reads, writes):
        raw = [r.w for r in reads]
        other = []
        for w in writes:
            other.append(w.w)
            other.extend(w.r.items())
        return raw, other

    def _commit(self, tok, reads, writes):
        k, v = tok
        for r in reads:
            if r.r.get(k, 0) < v:
                r.r[k] = v
        for w in writes:
            w.w = tok
            w.r = {}

    def op(self, eng, fn, reads=(), writes=()):
        reads = [getattr(x, 'res', x) for x in reads] + [x.extra for x in reads if hasattr(x, 'extra')]
        writes = [getattr(x, 'res', x) for x in writes]
        writes = writes + [r for r in reads if r.excl and r not in writes]
        raw, other = self._deps(reads, writes)
        self._wait(eng, raw, other)
        ins = fn()
        ins.then_inc(self.sem[eng], 1)
        self.cnt[eng] += 1
        tok = (eng, self.cnt[eng])
        self._commit(tok, reads, writes)
        return tok

    def dma(self, q, out, in_, reads=(), writes=(), key=None):
        reads = [getattr(x, 'res', x) for x in reads]
        writes = [getattr(x, 'res', x) for x in writes]
        kres = key if key is not None else (writes[0] if writes else reads[0])
        kres = getattr(kres, 'res', kres)
        if kres.sem is None:
            kres.sem = self.newsem()
            self.live.append(kres)
        raw, other = self._deps(reads, writes)
        other = list(other) + [(kres.sem, self.cnt[kres.sem])]
        self._wait(q, raw, other)
        ins = self.E[q].dma_start(out=out, in_=in_)
        ins.then_inc(self.sem[kres.sem], 16)
        self.cnt[kres.sem] += 16
        tok = (kres.sem, self.cnt[kres.sem])
        self._commit(tok, reads, writes)
        return tok

    def barrier(self):
        engs = ("pe", "act", "dve", "pool")
        for e in engs + ("sp",):
            for f in engs:
                if e == f and e == "pe":
                    continue
                v = self.cnt[f]
                if v > 0 and self.seen[e].get(f, 0) < v:
                    self.E[e].wait_ge(self.sem[f], v)
                    self.seen[e][f] = v
            for r in self.live:
                if r.persist:
                    continue
                v = self.cnt[r.sem]
                if v > 0 and self.seen[e].get(r.sem, 0) < v:
                    self.E[e].wait_ge(self.sem[r.sem], v)
                    self.seen[e][r.sem] = v
        self.recycle()

    def drain(self, q="sp"):
        for k in self.dsems:
            v = self.cnt[k]
            if v > 0 and self.seen[q].get(k, 0) < v:
                self.E[q].wait_ge(self.sem[k], v)
                self.seen[q][k] = v
        for f in ("pe", "act", "dve", "pool"):
            v = self.cnt[f]
            if v > 0 and self.seen[q].get(f, 0) < v:
                self.E[q].wait_ge(self.sem[f], v)
                self.seen[q][f] = v


def host_consts():
    ROPE_PAIRS = 16
    t = np.arange(1024)
    inv = (10000.0 ** (-np.arange(ROPE_PAIRS, dtype=np.float32) / ROPE_PAIRS)).astype(np.float32)
    ar = (t // 64).astype(np.float32)[:, None] * inv
    ac = (t % 64).astype(np.float32)[:, None] * inv
    cr, sr, cc, sc_ = np.cos(ar), np.sin(ar), np.cos(ac), np.sin(ac)
    cos64 = np.concatenate([cr, cr, cc, cc], axis=1).astype(np.float32)
    sin64 = np.concatenate([-sr, sr, -sc_, sc_], axis=1).astype(np.float32)
    k = np.arange(128)[:, None]
    q = np.arange(128)[None, :]
    dm = np.zeros((128, 4, 128), np.float32)
    dm[:, 0] = np.maximum(q - k, 0)
    dm[:, 1] = np.maximum(k - q, 0)
    dm[:, 2] = (q >= k)
    dm[:, 3] = (k >= q)
    pr = np.zeros((128, 2, 128), np.float32)
    pr[:, 0, :] = np.arange(128) + 1.0
    pr[:, 1, :] = 128.0 - np.arange(128)
    kp = np.zeros((128, 2), np.float32)
    kp[:, 0] = 127.0 - np.arange(128)
    kp[:, 1] = np.arange(128)
    return dict(cos64=cos64, sin64=sin64, dmat=dm, posrow=pr, kpos=kp)


IN_SPECS = [
    ("xp", (1024, 1024)), ("xs", (1024, 1024)), ("cvec", (2, 1024)),
    ("ck", (2, 256, 512)), ("cv", (2, 256, 512)), ("srf", (2, 4, 64, 64)), ("srb", (2, 4, 64, 64)),
    ("norm1", (2, 1024)), ("w_mod", (2, 1024, 6144)), ("b_mod", (2, 6144)), ("w_in", (2, 1024, 3072)),
    ("sgu_norm", (2, 256)), ("sgu_w", (2, 4, 128, 128)), ("sgu_b", (2, 4, 128)),
    ("rlf", (2, 4)), ("rlb", (2, 4)), ("ret_norm", (2, 256)), ("q_norm", (2, 64)), ("k_norm", (2, 64)),
    ("diff_lam", (2, 256)), ("diff_norm", (2, 128)), ("w_out", (2, 1024, 1024)), ("norm2", (2, 1024)),
    ("ffn_up", (2, 1024, 5632)), ("ffn_conv", (2, 3, 5632)), ("ffn_conv_b", (2, 5632)),
    ("ffn_down", (2, 2816, 1024)),
    ("cos64", (1024, 64)), ("sin64", (1024, 64)), ("dmat", (128, 4, 128)), ("posrow", (128, 2, 128)),
    ("kpos", (128, 2)),
]
OUT_SPECS = [
    ("yp", (1024, 1024)), ("ys", (1024, 1024)), ("nk", (4, 2, 256, 512)), ("nv", (4, 2, 256, 512)),
    ("nrf", (4, 2, 4, 64, 64)), ("nrb", (4, 2, 4, 64, 64)),
]


def build(cfg=None):
    cfg = cfg or {}
    NL = cfg.get("n_layers", 2)
    HALVES = cfg.get("halves", (0, 1))
    taps = cfg.get("taps", None)
    nc = bass.Bass("TRN2", target_bir_lowering=False)
    try:
        nc.allow_low_precision("bf16 matmul operands with fp32 accumulation")
    except Exception:
        pass
    din = {n: nc.dram_tensor(n, list(s), F32, kind="ExternalInput").ap() for n, s in IN_SPECS}
    dout = {n: nc.dram_tensor(n, list(s), F32, kind="ExternalOutput").ap() for n, s in OUT_SPECS}
    es = ExitStack()
    STOP = cfg.get("stop", None)

    class _Stop(Exception):
        pass
    try:
      with es:
        S = Sch(nc, es)

        def ckpt(k):
            if STOP == k:
                S.drain("sp")
                raise _Stop()

        def sb(name, shape, dt=F32):
            return TT(es.enter_context(nc.sbuf_tensor(name, list(shape), dt)), Res(persist=True))

        def tap(name, ap, shape, reads):
            if taps is None or name not in taps:
                return
            d = nc.dram_tensor("tap_" + name, list(shape), ap.dtype, kind="ExternalOutput").ap()
            taps[name] = (list(shape), ap.dtype)
            S.dma("sp", d, ap, reads=reads)

        PSt = [es.enter_context(nc.psum_tensor("ps%d" % i, [128, 1024], F32)) for i in range(4)]
        PB = []
        for i in range(8):
            PB.append(TT(PSt[i // 2], Res(excl=True)))

        def pbank(i):
            return PSt[i // 2][:, (i % 2) * 512:(i % 2) * 512 + 512]

        rot = {"n": 0}

        def nextbank(pool=(0, 1, 2, 3), key="n"):
            i = pool[rot.setdefault(key, 0) % len(pool)]
            rot[key] += 1
            return i

        xT = sb("xT", [128, 8, T])
        hT = sb("hT", [128, 8, T], BF16)
        yT = sb("yT", [128, 8, T], BF16)
        yT.halves = [Res(persist=True), Res(persist=True)]
        ring = [sb("ring%d" % i, [128, 4096], BF16) for i in range(NSLOT)]
        for r_ in ring:
            r_.extra = Res(persist=True)
            r_.extra.sem = S.newsem()
            r_.res.sem = S.newsem()
        ARENA_BYTES = 74 * 1024
        arena = sb("arena", [128, ARENA_BYTES // 4], F32)

        identF = sb("identF", [128, 128])
        identB = sb("identB", [128, 128], BF16)
        onesB = sb("onesB", [128, 128], BF16)
        sTb = sb("sTb", [128, 8, 2], BF16)
        MOD = [sb("MOD%d" % l, [128, 48, 2]) for l in range(2)]
        GS = [[sb("GS%d%d" % (l, n), [128, 8, 2]) for n in range(2)] for l in range(2)]
        NT = sb("NT", [128, 32])
        CW = sb("CW", [128, 2, 4, 44])
        WST = sb("WST", [128, 8, 128], BF16)
        BS = sb("BS", [128, 8])
        SGN = sb("SGN", [128, 2, 256])
        RN = sb("RN", [128, 2, 256])
        QN = sb("QN", [128, 2, 64])
        KN = sb("KN", [128, 2, 64])
        DN = sb("DN", [128, 2, 128])
        LG = sb("LG", [128, 16])
        KP = sb("KP", [128, 2])
        C128 = sb("C128", [128, 64])
        DTm = sb("DTm", [128, 2, 4, 128])
        QD = sb("QD", [128, 2, 2, 2, 128])
        KDE = sb("KDE", [128, 2, 2, 256])
        CD = sb("CD", [128, 2, 2, 2, 64])
        NLAM = sb("NLAM", [128, 2])
        COS = sb("COS", [128, 8, 64])
        SIN = sb("SIN", [128, 8, 64])
        small = [sb("small%d" % i, [128, 16]) for i in range(8)]
        srot = {"i": 0}

        def nsmall():
            s = small[srot["i"] % len(small)]
            srot["i"] += 1
            return s

        wq = []

        def slotv(slot, a, b):
            return slot.t[:, 0:a * b].rearrange("p (a b) -> p a b", b=b)

        def wsrc(name, l, c0, c1):
            return din[name][l, :, c0:c1].rearrange("(kc p) n -> p kc n", p=128)

        def plan_weights():
            def modt(l, which):
                for t2 in range(2):
                    c0 = which * 1024 + t2 * 512
                    wq.append([(lambda s: slotv(s, 8, 512), wsrc("w_mod", l, c0, c0 + 512))])
            first = True
            for half in HALVES:
                for l in range(NL):
                    if first:
                        modt(l, 0)
                        modt(l, 1)
                    for g in range(6):
                        wq.append([(lambda s: slotv(s, 8, 512), wsrc("w_in", l, g * 512, (g + 1) * 512))])
                    if first:
                        modt(l, 2)
                    for g in range(2):
                        wq.append([(lambda s: slotv(s, 8, 512), wsrc("w_out", l, g * 512, (g + 1) * 512))])
                    if first:
                        modt(l, 3)
                        modt(l, 4)
                    for t in range(11):
                        wq.append([
                            (lambda s: slotv(s, 8, 512)[:, :, 0:256], wsrc("ffn_up", l, t * 256, (t + 1) * 256)),
                            (lambda s: slotv(s, 8, 512)[:, :, 256:512],
                             wsrc("ffn_up", l, DFF + t * 256, DFF + (t + 1) * 256)),
                        ])
                    if first:
                        modt(l, 5)
                    for m in range(8):
                        wq.append([(lambda s: slotv(s, 22, 128), wsrc("ffn_down", l, m * 128, (m + 1) * 128))])
                first = False

        plan_weights()
        wstate = {"next_load": 0, "next_use": 0}

        def w_issue():
            i = wstate["next_load"]
            if i >= len(wq):
                return
            slot = ring[i % NSLOT]
            for n_, (dstf, src) in enumerate(wq[i]):
                S.dma("pool", dstf(slot), src, writes=[slot.res if n_ == 0 else slot.extra])
            wstate["next_load"] = i + 1

        def w_get():
            i = wstate["next_use"]
            wstate["next_use"] = i + 1
            assert i < wstate["next_load"]
            return ring[i % NSLOT]

        def w_done():
            w_issue()

        class Arena:
            def __init__(self):
                self.off = 0

            def reset(self):
                self.off = 0

            def get(self, shape, dt):
                n = 1
                for s_ in shape[1:]:
                    n *= s_
                nbytes = n * (4 if dt == F32 else 2)
                nbytes = (nbytes + 63) // 64 * 64
                w0 = self.off // 4
                w1 = (self.off + nbytes) // 4
                assert self.off + nbytes <= ARENA_BYTES, ("arena overflow", self.off, nbytes)
                self.off += nbytes
                ap = arena.t[0:shape[0], w0:w1]
                if dt == BF16:
                    ap = ap.bitcast(BF16)
                nfree = n
                ap = ap[:, 0:nfree]
                if len(shape) > 2:
                    names = " ".join("d%d" % i for i in range(len(shape) - 1))
                    kw = {"d%d" % i: shape[i + 1] for i in range(len(shape) - 1)}
                    ap = ap.rearrange("p (%s) -> p %s" % (names, names), **kw)
                return ap

        AR = Arena()

        class AV:
            def __init__(self, shape, dt=F32):
                self.ap = AR.get(shape, dt)
                self.res = Res()

            @property
            def t(self):
                return self.ap

        S.op("pool", lambda: nc.gpsimd.memset(identF.t[:], 0.0), writes=[identF])
        S.op("pool", lambda: nc.gpsimd.affine_select(out=identF.t[:], in_=identF.t[:], pattern=[[-1, 128]],
                                                      compare_op=ALU.not_equal, fill=1.0, base=0,
                                                      channel_multiplier=1), reads=[identF], writes=[identF])
        S.op("pool", lambda: nc.gpsimd.memset(onesB.t[:], 1.0), writes=[onesB])
        S.op("pool", lambda: nc.gpsimd.memset(C128.t[:], 128.0), writes=[C128])
        S.op("dve", lambda: nc.vector.tensor_copy(identB.t[:], identF.t[:]), reads=[identF], writes=[identB])

        ckpt(1)
        for _ in range(NSLOT):
            w_issue()

        cres = Res()
        crow = AV([2, 1024])
        bmrow = AV([96, 128])
        nrow = AV([32, 128])
        cvrow = AV([44, 2, 4, 128])
        wsraw = AV([128, 8, 128])
        bsrow = AV([8, 128])
        BMT = sb("BMT", [128, 96])
        DL = AV([128, 2, 256])
        DM = AV([128, 4, 128])
        PR = AV([128, 2, 128])

        def cload(dst_tt, dst_ap, src_ap):
            S.dma("sp", dst_ap, src_ap, writes=[dst_tt])

        cload(crow, crow.t[:], din["cvec"])
        cload(bmrow, bmrow.t[:], din["b_mod"].rearrange("l (j p) -> (l j) p", p=128))
        cload(nrow, nrow.t[0:16, :], din["norm1"].rearrange("l (kc p) -> (l kc) p", p=128))
        cload(nrow, nrow.t[16:32, :], din["norm2"].rearrange("l (kc p) -> (l kc) p", p=128))
        for l in range(2):
            cload(cvrow, cvrow.t[:, l, 0:3, :], din["ffn_conv"][l].rearrange("j (c p) -> c j p", p=128))
            cload(cvrow, cvrow.t[:, l, 3, :], din["ffn_conv_b"][l].rearrange("(c p) -> c p", p=128))
        cload(wsraw, wsraw.t[:], din["sgu_w"].rearrange("l g p q -> p (l g) q"))
        cload(bsrow, bsrow.t[:], din["sgu_b"].rearrange("l g p -> (l g) p"))
        cload(SGN, SGN.t[:], din["sgu_norm"].partition_broadcast(128))
        cload(RN, RN.t[:], din["ret_norm"].partition_broadcast(128))
        cload(QN, QN.t[:], din["q_norm"].partition_broadcast(128))
        cload(KN, KN.t[:], din["k_norm"].partition_broadcast(128))
        cload(DN, DN.t[:], din["diff_norm"].partition_broadcast(128))
        cload(DL, DL.t[:], din["diff_lam"].partition_broadcast(128))
        cload(LG, LG.t[:, 0:8], din["rlf"].rearrange("l h -> (l h)").partition_broadcast(128))
        cload(LG, LG.t[:, 8:16], din["rlb"].rearrange("l h -> (l h)").partition_broadcast(128))
        cload(DM, DM.t[:], din["dmat"])
        cload(PR, PR.t[:], din["posrow"])
        cload(KP, KP.t[:], din["kpos"])
        cload(COS, COS.t[:], din["cos64"].rearrange("(t p) c -> p t c", p=128))
        cload(SIN, SIN.t[:], din["sin64"].rearrange("(t p) c -> p t c", p=128))

        ckpt(2)
        csil = AV([2, 1024])
        S.op("act", lambda: nc.scalar.activation(out=csil.t[:], in_=crow.t[:], func=AF.Silu),
             reads=[crow], writes=[csil])
        b = nextbank()

        def f():
            ins = None
            for kc in range(8):
                ins = nc.tensor.transpose(pbank(b)[:, kc * 2:kc * 2 + 2], csil.t[0:2, kc * 128:(kc + 1) * 128],
                                          identF.t[0:2, 0:2])
            return ins
        S.op("pe", f, reads=[csil, identF], writes=[PB[b]])
        S.op("dve", lambda: nc.vector.tensor_copy(sTb.t[:].rearrange("p a b -> p (a b)"), pbank(b)[:, 0:16]),
             reads=[PB[b]], writes=[sTb])

        ckpt(3)
        b = nextbank()
        S.op("pe", lambda: nc.tensor.transpose(pbank(b)[:, 0:32], nrow.t[0:32, :], identF.t[0:32, 0:32]),
             reads=[nrow, identF], writes=[PB[b]])
        S.op("dve", lambda: nc.vector.tensor_copy(NT.t[:], pbank(b)[:, 0:32]), reads=[PB[b]], writes=[NT])
        b = nextbank()
        S.op("pe", lambda: nc.tensor.transpose(pbank(b)[:, 0:96], bmrow.t[0:96, :], identF.t[0:96, 0:96]),
             reads=[bmrow, identF], writes=[PB[b]])
        S.op("dve", lambda: nc.vector.tensor_copy(BMT.t[:], pbank(b)[:, 0:96]), reads=[PB[b]], writes=[BMT])
        ckpt(4)
        b = nextbank()

        def f():
            ins = None
            for l in range(2):
                for j in range(4):
                    c0 = (l * 4 + j) * 44
                    ins = nc.tensor.transpose(pbank(b)[:, c0:c0 + 44], cvrow.t[0:44, l, j, :], identF.t[0:44, 0:44])
            return ins
        S.op("pe", f, reads=[cvrow, identF], writes=[PB[b]])
        S.op("dve", lambda: nc.vector.tensor_copy(CW.t[:].rearrange("p a b c -> p (a b c)"), pbank(b)[:, 0:352]),
             reads=[PB[b]], writes=[CW])
        ckpt(5)
        for hb in range(2):
            b = nextbank()

            def f():
                ins = None
                for i in range(4):
                    ins = nc.tensor.transpose(pbank(b)[:, i * 128:(i + 1) * 128], wsraw.t[:, hb * 4 + i, :], identF.t[:])
                return ins
            S.op("pe", f, reads=[wsraw, identF], writes=[PB[b]])
            S.op("dve", lambda: nc.vector.tensor_copy(
                WST.t[:, hb * 4:(hb + 1) * 4, :].rearrange("p a b -> p (a b)"), pbank(b)[:, 0:512]),
                reads=[PB[b]], writes=[WST])
        b = nextbank()
        S.op("pe", lambda: nc.tensor.transpose(pbank(b)[:, 0:8], bsrow.t[0:8, :], identF.t[0:8, 0:8]),
             reads=[bsrow, identF], writes=[PB[b]])
        S.op("dve", lambda: nc.vector.tensor_copy(BS.t[:], pbank(b)[:, 0:8]), reads=[PB[b]], writes=[BS])

        ckpt(6)
        modrow = sb("modrow", [2, 1024])

        def mods_group(l, which):
            for t2 in range(2):
                slot = w_get()
                b = nextbank()
                wv = slotv(slot, 8, 512)

                def f():
                    ins = None
                    for kc in range(8):
                        ins = nc.tensor.matmul(pbank(b)[0:2, :], sTb.t[:, kc, :], wv[:, kc, :],
                                               start=(kc == 0), stop=(kc == 7))
                    return ins
                S.op("pe", f, reads=[sTb, slot], writes=[PB[b]])
                w_done()
                S.op("dve", lambda: nc.vector.tensor_copy(modrow.t[:, t2 * 512:(t2 + 1) * 512], pbank(b)[0:2, :]),
                     reads=[PB[b]], writes=[modrow])
            b = nextbank()

            def f():
                ins = None
                for jb in range(8):
                    ins = nc.tensor.transpose(pbank(b)[:, jb * 2:jb * 2 + 2], modrow.t[0:2, jb * 128:(jb + 1) * 128],
                                              identF.t[0:2, 0:2])
                return ins
            S.op("pe", f, reads=[modrow, identF], writes=[PB[b]])
            c0 = l * 48 + which * 8
            S.op("dve", lambda: nc.vector.tensor_tensor(
                out=MOD[l].t[:, which * 8:(which + 1) * 8, :], in0=pbank(b)[:, 0:16].rearrange("p (a b) -> p a b", b=2),
                in1=BMT.t[:, c0:c0 + 8].unsqueeze(2).broadcast_to([128, 8, 2]), op=ALU.add),
                 reads=[PB[b], BMT], writes=[MOD[l]])
            if which in (1, 4):
                n = 0 if which == 1 else 1
                ntv = NT.t[:, (n * 2 + l) * 8:(n * 2 + l) * 8 + 8]
                S.op("dve", lambda: nc.vector.scalar_tensor_tensor(
                    out=GS[l][n].t[:], in0=MOD[l].t[:, which * 8:(which + 1) * 8, :], scalar=1.0,
                    in1=ntv.unsqueeze(2).broadcast_to([128, 8, 2]), op0=ALU.add, op1=ALU.mult),
                    reads=[MOD[l], NT], writes=[GS[l][n]])

        ckpt(7)
        S.op("act", lambda: nc.scalar.activation(out=LG.t[:], in_=LG.t[:], func=AF.Exp, scale=-1.0),
             reads=[LG], writes=[LG])
        S.op("act", lambda: nc.scalar.activation(out=LG.t[:], in_=LG.t[:], func=AF.Ln, bias=1.0, scale=1.0),
             reads=[LG], writes=[LG])
        S.op("dve", lambda: nc.vector.tensor_scalar(LG.t[:], LG.t[:], -1.0, None, ALU.mult), reads=[LG], writes=[LG])

        ckpt(8)

        def lgi(d, l, h):
            return d * 8 + l * 4 + h

        dtmp = [AV([128, 128]) for i in range(4)]
        for l in range(NL):
            for h in range(4):
                tf, tb = dtmp[(h % 2) * 2], dtmp[(h % 2) * 2 + 1]
                i_f, i_b = lgi(0, l, h), lgi(1, l, h)
                S.op("act", lambda: nc.scalar.activation(out=tf.t[:], in_=DM.t[:, 0, :], func=AF.Exp,
                                                         scale=LG.t[:, i_f:i_f + 1]), reads=[DM, LG], writes=[tf])
                S.op("act", lambda: nc.scalar.activation(out=tb.t[:], in_=DM.t[:, 1, :], func=AF.Exp,
                                                         scale=LG.t[:, i_b:i_b + 1]), reads=[DM, LG], writes=[tb])
                S.op("dve", lambda: nc.vector.scalar_tensor_tensor(out=tf.t[:], in0=tf.t[:], scalar=0.125,
                                                                   in1=DM.t[:, 2, :], op0=ALU.mult, op1=ALU.mult),
                     reads=[tf, DM], writes=[tf])
                S.op("dve", lambda: nc.vector.scalar_tensor_tensor(out=tb.t[:], in0=tb.t[:], scalar=0.125,
                                                                   in1=DM.t[:, 3, :], op0=ALU.mult, op1=ALU.mult),
                     reads=[tb, DM], writes=[tb])
                S.op("dve", lambda: nc.vector.tensor_tensor(out=DTm.t[:, l, h, :], in0=tf.t[:], in1=tb.t[:],
                                                            op=ALU.add), reads=[tf, tb], writes=[DTm])
            for hp in range(2):
                for d in range(2):
                    for j in range(2):
                        ii = lgi(d, l, 2 * hp + j)
                        ps_ = slice(j * 64, (j + 1) * 64)
                        S.op("act", lambda: nc.scalar.activation(out=QD.t[ps_, l, hp, d, :], in_=PR.t[ps_, d, :],
                                                                 func=AF.Exp, scale=LG.t[ps_, ii:ii + 1]),
                             reads=[PR, LG], writes=[QD])
                        S.op("act", lambda: nc.scalar.activation(out=CD.t[ps_, l, d, hp, :], in_=C128.t[ps_, :],
                                                                 func=AF.Exp, scale=LG.t[ps_, ii:ii + 1]),
                             reads=[C128, LG], writes=[CD])
            for d in range(2):
                for h in range(4):
                    ii = lgi(d, l, h)
                    S.op("act", lambda: nc.scalar.activation(
                        out=KDE.t[:, l, d, h * 64:(h + 1) * 64], in_=KP.t[:, d:d + 1].broadcast_to([128, 64]),
                        func=AF.Exp, scale=LG.t[:, ii:ii + 1], bias=math.log(0.125)),
                        reads=[KP, LG], writes=[KDE])
            lam_init = 0.8 - 0.6 * math.exp(-0.3 * l)
            pr_ = nsmall()
            dlv = DL.t[:, l, :].rearrange("p (a b c) -> p a b c", a=2, b=2)
            lt = dtmp[0]
            S.op("dve", lambda: nc.vector.tensor_tensor(out=lt.t[:].rearrange("p (a c) -> p a c", a=2),
                                                        in0=dlv[:, :, 0, :], in1=dlv[:, :, 1, :], op=ALU.mult),
                 reads=[DL], writes=[lt])
            S.op("dve", lambda: nc.vector.tensor_reduce(out=pr_.t[:, 0:2],
                                                        in_=lt.t[:].rearrange("p (a c) -> p a c", a=2),
                                                        axis=AX.X, op=ALU.add), reads=[lt], writes=[pr_])
            S.op("act", lambda: nc.scalar.activation(out=pr_.t[:, 2:4], in_=pr_.t[:, 0:2], func=AF.Exp),
                 reads=[pr_], writes=[pr_])
            S.op("dve", lambda: nc.vector.tensor_tensor(out=pr_.t[:, 4:5], in0=pr_.t[:, 3:4], in1=pr_.t[:, 2:3],
                                                        op=ALU.subtract), reads=[pr_], writes=[pr_])
            S.op("dve", lambda: nc.vector.tensor_scalar(NLAM.t[:, l:l + 1], pr_.t[:, 4:5], -lam_init, None, ALU.add),
                 reads=[pr_], writes=[NLAM])
            S.op("dve", lambda: nc.vector.tensor_scalar(DN.t[:, l, :], DN.t[:, l, :], 1.0 - lam_init, None, ALU.mult),
                 reads=[DN], writes=[DN])

        S.barrier()
        ckpt(10)

        def rstd_from_ss(ss_ap, out_ap, scale, bias, R, W):
            S.op("act", lambda: nc.scalar.activation(out=out_ap, in_=ss_ap, func=AF.Sqrt, bias=bias, scale=scale),
                 reads=R, writes=W)
            S.op("dve", lambda: nc.vector.reciprocal(out_ap, out_ap), reads=W, writes=W)

        def load_x(src):
            xin = [AV([128, 1024]) for _ in range(2)]
            for tt in range(8):
                xi = xin[tt % 2]
                S.dma("sp", xi.t[:], src[tt * 128:(tt + 1) * 128, :], writes=[xi])
                for hb in range(2):
                    b = nextbank()

                    def f():
                        ins = None
                        for i in range(4):
                            c = hb * 4 + i
                            ins = nc.tensor.transpose(pbank(b)[:, i * 128:(i + 1) * 128],
                                                      xi.t[:, c * 128:(c + 1) * 128], identF.t[:])
                        return ins
                    S.op("pe", f, reads=[xi, identF], writes=[PB[b]])
                    eng = "act" if hb == 0 else "dve"
                    dst = xT.t[:, hb * 4:(hb + 1) * 4, tt * 128:(tt + 1) * 128]
                    srcp = pbank(b).rearrange("p (a b) -> p a b", b=128)
                    if eng == "act":
                        S.op("act", lambda: nc.scalar.copy(out=dst, in_=srcp), reads=[PB[b]], writes=[xT])
                    else:
                        S.op("dve", lambda: nc.vector.tensor_copy(dst, srcp), reads=[PB[b]], writes=[xT])

        def store_x(dst):
            xo = [AV([128, 1024]) for _ in range(2)]
            for tt in range(8):
                xi = xo[tt % 2]
                for hb in range(2):
                    b = nextbank()

                    def f():
                        ins = None
                        for i in range(4):
                            c = hb * 4 + i
                            ins = nc.tensor.transpose(pbank(b)[:, i * 128:(i + 1) * 128],
                                                      xT.t[:, c, tt * 128:(tt + 1) * 128], identF.t[:])
                        return ins
                    S.op("pe", f, reads=[xT, identF], writes=[PB[b]])
                    dstp = xi.t[:, hb * 512:(hb + 1) * 512]
                    if hb == 0:
                        S.op("act", lambda: nc.scalar.copy(out=dstp, in_=pbank(b)), reads=[PB[b]], writes=[xi])
                    else:
                        S.op("dve", lambda: nc.vector.tensor_copy(dstp, pbank(b)), reads=[PB[b]], writes=[xi])
                S.dma("sp", dst[tt * 128:(tt + 1) * 128, :], xi.t[:], reads=[xi])

        def norm_mod(l, n, cond):
            sq = yT
            RB = AV([128, T])
            S.op("act", lambda: nc.scalar.activation(out=sq.t[:], in_=xT.t[:], func=AF.Square),
                 reads=[xT], writes=[sq] + yT.halves)
            for tg in range(2):
                b = nextbank()

                def f():
                    ins = None
                    for kc in range(8):
                        ins = nc.tensor.matmul(pbank(b), onesB.t[:], sq.t[:, kc, tg * 512:(tg + 1) * 512],
                                               start=(kc == 0), stop=(kc == 7))
                    return ins
                S.op("pe", f, reads=[sq, onesB], writes=[PB[b]])
                rstd_from_ss(pbank(b), RB.t[:, tg * 512:(tg + 1) * 512], 1.0 / D, EPS, [PB[b]], [RB])
            shi = 0 if n == 0 else 3
            tmp = [AV([128, 1024]) for _ in range(2)]
            for kc in range(8):
                tm = tmp[kc % 2]
                S.op("dve", lambda: nc.vector.scalar_tensor_tensor(
                    out=tm.t[:], in0=xT.t[:, kc, :], scalar=GS[l][n].t[:, kc, cond:cond + 1], in1=RB.t[:],
                    op0=ALU.mult, op1=ALU.mult), reads=[xT, GS[l][n], RB], writes=[tm])
                S.op("act", lambda: nc.scalar.activation(out=hT.t[:, kc, :], in_=tm.t[:], func=AF.Identity,
                                                         bias=MOD[l].t[:, shi * 8 + kc, cond:cond + 1], scale=1.0),
                     reads=[tm, MOD[l]], writes=[hT])

        def zmm(slot, tt, b):
            wv = slotv(slot, 8, 512)

            def f():
                ins = None
                for kc in range(8):
                    ins = nc.tensor.matmul(pbank(b), hT.t[:, kc, tt * 128:(tt + 1) * 128], wv[:, kc, :],
                                           start=(kc == 0), stop=(kc == 7))
                return ins
            S.op("pe", f, reads=[hT, slot], writes=[PB[b]])

        def group_rstd(src_ap, ngrp, gsz, scale, bias, sqt, ss):
            src_tt, sq_tt = sqt
            S.op("dve", lambda: nc.vector.tensor_tensor(out=sq_tt.t[:, 0:ngrp * gsz], in0=src_ap, in1=src_ap,
                                                        op=ALU.mult), reads=[src_tt], writes=[sq_tt])
            S.op("dve", lambda: nc.vector.tensor_reduce(
                out=ss.t[:, 0:ngrp], in_=sq_tt.t[:, 0:ngrp * gsz].rearrange("p (a b) -> p a b", b=gsz),
                axis=AX.X, op=ALU.add), reads=[sq_tt], writes=[ss])
            rstd_from_ss(ss.t[:, 0:ngrp], ss.t[:, 0:ngrp], scale, bias, [ss], [ss])

        def transposes_to(src_tt, src_ap_fn, nblk, dst_tt, dst_ap, pool=(0, 1, 2, 3)):
            b = nextbank(pool)
            pv = pbank(b).bitcast(BF16)

            def f():
                ins = None
                for i in range(nblk):
                    ins = nc.tensor.transpose(pv[:, i * 128:(i + 1) * 128], src_ap_fn(i), identB.t[:])
                return ins
            S.op("pe", f, reads=[src_tt, identB], writes=[PB[b]])
            S.op("dve", lambda: nc.vector.tensor_copy(dst_ap, pv[:, 0:nblk * 128].rearrange("p (a b) -> p a b", b=128)),
                 reads=[PB[b]], writes=[dst_tt])

        def mixer(l, half):
            cond = half
            nseq, L = (4, 256) if half == 0 else (1, 1024)
            cpl = L // 128
            AR.reset()
            ytok = AV([128, 8, 1024], BF16)
            ar_mark = AR.off

            class _V:
                pass
            SCR = _V()
            SCR.ap = yT.t[:].bitcast(F32)
            SCR.t = SCR.ap
            SCR.res = yT.res
            slot = w_get()
            GE = AV([128, 8, 512])
            VN = AV([128, 8, 256], BF16)
            ssG = AV([128, 8])
            for tt in range(8):
                b = nextbank()
                zmm(slot, tt, b)
                S.op("act", lambda: nc.scalar.activation(out=GE.t[:, tt, :], in_=pbank(b), func=AF.Gelu_apprx_tanh),
                     reads=[PB[b]], writes=[GE])
            w_done()
            gv = GE.t[:, :, 256:512]
            sv = SCR.t[:, :, 0:256]

            def g0_stageB():
                S.op("dve", lambda: nc.vector.tensor_tensor(out=sv, in0=gv, in1=gv, op=ALU.mult), reads=[GE], writes=[SCR])
                S.op("dve", lambda: nc.vector.tensor_reduce(out=ssG.t[:], in_=sv, axis=AX.X, op=ALU.add),
                     reads=[SCR], writes=[ssG])
                rstd_from_ss(ssG.t[:], ssG.t[:], 1.0 / 256, EPS, [ssG], [ssG])
                S.op("dve", lambda: nc.vector.tensor_tensor(out=gv, in0=gv,
                                                            in1=ssG.t[:].unsqueeze(2).broadcast_to([128, 8, 256]),
                                                            op=ALU.mult), reads=[GE, ssG], writes=[GE])
                S.op("dve", lambda: nc.vector.tensor_tensor(
                    out=VN.t[:], in0=gv, in1=SGN.t[:, l, :].unsqueeze(1).broadcast_to([128, 8, 256]), op=ALU.mult),
                    reads=[GE, SGN], writes=[VN])

            def g0_stageC(tt):
                b2 = nextbank()

                def f():
                    ins = None
                    for g in range(4):
                        ins = nc.tensor.matmul(pbank(b2)[:, g * 64:(g + 1) * 64], WST.t[:, l * 4 + g, :],
                                               VN.t[:, tt, g * 64:(g + 1) * 64], start=True, stop=True)
                    return ins
                S.op("pe", f, reads=[WST, VN], writes=[PB[b2]])
                for g in range(4):
                    S.op("dve", lambda: nc.vector.scalar_tensor_tensor(
                        out=ytok.t[:, tt, g * 64:(g + 1) * 64], in0=pbank(b2)[:, g * 64:(g + 1) * 64],
                        scalar=BS.t[:, l * 4 + g:l * 4 + g + 1], in1=GE.t[:, tt, g * 64:(g + 1) * 64],
                        op0=ALU.add, op1=ALU.mult), reads=[PB[b2], BS, GE], writes=[ytok])

            ckpt(13)

            QT = AV([128, 2, 1024], BF16)
            KT = AV([128, 2, 1024], BF16)
            VB = AV([128, 8, 256], BF16)
            SG = AV([128, 8, 256], BF16)
            KVS = AV([128, 8, 2, 2, 64])
            RS = AV([128, 2, 2, 64])
            RSb = AV([128, 8, 2, 2, 64], BF16)

            class _W:
                pass
            RSd = []
            RSbd = []
            for d_ in range(2):
                v_ = _W()
                v_.ap = RS.t[:, d_, :, :]
                v_.t = v_.ap
                v_.res = Res()
                RSd.append(v_)
                w_ = _W()
                w_.res = Res()
                RSbd.append(w_)
            qkb = [AV([128, 512], BF16) for _ in range(2)]
            KF = [AV([128, 2, 256], BF16) for _ in range(2)]
            gtmp = [AV([128, 256]) for _ in range(2)]
            st_stage = [AV([128, 2, 2, 64]) for _ in range(2)]
            slot1 = w_get()
            slot2 = w_get()
            g0_stageB()
            for tt in range(8):
                if tt >= 1:
                    g0_stageC(tt - 1)
                b1 = nextbank()
                zmm(slot1, tt, b1)
                if tt == 0:
                    ckpt(1301)
                b2 = nextbank()
                zmm(slot2, tt, b2)
                if tt == 0:
                    ckpt(1302)
                qk = qkb[tt % 2]
                S.op("act", lambda: nc.scalar.copy(out=qk.t[:], in_=pbank(b1)), reads=[PB[b1]], writes=[qk])
                if tt == 0:
                    ckpt(1303)
                kf = KF[tt % 2]
                for d in range(2):
                    S.op("dve", lambda: nc.vector.tensor_tensor(
                        out=kf.t[:, d, :], in0=pbank(b1)[:, 256:512], in1=KDE.t[:, l, d, :], op=ALU.mult),
                        reads=[PB[b1], KDE], writes=[kf])
                    if tt == 0 and d == 0:
                        ckpt(1304)
                if tt == 0:
                    ckpt(131)
                bt = nextbank((6, 7), "t67")
                pv = pbank(bt).bitcast(BF16)

                def f():
                    ins = None
                    for i in range(4):
                        ins = nc.tensor.transpose(pv[:, i * 128:(i + 1) * 128], qk.t[:, i * 128:(i + 1) * 128],
                                                  identB.t[:])
                    return ins
                S.op("pe", f, reads=[qk, identB], writes=[PB[bt]])
                if tt == 0:
                    ckpt(132)
                S.op("dve", lambda: nc.vector.tensor_copy(
                    QT.t[:, :, tt * 128:(tt + 1) * 128], pv[:, 0:256].rearrange("p (a b) -> p a b", b=128)),
                    reads=[PB[bt]], writes=[QT])
                S.op("dve", lambda: nc.vector.tensor_copy(
                    KT.t[:, :, tt * 128:(tt + 1) * 128], pv[:, 256:512].rearrange("p (a b) -> p a b", b=128)),
                    reads=[PB[bt]], writes=[KT])
                if tt == 0:
                    ckpt(133)
                S.op("act", lambda: nc.scalar.copy(out=VB.t[:, tt, :], in_=pbank(b2)[:, 0:256]),
                     reads=[PB[b2]], writes=[VB])
                gt_ = gtmp[tt % 2]
                S.op("act", lambda: nc.scalar.activation(out=gt_.t[:], in_=pbank(b2)[:, 256:512], func=AF.Silu),
                     reads=[PB[b2]], writes=[gt_])
                S.op("dve", lambda: nc.vector.tensor_tensor(out=SG.t[:, tt, :], in0=gt_.t[:], in1=RN.t[:, l, :],
                                                            op=ALU.mult), reads=[gt_, RN], writes=[SG])
                if tt == 0:
                    ckpt(134)
                bk = nextbank((4, 5), "t45")

                def f():
                    ins = None
                    for d in range(2):
                        for hp in range(2):
                            ins = nc.tensor.matmul(pbank(bk)[:, (d * 2 + hp) * 128:(d * 2 + hp + 1) * 128],
                                                   kf.t[:, d, hp * 128:(hp + 1) * 128],
                                                   VB.t[:, tt, hp * 128:(hp + 1) * 128], start=True, stop=True)
                    return ins
                S.op("pe", f, reads=[kf, VB], writes=[PB[bk]])
                if tt == 0:
                    ckpt(135)
                pk = pbank(bk).rearrange("p (a b) -> p a b", b=128)
                for j in range(2):
                    ps_ = slice(j * 64, (j + 1) * 64)
                    S.op("dve", lambda: nc.vector.tensor_copy(
                        KVS.t[ps_, tt, :, :, :].rearrange("p a b c -> p (a b) c"), pk[ps_, :, j * 64:(j + 1) * 64]),
                        reads=[PB[bk]], writes=[KVS])
            g0_stageC(7)
            w_done()
            w_done()

            ckpt(14)
            for s_ in range(nseq):
                if half == 0:
                    S.op("dve", lambda: nc.vector.memset(RS.t[:], 0.0), writes=[RSd[0], RSd[1]])
                else:
                    for d, nm in ((0, "srf"), (1, "srb")):
                        for j in range(2):
                            srcs = din[nm][l].rearrange("(hp j) d e -> j d hp e", j=2)[j]
                            S.dma("sp", RS.t[j * 64:(j + 1) * 64, d, :, :], srcs, writes=[RSd[d]])
                def step(d, tt):
                    S.op("dve", lambda: nc.vector.tensor_copy(RSb.t[:, tt, d, :, :], RSd[d].t[:]),
                         reads=[RSd[d]], writes=[RSbd[d]])
                    S.op("dve", lambda: nc.vector.tensor_tensor(out=RSd[d].t[:], in0=RSd[d].t[:],
                                                                in1=CD.t[:, l, d, :, :], op=ALU.mult),
                         reads=[RSd[d], CD], writes=[RSd[d]])
                    S.op("dve", lambda: nc.vector.tensor_tensor(out=RSd[d].t[:], in0=RSd[d].t[:],
                                                                in1=KVS.t[:, tt, d, :, :], op=ALU.add),
                         reads=[RSd[d], KVS], writes=[RSd[d]])
                for c in range(cpl):
                    step(0, s_ * cpl + c)
                    step(1, s_ * cpl + (cpl - 1 - c))
                if half == 0:
                    stg = st_stage[s_ % 2]
                    S.op("dve", lambda: nc.vector.tensor_copy(stg.t[:], RS.t[:]), reads=[RSd[0], RSd[1]], writes=[stg])
                    for d, nm in ((0, "nrf"), (1, "nrb")):
                        for j in range(2):
                            dsts = dout[nm][s_, l].rearrange("(hp j) d e -> j d hp e", j=2)[j]
                            S.dma("sp", dsts, stg.t[j * 64:(j + 1) * 64, d, :, :], reads=[stg])

            ckpt(15)
            S.barrier()
            ar_keep = AR.off
            AR.off = ar_mark
            MT = [AV([128, 512], BF16) for _ in range(2)]
            QF = [AV([128, 2, 2, 128], BF16) for _ in range(2)]
            OR = AV([128, 8, 256])
            for tt in range(8):
                c = tt % cpl
                tsl = slice(tt * 128, (tt + 1) * 128)
                bia = nextbank((0, 1), "t01")
                bib = nextbank((0, 1), "t01")

                def f():
                    ins = None
                    for j, bnk in ((0, bia), (1, bib)):
                        ps_ = slice(j * 64, (j + 1) * 64)
                        for hp in range(2):
                            ins = nc.tensor.matmul(pbank(bnk)[:, hp * 128:(hp + 1) * 128], KT.t[ps_, hp, tsl],
                                                   QT.t[ps_, hp, tsl], start=True, stop=True)
                    return ins
                S.op("pe", f, reads=[KT, QT], writes=[PB[bia], PB[bib]])
                mt = MT[tt % 2]
                mt4 = mt.t[:].rearrange("p (hp j q) -> p hp j q", hp=2, j=2)
                for j, bnk in ((0, bia), (1, bib)):
                    S.op("dve", lambda: nc.vector.tensor_tensor(
                        out=mt4[:, :, j, :], in0=pbank(bnk)[:, 0:256].rearrange("p (a b) -> p a b", b=128),
                        in1=DTm.t[:, l, :, :].rearrange("p (hp j) q -> p hp j q", j=2)[:, :, j, :], op=ALU.mult),
                        reads=[PB[bnk], DTm], writes=[mt])
                qf = QF[tt % 2]
                S.op("dve", lambda: nc.vector.tensor_tensor(
                    out=qf.t[:], in0=QD.t[:, l, :, :, :],
                    in1=QT.t[:, :, tsl].unsqueeze(2).broadcast_to([128, 2, 2, 128]), op=ALU.mult),
                    reads=[QT, QD], writes=[qf])
                use_f = not (half == 0 and c == 0)
                use_b = not (half == 0 and c == cpl - 1)
                bo = nextbank((2, 3), "t23")

                def f():
                    ins = None
                    for h in range(4):
                        hp, j = h // 2, h % 2
                        ps_ = slice(j * 64, (j + 1) * 64)
                        o_ap = pbank(bo)[:, h * 64:(h + 1) * 64]
                        last = not (use_f or use_b)
                        ins = nc.tensor.matmul(o_ap, mt.t[:, h * 128:(h + 1) * 128], VB.t[:, tt, h * 64:(h + 1) * 64],
                                               start=True, stop=last)
                        if use_f:
                            ins = nc.tensor.matmul(o_ap, qf.t[ps_, hp, 0, :], RSb.t[ps_, tt, 0, hp, :],
                                                   start=False, stop=not use_b)
                        if use_b:
                            ins = nc.tensor.matmul(o_ap, qf.t[ps_, hp, 1, :], RSb.t[ps_, tt, 1, hp, :],
                                                   start=False, stop=True)
                    return ins
                S.op("pe", f, reads=[mt, VB, qf, RSbd[0], RSbd[1]], writes=[PB[bo]])
                S.op("act", lambda: nc.scalar.copy(out=OR.t[:, tt, :], in_=pbank(bo)[:, 0:256]),
                     reads=[PB[bo]], writes=[OR])
            sv = SCR.t[:, :, 0:256]
            S.op("act", lambda: nc.scalar.activation(out=sv, in_=OR.t[:], func=AF.Square), reads=[OR], writes=[SCR])
            ssR = AV([128, 8, 4])
            S.op("dve", lambda: nc.vector.tensor_reduce(out=ssR.t[:], in_=sv.rearrange("p t (h c) -> p t h c", c=64),
                                                        axis=AX.X, op=ALU.add), reads=[SCR], writes=[ssR])
            rstd_from_ss(ssR.t[:], ssR.t[:], 1.0 / 64, EPS, [ssR], [ssR])
            o4 = OR.t[:].rearrange("p t (h c) -> p t h c", c=64)
            S.op("dve", lambda: nc.vector.tensor_tensor(out=o4, in0=o4,
                                                        in1=ssR.t[:].unsqueeze(3).broadcast_to([128, 8, 4, 64]),
                                                        op=ALU.mult), reads=[OR, ssR], writes=[OR])
            S.op("dve", lambda: nc.vector.tensor_tensor(out=ytok.t[:, :, 256:512], in0=OR.t[:], in1=SG.t[:],
                                                        op=ALU.mult), reads=[OR, SG], writes=[ytok])
            S.barrier()
            ckpt(16)
            AR.off = ar_mark

            NKC = 2 if half == 0 else 10
            KOFF = 0 if half == 0 else 256
            NKEY = 1024 + KOFF
            QTa = AV([128, 4, 1024], BF16)
            KTa = AV([128, 4, NKEY], BF16)
            VA = AV([128, NKEY // 128, 4, 132], BF16)
            S.op("dve", lambda: nc.vector.memset(VA.t[:, :, :, 128:129], 1.0), writes=[VA])
            ar_att = AR.off
            stg = [AV([128, 512]) for _ in range(2)]

            class _G:
                pass
            GR = []
            for g_ in range(2):
                o_ = _G()
                o_.QA = AV([128, 4, 512])
                o_.QB = AV([128, 4, 512], BF16)
                o_.ss = AV([128, 32])
                o_.SQ = _G()
                o_.SQ.ap = SCR.t[:, g_ * 4:(g_ + 1) * 4, :]
                o_.SQ.t = o_.SQ.ap
                o_.SQ.res = Res()
                GR.append(o_)
            S.op("dve", lambda: nc.vector.memset(GR[0].ss.t[:], 0.0), reads=[yT], writes=[GR[0].ss, GR[0].SQ, GR[1].SQ])
            if half == 1:
                for kc in range(2):
                    st = stg[kc % 2]
                    S.dma("sp", st.t[:], din["ck"][l, kc * 128:(kc + 1) * 128, :], writes=[st])
                    S.op("act", lambda: nc.scalar.copy(out=GR[0].QB.t[:, kc, :], in_=st.t[:]), reads=[st],
                         writes=[GR[0].QB])
                    transposes_to(GR[0].QB, lambda i: GR[0].QB.t[:, kc, i * 128:(i + 1) * 128], 4, KTa,
                                  KTa.t[:, :, kc * 128:(kc + 1) * 128])
                for kc in range(2):
                    st = stg[kc % 2]
                    S.dma("sp", st.t[:], din["cv"][l, kc * 128:(kc + 1) * 128, :], writes=[st])
                    S.op("act", lambda: nc.scalar.copy(out=VA.t[:, kc, :, 0:128],
                                                       in_=st.t[:].rearrange("p (a b) -> p a b", b=128)),
                         reads=[st], writes=[VA])

            def stageA1(slot, g):
                G = GR[g]
                for t in range(4):
                    tt = g * 4 + t
                    b = nextbank()
                    zmm(slot, tt, b)
                    S.op("act", lambda: nc.scalar.copy(out=G.QA.t[:, t, :], in_=pbank(b)), reads=[PB[b]], writes=[G.QA])
                    S.op("act", lambda: nc.scalar.activation(out=G.SQ.t[:, t, :], in_=pbank(b), func=AF.Square),
                         reads=[PB[b]], writes=[G.SQ])

            def stageA2(g):
                G = GR[g]
                S.op("dve", lambda: nc.vector.tensor_reduce(
                    out=G.ss.t[:], in_=G.SQ.t[:].rearrange("p t (a b) -> p (t a) b", b=64),
                    axis=AX.X, op=ALU.add), reads=[G.SQ], writes=[G.ss])

            def stageB(g, gain_tt, which):
                G = GR[g]
                if which == "q":
                    rstd_from_ss(G.ss.t[:], G.ss.t[:], 1.0, 64 * EPS, [G.ss], [G.ss])
                else:
                    rstd_from_ss(G.ss.t[:], G.ss.t[:], 1.0 / 64, EPS, [G.ss], [G.ss])
                qf_ = G.QA.t[:].rearrange("p a b -> p (a b)")
                sf_ = G.SQ.t[:].rearrange("p a b -> p (a b)")
                q3 = qf_.rearrange("p (a b) -> p a b", b=64)
                S.op("dve", lambda: nc.vector.tensor_tensor(out=q3, in0=q3,
                                                            in1=G.ss.t[:].unsqueeze(2).broadcast_to([128, 32, 64]),
                                                            op=ALU.mult), reads=[G.QA, G.ss], writes=[G.QA])
                gbc = gain_tt.t[:, l, :].unsqueeze(1).broadcast_to([128, 32, 64])
                if half == 0 and which == "q":
                    S.op("dve", lambda: nc.vector.tensor_tensor(
                        out=G.QB.t[:].rearrange("p a (g c) -> p (a g) c", c=64), in0=q3, in1=gbc, op=ALU.mult),
                        reads=[G.QA, gain_tt], writes=[G.QB])
                else:
                    S.op("dve", lambda: nc.vector.tensor_tensor(out=q3, in0=q3, in1=gbc, op=ALU.mult),
                         reads=[G.QA, gain_tt], writes=[G.QA])
                    if half == 0:
                        for s2 in range(2):
                            s_ = g * 2 + s2
                            S.dma("sp", dout["nk"][s_, l].rearrange("(t p) c -> p t c", p=128),
                                  G.QA.t[:, 2 * s2:2 * s2 + 2, :], reads=[G.QA])
                        S.op("act", lambda: nc.scalar.copy(out=G.QB.t[:].rearrange("p a b -> p (a b)"), in_=qf_),
                             reads=[G.QA], writes=[G.QB])
                    else:
                        x5 = G.QA.t[:].rearrange("p t (g a s c) -> p t g a s c", a=2, s=2, c=16)
                        r5 = G.SQ.t[:].rearrange("p t (g a s c) -> p t g a s c", a=2, s=2, c=16)
                        s5 = SIN.t[:, g * 4:(g + 1) * 4, :].rearrange("p t (a s c) -> p t a s c", s=2, c=16)
                        for sidx in range(2):
                            for ax_ in range(2):
                                S.op("dve", lambda: nc.vector.tensor_tensor(
                                    out=r5[:, :, :, ax_, sidx, :], in0=x5[:, :, :, ax_, 1 - sidx, :],
                                    in1=s5[:, :, ax_, sidx, :].unsqueeze(2).broadcast_to([128, 4, 8, 16]), op=ALU.mult),
                                    reads=[G.QA, SIN], writes=[G.SQ])
                        q4 = G.QA.t[:].rearrange("p t (g c) -> p t g c", c=64)
                        S.op("dve", lambda: nc.vector.tensor_tensor(
                            out=q4, in0=q4,
                            in1=COS.t[:, g * 4:(g + 1) * 4, :].unsqueeze(2).broadcast_to([128, 4, 8, 64]), op=ALU.mult),
                            reads=[G.QA, COS], writes=[G.QA])
                        S.op("dve", lambda: nc.vector.tensor_tensor(out=G.QB.t[:].rearrange("p a b -> p (a b)"),
                                                                    in0=qf_, in1=sf_, op=ALU.add),
                             reads=[G.QA, G.SQ], writes=[G.QB])

            def stageC(g, dstT, doff):
                G = GR[g]
                for t in range(4):
                    tt = g * 4 + t
                    b = nextbank()
                    pv = pbank(b).bitcast(BF16)

                    def f():
                        ins = None
                        for i in range(4):
                            ins = nc.tensor.transpose(pv[:, i * 128:(i + 1) * 128], G.QB.t[:, t, i * 128:(i + 1) * 128],
                                                      identB.t[:])
                        return ins
                    S.op("pe", f, reads=[G.QB, identB], writes=[PB[b]])
                    S.op("act", lambda: nc.scalar.copy(out=dstT.t[:, :, doff + tt * 128:doff + (tt + 1) * 128],
                                                       in_=pv[:, 0:512].rearrange("p (a b) -> p a b", b=128)),
                         reads=[PB[b]], writes=[dstT])

            def stageV(slot, g):
                for t in range(4):
                    tt = g * 4 + t
                    b = nextbank()
                    zmm(slot, tt, b)
                    kci = KOFF // 128 + tt
                    if half == 0:
                        st = stg[tt % 2]
                        S.op("act", lambda: nc.scalar.copy(out=st.t[:], in_=pbank(b)), reads=[PB[b]], writes=[st])
                        S.dma("sp", dout["nv"][tt // 2, l, (tt % 2) * 128:(tt % 2) * 128 + 128, :], st.t[:], reads=[st])
                        S.op("act", lambda: nc.scalar.copy(out=VA.t[:, kci, :, 0:128],
                                                           in_=st.t[:].rearrange("p (a b) -> p a b", b=128)),
                             reads=[st], writes=[VA])
                    else:
                        S.op("act", lambda: nc.scalar.copy(out=VA.t[:, kci, :, 0:128],
                                                           in_=pbank(b).rearrange("p (a b) -> p a b", b=128)),
                             reads=[PB[b]], writes=[VA])

            slot3 = w_get()
            stageA1(slot3, 0)
            stageA2(0)
            stageB(0, QN, "q")
            stageA1(slot3, 1)
            stageA2(1)
            w_done()
            slot4 = w_get()
            stageC(0, QTa, 0)
            stageB(1, QN, "q")
            stageA1(slot4, 0)
            stageA2(0)
            stageC(1, QTa, 0)
            stageB(0, KN, "k")
            stageA1(slot4, 1)
            stageA2(1)
            w_done()
            slot5 = w_get()
            stageC(0, KTa, KOFF)
            stageV(slot5, 0)
            stageB(1, KN, "k")
            stageV(slot5, 1)
            stageC(1, KTa, KOFF)
            w_done()
            S.barrier()
            ckpt(17)
            AR.off = ar_att
            OALL = AV([128, 8, 4, 128])
            ETP = [AV([128, 2, 2, 256], BF16) for _ in range(2)]
            ot2 = [AV([128, 128]) for _ in range(2)]
            work = []
            for qg in range(4):
                kcs = [qg * 2, qg * 2 + 1] if half == 0 else list(range(10))
                for h in range(4):
                    for pi in range(len(kcs) // 2):
                        work.append((qg, h, pi, len(kcs) // 2, (kcs[2 * pi], kcs[2 * pi + 1])))
            st_banks = {}

            def emit_st(widx):
                qg, h, pi, npair, kc2 = work[widx]
                q0 = qg * 256
                bsA, bsB = ((0, 1), (2, 3))[widx % 2]
                st_banks[widx] = (bsA, bsB)

                def f():
                    ins = None
                    for i, bnk in ((0, bsA), (1, bsB)):
                        ps_ = slice(i * 64, (i + 1) * 64)
                        for kk in range(2):
                            kc = kc2[kk]
                            ins = nc.tensor.matmul(pbank(bnk)[:, kk * 256:(kk + 1) * 256],
                                                   KTa.t[ps_, h, kc * 128:(kc + 1) * 128],
                                                   QTa.t[ps_, h, q0:q0 + 256], start=True, stop=True)
                    return ins
                S.op("pe", f, reads=[KTa, QTa], writes=[PB[bsA], PB[bsB]])

            OSQh = AV([128, 2048])

            def norm_half(hh):
                ov = OALL.t[:, hh * 4:(hh + 1) * 4, :, :]
                of_ = ov.rearrange("p a b c -> p (a b c)")
                ss2 = AV([128, 16])
                S.op("dve", lambda: nc.vector.tensor_tensor(out=OSQh.t[:], in0=of_, in1=of_, op=ALU.mult),
                     reads=[OALLh[hh]], writes=[OSQh])
                S.op("dve", lambda: nc.vector.tensor_reduce(out=ss2.t[:], in_=OSQh.t[:].rearrange("p (a b) -> p a b", b=128),
                                                            axis=AX.X, op=ALU.add), reads=[OSQh], writes=[ss2])
                rstd_from_ss(ss2.t[:], ss2.t[:], 1.0 / 128, EPS, [ss2], [ss2])
                o3 = of_.rearrange("p (a b) -> p a b", b=128)
                S.op("dve", lambda: nc.vector.tensor_tensor(out=o3, in0=o3,
                                                            in1=ss2.t[:].unsqueeze(2).broadcast_to([128, 16, 128]),
                                                            op=ALU.mult), reads=[OALLh[hh], ss2], writes=[OALLh[hh]])
                S.op("dve", lambda: nc.vector.tensor_tensor(
                    out=ytok.t[:, hh * 4:(hh + 1) * 4, 512:1024].rearrange("p t (h c) -> p t h c", c=128), in0=ov,
                    in1=DN.t[:, l, :].unsqueeze(1).unsqueeze(1).broadcast_to([128, 4, 4, 128]), op=ALU.mult),
                    reads=[OALLh[hh], DN], writes=[ytok])

            OALLh = [Res(), Res()]
            emit_st(0)
            itn = 0
            for widx in range(len(work)):
                qg, h, pi, npair, kc2 = work[widx]
                q0 = qg * 256
                if pi == 0:
                    accs = ((4, 5), (6, 7))[itn % 2]
                    itn += 1
                if widx + 1 < len(work):
                    emit_st(widx + 1)
                bsA, bsB = st_banks.pop(widx)
                etp = ETP[widx % 2]
                S.op("act", lambda: nc.scalar.activation(out=etp.t[:].rearrange("p i a b -> p (i a b)"),
                                                         in_=PSt[bsA // 2][:, 0:1024], func=AF.Exp),
                     reads=[PB[bsA], PB[bsB]], writes=[etp])

                def f():
                    ins = None
                    for kk in range(2):
                        kc = kc2[kk]
                        for qb_ in range(2):
                            for i in range(2):
                                first = (pi == 0 and kk == 0 and i == 0)
                                last = (pi == npair - 1 and kk == 1 and i == 1)
                                ins = nc.tensor.matmul(pbank(accs[qb_])[:, i * 129:(i + 1) * 129],
                                                       etp.t[:, i, kk, qb_ * 128:(qb_ + 1) * 128],
                                                       VA.t[:, kc, h, 0:129], start=first, stop=last)
                    return ins
                S.op("pe", f, reads=[etp, VA], writes=[PB[accs[0]], PB[accs[1]]])
                if pi == npair - 1:
                    for qb_ in range(2):
                        tt = (q0 + qb_ * 128) // 128
                        ab = accs[qb_]
                        acc = pbank(ab)
                        rc = nsmall()
                        S.op("dve", lambda: nc.vector.reciprocal(
                            rc.t[:, 0:2], acc[:, 0:258].rearrange("p (a b) -> p a b", b=129)[:, :, 128]),
                            reads=[PB[ab]], writes=[rc])
                        S.op("dve", lambda: nc.vector.tensor_tensor(out=rc.t[:, 2:3], in0=rc.t[:, 1:2],
                                                                    in1=NLAM.t[:, l:l + 1], op=ALU.mult),
                             reads=[rc, NLAM], writes=[rc])
                        o1 = ot2[qb_]
                        S.op("dve", lambda: nc.vector.tensor_scalar(o1.t[:], acc[:, 129:257], rc.t[:, 2:3], None,
                                                                    ALU.mult), reads=[PB[ab], rc], writes=[o1])
                        S.op("dve", lambda: nc.vector.scalar_tensor_tensor(
                            out=OALL.t[:, tt, h, :], in0=acc[:, 0:128], scalar=rc.t[:, 0:1], in1=o1.t[:],
                            op0=ALU.mult, op1=ALU.add), reads=[PB[ab], rc, o1], writes=[OALLh[qg // 2]])
                        if qb_ == 1 and h == 3 and qg in (1, 3):
                            norm_half(qg // 2)
            ckpt(18)
            if half == HALVES[0]:
                mods_group(l, 2)
            slots_o = [w_get(), w_get()]
            for tg in range(2):
                for tt in range(tg * 4, tg * 4 + 4):
                    b = nextbank()
                    pv = pbank(b).bitcast(BF16)

                    def f():
                        ins = None
                        for i in range(8):
                            ins = nc.tensor.transpose(pv[:, i * 128:(i + 1) * 128], ytok.t[:, tt, i * 128:(i + 1) * 128],
                                                      identB.t[:])
                        return ins
                    S.op("pe", f, reads=[ytok, identB], writes=[PB[b]])
                    S.op("dve", lambda: nc.vector.tensor_copy(
                        yT.t[:, :, tt * 128:(tt + 1) * 128], pv[:, 0:1024].rearrange("p (a b) -> p a b", b=128)),
                        reads=[PB[b]], writes=[yT.halves[tg], yT])
                for cg in range(2):
                    slot = slots_o[cg]
                    wv = slotv(slot, 8, 512)
                    for m in range(4):
                        mm = cg * 4 + m
                        b = nextbank()

                        def f():
                            ins = None
                            for kc in range(8):
                                ins = nc.tensor.matmul(pbank(b), wv[:, kc, m * 128:(m + 1) * 128],
                                                       yT.t[:, kc, tg * 512:(tg + 1) * 512],
                                                       start=(kc == 0), stop=(kc == 7))
                            return ins
                        S.op("pe", f, reads=[slot, yT.halves[tg]], writes=[PB[b]])
                        xs_ = xT.t[:, mm, tg * 512:(tg + 1) * 512]
                        S.op("dve", lambda: nc.vector.scalar_tensor_tensor(
                            out=xs_, in0=pbank(b), scalar=MOD[l].t[:, 2 * 8 + mm, cond:cond + 1], in1=xs_,
                            op0=ALU.mult, op1=ALU.add), reads=[PB[b], MOD[l], xT], writes=[xT])
            w_done()
            w_done()
            S.barrier()

        def ffn(l, half):
            cond = half
            nseq, L = (4, 256) if half == 0 else (1, 1024)
            AR.reset()
            gT = AV([128, 22, 1024], BF16)
            y0 = [AV([128, 1024]) for _ in range(3)]
            sa = [AV([128, 1024]) for _ in range(2)]
            upb = ((0, 1), (2, 3), (4, 5))
            ui = 0
            for t in range(11):
                slot = w_get()
                wv = slotv(slot, 8, 512)
                for sub in range(2):
                    j = 2 * t + sub
                    for ab in range(2):
                        cols = ab * 256 + sub * 128
                        jj = ab * 22 + j
                        bb = upb[ui % 3]
                        yy = y0[ui % 3]
                        ui += 1
                        for tg in range(2):
                            bnk = bb[tg]

                            def f():
                                ins = None
                                for kc in range(8):
                                    ins = nc.tensor.matmul(pbank(bnk), wv[:, kc, cols:cols + 128],
                                                           hT.t[:, kc, tg * 512:(tg + 1) * 512],
                                                           start=(kc == 0), stop=(kc == 7))
                                return ins
                            S.op("pe", f, reads=[slot, hT], writes=[PB[bnk]])
                        pfull = PSt[bb[0] // 2][:, 0:1024]
                        S.op("act", lambda: nc.scalar.activation(out=yy.t[:], in_=pfull, func=AF.Identity,
                                                                 bias=CW.t[:, l, 3, jj:jj + 1],
                                                                 scale=CW.t[:, l, 1, jj:jj + 1]),
                             reads=[PB[bb[0]], PB[bb[1]], CW], writes=[yy])
                        pv3 = pfull.rearrange("p (s t) -> p s t", t=L)
                        yv3 = yy.t[:].rearrange("p (s t) -> p s t", t=L)
                        S.op("dve", lambda: nc.vector.scalar_tensor_tensor(
                            out=yv3[:, :, 1:L], in0=pv3[:, :, 0:L - 1], scalar=CW.t[:, l, 0, jj:jj + 1],
                            in1=yv3[:, :, 1:L], op0=ALU.mult, op1=ALU.add),
                            reads=[PB[bb[0]], PB[bb[1]], CW, yy], writes=[yy])
                        S.op("dve", lambda: nc.vector.scalar_tensor_tensor(
                            out=yv3[:, :, 0:L - 1], in0=pv3[:, :, 1:L], scalar=CW.t[:, l, 2, jj:jj + 1],
                            in1=yv3[:, :, 0:L - 1], op0=ALU.mult, op1=ALU.add),
                            reads=[PB[bb[0]], PB[bb[1]], CW, yy], writes=[yy])
                        if ab == 0:
                            sa_ = sa[j % 2]
                            S.op("act", lambda: nc.scalar.activation(out=sa_.t[:], in_=yy.t[:], func=AF.Silu),
                                 reads=[yy], writes=[sa_])
                        else:
                            sa_ = sa[j % 2]
                            S.op("pool", lambda: nc.gpsimd.tensor_tensor(out=gT.t[:, j, :], in0=sa_.t[:], in1=yy.t[:],
                                                                         op=ALU.mult), reads=[sa_, yy], writes=[gT])
                w_done()
            if half == HALVES[0]:
                mods_group(l, 5)
            for m in range(8):
                slot = w_get()
                wv = slotv(slot, 22, 128)
                for tg in range(2):
                    b = nextbank((6, 7), "t67")

                    def f():
                        ins = None
                        for kc in range(22):
                            ins = nc.tensor.matmul(pbank(b), wv[:, kc, :], gT.t[:, kc, tg * 512:(tg + 1) * 512],
                                                   start=(kc == 0), stop=(kc == 21))
                        return ins
                    S.op("pe", f, reads=[slot, gT], writes=[PB[b]])
                    xs_ = xT.t[:, m, tg * 512:(tg + 1) * 512]
                    S.op("dve", lambda: nc.vector.scalar_tensor_tensor(
                        out=xs_, in0=pbank(b), scalar=MOD[l].t[:, 5 * 8 + m, cond:cond + 1], in1=xs_,
                        op0=ALU.mult, op1=ALU.add), reads=[PB[b], MOD[l], xT], writes=[xT])
                w_done()
            S.barrier()

        for half in HALVES:
            AR.reset()
            load_x(din["xp"] if half == 0 else din["xs"])
            S.barrier()
            ckpt(11)
            for l in range(NL):
                AR.reset()
                if half == HALVES[0]:
                    mods_group(l, 0)
                    mods_group(l, 1)
                norm_mod(l, 0, half)
                S.barrier()
                ckpt(12)
                mixer(l, half)
                ckpt(19)
                AR.reset()
                if half == HALVES[0]:
                    mods_group(l, 3)
                    mods_group(l, 4)
                norm_mod(l, 1, half)
                S.barrier()
                ckpt(20)
                ffn(l, half)
                ckpt(21)
            AR.reset()
            store_x(dout["yp"] if half == 0 else dout["ys"])
            S.barrier()
        S.drain("sp")
    except _Stop:
        pass
    return nc


_CACHE = {}


def _prep_inputs(inputs):
    f32 = lambda a: np.ascontiguousarray(np.asarray(a, dtype=np.float32))
    I = {k: f32(v) for k, v in inputs.items()}
    hc = host_consts()
    shared = {
        "norm1": I["norm1"], "w_mod": I["w_mod"], "b_mod": I["b_mod"], "w_in": I["w_in"],
        "sgu_norm": I["sgu_norm"], "sgu_w": I["sgu_w"], "sgu_b": I["sgu_b"],
        "rlf": I["ret_logit_fwd"], "rlb": I["ret_logit_bwd"], "ret_norm": I["ret_norm"].reshape(2, 256),
        "q_norm": I["q_norm"], "k_norm": I["k_norm"], "diff_lam": I["diff_lam"].reshape(2, 256),
        "diff_norm": I["diff_norm"], "w_out": I["w_out"], "norm2": I["norm2"], "ffn_up": I["ffn_up"],
        "ffn_conv": I["ffn_conv"], "ffn_conv_b": I["ffn_conv_b"], "ffn_down": I["ffn_down"],
    }
    shared.update(hc)
    maps = []
    for c in range(8):
        m = dict(shared)
        m["xp"] = np.ascontiguousarray(I["x_prompt"][4 * c:4 * c + 4].reshape(1024, 1024))
        m["xs"] = np.ascontiguousarray(I["x_sample"][c])
        m["cvec"] = np.ascontiguousarray(np.stack([I["c_ctx"], I["c"][c]], axis=0))
        m["ck"] = np.ascontiguousarray(I["cache_k"][c].reshape(2, 256, 512))
        m["cv"] = np.ascontiguousarray(I["cache_v"][c].reshape(2, 256, 512))
        m["srf"] = np.ascontiguousarray(I["state_ret_fwd"][c])
        m["srb"] = np.ascontiguousarray(I["state_ret_bwd"][c])
        maps.append(m)
    return maps


def kernel(**inputs):
    maps = _prep_inputs(inputs)
    if "nc" not in _CACHE:
        _CACHE["nc"] = build()
    nc = _CACHE["nc"]
    res = run_bass_kernel_spmd(nc, maps, core_ids=list(range(8)))
    R = res.results
    yp = np.concatenate([np.asarray(R[c]["yp"]).reshape(4, 256, 1024) for c in range(8)], axis=0)
    ys = np.stack([np.asarray(R[c]["ys"]) for c in range(8)], axis=0)
    nk = np.concatenate([np.asarray(R[c]["nk"]).reshape(4, 2, 256, 4, 2, 64) for c in range(8)], axis=0)
    nv = np.concatenate([np.asarray(R[c]["nv"]).reshape(4, 2, 256, 4, 128) for c in range(8)], axis=0)
    nrf = np.concatenate([np.asarray(R[c]["nrf"]) for c in range(8)], axis=0)
    nrb = np.concatenate([np.asarray(R[c]["nrb"]) for c in range(8)], axis=0)
    return (yp.astype(np.float32), ys.astype(np.float32), nk.astype(np.float32), nv.astype(np.float32),
            nrf.astype(np.float32), nrb.astype(np.float32))
```

```python
import math
from contextlib import ExitStack
import numpy as np
import concourse.bass as bass
import concourse.mybir as mybir
from concourse.bass_utils import run_bass_kernel_spmd

F32 = mybir.dt.float32
BF16 = mybir.dt.bfloat16
AF = mybir.ActivationFunctionType
ALU = mybir.AluOpType
AX = mybir.AxisListType

D = 1024
T = 1024
DFF = 2816
EPS = 1e-6
NSLOT = 4


class Res:
    __slots__ = ("w", "r", "sem", "persist", "excl")

    def __init__(self, persist=False, excl=False):
        self.excl = excl
        self.w = None
        self.r = {}
        self.sem = None
        self.persist = persist


class TT:
    def __init__(self, t, res=None):
        self.t = t
        self.res = res if res is not None else Res()


class Sch:
    def __init__(self, nc, es):
        self.nc = nc
        self.es = es
        self.E = {"pe": nc.tensor, "act": nc.scalar, "dve": nc.vector, "pool": nc.gpsimd, "sp": nc.sync}
        self.sem = {}
        self.cnt = {}
        self.seen = {k: {} for k in self.E}
        for k in ("pe", "act", "dve", "pool"):
            self.sem[k] = es.enter_context(nc.semaphore("s_" + k))
            self.cnt[k] = 0
        self.nsem = 0
        self.dsems = []
        self.free = []
        self.live = []

    def newsem(self):
        if self.free:
            return self.free.pop()
        name = "d%d" % self.nsem
        self.nsem += 1
        self.sem[name] = self.es.enter_context(self.nc.semaphore(name))
        self.cnt[name] = 0
        self.dsems.append(name)
        return name

    def recycle(self):
        keep = []
        for r in self.live:
            if r.persist:
                keep.append(r)
            else:
                self.free.append(r.sem)
                r.sem = None
        self.live = keep

    def _wait(self, eng, raw, other):
        best = {}
        for t in raw:
            if t is None:
                continue
            k, v = t
            if k == eng and eng in ("pe", "sp"):
                continue
            if v > best.get(k, 0):
                best[k] = v
        for t in other:
            if t is None:
                continue
            k, v = t
            if k == eng and eng in ("pe", "sp"):
                continue
            if v > best.get(k, 0):
                best[k] = v
        sn = self.seen[eng]
        for k, v in best.items():
            if sn.get(k, 0) >= v:
                continue
            self.E[eng].wait_ge(self.sem[k], v)
            sn[k] = v

    def _deps(self, reads, writes):
        raw = [r.w for r in reads]
        other = []
        for w in writes:
            other.append(w.w)
            other.extend(w.r.items())
        return raw, other

    def _commit(self, tok, reads, writes):
        k, v = tok
        for r in reads:
            if r.r.get(k, 0) < v:
                r.r[k] = v
        for w in writes:
            w.w = tok
            w.r = {}

    def op(self, eng, fn, reads=(), writes=()):
        reads = [getattr(x, 'res', x) for x in reads] + [x.extra for x in reads if hasattr(x, 'extra')]
        writes = [getattr(x, 'res', x) for x in writes]
        writes = writes + [r for r in reads if r.excl and r not in writes]
        raw, other = self._deps(reads, writes)
        self._wait(eng, raw, other)
        ins = fn()
        ins.then_inc(self.sem[eng], 1)
        self.cnt[eng] += 1
        tok = (eng, self.cnt[eng])
        self._commit(tok, reads, writes)
        return tok

    def dma(self, q, out, in_, reads=(), writes=(), key=None):
        reads = [getattr(x, 'res', x) for x in reads]
        writes = [getattr(x, 'res', x) for x in writes]
        kres = key if key is not None else (writes[0] if writes else reads[0])
        kres = getattr(kres, 'res', kres)
        if kres.sem is None:
            kres.sem = self.newsem()
            self.live.append(kres)
        raw, other = self._deps(reads, writes)
        other = list(other) + [(kres.sem, self.cnt[kres.sem])]
        self._wait(q, raw, other)
        ins = self.E[q].dma_start(out=out, in_=in_)
        ins.then_inc(self.sem[kres.sem], 16)
        self.cnt[kres.sem] += 16
        tok = (kres.sem, self.cnt[kres.sem])
        self._commit(tok, reads, writes)
        return tok

    def barrier(self):
        engs = ("pe", "act", "dve", "pool")
        for e in engs + ("sp",):
            for f in engs:
                if e == f and e == "pe":
                    continue
                v = self.cnt[f]
                if v > 0 and self.seen[e].get(f, 0) < v:
                    self.E[e].wait_ge(self.sem[f], v)
                    self.seen[e][f] = v
            for r in self.live:
                if r.persist:
                    continue
                v = self.cnt[r.sem]
                if v > 0 and self.seen[e].get(r.sem, 0) < v:
                    self.E[e].wait_ge(self.sem[r.sem], v)
                    self.seen[e][r.sem] = v
        self.recycle()

    def drain(self, q="sp"):
        for k in self.dsems:
            v = self.cnt[k]
            if v > 0 and self.seen[q].get(k, 0) < v:
                self.E[q].wait_ge(self.sem[k], v)
                self.seen[q][k] = v
        for f in ("pe", "act", "dve", "pool"):
            v = self.cnt[f]
            if v > 0 and self.seen[q].get(f, 0) < v:
                self.E[q].wait_ge(self.sem[f], v)
                self.seen[q][f] = v


def host_consts():
    ROPE_PAIRS = 16
    t = np.arange(1024)
    inv = (10000.0 ** (-np.arange(ROPE_PAIRS, dtype=np.float32) / ROPE_PAIRS)).astype(np.float32)
    ar = (t // 64).astype(np.float32)[:, None] * inv
    ac = (t % 64).astype(np.float32)[:, None] * inv
    cr, sr, cc, sc_ = np.cos(ar), np.sin(ar), np.cos(ac), np.sin(ac)
    cos64 = np.concatenate([cr, cr, cc, cc], axis=1).astype(np.float32)
    sin64 = np.concatenate([-sr, sr, -sc_, sc_], axis=1).astype(np.float32)
    k = np.arange(128)[:, None]
    q = np.arange(128)[None, :]
    dm = np.zeros((128, 4, 128), np.float32)
    dm[:, 0] = np.maximum(q - k, 0)
    dm[:, 1] = np.maximum(k - q, 0)
    dm[:, 2] = (q >= k)
    dm[:, 3] = (k >= q)
    pr = np.zeros((128, 2, 128), np.float32)
    pr[:, 0, :] = np.arange(128) + 1.0
    pr[:, 1, :] = 128.0 - np.arange(128)
    kp = np.zeros((128, 2), np.float32)
    kp[:, 0] = 127.0 - np.arange(128)
    kp[:, 1] = np.arange(128)
    return dict(cos64=cos64, sin64=sin64, dmat=dm, posrow=pr, kpos=kp)


IN_SPECS = [
    ("xp", (1024, 1024)), ("xs", (1024, 1024)), ("cvec", (2, 1024)),
    ("ck", (2, 256, 512)), ("cv", (2, 256, 512)), ("srf", (2, 4, 64, 64)), ("srb", (2, 4, 64, 64)),
    ("norm1", (2, 1024)), ("w_mod", (2, 1024, 6144)), ("b_mod", (2, 6144)), ("w_in", (2, 1024, 3072)),
    ("sgu_norm", (2, 256)), ("sgu_w", (2, 4, 128, 128)), ("sgu_b", (2, 4, 128)),
    ("rlf", (2, 4)), ("rlb", (2, 4)), ("ret_norm", (2, 256)), ("q_norm", (2, 64)), ("k_norm", (2, 64)),
    ("diff_lam", (2, 256)), ("diff_norm", (2, 128)), ("w_out", (2, 1024, 1024)), ("norm2", (2, 1024)),
    ("ffn_up", (2, 1024, 5632)), ("ffn_conv", (2, 3, 5632)), ("ffn_conv_b", (2, 5632)),
    ("ffn_down", (2, 2816, 1024)),
    ("cos64", (1024, 64)), ("sin64", (1024, 64)), ("dmat", (128, 4, 128)), ("posrow", (128, 2, 128)),
    ("kpos", (128, 2)),
]
OUT_SPECS = [
    ("yp", (1024, 1024)), ("ys", (1024, 1024)), ("nk", (4, 2, 256, 512)), ("nv", (4, 2, 256, 512)),
    ("nrf", (4, 2, 4, 64, 64)), ("nrb", (4, 2, 4, 64, 64)),
]


def build(cfg=None):
    cfg = cfg or {}
    NL = cfg.get("n_layers", 2)
    HALVES = cfg.get("halves", (0, 1))
    taps = cfg.get("taps", None)
    nc = bass.Bass("TRN2", target_bir_lowering=False)
    try:
        nc.allow_low_precision("bf16 matmul operands with fp32 accumulation")
    except Exception:
        pass
    din = {n: nc.dram_tensor(n, list(s), F32, kind="ExternalInput").ap() for n, s in IN_SPECS}
    dout = {n: nc.dram_tensor(n, list(s), F32, kind="ExternalOutput").ap() for n, s in OUT_SPECS}
    es = ExitStack()
    STOP = cfg.get("stop", None)

    class _Stop(Exception):
        pass
    try:
      with es:
        S = Sch(nc, es)

        def ckpt(k):
            if STOP == k:
                S.drain("sp")
                raise _Stop()

        def sb(name, shape, dt=F32):
            return TT(es.enter_context(nc.sbuf_tensor(name, list(shape), dt)), Res(persist=True))

        def tap(name, ap, shape, reads):
            if taps is None or name not in taps:
                return
            d = nc.dram_tensor("tap_" + name, list(shape), ap.dtype, kind="ExternalOutput").ap()
            taps[name] = (list(shape), ap.dtype)
            S.dma("sp", d, ap, reads=reads)

        PSt = [es.enter_context(nc.psum_tensor("ps%d" % i, [128, 1024], F32)) for i in range(4)]
        PB = []
        for i in range(8):
            PB.append(TT(PSt[i // 2], Res(excl=True)))

        def pbank(i):
            return PSt[i // 2][:, (i % 2) * 512:(i % 2) * 512 + 512]

        rot = {"n": 0}

        def nextbank(pool=(0, 1, 2, 3), key="n"):
            i = pool[rot.setdefault(key, 0) % len(pool)]
            rot[key] += 1
            return i

        xT = sb("xT", [128, 8, T])
        hT = sb("hT", [128, 8, T], BF16)
        yT = sb("yT", [128, 8, T], BF16)
        yT.halves = [Res(persist=True), Res(persist=True)]
        ring = [sb("ring%d" % i, [128, 4096], BF16) for i in range(NSLOT)]
        for r_ in ring:
            r_.extra = Res(persist=True)
            r_.extra.sem = S.newsem()
            r_.res.sem = S.newsem()
        ARENA_BYTES = 74 * 1024
        arena = sb("arena", [128, ARENA_BYTES // 4], F32)

        identF = sb("identF", [128, 128])
        identB = sb("identB", [128, 128], BF16)
        onesB = sb("onesB", [128, 128], BF16)
        sTb = sb("sTb", [128, 8, 2], BF16)
        MOD = [sb("MOD%d" % l, [128, 48, 2]) for l in range(2)]
        GS = [[sb("GS%d%d" % (l, n), [128, 8, 2]) for n in range(2)] for l in range(2)]
        NT = sb("NT", [128, 32])
        CW = sb("CW", [128, 2, 4, 44])
        WST = sb("WST", [128, 8, 128], BF16)
        BS = sb("BS", [128, 8])
        SGN = sb("SGN", [128, 2, 256])
        RN = sb("RN", [128, 2, 256])
        QN = sb("QN", [128, 2, 64])
        KN = sb("KN", [128, 2, 64])
        DN = sb("DN", [128, 2, 128])
        LG = sb("LG", [128, 16])
        KP = sb("KP", [128, 2])
        C128 = sb("C128", [128, 64])
        DTm = sb("DTm", [128, 2, 4, 128])
        QD = sb("QD", [128, 2, 2, 2, 128])
        KDE = sb("KDE", [128, 2, 2, 256])
        CD = sb("CD", [128, 2, 2, 2, 64])
        NLAM = sb("NLAM", [128, 2])
        COS = sb("COS", [128, 8, 64])
        SIN = sb("SIN", [128, 8, 64])
        small = [sb("small%d" % i, [128, 16]) for i in range(8)]
        srot = {"i": 0}

        def nsmall():
            s = small[srot["i"] % len(small)]
            srot["i"] += 1
            return s

        wq = []

        def slotv(slot, a, b):
            return slot.t[:, 0:a * b].rearrange("p (a b) -> p a b", b=b)

        def wsrc(name, l, c0, c1):
            return din[name][l, :, c0:c1].rearrange("(kc p) n -> p kc n", p=128)

        def plan_weights():
            def modt(l, which):
                for t2 in range(2):
                    c0 = which * 1024 + t2 * 512
                    wq.append([(lambda s: slotv(s, 8, 512), wsrc("w_mod", l, c0, c0 + 512))])
            first = True
            for half in HALVES:
                for l in range(NL):
                    if first:
                        modt(l, 0)
                        modt(l, 1)
                    for g in range(6):
                        wq.append([(lambda s: slotv(s, 8, 512), wsrc("w_in", l, g * 512, (g + 1) * 512))])
                    if first:
                        modt(l, 2)
                    for g in range(2):
                        wq.append([(lambda s: slotv(s, 8, 512), wsrc("w_out", l, g * 512, (g + 1) * 512))])
                    if first:
                        modt(l, 3)
                        modt(l, 4)
                    for t in range(11):
                        wq.append([
                            (lambda s: slotv(s, 8, 512)[:, :, 0:256], wsrc("ffn_up", l, t * 256, (t + 1) * 256)),
                            (lambda s: slotv(s, 8, 512)[:, :, 256:512],
                             wsrc("ffn_up", l, DFF + t * 256, DFF + (t + 1) * 256)),
                        ])
                    if first:
                        modt(l, 5)
                    for m in range(8):
                        wq.append([(lambda s: slotv(s, 22, 128), wsrc("ffn_down", l, m * 128, (m + 1) * 128))])
                first = False

        plan_weights()
        wstate = {"next_load": 0, "next_use": 0}

        def w_issue():
            i = wstate["next_load"]
            if i >= len(wq):
                return
            slot = ring[i % NSLOT]
            for n_, (dstf, src) in enumerate(wq[i]):
                S.dma("pool", dstf(slot), src, writes=[slot.res if n_ == 0 else slot.extra])
            wstate["next_load"] = i + 1

        def w_get():
            i = wstate["next_use"]
            wstate["next_use"] = i + 1
            assert i < wstate["next_load"]
            return ring[i % NSLOT]

        def w_done():
            w_issue()

        class Arena:
            def __init__(self):
                self.off = 0

            def reset(self):
                self.off = 0

            def get(self, shape, dt):
                n = 1
                for s_ in shape[1:]:
                    n *= s_
                nbytes = n * (4 if dt == F32 else 2)
                nbytes = (nbytes + 63) // 64 * 64
                w0 = self.off // 4
                w1 = (self.off + nbytes) // 4
                assert self.off + nbytes <= ARENA_BYTES, ("arena overflow", self.off, nbytes)
                self.off += nbytes
                ap = arena.t[0:shape[0], w0:w1]
                if dt == BF16:
                    ap = ap.bitcast(BF16)
                nfree = n
                ap = ap[:, 0:nfree]
                if len(shape) > 2:
                    names = " ".join("d%d" % i for i in range(len(shape) - 1))
                    kw = {"d%d" % i: shape[i + 1] for i in range(len(shape) - 1)}
                    ap = ap.rearrange("p (%s) -> p %s" % (names, names), **kw)
                return ap

        AR = Arena()

        class AV:
            def __init__(self, shape, dt=F32):
                self.ap = AR.get(shape, dt)
                self.res = Res()

            @property
            def t(self):
                return self.ap

        S.op("pool", lambda: nc.gpsimd.memset(identF.t[:], 0.0), writes=[identF])
        S.op("pool", lambda: nc.gpsimd.affine_select(out=identF.t[:], in_=identF.t[:], pattern=[[-1, 128]],
                                                      compare_op=ALU.not_equal, fill=1.0, base=0,
                                                      channel_multiplier=1), reads=[identF], writes=[identF])
        S.op("pool", lambda: nc.gpsimd.memset(onesB.t[:], 1.0), writes=[onesB])
        S.op("pool", lambda: nc.gpsimd.memset(C128.t[:], 128.0), writes=[C128])
        S.op("dve", lambda: nc.vector.tensor_copy(identB.t[:], identF.t[:]), reads=[identF], writes=[identB])

        ckpt(1)
        for _ in range(NSLOT):
            w_issue()

        cres = Res()
        crow = AV([2, 1024])
        bmrow = AV([96, 128])
        nrow = AV([32, 128])
        cvrow = AV([44, 2, 4, 128])
        wsraw = AV([128, 8, 128])
        bsrow = AV([8, 128])
        BMT = sb("BMT", [128, 96])
        DL = AV([128, 2, 256])
        DM = AV([128, 4, 128])
        PR = AV([128, 2, 128])

        def cload(dst_tt, dst_ap, src_ap):
            S.dma("sp", dst_ap, src_ap, writes=[dst_tt])

        cload(crow, crow.t[:], din["cvec"])
        cload(bmrow, bmrow.t[:], din["b_mod"].rearrange("l (j p) -> (l j) p", p=128))
        cload(nrow, nrow.t[0:16, :], din["norm1"].rearrange("l (kc p) -> (l kc) p", p=128))
        cload(nrow, nrow.t[16:32, :], din["norm2"].rearrange("l (kc p) -> (l kc) p", p=128))
        for l in range(2):
            cload(cvrow, cvrow.t[:, l, 0:3, :], din["ffn_conv"][l].rearrange("j (c p) -> c j p", p=128))
            cload(cvrow, cvrow.t[:, l, 3, :], din["ffn_conv_b"][l].rearrange("(c p) -> c p", p=128))
        cload(wsraw, wsraw.t[:], din["sgu_w"].rearrange("l g p q -> p (l g) q"))
        cload(bsrow, bsrow.t[:], din["sgu_b"].rearrange("l g p -> (l g) p"))
        cload(SGN, SGN.t[:], din["sgu_norm"].partition_broadcast(128))
        cload(RN, RN.t[:], din["ret_norm"].partition_broadcast(128))
        cload(QN, QN.t[:], din["q_norm"].partition_broadcast(128))
        cload(KN, KN.t[:], din["k_norm"].partition_broadcast(128))
        cload(DN, DN.t[:], din["diff_norm"].partition_broadcast(128))
        cload(DL, DL.t[:], din["diff_lam"].partition_broadcast(128))
        cload(LG, LG.t[:, 0:8], din["rlf"].rearrange("l h -> (l h)").partition_broadcast(128))
        cload(LG, LG.t[:, 8:16], din["rlb"].rearrange("l h -> (l h)").partition_broadcast(128))
        cload(DM, DM.t[:], din["dmat"])
        cload(PR, PR.t[:], din["posrow"])
        cload(KP, KP.t[:], din["kpos"])
        cload(COS, COS.t[:], din["cos64"].rearrange("(t p) c -> p t c", p=128))
        cload(SIN, SIN.t[:], din["sin64"].rearrange("(t p) c -> p t c", p=128))

        ckpt(2)
        csil = AV([2, 1024])
        S.op("act", lambda: nc.scalar.activation(out=csil.t[:], in_=crow.t[:], func=AF.Silu),
             reads=[crow], writes=[csil])
        b = nextbank()

        def f():
            ins = None
            for kc in range(8):
                ins = nc.tensor.transpose(pbank(b)[:, kc * 2:kc * 2 + 2], csil.t[0:2, kc * 128:(kc + 1) * 128],
                                          identF.t[0:2, 0:2])
            return ins
        S.op("pe", f, reads=[csil, identF], writes=[PB[b]])
        S.op("dve", lambda: nc.vector.tensor_copy(sTb.t[:].rearrange("p a b -> p (a b)"), pbank(b)[:, 0:16]),
             reads=[PB[b]], writes=[sTb])

        ckpt(3)
        b = nextbank()
        S.op("pe", lambda: nc.tensor.transpose(pbank(b)[:, 0:32], nrow.t[0:32, :], identF.t[0:32, 0:32]),
             reads=[nrow, identF], writes=[PB[b]])
        S.op("dve", lambda: nc.vector.tensor_copy(NT.t[:], pbank(b)[:, 0:32]), reads=[PB[b]], writes=[NT])
        b = nextbank()
        S.op("pe", lambda: nc.tensor.transpose(pbank(b)[:, 0:96], bmrow.t[0:96, :], identF.t[0:96, 0:96]),
             reads=[bmrow, identF], writes=[PB[b]])
        S.op("dve", lambda: nc.vector.tensor_copy(BMT.t[:], pbank(b)[:, 0:96]), reads=[PB[b]], writes=[BMT])
        ckpt(4)
        b = nextbank()

        def f():
            ins = None
            for l in range(2):
                for j in range(4):
                    c0 = (l * 4 + j) * 44
                    ins = nc.tensor.transpose(pbank(b)[:, c0:c0 + 44], cvrow.t[0:44, l, j, :], identF.t[0:44, 0:44])
            return ins
        S.op("pe", f, reads=[cvrow, identF], writes=[PB[b]])
        S.op("dve", lambda: nc.vector.tensor_copy(CW.t[:].rearrange("p a b c -> p (a b c)"), pbank(b)[:, 0:352]),
             reads=[PB[b]], writes=[CW])
        ckpt(5)
        for hb in range(2):
            b = nextbank()

            def f():
                ins = None
                for i in range(4):
                    ins = nc.tensor.transpose(pbank(b)[:, i * 128:(i + 1) * 128], wsraw.t[:, hb * 4 + i, :], identF.t[:])
                return ins
            S.op("pe", f, reads=[wsraw, identF], writes=[PB[b]])
            S.op("dve", lambda: nc.vector.tensor_copy(
                WST.t[:, hb * 4:(hb + 1) * 4, :].rearrange("p a b -> p (a b)"), pbank(b)[:, 0:512]),
                reads=[PB[b]], writes=[WST])
        b = nextbank()
        S.op("pe", lambda: nc.tensor.transpose(pbank(b)[:, 0:8], bsrow.t[0:8, :], identF.t[0:8, 0:8]),
             reads=[bsrow, identF], writes=[PB[b]])
        S.op("dve", lambda: nc.vector.tensor_copy(BS.t[:], pbank(b)[:, 0:8]), reads=[PB[b]], writes=[BS])

        ckpt(6)
        modrow = sb("modrow", [2, 1024])

        def mods_group(l, which):
            for t2 in range(2):
                slot = w_get()
                b = nextbank()
                wv = slotv(slot, 8, 512)

                def f():
                    ins = None
                    for kc in range(8):
                        ins = nc.tensor.matmul(pbank(b)[0:2, :], sTb.t[:, kc, :], wv[:, kc, :],
                                               start=(kc == 0), stop=(kc == 7))
                    return ins
                S.op("pe", f, reads=[sTb, slot], writes=[PB[b]])
                w_done()
                S.op("dve", lambda: nc.vector.tensor_copy(modrow.t[:, t2 * 512:(t2 + 1) * 512], pbank(b)[0:2, :]),
                     reads=[PB[b]], writes=[modrow])
            b = nextbank()

            def f():
                ins = None
                for jb in range(8):
                    ins = nc.tensor.transpose(pbank(b)[:, jb * 2:jb * 2 + 2], modrow.t[0:2, jb * 128:(jb + 1) * 128],
                                              identF.t[0:2, 0:2])
                return ins
            S.op("pe", f, reads=[modrow, identF], writes=[PB[b]])
            c0 = l * 48 + which * 8
            S.op("dve", lambda: nc.vector.tensor_tensor(
                out=MOD[l].t[:, which * 8:(which + 1) * 8, :], in0=pbank(b)[:, 0:16].rearrange("p (a b) -> p a b", b=2),
                in1=BMT.t[:, c0:c0 + 8].unsqueeze(2).broadcast_to([128, 8, 2]), op=ALU.add),
                 reads=[PB[b], BMT], writes=[MOD[l]])
            if which in (1, 4):
                n = 0 if which == 1 else 1
                ntv = NT.t[:, (n * 2 + l) * 8:(n * 2 + l) * 8 + 8]
                S.op("dve", lambda: nc.vector.scalar_tensor_tensor(
                    out=GS[l][n].t[:], in0=MOD[l].t[:, which * 8:(which + 1) * 8, :], scalar=1.0,
                    in1=ntv.unsqueeze(2).broadcast_to([128, 8, 2]), op0=ALU.add, op1=ALU.mult),
                    reads=[MOD[l], NT], writes=[GS[l][n]])

        ckpt(7)
        S.op("act", lambda: nc.scalar.activation(out=LG.t[:], in_=LG.t[:], func=AF.Exp, scale=-1.0),
             reads=[LG], writes=[LG])
        S.op("act", lambda: nc.scalar.activation(out=LG.t[:], in_=LG.t[:], func=AF.Ln, bias=1.0, scale=1.0),
             reads=[LG], writes=[LG])
        S.op("dve", lambda: nc.vector.tensor_scalar(LG.t[:], LG.t[:], -1.0, None, ALU.mult), reads=[LG], writes=[LG])

        ckpt(8)

        def lgi(d, l, h):
            return d * 8 + l * 4 + h

        dtmp = [AV([128, 128]) for i in range(4)]
        for l in range(NL):
            for h in range(4):
                tf, tb = dtmp[(h % 2) * 2], dtmp[(h % 2) * 2 + 1]
                i_f, i_b = lgi(0, l, h), lgi(1, l, h)
                S.op("act", lambda: nc.scalar.activation(out=tf.t[:], in_=DM.t[:, 0, :], func=AF.Exp,
                                                         scale=LG.t[:, i_f:i_f + 1]), reads=[DM, LG], writes=[tf])
                S.op("act", lambda: nc.scalar.activation(out=tb.t[:], in_=DM.t[:, 1, :], func=AF.Exp,
                                                         scale=LG.t[:, i_b:i_b + 1]), reads=[DM, LG], writes=[tb])
                S.op("dve", lambda: nc.vector.scalar_tensor_tensor(out=tf.t[:], in0=tf.t[:], scalar=0.125,
                                                                   in1=DM.t[:, 2, :], op0=ALU.mult, op1=ALU.mult),
                     reads=[tf, DM], writes=[tf])
                S.op("dve", lambda: nc.vector.scalar_tensor_tensor(out=tb.t[:], in0=tb.t[:], scalar=0.125,
                                                                   in1=DM.t[:, 3, :], op0=ALU.mult, op1=ALU.mult),
                     reads=[tb, DM], writes=[tb])
                S.op("dve", lambda: nc.vector.tensor_tensor(out=DTm.t[:, l, h, :], in0=tf.t[:], in1=tb.t[:],
                                                            op=ALU.add), reads=[tf, tb], writes=[DTm])
            for hp in range(2):
                for d in range(2):
                    for j in range(2):
                        ii = lgi(d, l, 2 * hp + j)
                        ps_ = slice(j * 64, (j + 1) * 64)
                        S.op("act", lambda: nc.scalar.activation(out=QD.t[ps_, l, hp, d, :], in_=PR.t[ps_, d, :],
                                                                 func=AF.Exp, scale=LG.t[ps_, ii:ii + 1]),
                             reads=[PR, LG], writes=[QD])
                        S.op("act", lambda: nc.scalar.activation(out=CD.t[ps_, l, d, hp, :], in_=C128.t[ps_, :],
                                                                 func=AF.Exp, scale=LG.t[ps_, ii:ii + 1]),
                             reads=[C128, LG], writes=[CD])
            for d in range(2):
                for h in range(4):
                    ii = lgi(d, l, h)
                    S.op("act", lambda: nc.scalar.activation(
                        out=KDE.t[:, l, d, h * 64:(h + 1) * 64], in_=KP.t[:, d:d + 1].broadcast_to([128, 64]),
                        func=AF.Exp, scale=LG.t[:, ii:ii + 1], bias=math.log(0.125)),
                        reads=[KP, LG], writes=[KDE])
            lam_init = 0.8 - 0.6 * math.exp(-0.3 * l)
            pr_ = nsmall()
            dlv = DL.t[:, l, :].rearrange("p (a b c) -> p a b c", a=2, b=2)
            lt = dtmp[0]
            S.op("dve", lambda: nc.vector.tensor_tensor(out=lt.t[:].rearrange("p (a c) -> p a c", a=2),
                                                        in0=dlv[:, :, 0, :], in1=dlv[:, :, 1, :], op=ALU.mult),
                 reads=[DL], writes=[lt])
            S.op("dve", lambda: nc.vector.tensor_reduce(out=pr_.t[:, 0:2],
                                                        in_=lt.t[:].rearrange("p (a c) -> p a c", a=2),
                                                        axis=AX.X, op=ALU.add), reads=[lt], writes=[pr_])
            S.op("act", lambda: nc.scalar.activation(out=pr_.t[:, 2:4], in_=pr_.t[:, 0:2], func=AF.Exp),
                 reads=[pr_], writes=[pr_])
            S.op("dve", lambda: nc.vector.tensor_tensor(out=pr_.t[:, 4:5], in0=pr_.t[:, 3:4], in1=pr_.t[:, 2:3],
                                                        op=ALU.subtract), reads=[pr_], writes=[pr_])
            S.op("dve", lambda: nc.vector.tensor_scalar(NLAM.t[:, l:l + 1], pr_.t[:, 4:5], -lam_init, None, ALU.add),
                 reads=[pr_], writes=[NLAM])
            S.op("dve", lambda: nc.vector.tensor_scalar(DN.t[:, l, :], DN.t[:, l, :], 1.0 - lam_init, None, ALU.mult),
                 reads=[DN], writes=[DN])

        S.barrier()
        ckpt(10)

        def rstd_from_ss(ss_ap, out_ap, scale, bias, R, W):
            S.op("act", lambda: nc.scalar.activation(out=out_ap, in_=ss_ap, func=AF.Sqrt, bias=bias, scale=scale),
                 reads=R, writes=W)
            S.op("dve", lambda: nc.vector.reciprocal(out_ap, out_ap), reads=W, writes=W)

        def load_x(src):
            xin = [AV([128, 1024]) for _ in range(2)]
            for tt in range(8):
                xi = xin[tt % 2]
                S.dma("sp", xi.t[:], src[tt * 128:(tt + 1) * 128, :], writes=[xi])
                for hb in range(2):
                    b = nextbank()

                    def f():
                        ins = None
                        for i in range(4):
                            c = hb * 4 + i
                            ins = nc.tensor.transpose(pbank(b)[:, i * 128:(i + 1) * 128],
                                                      xi.t[:, c * 128:(c + 1) * 128], identF.t[:])
                        return ins
                    S.op("pe", f, reads=[xi, identF], writes=[PB[b]])
                    eng = "act" if hb == 0 else "dve"
                    dst = xT.t[:, hb * 4:(hb + 1) * 4, tt * 128:(tt + 1) * 128]
                    srcp = pbank(b).rearrange("p (a b) -> p a b", b=128)
                    if eng == "act":
                        S.op("act", lambda: nc.scalar.copy(out=dst, in_=srcp), reads=[PB[b]], writes=[xT])
                    else:
                        S.op("dve", lambda: nc.vector.tensor_copy(dst, srcp), reads=[PB[b]], writes=[xT])

        def store_x(dst):
            xo = [AV([128, 1024]) for _ in range(2)]
            for tt in range(8):
                xi = xo[tt % 2]
                for hb in range(2):
                    b = nextbank()

                    def f():
                        ins = None
                        for i in range(4):
                            c = hb * 4 + i
                            ins = nc.tensor.transpose(pbank(b)[:, i * 128:(i + 1) * 128],
                                                      xT.t[:, c, tt * 128:(tt + 1) * 128], identF.t[:])
                        return ins
                    S.op("pe", f, reads=[xT, identF], writes=[PB[b]])
                    dstp = xi.t[:, hb * 512:(hb + 1) * 512]
                    if hb == 0:
                        S.op("act", lambda: nc.scalar.copy(out=dstp, in_=pbank(b)), reads=[PB[b]], writes=[xi])
                    else:
                        S.op("dve", lambda: nc.vector.tensor_copy(dstp, pbank(b)), reads=[PB[b]], writes=[xi])
                S.dma("sp", dst[tt * 128:(tt + 1) * 128, :], xi.t[:], reads=[xi])

        def norm_mod(l, n, cond, presq=None):
            RB = AV([128, T])
            if presq is None:
                sq = yT
                S.op("act", lambda: nc.scalar.activation(out=sq.t[:], in_=xT.t[:], func=AF.Square),
                     reads=[xT], writes=[sq] + yT.halves)
            else:
                sq = presq
            for tg in range(2):
                b = nextbank()

                def f():
                    ins = None
                    for kc in range(8):
                        ins = nc.tensor.matmul(pbank(b), onesB.t[:], sq.t[:, kc, tg * 512:(tg + 1) * 512],
                                               start=(kc == 0), stop=(kc == 7))
                    return ins
                S.op("pe", f, reads=[sq, onesB], writes=[PB[b]])
                rstd_from_ss(pbank(b), RB.t[:, tg * 512:(tg + 1) * 512], 1.0 / D, EPS, [PB[b]], [RB])
            shi = 0 if n == 0 else 3
            tmp = [AV([128, 1024]) for _ in range(2)]
            for kc in range(8):
                tm = tmp[kc % 2]
                S.op("dve", lambda: nc.vector.scalar_tensor_tensor(
                    out=tm.t[:], in0=xT.t[:, kc, :], scalar=GS[l][n].t[:, kc, cond:cond + 1], in1=RB.t[:],
                    op0=ALU.mult, op1=ALU.mult), reads=[xT, GS[l][n], RB], writes=[tm])
                S.op("act", lambda: nc.scalar.activation(out=hT.t[:, kc, :], in_=tm.t[:], func=AF.Identity,
                                                         bias=MOD[l].t[:, shi * 8 + kc, cond:cond + 1], scale=1.0),
                     reads=[tm, MOD[l]], writes=[hT])

        def zmm(slot, tt, b):
            wv = slotv(slot, 8, 512)

            def f():
                ins = None
                for kc in range(8):
                    ins = nc.tensor.matmul(pbank(b), hT.t[:, kc, tt * 128:(tt + 1) * 128], wv[:, kc, :],
                                           start=(kc == 0), stop=(kc == 7))
                return ins
            S.op("pe", f, reads=[hT, slot], writes=[PB[b]])

        def group_rstd(src_ap, ngrp, gsz, scale, bias, sqt, ss):
            src_tt, sq_tt = sqt
            S.op("dve", lambda: nc.vector.tensor_tensor(out=sq_tt.t[:, 0:ngrp * gsz], in0=src_ap, in1=src_ap,
                                                        op=ALU.mult), reads=[src_tt], writes=[sq_tt])
            S.op("dve", lambda: nc.vector.tensor_reduce(
                out=ss.t[:, 0:ngrp], in_=sq_tt.t[:, 0:ngrp * gsz].rearrange("p (a b) -> p a b", b=gsz),
                axis=AX.X, op=ALU.add), reads=[sq_tt], writes=[ss])
            rstd_from_ss(ss.t[:, 0:ngrp], ss.t[:, 0:ngrp], scale, bias, [ss], [ss])

        def transposes_to(src_tt, src_ap_fn, nblk, dst_tt, dst_ap, pool=(0, 1, 2, 3)):
            b = nextbank(pool)
            pv = pbank(b).bitcast(BF16)

            def f():
                ins = None
                for i in range(nblk):
                    ins = nc.tensor.transpose(pv[:, i * 128:(i + 1) * 128], src_ap_fn(i), identB.t[:])
                return ins
            S.op("pe", f, reads=[src_tt, identB], writes=[PB[b]])
            S.op("dve", lambda: nc.vector.tensor_copy(dst_ap, pv[:, 0:nblk * 128].rearrange("p (a b) -> p a b", b=128)),
                 reads=[PB[b]], writes=[dst_tt])

        def mixer(l, half):
            cond = half
            nseq, L = (4, 256) if half == 0 else (1, 1024)
            cpl = L // 128
            AR.reset()
            ytok = AV([128, 8, 1024], BF16)
            ar_mark = AR.off

            class _V:
                pass
            SCR = _V()
            SCR.ap = yT.t[:].bitcast(F32)
            SCR.t = SCR.ap
            SCR.res = yT.res
            slot = w_get()
            GE = AV([128, 8, 512])
            VN = AV([128, 8, 256], BF16)
            ssG = AV([128, 8])
            for tt in range(8):
                b = nextbank()
                zmm(slot, tt, b)
                S.op("act", lambda: nc.scalar.activation(out=GE.t[:, tt, :], in_=pbank(b), func=AF.Gelu_apprx_tanh),
                     reads=[PB[b]], writes=[GE])
            w_done()
            gv = GE.t[:, :, 256:512]
            sv = SCR.t[:, :, 0:256]

            def g0_stageB():
                S.op("dve", lambda: nc.vector.tensor_tensor(out=sv, in0=gv, in1=gv, op=ALU.mult), reads=[GE], writes=[SCR])
                S.op("dve", lambda: nc.vector.tensor_reduce(out=ssG.t[:], in_=sv, axis=AX.X, op=ALU.add),
                     reads=[SCR], writes=[ssG])
                rstd_from_ss(ssG.t[:], ssG.t[:], 1.0 / 256, EPS, [ssG], [ssG])
                S.op("dve", lambda: nc.vector.tensor_tensor(out=gv, in0=gv,
                                                            in1=ssG.t[:].unsqueeze(2).broadcast_to([128, 8, 256]),
                                                            op=ALU.mult), reads=[GE, ssG], writes=[GE])
                S.op("dve", lambda: nc.vector.tensor_tensor(
                    out=VN.t[:], in0=gv, in1=SGN.t[:, l, :].unsqueeze(1).broadcast_to([128, 8, 256]), op=ALU.mult),
                    reads=[GE, SGN], writes=[VN])

            def g0_stageC(tt):
                b2 = nextbank()

                def f():
                    ins = None
                    for g in range(4):
                        ins = nc.tensor.matmul(pbank(b2)[:, g * 64:(g + 1) * 64], WST.t[:, l * 4 + g, :],
                                               VN.t[:, tt, g * 64:(g + 1) * 64], start=True, stop=True)
                    return ins
                S.op("pe", f, reads=[WST, VN], writes=[PB[b2]])
                for g in range(4):
                    S.op("dve", lambda: nc.vector.scalar_tensor_tensor(
                        out=ytok.t[:, tt, g * 64:(g + 1) * 64], in0=pbank(b2)[:, g * 64:(g + 1) * 64],
                        scalar=BS.t[:, l * 4 + g:l * 4 + g + 1], in1=GE.t[:, tt, g * 64:(g + 1) * 64],
                        op0=ALU.add, op1=ALU.mult), reads=[PB[b2], BS, GE], writes=[ytok])

            ckpt(13)

            QT = AV([128, 2, 1024], BF16)
            KT = AV([128, 2, 1024], BF16)
            VB = AV([128, 8, 256], BF16)
            SG = AV([128, 8, 256], BF16)
            KVS = AV([128, 8, 2, 2, 64])
            RS = AV([128, 2, 2, 64])
            RSb = AV([128, 8, 2, 2, 64], BF16)

            class _W:
                pass
            RSd = []
            RSbd = []
            for d_ in range(2):
                v_ = _W()
                v_.ap = RS.t[:, d_, :, :]
                v_.t = v_.ap
                v_.res = Res()
                RSd.append(v_)
                w_ = _W()
                w_.res = Res()
                RSbd.append(w_)
            qkb = [AV([128, 512], BF16) for _ in range(2)]
            KF = [AV([128, 2, 256], BF16) for _ in range(2)]
            gtmp = [AV([128, 256]) for _ in range(2)]
            st_stage = [AV([128, 2, 2, 64]) for _ in range(2)]
            slot1 = w_get()
            slot2 = w_get()
            g0_stageB()
            for tt in range(8):
                if tt >= 1:
                    g0_stageC(tt - 1)
                b1 = nextbank()
                zmm(slot1, tt, b1)
                if tt == 0:
                    ckpt(1301)
                b2 = nextbank()
                zmm(slot2, tt, b2)
                if tt == 0:
                    ckpt(1302)
                qk = qkb[tt % 2]
                S.op("act", lambda: nc.scalar.copy(out=qk.t[:], in_=pbank(b1)), reads=[PB[b1]], writes=[qk])
                if tt == 0:
                    ckpt(1303)
                kf = KF[tt % 2]
                for d in range(2):
                    S.op("dve", lambda: nc.vector.tensor_tensor(
                        out=kf.t[:, d, :], in0=pbank(b1)[:, 256:512], in1=KDE.t[:, l, d, :], op=ALU.mult),
                        reads=[PB[b1], KDE], writes=[kf])
                    if tt == 0 and d == 0:
                        ckpt(1304)
                if tt == 0:
                    ckpt(131)
                bt = nextbank((6, 7), "t67")
                pv = pbank(bt).bitcast(BF16)

                def f():
                    ins = None
                    for i in range(4):
                        ins = nc.tensor.transpose(pv[:, i * 128:(i + 1) * 128], qk.t[:, i * 128:(i + 1) * 128],
                                                  identB.t[:])
                    return ins
                S.op("pe", f, reads=[qk, identB], writes=[PB[bt]])
                if tt == 0:
                    ckpt(132)
                S.op("dve", lambda: nc.vector.tensor_copy(
                    QT.t[:, :, tt * 128:(tt + 1) * 128], pv[:, 0:256].rearrange("p (a b) -> p a b", b=128)),
                    reads=[PB[bt]], writes=[QT])
                S.op("dve", lambda: nc.vector.tensor_copy(
                    KT.t[:, :, tt * 128:(tt + 1) * 128], pv[:, 256:512].rearrange("p (a b) -> p a b", b=128)),
                    reads=[PB[bt]], writes=[KT])
                if tt == 0:
                    ckpt(133)
                S.op("act", lambda: nc.scalar.copy(out=VB.t[:, tt, :], in_=pbank(b2)[:, 0:256]),
                     reads=[PB[b2]], writes=[VB])
                gt_ = gtmp[tt % 2]
                S.op("act", lambda: nc.scalar.activation(out=gt_.t[:], in_=pbank(b2)[:, 256:512], func=AF.Silu),
                     reads=[PB[b2]], writes=[gt_])
                S.op("dve", lambda: nc.vector.tensor_tensor(out=SG.t[:, tt, :], in0=gt_.t[:], in1=RN.t[:, l, :],
                                                            op=ALU.mult), reads=[gt_, RN], writes=[SG])
                if tt == 0:
                    ckpt(134)
                bk = nextbank((4, 5), "t45")

                def f():
                    ins = None
                    for d in range(2):
                        for hp in range(2):
                            ins = nc.tensor.matmul(pbank(bk)[:, (d * 2 + hp) * 128:(d * 2 + hp + 1) * 128],
                                                   kf.t[:, d, hp * 128:(hp + 1) * 128],
                                                   VB.t[:, tt, hp * 128:(hp + 1) * 128], start=True, stop=True)
                    return ins
                S.op("pe", f, reads=[kf, VB], writes=[PB[bk]])
                if tt == 0:
                    ckpt(135)
                pk = pbank(bk).rearrange("p (a b) -> p a b", b=128)
                for j in range(2):
                    ps_ = slice(j * 64, (j + 1) * 64)
                    S.op("dve", lambda: nc.vector.tensor_copy(
                        KVS.t[ps_, tt, :, :, :].rearrange("p a b c -> p (a b) c"), pk[ps_, :, j * 64:(j + 1) * 64]),
                        reads=[PB[bk]], writes=[KVS])
            g0_stageC(7)
            w_done()
            w_done()

            ckpt(14)
            for s_ in range(nseq):
                if half == 0:
                    S.op("dve", lambda: nc.vector.memset(RS.t[:], 0.0), writes=[RSd[0], RSd[1]])
                else:
                    for d, nm in ((0, "srf"), (1, "srb")):
                        for j in range(2):
                            srcs = din[nm][l].rearrange("(hp j) d e -> j d hp e", j=2)[j]
                            S.dma("sp", RS.t[j * 64:(j + 1) * 64, d, :, :], srcs, writes=[RSd[d]])
                def step(d, tt):
                    S.op("dve", lambda: nc.vector.tensor_copy(RSb.t[:, tt, d, :, :], RSd[d].t[:]),
                         reads=[RSd[d]], writes=[RSbd[d]])
                    S.op("dve", lambda: nc.vector.tensor_tensor(out=RSd[d].t[:], in0=RSd[d].t[:],
                                                                in1=CD.t[:, l, d, :, :], op=ALU.mult),
                         reads=[RSd[d], CD], writes=[RSd[d]])
                    S.op("dve", lambda: nc.vector.tensor_tensor(out=RSd[d].t[:], in0=RSd[d].t[:],
                                                                in1=KVS.t[:, tt, d, :, :], op=ALU.add),
                         reads=[RSd[d], KVS], writes=[RSd[d]])
                for c in range(cpl):
                    step(0, s_ * cpl + c)
                    step(1, s_ * cpl + (cpl - 1 - c))
                if half == 0:
                    stg = st_stage[s_ % 2]
                    S.op("dve", lambda: nc.vector.tensor_copy(stg.t[:], RS.t[:]), reads=[RSd[0], RSd[1]], writes=[stg])
                    for d, nm in ((0, "nrf"), (1, "nrb")):
                        for j in range(2):
                            dsts = dout[nm][s_, l].rearrange("(hp j) d e -> j d hp e", j=2)[j]
                            S.dma("sp", dsts, stg.t[j * 64:(j + 1) * 64, d, :, :], reads=[stg])

            ckpt(15)
            S.barrier()
            ar_keep = AR.off
            AR.off = ar_mark
            MT = [AV([128, 512], BF16) for _ in range(2)]
            QF = [AV([128, 2, 2, 128], BF16) for _ in range(2)]
            OR = AV([128, 8, 256])
            for tt in range(8):
                c = tt % cpl
                tsl = slice(tt * 128, (tt + 1) * 128)
                bia = nextbank((0, 1), "t01")
                bib = nextbank((0, 1), "t01")

                def f():
                    ins = None
                    for j, bnk in ((0, bia), (1, bib)):
                        ps_ = slice(j * 64, (j + 1) * 64)
                        for hp in range(2):
                            ins = nc.tensor.matmul(pbank(bnk)[:, hp * 128:(hp + 1) * 128], KT.t[ps_, hp, tsl],
                                                   QT.t[ps_, hp, tsl], start=True, stop=True)
                    return ins
                S.op("pe", f, reads=[KT, QT], writes=[PB[bia], PB[bib]])
                mt = MT[tt % 2]
                mt4 = mt.t[:].rearrange("p (hp j q) -> p hp j q", hp=2, j=2)
                for j, bnk in ((0, bia), (1, bib)):
                    S.op("dve", lambda: nc.vector.tensor_tensor(
                        out=mt4[:, :, j, :], in0=pbank(bnk)[:, 0:256].rearrange("p (a b) -> p a b", b=128),
                        in1=DTm.t[:, l, :, :].rearrange("p (hp j) q -> p hp j q", j=2)[:, :, j, :], op=ALU.mult),
                        reads=[PB[bnk], DTm], writes=[mt])
                qf = QF[tt % 2]
                S.op("dve", lambda: nc.vector.tensor_tensor(
                    out=qf.t[:], in0=QD.t[:, l, :, :, :],
                    in1=QT.t[:, :, tsl].unsqueeze(2).broadcast_to([128, 2, 2, 128]), op=ALU.mult),
                    reads=[QT, QD], writes=[qf])
                use_f = not (half == 0 and c == 0)
                use_b = not (half == 0 and c == cpl - 1)
                bo = nextbank((2, 3), "t23")

                def f():
                    ins = None
                    for h in range(4):
                        hp, j = h // 2, h % 2
                        ps_ = slice(j * 64, (j + 1) * 64)
                        o_ap = pbank(bo)[:, h * 64:(h + 1) * 64]
                        last = not (use_f or use_b)
                        ins = nc.tensor.matmul(o_ap, mt.t[:, h * 128:(h + 1) * 128], VB.t[:, tt, h * 64:(h + 1) * 64],
                                               start=True, stop=last)
                        if use_f:
                            ins = nc.tensor.matmul(o_ap, qf.t[ps_, hp, 0, :], RSb.t[ps_, tt, 0, hp, :],
                                                   start=False, stop=not use_b)
                        if use_b:
                            ins = nc.tensor.matmul(o_ap, qf.t[ps_, hp, 1, :], RSb.t[ps_, tt, 1, hp, :],
                                                   start=False, stop=True)
                    return ins
                S.op("pe", f, reads=[mt, VB, qf, RSbd[0], RSbd[1]], writes=[PB[bo]])
                S.op("act", lambda: nc.scalar.copy(out=OR.t[:, tt, :], in_=pbank(bo)[:, 0:256]),
                     reads=[PB[bo]], writes=[OR])
            sv = SCR.t[:, :, 0:256]
            S.op("act", lambda: nc.scalar.activation(out=sv, in_=OR.t[:], func=AF.Square), reads=[OR], writes=[SCR])
            ssR = AV([128, 8, 4])
            S.op("dve", lambda: nc.vector.tensor_reduce(out=ssR.t[:], in_=sv.rearrange("p t (h c) -> p t h c", c=64),
                                                        axis=AX.X, op=ALU.add), reads=[SCR], writes=[ssR])
            rstd_from_ss(ssR.t[:], ssR.t[:], 1.0 / 64, EPS, [ssR], [ssR])
            o4 = OR.t[:].rearrange("p t (h c) -> p t h c", c=64)
            S.op("dve", lambda: nc.vector.tensor_tensor(out=o4, in0=o4,
                                                        in1=ssR.t[:].unsqueeze(3).broadcast_to([128, 8, 4, 64]),
                                                        op=ALU.mult), reads=[OR, ssR], writes=[OR])
            S.op("dve", lambda: nc.vector.tensor_tensor(out=ytok.t[:, :, 256:512], in0=OR.t[:], in1=SG.t[:],
                                                        op=ALU.mult), reads=[OR, SG], writes=[ytok])
            S.barrier()
            ckpt(16)
            AR.off = ar_mark

            NKC = 2 if half == 0 else 10
            KOFF = 0 if half == 0 else 256
            NKEY = 1024 + KOFF
            QTa = AV([128, 4, 1024], BF16)
            KTa = AV([128, 4, NKEY], BF16)
            VA = AV([128, NKEY // 128, 4, 132], BF16)
            S.op("dve", lambda: nc.vector.memset(VA.t[:, :, :, 128:129], 1.0), writes=[VA])
            ar_att = AR.off
            stg = [AV([128, 512]) for _ in range(2)]

            class _G:
                pass
            GR = []
            for g_ in range(2):
                o_ = _G()
                o_.QA = AV([128, 4, 512])
                o_.QB = AV([128, 4, 512], BF16)
                o_.ss = AV([128, 32])
                o_.SQ = _G()
                o_.SQ.ap = SCR.t[:, g_ * 4:(g_ + 1) * 4, :]
                o_.SQ.t = o_.SQ.ap
                o_.SQ.res = Res()
                GR.append(o_)
            S.op("dve", lambda: nc.vector.memset(GR[0].ss.t[:], 0.0), reads=[yT], writes=[GR[0].ss, GR[0].SQ, GR[1].SQ])
            if half == 1:
                for kc in range(2):
                    st = stg[kc % 2]
                    S.dma("sp", st.t[:], din["ck"][l, kc * 128:(kc + 1) * 128, :], writes=[st])
                    S.op("act", lambda: nc.scalar.copy(out=GR[0].QB.t[:, kc, :], in_=st.t[:]), reads=[st],
                         writes=[GR[0].QB])
                    transposes_to(GR[0].QB, lambda i: GR[0].QB.t[:, kc, i * 128:(i + 1) * 128], 4, KTa,
                                  KTa.t[:, :, kc * 128:(kc + 1) * 128])
                for kc in range(2):
                    st = stg[kc % 2]
                    S.dma("sp", st.t[:], din["cv"][l, kc * 128:(kc + 1) * 128, :], writes=[st])
                    S.op("act", lambda: nc.scalar.copy(out=VA.t[:, kc, :, 0:128],
                                                       in_=st.t[:].rearrange("p (a b) -> p a b", b=128)),
                         reads=[st], writes=[VA])

            def stageA1(slot, g):
                G = GR[g]
                for t in range(4):
                    tt = g * 4 + t
                    b = nextbank()
                    zmm(slot, tt, b)
                    S.op("act", lambda: nc.scalar.copy(out=G.QA.t[:, t, :], in_=pbank(b)), reads=[PB[b]], writes=[G.QA])
                    S.op("act", lambda: nc.scalar.activation(out=G.SQ.t[:, t, :], in_=pbank(b), func=AF.Square),
                         reads=[PB[b]], writes=[G.SQ])

            def stageA2(g):
                G = GR[g]
                S.op("dve", lambda: nc.vector.tensor_reduce(
                    out=G.ss.t[:], in_=G.SQ.t[:].rearrange("p t (a b) -> p (t a) b", b=64),
                    axis=AX.X, op=ALU.add), reads=[G.SQ], writes=[G.ss])

            def stageB(g, gain_tt, which):
                G = GR[g]
                if which == "q":
                    rstd_from_ss(G.ss.t[:], G.ss.t[:], 1.0, 64 * EPS, [G.ss], [G.ss])
                else:
                    rstd_from_ss(G.ss.t[:], G.ss.t[:], 1.0 / 64, EPS, [G.ss], [G.ss])
                qf_ = G.QA.t[:].rearrange("p a b -> p (a b)")
                sf_ = G.SQ.t[:].rearrange("p a b -> p (a b)")
                q3 = qf_.rearrange("p (a b) -> p a b", b=64)
                S.op("dve", lambda: nc.vector.tensor_tensor(out=q3, in0=q3,
                                                            in1=G.ss.t[:].unsqueeze(2).broadcast_to([128, 32, 64]),
                                                            op=ALU.mult), reads=[G.QA, G.ss], writes=[G.QA])
                gbc = gain_tt.t[:, l, :].unsqueeze(1).broadcast_to([128, 32, 64])
                if half == 0 and which == "q":
                    S.op("dve", lambda: nc.vector.tensor_tensor(
                        out=G.QB.t[:].rearrange("p a (g c) -> p (a g) c", c=64), in0=q3, in1=gbc, op=ALU.mult),
                        reads=[G.QA, gain_tt], writes=[G.QB])
                else:
                    S.op("dve", lambda: nc.vector.tensor_tensor(out=q3, in0=q3, in1=gbc, op=ALU.mult),
                         reads=[G.QA, gain_tt], writes=[G.QA])
                    if half == 0:
                        for s2 in range(2):
                            s_ = g * 2 + s2
                            S.dma("sp", dout["nk"][s_, l].rearrange("(t p) c -> p t c", p=128),
                                  G.QA.t[:, 2 * s2:2 * s2 + 2, :], reads=[G.QA])
                        S.op("act", lambda: nc.scalar.copy(out=G.QB.t[:].rearrange("p a b -> p (a b)"), in_=qf_),
                             reads=[G.QA], writes=[G.QB])
                    else:
                        x5 = G.QA.t[:].rearrange("p t (g a s c) -> p t g a s c", a=2, s=2, c=16)
                        r5 = G.SQ.t[:].rearrange("p t (g a s c) -> p t g a s c", a=2, s=2, c=16)
                        s5 = SIN.t[:, g * 4:(g + 1) * 4, :].rearrange("p t (a s c) -> p t a s c", s=2, c=16)
                        for sidx in range(2):
                            for ax_ in range(2):
                                S.op("dve", lambda: nc.vector.tensor_tensor(
                                    out=r5[:, :, :, ax_, sidx, :], in0=x5[:, :, :, ax_, 1 - sidx, :],
                                    in1=s5[:, :, ax_, sidx, :].unsqueeze(2).broadcast_to([128, 4, 8, 16]), op=ALU.mult),
                                    reads=[G.QA, SIN], writes=[G.SQ])
                        q4 = G.QA.t[:].rearrange("p t (g c) -> p t g c", c=64)
                        S.op("dve", lambda: nc.vector.tensor_tensor(
                            out=q4, in0=q4,
                            in1=COS.t[:, g * 4:(g + 1) * 4, :].unsqueeze(2).broadcast_to([128, 4, 8, 64]), op=ALU.mult),
                            reads=[G.QA, COS], writes=[G.QA])
                        S.op("dve", lambda: nc.vector.tensor_tensor(out=G.QB.t[:].rearrange("p a b -> p (a b)"),
                                                                    in0=qf_, in1=sf_, op=ALU.add),
                             reads=[G.QA, G.SQ], writes=[G.QB])

            def stageC(g, dstT, doff):
                G = GR[g]
                for t in range(4):
                    tt = g * 4 + t
                    b = nextbank()
                    pv = pbank(b).bitcast(BF16)

                    def f():
                        ins = None
                        for i in range(4):
                            ins = nc.tensor.transpose(pv[:, i * 128:(i + 1) * 128], G.QB.t[:, t, i * 128:(i + 1) * 128],
                                                      identB.t[:])
                        return ins
                    S.op("pe", f, reads=[G.QB, identB], writes=[PB[b]])
                    S.op("act", lambda: nc.scalar.copy(out=dstT.t[:, :, doff + tt * 128:doff + (tt + 1) * 128],
                                                       in_=pv[:, 0:512].rearrange("p (a b) -> p a b", b=128)),
                         reads=[PB[b]], writes=[dstT])

            def stageV(slot, g):
                for t in range(4):
                    tt = g * 4 + t
                    b = nextbank()
                    zmm(slot, tt, b)
                    kci = KOFF // 128 + tt
                    if half == 0:
                        st = stg[tt % 2]
                        S.op("act", lambda: nc.scalar.copy(out=st.t[:], in_=pbank(b)), reads=[PB[b]], writes=[st])
                        S.dma("sp", dout["nv"][tt // 2, l, (tt % 2) * 128:(tt % 2) * 128 + 128, :], st.t[:], reads=[st])
                        S.op("act", lambda: nc.scalar.copy(out=VA.t[:, kci, :, 0:128],
                                                           in_=st.t[:].rearrange("p (a b) -> p a b", b=128)),
                             reads=[st], writes=[VA])
                    else:
                        S.op("act", lambda: nc.scalar.copy(out=VA.t[:, kci, :, 0:128],
                                                           in_=pbank(b).rearrange("p (a b) -> p a b", b=128)),
                             reads=[PB[b]], writes=[VA])

            slot3 = w_get()
            stageA1(slot3, 0)
            stageA2(0)
            stageB(0, QN, "q")
            stageA1(slot3, 1)
            stageA2(1)
            w_done()
            slot4 = w_get()
            stageC(0, QTa, 0)
            stageB(1, QN, "q")
            stageA1(slot4, 0)
            stageA2(0)
            stageC(1, QTa, 0)
            stageB(0, KN, "k")
            stageA1(slot4, 1)
            stageA2(1)
            w_done()
            slot5 = w_get()
            stageC(0, KTa, KOFF)
            stageV(slot5, 0)
            stageB(1, KN, "k")
            stageV(slot5, 1)
            stageC(1, KTa, KOFF)
            w_done()
            S.barrier()
            ckpt(17)
            AR.off = ar_att
            OALL = AV([128, 8, 4, 128])
            ETP = [AV([128, 2, 2, 256], BF16) for _ in range(2)]
            ot2 = [AV([128, 128]) for _ in range(2)]
            work = []
            for qg in range(4):
                kcs = [qg * 2, qg * 2 + 1] if half == 0 else list(range(10))
                for h in range(4):
                    for pi in range(len(kcs) // 2):
                        work.append((qg, h, pi, len(kcs) // 2, (kcs[2 * pi], kcs[2 * pi + 1])))
            st_banks = {}

            def emit_st(widx):
                qg, h, pi, npair, kc2 = work[widx]
                q0 = qg * 256
                bsA, bsB = ((0, 1), (2, 3))[widx % 2]
                st_banks[widx] = (bsA, bsB)

                def f():
                    ins = None
                    for i, bnk in ((0, bsA), (1, bsB)):
                        ps_ = slice(i * 64, (i + 1) * 64)
                        for kk in range(2):
                            kc = kc2[kk]
                            ins = nc.tensor.matmul(pbank(bnk)[:, kk * 256:(kk + 1) * 256],
                                                   KTa.t[ps_, h, kc * 128:(kc + 1) * 128],
                                                   QTa.t[ps_, h, q0:q0 + 256], start=True, stop=True)
                    return ins
                S.op("pe", f, reads=[KTa, QTa], writes=[PB[bsA], PB[bsB]])

            OSQh = AV([128, 2048])

            def norm_half(hh):
                ov = OALL.t[:, hh * 4:(hh + 1) * 4, :, :]
                of_ = ov.rearrange("p a b c -> p (a b c)")
                ss2 = AV([128, 16])
                S.op("dve", lambda: nc.vector.tensor_tensor(out=OSQh.t[:], in0=of_, in1=of_, op=ALU.mult),
                     reads=[OALLh[hh]], writes=[OSQh])
                S.op("dve", lambda: nc.vector.tensor_reduce(out=ss2.t[:], in_=OSQh.t[:].rearrange("p (a b) -> p a b", b=128),
                                                            axis=AX.X, op=ALU.add), reads=[OSQh], writes=[ss2])
                rstd_from_ss(ss2.t[:], ss2.t[:], 1.0 / 128, EPS, [ss2], [ss2])
                o3 = of_.rearrange("p (a b) -> p a b", b=128)
                S.op("dve", lambda: nc.vector.tensor_tensor(out=o3, in0=o3,
                                                            in1=ss2.t[:].unsqueeze(2).broadcast_to([128, 16, 128]),
                                                            op=ALU.mult), reads=[OALLh[hh], ss2], writes=[OALLh[hh]])
                S.op("dve", lambda: nc.vector.tensor_tensor(
                    out=ytok.t[:, hh * 4:(hh + 1) * 4, 512:1024].rearrange("p t (h c) -> p t h c", c=128), in0=ov,
                    in1=DN.t[:, l, :].unsqueeze(1).unsqueeze(1).broadcast_to([128, 4, 4, 128]), op=ALU.mult),
                    reads=[OALLh[hh], DN], writes=[ytok])

            OALLh = [Res(), Res()]
            emit_st(0)
            itn = 0
            for widx in range(len(work)):
                qg, h, pi, npair, kc2 = work[widx]
                q0 = qg * 256
                if pi == 0:
                    accs = ((4, 5), (6, 7))[itn % 2]
                    itn += 1
                if widx + 1 < len(work):
                    emit_st(widx + 1)
                bsA, bsB = st_banks.pop(widx)
                etp = ETP[widx % 2]
                S.op("act", lambda: nc.scalar.activation(out=etp.t[:].rearrange("p i a b -> p (i a b)"),
                                                         in_=PSt[bsA // 2][:, 0:1024], func=AF.Exp),
                     reads=[PB[bsA], PB[bsB]], writes=[etp])

                def f():
                    ins = None
                    for kk in range(2):
                        kc = kc2[kk]
                        for qb_ in range(2):
                            for i in range(2):
                                first = (pi == 0 and kk == 0 and i == 0)
                                last = (pi == npair - 1 and kk == 1 and i == 1)
                                ins = nc.tensor.matmul(pbank(accs[qb_])[:, i * 129:(i + 1) * 129],
                                                       etp.t[:, i, kk, qb_ * 128:(qb_ + 1) * 128],
                                                       VA.t[:, kc, h, 0:129], start=first, stop=last)
                    return ins
                S.op("pe", f, reads=[etp, VA], writes=[PB[accs[0]], PB[accs[1]]])
                if pi == npair - 1:
                    for qb_ in range(2):
                        tt = (q0 + qb_ * 128) // 128
                        ab = accs[qb_]
                        acc = pbank(ab)
                        rc = nsmall()
                        S.op("dve", lambda: nc.vector.reciprocal(
                            rc.t[:, 0:2], acc[:, 0:258].rearrange("p (a b) -> p a b", b=129)[:, :, 128]),
                            reads=[PB[ab]], writes=[rc])
                        S.op("dve", lambda: nc.vector.tensor_tensor(out=rc.t[:, 2:3], in0=rc.t[:, 1:2],
                                                                    in1=NLAM.t[:, l:l + 1], op=ALU.mult),
                             reads=[rc, NLAM], writes=[rc])
                        o1 = ot2[qb_]
                        S.op("dve", lambda: nc.vector.tensor_scalar(o1.t[:], acc[:, 129:257], rc.t[:, 2:3], None,
                                                                    ALU.mult), reads=[PB[ab], rc], writes=[o1])
                        S.op("dve", lambda: nc.vector.scalar_tensor_tensor(
                            out=OALL.t[:, tt, h, :], in0=acc[:, 0:128], scalar=rc.t[:, 0:1], in1=o1.t[:],
                            op0=ALU.mult, op1=ALU.add), reads=[PB[ab], rc, o1], writes=[OALLh[qg // 2]])
                        if qb_ == 1 and h == 3 and qg in (1, 3):
                            norm_half(qg // 2)
            ckpt(18)
            if half == HALVES[0]:
                mods_group(l, 2)
            slots_o = [w_get(), w_get()]
            for tg in range(2):
                for tt in range(tg * 4, tg * 4 + 4):
                    b = nextbank()
                    pv = pbank(b).bitcast(BF16)

                    def f():
                        ins = None
                        for i in range(8):
                            ins = nc.tensor.transpose(pv[:, i * 128:(i + 1) * 128], ytok.t[:, tt, i * 128:(i + 1) * 128],
                                                      identB.t[:])
                        return ins
                    S.op("pe", f, reads=[ytok, identB], writes=[PB[b]])
                    S.op("dve", lambda: nc.vector.tensor_copy(
                        yT.t[:, :, tt * 128:(tt + 1) * 128], pv[:, 0:1024].rearrange("p (a b) -> p a b", b=128)),
                        reads=[PB[b]], writes=[yT.halves[tg], yT])
                for cg in range(2):
                    slot = slots_o[cg]
                    wv = slotv(slot, 8, 512)
                    for m in range(4):
                        mm = cg * 4 + m
                        b = nextbank()

                        def f():
                            ins = None
                            for kc in range(8):
                                ins = nc.tensor.matmul(pbank(b), wv[:, kc, m * 128:(m + 1) * 128],
                                                       yT.t[:, kc, tg * 512:(tg + 1) * 512],
                                                       start=(kc == 0), stop=(kc == 7))
                            return ins
                        S.op("pe", f, reads=[slot, yT.halves[tg]], writes=[PB[b]])
                        xs_ = xT.t[:, mm, tg * 512:(tg + 1) * 512]
                        S.op("dve", lambda: nc.vector.scalar_tensor_tensor(
                            out=xs_, in0=pbank(b), scalar=MOD[l].t[:, 2 * 8 + mm, cond:cond + 1], in1=xs_,
                            op0=ALU.mult, op1=ALU.add), reads=[PB[b], MOD[l], xT], writes=[xT])
                        S.op("act", lambda: nc.scalar.activation(out=hT.t[:, mm, tg * 512:(tg + 1) * 512], in_=xs_,
                                                                 func=AF.Square), reads=[xT], writes=[hT])
            w_done()
            w_done()
            S.barrier()

        def ffn(l, half):
            cond = half
            nseq, L = (4, 256) if half == 0 else (1, 1024)
            AR.reset()
            gT = AV([128, 22, 1024], BF16)
            y0 = [AV([128, 1024]) for _ in range(3)]
            sa = [AV([128, 1024]) for _ in range(2)]
            upb = ((0, 1), (2, 3), (4, 5))
            ui = 0
            for t in range(11):
                slot = w_get()
                wv = slotv(slot, 8, 512)
                for sub in range(2):
                    j = 2 * t + sub
                    for ab in range(2):
                        cols = ab * 256 + sub * 128
                        jj = ab * 22 + j
                        bb = upb[ui % 3]
                        yy = y0[ui % 3]
                        ui += 1
                        for tg in range(2):
                            bnk = bb[tg]

                            def f():
                                ins = None
                                for kc in range(8):
                                    ins = nc.tensor.matmul(pbank(bnk), wv[:, kc, cols:cols + 128],
                                                           hT.t[:, kc, tg * 512:(tg + 1) * 512],
                                                           start=(kc == 0), stop=(kc == 7))
                                return ins
                            S.op("pe", f, reads=[slot, hT], writes=[PB[bnk]])
                        pfull = PSt[bb[0] // 2][:, 0:1024]
                        S.op("act", lambda: nc.scalar.activation(out=yy.t[:], in_=pfull, func=AF.Identity,
                                                                 bias=CW.t[:, l, 3, jj:jj + 1],
                                                                 scale=CW.t[:, l, 1, jj:jj + 1]),
                             reads=[PB[bb[0]], PB[bb[1]], CW], writes=[yy])
                        pv3 = pfull.rearrange("p (s t) -> p s t", t=L)
                        yv3 = yy.t[:].rearrange("p (s t) -> p s t", t=L)
                        S.op("dve", lambda: nc.vector.scalar_tensor_tensor(
                            out=yv3[:, :, 1:L], in0=pv3[:, :, 0:L - 1], scalar=CW.t[:, l, 0, jj:jj + 1],
                            in1=yv3[:, :, 1:L], op0=ALU.mult, op1=ALU.add),
                            reads=[PB[bb[0]], PB[bb[1]], CW, yy], writes=[yy])
                        S.op("dve", lambda: nc.vector.scalar_tensor_tensor(
                            out=yv3[:, :, 0:L - 1], in0=pv3[:, :, 1:L], scalar=CW.t[:, l, 2, jj:jj + 1],
                            in1=yv3[:, :, 0:L - 1], op0=ALU.mult, op1=ALU.add),
                            reads=[PB[bb[0]], PB[bb[1]], CW, yy], writes=[yy])
                        if ab == 0:
                            sa_ = sa[j % 2]
                            S.op("act", lambda: nc.scalar.activation(out=sa_.t[:], in_=yy.t[:], func=AF.Silu),
                                 reads=[yy], writes=[sa_])
                        else:
                            sa_ = sa[j % 2]
                            S.op("pool", lambda: nc.gpsimd.tensor_tensor(out=gT.t[:, j, :], in0=sa_.t[:], in1=yy.t[:],
                                                                         op=ALU.mult), reads=[sa_, yy], writes=[gT])
                w_done()
            if half == HALVES[0]:
                mods_group(l, 5)
            for m in range(8):
                slot = w_get()
                wv = slotv(slot, 22, 128)
                for tg in range(2):
                    b = nextbank((6, 7), "t67")

                    def f():
                        ins = None
                        for kc in range(22):
                            ins = nc.tensor.matmul(pbank(b), wv[:, kc, :], gT.t[:, kc, tg * 512:(tg + 1) * 512],
                                                   start=(kc == 0), stop=(kc == 21))
                        return ins
                    S.op("pe", f, reads=[slot, gT], writes=[PB[b]])
                    xs_ = xT.t[:, m, tg * 512:(tg + 1) * 512]
                    S.op("dve", lambda: nc.vector.scalar_tensor_tensor(
                        out=xs_, in0=pbank(b), scalar=MOD[l].t[:, 5 * 8 + m, cond:cond + 1], in1=xs_,
                        op0=ALU.mult, op1=ALU.add), reads=[PB[b], MOD[l], xT], writes=[xT])
                    if l < NL - 1:
                        S.op("act", lambda: nc.scalar.activation(out=yT.t[:, m, tg * 512:(tg + 1) * 512], in_=xs_,
                                                                 func=AF.Square), reads=[xT], writes=[yT] + yT.halves)
                w_done()
            S.barrier()

        for half in HALVES:
            AR.reset()
            load_x(din["xp"] if half == 0 else din["xs"])
            S.barrier()
            ckpt(11)
            for l in range(NL):
                AR.reset()
                if half == HALVES[0]:
                    mods_group(l, 0)
                    mods_group(l, 1)
                norm_mod(l, 0, half, presq=(yT if l > 0 else None))
                S.barrier()
                ckpt(12)
                mixer(l, half)
                ckpt(19)
                AR.reset()
                if half == HALVES[0]:
                    mods_group(l, 3)
                    mods_group(l, 4)
                norm_mod(l, 1, half, presq=hT)
                S.barrier()
                ckpt(20)
                ffn(l, half)
                ckpt(21)
            AR.reset()
            store_x(dout["yp"] if half == 0 else dout["ys"])
            S.barrier()
        S.drain("sp")
    except _Stop:
        pass
    return nc


_CACHE = {}


def _prep_inputs(inputs):
    f32 = lambda a: np.ascontiguousarray(np.asarray(a, dtype=np.float32))
    I = {k: f32(v) for k, v in inputs.items()}
    hc = host_consts()
    shared = {
        "norm1": I["norm1"], "w_mod": I["w_mod"], "b_mod": I["b_mod"], "w_in": I["w_in"],
        "sgu_norm": I["sgu_norm"], "sgu_w": I["sgu_w"], "sgu_b": I["sgu_b"],
        "rlf": I["ret_logit_fwd"], "rlb": I["ret_logit_bwd"], "ret_norm": I["ret_norm"].reshape(2, 256),
        "q_norm": I["q_norm"], "k_norm": I["k_norm"], "diff_lam": I["diff_lam"].reshape(2, 256),
        "diff_norm": I["diff_norm"], "w_out": I["w_out"], "norm2": I["norm2"], "ffn_up": I["ffn_up"],
        "ffn_conv": I["ffn_conv"], "ffn_conv_b": I["ffn_conv_b"], "ffn_down": I["ffn_down"],
    }
    shared.update(hc)
    maps = []
    for c in range(8):
        m = dict(shared)
        m["xp"] = np.ascontiguousarray(I["x_prompt"][4 * c:4 * c + 4].reshape(1024, 1024))
        m["xs"] = np.ascontiguousarray(I["x_sample"][c])
        m["cvec"] = np.ascontiguousarray(np.stack([I["c_ctx"], I["c"][c]], axis=0))
        m["ck"] = np.ascontiguousarray(I["cache_k"][c].reshape(2, 256, 512))
        m["cv"] = np.ascontiguousarray(I["cache_v"][c].reshape(2, 256, 512))
        m["srf"] = np.ascontiguousarray(I["state_ret_fwd"][c])
        m["srb"] = np.ascontiguousarray(I["state_ret_bwd"][c])
        maps.append(m)
    return maps


def kernel(**inputs):
    maps = _prep_inputs(inputs)
    if "nc" not in _CACHE:
        _CACHE["nc"] = build()
    nc = _CACHE["nc"]
    res = run_bass_kernel_spmd(nc, maps, core_ids=list(range(8)))
    R = res.results
    yp = np.concatenate([np.asarray(R[c]["yp"]).reshape(4, 256, 1024) for c in range(8)], axis=0)
    ys = np.stack([np.asarray(R[c]["ys"]) for c in range(8)], axis=0)
    nk = np.concatenate([np.asarray(R[c]["nk"]).reshape(4, 2, 256, 4, 2, 64) for c in range(8)], axis=0)
    nv = np.concatenate([np.asarray(R[c]["nv"]).reshape(4, 2, 256, 4, 128) for c in range(8)], axis=0)
    nrf = np.concatenate([np.asarray(R[c]["nrf"]) for c in range(8)], axis=0)
    nrb = np.concatenate([np.asarray(R[c]["nrb"]) for c in range(8)], axis=0)
    return (yp.astype(np.float32), ys.astype(np.float32), nk.astype(np.float32), nv.astype(np.float32),
            nrf.astype(np.float32), nrb.astype(np.float32))
```

```python
import math
from contextlib import ExitStack
import numpy as np
import concourse.bass as bass
import concourse.mybir as mybir
from concourse.bass_utils import run_bass_kernel_spmd

F32 = mybir.dt.float32
BF16 = mybir.dt.bfloat16
AF = mybir.ActivationFunctionType
ALU = mybir.AluOpType
AX = mybir.AxisListType

D = 1024
T = 1024
DFF = 2816
EPS = 1e-6
NSLOT = 4


class Res:
    __slots__ = ("w", "r", "sem", "persist", "excl")

    def __init__(self, persist=False, excl=False):
        self.excl = excl
        self.w = None
        self.r = {}
        self.sem = None
        self.persist = persist


class TT:
    def __init__(self, t, res=None):
        self.t = t
        self.res = res if res is not None else Res()


class Sch:
    def __init__(self, nc, es):
        self.nc = nc
        self.es = es
        self.E = {"pe": nc.tensor, "act": nc.scalar, "dve": nc.vector, "pool": nc.gpsimd, "sp": nc.sync}
        self.sem = {}
        self.cnt = {}
        self.seen = {k: {} for k in self.E}
        for k in ("pe", "act", "dve", "pool"):
            self.sem[k] = es.enter_context(nc.semaphore("s_" + k))
            self.cnt[k] = 0
        self.nsem = 0
        self.dsems = []
        self.free = []
        self.live = []

    def newsem(self):
        if self.free:
            return self.free.pop()
        name = "d%d" % self.nsem
        self.nsem += 1
        self.sem[name] = self.es.enter_context(self.nc.semaphore(name))
        self.cnt[name] = 0
        self.dsems.append(name)
        return name

    def recycle(self):
        keep = []
        for r in self.live:
            if r.persist:
                keep.append(r)
            else:
                self.free.append(r.sem)
                r.sem = None
        self.live = keep

    def _wait(self, eng, raw, other):
        best = {}
        for t in raw:
            if t is None:
                continue
            k, v = t
            if k == eng and eng in ("pe", "sp"):
                continue
            if v > best.get(k, 0):
                best[k] = v
        for t in other:
            if t is None:
                continue
            k, v = t
            if k == eng and eng in ("pe", "sp"):
                continue
            if v > best.get(k, 0):
                best[k] = v
        sn = self.seen[eng]
        for k, v in best.items():
            if sn.get(k, 0) >= v:
                continue
            self.E[eng].wait_ge(self.sem[k], v)
            sn[k] = v

    def _deps(self, reads, writes):
        raw = [r.w for r in reads]
        other = []
        for w in writes:
            other.append(w.w)
            other.extend(w.r.items())
        return raw, other

    def _commit(self, tok, reads, writes):
        k, v = tok
        for r in reads:
            if r.r.get(k, 0) < v:
                r.r[k] = v
        for w in writes:
            w.w = tok
            w.r = {}

    def op(self, eng, fn, reads=(), writes=()):
        reads = [getattr(x, 'res', x) for x in reads] + [x.extra for x in reads if hasattr(x, 'extra')]
        writes = [getattr(x, 'res', x) for x in writes]
        writes = writes + [r for r in reads if r.excl and r not in writes]
        raw, other = self._deps(reads, writes)
        self._wait(eng, raw, other)
        ins = fn()
        ins.then_inc(self.sem[eng], 1)
        self.cnt[eng] += 1
        tok = (eng, self.cnt[eng])
        self._commit(tok, reads, writes)
        return tok

    def dma(self, q, out, in_, reads=(), writes=(), key=None):
        reads = [getattr(x, 'res', x) for x in reads]
        writes = [getattr(x, 'res', x) for x in writes]
        kres = key if key is not None else (writes[0] if writes else reads[0])
        kres = getattr(kres, 'res', kres)
        if kres.sem is None:
            kres.sem = self.newsem()
            self.live.append(kres)
        raw, other = self._deps(reads, writes)
        other = list(other) + [(kres.sem, self.cnt[kres.sem])]
        self._wait(q, raw, other)
        ins = self.E[q].dma_start(out=out, in_=in_)
        ins.then_inc(self.sem[kres.sem], 16)
        self.cnt[kres.sem] += 16
        tok = (kres.sem, self.cnt[kres.sem])
        self._commit(tok, reads, writes)
        return tok

    def barrier(self):
        engs = ("pe", "act", "dve", "pool")
        for e in engs + ("sp",):
            for f in engs:
                if e == f and e == "pe":
                    continue
                v = self.cnt[f]
                if v > 0 and self.seen[e].get(f, 0) < v:
                    self.E[e].wait_ge(self.sem[f], v)
                    self.seen[e][f] = v
            for r in self.live:
                if r.persist:
                    continue
                v = self.cnt[r.sem]
                if v > 0 and self.seen[e].get(r.sem, 0) < v:
                    self.E[e].wait_ge(self.sem[r.sem], v)
                    self.seen[e][r.sem] = v
        self.recycle()

    def drain(self, q="sp"):
        for k in self.dsems:
            v = self.cnt[k]
            if v > 0 and self.seen[q].get(k, 0) < v:
                self.E[q].wait_ge(self.sem[k], v)
                self.seen[q][k] = v
        for f in ("pe", "act", "dve", "pool"):
            v = self.cnt[f]
            if v > 0 and self.seen[q].get(f, 0) < v:
                self.E[q].wait_ge(self.sem[f], v)
                self.seen[q][f] = v


def host_consts():
    ROPE_PAIRS = 16
    t = np.arange(1024)
    inv = (10000.0 ** (-np.arange(ROPE_PAIRS, dtype=np.float32) / ROPE_PAIRS)).astype(np.float32)
    ar = (t // 64).astype(np.float32)[:, None] * inv
    ac = (t % 64).astype(np.float32)[:, None] * inv
    cr, sr, cc, sc_ = np.cos(ar), np.sin(ar), np.cos(ac), np.sin(ac)
    cos64 = np.concatenate([cr, cr, cc, cc], axis=1).astype(np.float32)
    sin64 = np.concatenate([-sr, sr, -sc_, sc_], axis=1).astype(np.float32)
    k = np.arange(128)[:, None]
    q = np.arange(128)[None, :]
    dm = np.zeros((128, 4, 128), np.float32)
    dm[:, 0] = np.maximum(q - k, 0)
    dm[:, 1] = np.maximum(k - q, 0)
    dm[:, 2] = (q >= k)
    dm[:, 3] = (k >= q)
    pr = np.zeros((128, 2, 128), np.float32)
    pr[:, 0, :] = np.arange(128) + 1.0
    pr[:, 1, :] = 128.0 - np.arange(128)
    kp = np.zeros((128, 2), np.float32)
    kp[:, 0] = 127.0 - np.arange(128)
    kp[:, 1] = np.arange(128)
    return dict(cos64=cos64, sin64=sin64, dmat=dm, posrow=pr, kpos=kp)


IN_SPECS = [
    ("xp", (1024, 1024)), ("xs", (1024, 1024)), ("cvec", (2, 1024)),
    ("ck", (2, 256, 512)), ("cv", (2, 256, 512)), ("srf", (2, 4, 64, 64)), ("srb", (2, 4, 64, 64)),
    ("norm1", (2, 1024)), ("w_mod", (2, 1024, 6144)), ("b_mod", (2, 6144)), ("w_in", (2, 1024, 3072)),
    ("sgu_norm", (2, 256)), ("sgu_w", (2, 4, 128, 128)), ("sgu_b", (2, 4, 128)),
    ("rlf", (2, 4)), ("rlb", (2, 4)), ("ret_norm", (2, 256)), ("q_norm", (2, 64)), ("k_norm", (2, 64)),
    ("diff_lam", (2, 256)), ("diff_norm", (2, 128)), ("w_out", (2, 1024, 1024)), ("norm2", (2, 1024)),
    ("ffn_up", (2, 1024, 5632)), ("ffn_conv", (2, 3, 5632)), ("ffn_conv_b", (2, 5632)),
    ("ffn_down", (2, 2816, 1024)),
    ("cos64", (1024, 64)), ("sin64", (1024, 64)), ("dmat", (128, 4, 128)), ("posrow", (128, 2, 128)),
    ("kpos", (128, 2)),
]
OUT_SPECS = [
    ("yp", (1024, 1024)), ("ys", (1024, 1024)), ("nk", (4, 2, 256, 512)), ("nv", (4, 2, 256, 512)),
    ("nrf", (4, 2, 4, 64, 64)), ("nrb", (4, 2, 4, 64, 64)),
]


def build(cfg=None):
    cfg = cfg or {}
    NL = cfg.get("n_layers", 2)
    HALVES = cfg.get("halves", (0, 1))
    taps = cfg.get("taps", None)
    nc = bass.Bass("TRN2", target_bir_lowering=False)
    try:
        nc.allow_low_precision("bf16 matmul operands with fp32 accumulation")
    except Exception:
        pass
    din = {n: nc.dram_tensor(n, list(s), F32, kind="ExternalInput").ap() for n, s in IN_SPECS}
    dout = {n: nc.dram_tensor(n, list(s), F32, kind="ExternalOutput").ap() for n, s in OUT_SPECS}
    es = ExitStack()
    STOP = cfg.get("stop", None)

    class _Stop(Exception):
        pass
    try:
      with es:
        S = Sch(nc, es)

        def ckpt(k):
            if STOP == k:
                S.drain("sp")
                raise _Stop()

        def sb(name, shape, dt=F32):
            return TT(es.enter_context(nc.sbuf_tensor(name, list(shape), dt)), Res(persist=True))

        def tap(name, ap, shape, reads):
            if taps is None or name not in taps:
                return
            d = nc.dram_tensor("tap_" + name, list(shape), ap.dtype, kind="ExternalOutput").ap()
            taps[name] = (list(shape), ap.dtype)
            S.dma("sp", d, ap, reads=reads)

        PSt = [es.enter_context(nc.psum_tensor("ps%d" % i, [128, 1024], F32)) for i in range(4)]
        PB = []
        for i in range(8):
            PB.append(TT(PSt[i // 2], Res(excl=True)))

        def pbank(i):
            return PSt[i // 2][:, (i % 2) * 512:(i % 2) * 512 + 512]

        rot = {"n": 0}

        def nextbank(pool=(0, 1, 2, 3), key="n"):
            i = pool[rot.setdefault(key, 0) % len(pool)]
            rot[key] += 1
            return i

        xT = sb("xT", [128, 8, T])
        hT = sb("hT", [128, 8, T], BF16)
        yT = sb("yT", [128, 8, T], BF16)
        yT.halves = [Res(persist=True), Res(persist=True)]
        ring = [sb("ring%d" % i, [128, 4096], BF16) for i in range(NSLOT)]
        for r_ in ring:
            r_.extra = Res(persist=True)
            r_.extra.sem = S.newsem()
            r_.res.sem = S.newsem()
        ARENA_BYTES = 74 * 1024
        arena = sb("arena", [128, ARENA_BYTES // 4], F32)

        identF = sb("identF", [128, 128])
        identB = sb("identB", [128, 128], BF16)
        onesB = sb("onesB", [128, 128], BF16)
        sTb = sb("sTb", [128, 8, 2], BF16)
        MOD = [sb("MOD%d" % l, [128, 48, 2]) for l in range(2)]
        GS = [[sb("GS%d%d" % (l, n), [128, 8, 2]) for n in range(2)] for l in range(2)]
        NT = sb("NT", [128, 32])
        CW = sb("CW", [128, 2, 4, 44])
        WST = sb("WST", [128, 8, 128], BF16)
        BS = sb("BS", [128, 8])
        SGN = sb("SGN", [128, 2, 256])
        RN = sb("RN", [128, 2, 256])
        QN = sb("QN", [128, 2, 64])
        KN = sb("KN", [128, 2, 64])
        DN = sb("DN", [128, 2, 128])
        LG = sb("LG", [128, 16])
        KP = sb("KP", [128, 2])
        C128 = sb("C128", [128, 64])
        DTm = sb("DTm", [128, 2, 4, 128])
        QD = sb("QD", [128, 2, 2, 2, 128])
        KDE = sb("KDE", [128, 2, 2, 256])
        CD = sb("CD", [128, 2, 2, 2, 64])
        NLAM = sb("NLAM", [128, 2])
        COS = sb("COS", [128, 8, 64])
        SIN = sb("SIN", [128, 8, 64])
        small = [sb("small%d" % i, [128, 16]) for i in range(8)]
        srot = {"i": 0}

        def nsmall():
            s = small[srot["i"] % len(small)]
            srot["i"] += 1
            return s

        wq = []

        def slotv(slot, a, b):
            return slot.t[:, 0:a * b].rearrange("p (a b) -> p a b", b=b)

        def wsrc(name, l, c0, c1):
            return din[name][l, :, c0:c1].rearrange("(kc p) n -> p kc n", p=128)

        def plan_weights():
            def modt(l, which):
                for t2 in range(2):
                    c0 = which * 1024 + t2 * 512
                    wq.append([(lambda s: slotv(s, 8, 512), wsrc("w_mod", l, c0, c0 + 512))])
            first = True
            for half in HALVES:
                for l in range(NL):
                    if first:
                        modt(l, 0)
                        modt(l, 1)
                    for g in range(6):
                        wq.append([(lambda s: slotv(s, 8, 512), wsrc("w_in", l, g * 512, (g + 1) * 512))])
                    if first:
                        modt(l, 2)
                    for g in range(2):
                        wq.append([(lambda s: slotv(s, 8, 512), wsrc("w_out", l, g * 512, (g + 1) * 512))])
                    if first:
                        modt(l, 3)
                        modt(l, 4)
                    for t in range(11):
                        wq.append([
                            (lambda s: slotv(s, 8, 512)[:, :, 0:256], wsrc("ffn_up", l, t * 256, (t + 1) * 256)),
                            (lambda s: slotv(s, 8, 512)[:, :, 256:512],
                             wsrc("ffn_up", l, DFF + t * 256, DFF + (t + 1) * 256)),
                        ])
                    if first:
                        modt(l, 5)
                    for m in range(8):
                        wq.append([(lambda s: slotv(s, 22, 128), wsrc("ffn_down", l, m * 128, (m + 1) * 128))])
                first = False

        plan_weights()
        wstate = {"next_load": 0, "next_use": 0}

        def w_issue():
            i = wstate["next_load"]
            if i >= len(wq):
                return
            slot = ring[i % NSLOT]
            for n_, (dstf, src) in enumerate(wq[i]):
                S.dma("pool", dstf(slot), src, writes=[slot.res if n_ == 0 else slot.extra])
            wstate["next_load"] = i + 1

        def w_get():
            i = wstate["next_use"]
            wstate["next_use"] = i + 1
            assert i < wstate["next_load"]
            return ring[i % NSLOT]

        def w_done():
            w_issue()

        class Arena:
            def __init__(self):
                self.off = 0

            def reset(self):
                self.off = 0

            def get(self, shape, dt):
                n = 1
                for s_ in shape[1:]:
                    n *= s_
                nbytes = n * (4 if dt == F32 else 2)
                nbytes = (nbytes + 63) // 64 * 64
                w0 = self.off // 4
                w1 = (self.off + nbytes) // 4
                assert self.off + nbytes <= ARENA_BYTES, ("arena overflow", self.off, nbytes)
                self.off += nbytes
                ap = arena.t[0:shape[0], w0:w1]
                if dt == BF16:
                    ap = ap.bitcast(BF16)
                nfree = n
                ap = ap[:, 0:nfree]
                if len(shape) > 2:
                    names = " ".join("d%d" % i for i in range(len(shape) - 1))
                    kw = {"d%d" % i: shape[i + 1] for i in range(len(shape) - 1)}
                    ap = ap.rearrange("p (%s) -> p %s" % (names, names), **kw)
                return ap

        AR = Arena()

        class AV:
            def __init__(self, shape, dt=F32):
                self.ap = AR.get(shape, dt)
                self.res = Res()

            @property
            def t(self):
                return self.ap

        S.op("pool", lambda: nc.gpsimd.memset(identF.t[:], 0.0), writes=[identF])
        S.op("pool", lambda: nc.gpsimd.affine_select(out=identF.t[:], in_=identF.t[:], pattern=[[-1, 128]],
                                                      compare_op=ALU.not_equal, fill=1.0, base=0,
                                                      channel_multiplier=1), reads=[identF], writes=[identF])
        S.op("pool", lambda: nc.gpsimd.memset(onesB.t[:], 1.0), writes=[onesB])
        S.op("pool", lambda: nc.gpsimd.memset(C128.t[:], 128.0), writes=[C128])
        S.op("dve", lambda: nc.vector.tensor_copy(identB.t[:], identF.t[:]), reads=[identF], writes=[identB])

        ckpt(1)
        for _ in range(NSLOT):
            w_issue()

        cres = Res()
        crow = AV([2, 1024])
        bmrow = AV([96, 128])
        nrow = AV([32, 128])
        cvrow = AV([44, 2, 4, 128])
        wsraw = AV([128, 8, 128])
        bsrow = AV([8, 128])
        BMT = sb("BMT", [128, 96])
        DL = AV([128, 2, 256])
        DM = AV([128, 4, 128])
        PR = AV([128, 2, 128])

        def cload(dst_tt, dst_ap, src_ap):
            S.dma("sp", dst_ap, src_ap, writes=[dst_tt])

        cload(crow, crow.t[:], din["cvec"])
        cload(bmrow, bmrow.t[:], din["b_mod"].rearrange("l (j p) -> (l j) p", p=128))
        cload(nrow, nrow.t[0:16, :], din["norm1"].rearrange("l (kc p) -> (l kc) p", p=128))
        cload(nrow, nrow.t[16:32, :], din["norm2"].rearrange("l (kc p) -> (l kc) p", p=128))
        for l in range(2):
            cload(cvrow, cvrow.t[:, l, 0:3, :], din["ffn_conv"][l].rearrange("j (c p) -> c j p", p=128))
            cload(cvrow, cvrow.t[:, l, 3, :], din["ffn_conv_b"][l].rearrange("(c p) -> c p", p=128))
        cload(wsraw, wsraw.t[:], din["sgu_w"].rearrange("l g p q -> p (l g) q"))
        cload(bsrow, bsrow.t[:], din["sgu_b"].rearrange("l g p -> (l g) p"))
        cload(SGN, SGN.t[:], din["sgu_norm"].partition_broadcast(128))
        cload(RN, RN.t[:], din["ret_norm"].partition_broadcast(128))
        cload(QN, QN.t[:], din["q_norm"].partition_broadcast(128))
        cload(KN, KN.t[:], din["k_norm"].partition_broadcast(128))
        cload(DN, DN.t[:], din["diff_norm"].partition_broadcast(128))
        cload(DL, DL.t[:], din["diff_lam"].partition_broadcast(128))
        cload(LG, LG.t[:, 0:8], din["rlf"].rearrange("l h -> (l h)").partition_broadcast(128))
        cload(LG, LG.t[:, 8:16], din["rlb"].rearrange("l h -> (l h)").partition_broadcast(128))
        cload(DM, DM.t[:], din["dmat"])
        cload(PR, PR.t[:], din["posrow"])
        cload(KP, KP.t[:], din["kpos"])
        cload(COS, COS.t[:], din["cos64"].rearrange("(t p) c -> p t c", p=128))
        cload(SIN, SIN.t[:], din["sin64"].rearrange("(t p) c -> p t c", p=128))

        ckpt(2)
        csil = AV([2, 1024])
        S.op("act", lambda: nc.scalar.activation(out=csil.t[:], in_=crow.t[:], func=AF.Silu),
             reads=[crow], writes=[csil])
        b = nextbank()

        def f():
            ins = None
            for kc in range(8):
                ins = nc.tensor.transpose(pbank(b)[:, kc * 2:kc * 2 + 2], csil.t[0:2, kc * 128:(kc + 1) * 128],
                                          identF.t[0:2, 0:2])
            return ins
        S.op("pe", f, reads=[csil, identF], writes=[PB[b]])
        S.op("dve", lambda: nc.vector.tensor_copy(sTb.t[:].rearrange("p a b -> p (a b)"), pbank(b)[:, 0:16]),
             reads=[PB[b]], writes=[sTb])

        ckpt(3)
        b = nextbank()
        S.op("pe", lambda: nc.tensor.transpose(pbank(b)[:, 0:32], nrow.t[0:32, :], identF.t[0:32, 0:32]),
             reads=[nrow, identF], writes=[PB[b]])
        S.op("dve", lambda: nc.vector.tensor_copy(NT.t[:], pbank(b)[:, 0:32]), reads=[PB[b]], writes=[NT])
        b = nextbank()
        S.op("pe", lambda: nc.tensor.transpose(pbank(b)[:, 0:96], bmrow.t[0:96, :], identF.t[0:96, 0:96]),
             reads=[bmrow, identF], writes=[PB[b]])
        S.op("dve", lambda: nc.vector.tensor_copy(BMT.t[:], pbank(b)[:, 0:96]), reads=[PB[b]], writes=[BMT])
        ckpt(4)
        b = nextbank()

        def f():
            ins = None
            for l in range(2):
                for j in range(4):
                    c0 = (l * 4 + j) * 44
                    ins = nc.tensor.transpose(pbank(b)[:, c0:c0 + 44], cvrow.t[0:44, l, j, :], identF.t[0:44, 0:44])
            return ins
        S.op("pe", f, reads=[cvrow, identF], writes=[PB[b]])
        S.op("dve", lambda: nc.vector.tensor_copy(CW.t[:].rearrange("p a b c -> p (a b c)"), pbank(b)[:, 0:352]),
             reads=[PB[b]], writes=[CW])
        ckpt(5)
        for hb in range(2):
            b = nextbank()

            def f():
                ins = None
                for i in range(4):
                    ins = nc.tensor.transpose(pbank(b)[:, i * 128:(i + 1) * 128], wsraw.t[:, hb * 4 + i, :], identF.t[:])
                return ins
            S.op("pe", f, reads=[wsraw, identF], writes=[PB[b]])
            S.op("dve", lambda: nc.vector.tensor_copy(
                WST.t[:, hb * 4:(hb + 1) * 4, :].rearrange("p a b -> p (a b)"), pbank(b)[:, 0:512]),
                reads=[PB[b]], writes=[WST])
        b = nextbank()
        S.op("pe", lambda: nc.tensor.transpose(pbank(b)[:, 0:8], bsrow.t[0:8, :], identF.t[0:8, 0:8]),
             reads=[bsrow, identF], writes=[PB[b]])
        S.op("dve", lambda: nc.vector.tensor_copy(BS.t[:], pbank(b)[:, 0:8]), reads=[PB[b]], writes=[BS])

        ckpt(6)
        modrow = sb("modrow", [2, 1024])

        def mods_group(l, which):
            for t2 in range(2):
                slot = w_get()
                b = nextbank()
                wv = slotv(slot, 8, 512)

                def f():
                    ins = None
                    for kc in range(8):
                        ins = nc.tensor.matmul(pbank(b)[0:2, :], sTb.t[:, kc, :], wv[:, kc, :],
                                               start=(kc == 0), stop=(kc == 7))
                    return ins
                S.op("pe", f, reads=[sTb, slot], writes=[PB[b]])
                w_done()
                S.op("dve", lambda: nc.vector.tensor_copy(modrow.t[:, t2 * 512:(t2 + 1) * 512], pbank(b)[0:2, :]),
                     reads=[PB[b]], writes=[modrow])
            b = nextbank()

            def f():
                ins = None
                for jb in range(8):
                    ins = nc.tensor.transpose(pbank(b)[:, jb * 2:jb * 2 + 2], modrow.t[0:2, jb * 128:(jb + 1) * 128],
                                              identF.t[0:2, 0:2])
                return ins
            S.op("pe", f, reads=[modrow, identF], writes=[PB[b]])
            c0 = l * 48 + which * 8
            S.op("dve", lambda: nc.vector.tensor_tensor(
                out=MOD[l].t[:, which * 8:(which + 1) * 8, :], in0=pbank(b)[:, 0:16].rearrange("p (a b) -> p a b", b=2),
                in1=BMT.t[:, c0:c0 + 8].unsqueeze(2).broadcast_to([128, 8, 2]), op=ALU.add),
                 reads=[PB[b], BMT], writes=[MOD[l]])
            if which in (1, 4):
                n = 0 if which == 1 else 1
                ntv = NT.t[:, (n * 2 + l) * 8:(n * 2 + l) * 8 + 8]
                S.op("dve", lambda: nc.vector.scalar_tensor_tensor(
                    out=GS[l][n].t[:], in0=MOD[l].t[:, which * 8:(which + 1) * 8, :], scalar=1.0,
                    in1=ntv.unsqueeze(2).broadcast_to([128, 8, 2]), op0=ALU.add, op1=ALU.mult),
                    reads=[MOD[l], NT], writes=[GS[l][n]])

        ckpt(7)
        S.op("act", lambda: nc.scalar.activation(out=LG.t[:], in_=LG.t[:], func=AF.Exp, scale=-1.0),
             reads=[LG], writes=[LG])
        S.op("act", lambda: nc.scalar.activation(out=LG.t[:], in_=LG.t[:], func=AF.Ln, bias=1.0, scale=1.0),
             reads=[LG], writes=[LG])
        S.op("dve", lambda: nc.vector.tensor_scalar(LG.t[:], LG.t[:], -1.0, None, ALU.mult), reads=[LG], writes=[LG])

        ckpt(8)

        def lgi(d, l, h):
            return d * 8 + l * 4 + h

        dtmp = [AV([128, 128]) for i in range(4)]
        for l in range(NL):
            for h in range(4):
                tf, tb = dtmp[(h % 2) * 2], dtmp[(h % 2) * 2 + 1]
                i_f, i_b = lgi(0, l, h), lgi(1, l, h)
                S.op("act", lambda: nc.scalar.activation(out=tf.t[:], in_=DM.t[:, 0, :], func=AF.Exp,
                                                         scale=LG.t[:, i_f:i_f + 1]), reads=[DM, LG], writes=[tf])
                S.op("act", lambda: nc.scalar.activation(out=tb.t[:], in_=DM.t[:, 1, :], func=AF.Exp,
                                                         scale=LG.t[:, i_b:i_b + 1]), reads=[DM, LG], writes=[tb])
                S.op("dve", lambda: nc.vector.scalar_tensor_tensor(out=tf.t[:], in0=tf.t[:], scalar=0.125,
                                                                   in1=DM.t[:, 2, :], op0=ALU.mult, op1=ALU.mult),
                     reads=[tf, DM], writes=[tf])
                S.op("dve", lambda: nc.vector.scalar_tensor_tensor(out=tb.t[:], in0=tb.t[:], scalar=0.125,
                                                                   in1=DM.t[:, 3, :], op0=ALU.mult, op1=ALU.mult),
                     reads=[tb, DM], writes=[tb])
                S.op("dve", lambda: nc.vector.tensor_tensor(out=DTm.t[:, l, h, :], in0=tf.t[:], in1=tb.t[:],
                                                            op=ALU.add), reads=[tf, tb], writes=[DTm])
            for hp in range(2):
                for d in range(2):
                    for j in range(2):
                        ii = lgi(d, l, 2 * hp + j)
                        ps_ = slice(j * 64, (j + 1) * 64)
                        S.op("act", lambda: nc.scalar.activation(out=QD.t[ps_, l, hp, d, :], in_=PR.t[ps_, d, :],
                                                                 func=AF.Exp, scale=LG.t[ps_, ii:ii + 1]),
                             reads=[PR, LG], writes=[QD])
                        S.op("act", lambda: nc.scalar.activation(out=CD.t[ps_, l, d, hp, :], in_=C128.t[ps_, :],
                                                                 func=AF.Exp, scale=LG.t[ps_, ii:ii + 1]),
                             reads=[C128, LG], writes=[CD])
            for d in range(2):
                for h in range(4):
                    ii = lgi(d, l, h)
                    S.op("act", lambda: nc.scalar.activation(
                        out=KDE.t[:, l, d, h * 64:(h + 1) * 64], in_=KP.t[:, d:d + 1].broadcast_to([128, 64]),
                        func=AF.Exp, scale=LG.t[:, ii:ii + 1], bias=math.log(0.125)),
                        reads=[KP, LG], writes=[KDE])
            lam_init = 0.8 - 0.6 * math.exp(-0.3 * l)
            pr_ = nsmall()
            dlv = DL.t[:, l, :].rearrange("p (a b c) -> p a b c", a=2, b=2)
            lt = dtmp[0]
            S.op("dve", lambda: nc.vector.tensor_tensor(out=lt.t[:].rearrange("p (a c) -> p a c", a=2),
                                                        in0=dlv[:, :, 0, :], in1=dlv[:, :, 1, :], op=ALU.mult),
                 reads=[DL], writes=[lt])
            S.op("dve", lambda: nc.vector.tensor_reduce(out=pr_.t[:, 0:2],
                                                        in_=lt.t[:].rearrange("p (a c) -> p a c", a=2),
                                                        axis=AX.X, op=ALU.add), reads=[lt], writes=[pr_])
            S.op("act", lambda: nc.scalar.activation(out=pr_.t[:, 2:4], in_=pr_.t[:, 0:2], func=AF.Exp),
                 reads=[pr_], writes=[pr_])
            S.op("dve", lambda: nc.vector.tensor_tensor(out=pr_.t[:, 4:5], in0=pr_.t[:, 3:4], in1=pr_.t[:, 2:3],
                                                        op=ALU.subtract), reads=[pr_], writes=[pr_])
            S.op("dve", lambda: nc.vector.tensor_scalar(NLAM.t[:, l:l + 1], pr_.t[:, 4:5], -lam_init, None, ALU.add),
                 reads=[pr_], writes=[NLAM])
            S.op("dve", lambda: nc.vector.tensor_scalar(DN.t[:, l, :], DN.t[:, l, :], 1.0 - lam_init, None, ALU.mult),
                 reads=[DN], writes=[DN])

        S.barrier()
        ckpt(10)

        def rstd_from_ss(ss_ap, out_ap, scale, bias, R, W):
            S.op("act", lambda: nc.scalar.activation(out=out_ap, in_=ss_ap, func=AF.Sqrt, bias=bias, scale=scale),
                 reads=R, writes=W)
            S.op("dve", lambda: nc.vector.reciprocal(out_ap, out_ap), reads=W, writes=W)

        def load_x(src):
            xin = [AV([128, 1024]) for _ in range(2)]
            for tt in range(8):
                xi = xin[tt % 2]
                S.dma("sp", xi.t[:], src[tt * 128:(tt + 1) * 128, :], writes=[xi])
                for hb in range(2):
                    b = nextbank()

                    def f():
                        ins = None
                        for i in range(4):
                            c = hb * 4 + i
                            ins = nc.tensor.transpose(pbank(b)[:, i * 128:(i + 1) * 128],
                                                      xi.t[:, c * 128:(c + 1) * 128], identF.t[:])
                        return ins
                    S.op("pe", f, reads=[xi, identF], writes=[PB[b]])
                    eng = "act" if hb == 0 else "dve"
                    dst = xT.t[:, hb * 4:(hb + 1) * 4, tt * 128:(tt + 1) * 128]
                    srcp = pbank(b).rearrange("p (a b) -> p a b", b=128)
                    if eng == "act":
                        S.op("act", lambda: nc.scalar.copy(out=dst, in_=srcp), reads=[PB[b]], writes=[xT])
                    else:
                        S.op("dve", lambda: nc.vector.tensor_copy(dst, srcp), reads=[PB[b]], writes=[xT])
                S.op("act", lambda: nc.scalar.activation(out=yT.t[:, :, tt * 128:(tt + 1) * 128],
                                                         in_=xT.t[:, :, tt * 128:(tt + 1) * 128], func=AF.Square),
                     reads=[xT], writes=[yT] + yT.halves)

        def store_x(dst):
            xo = [AV([128, 1024]) for _ in range(2)]
            for tt in range(8):
                xi = xo[tt % 2]
                for hb in range(2):
                    b = nextbank()

                    def f():
                        ins = None
                        for i in range(4):
                            c = hb * 4 + i
                            ins = nc.tensor.transpose(pbank(b)[:, i * 128:(i + 1) * 128],
                                                      xT.t[:, c, tt * 128:(tt + 1) * 128], identF.t[:])
                        return ins
                    S.op("pe", f, reads=[xT, identF], writes=[PB[b]])
                    dstp = xi.t[:, hb * 512:(hb + 1) * 512]
                    if hb == 0:
                        S.op("act", lambda: nc.scalar.copy(out=dstp, in_=pbank(b)), reads=[PB[b]], writes=[xi])
                    else:
                        S.op("dve", lambda: nc.vector.tensor_copy(dstp, pbank(b)), reads=[PB[b]], writes=[xi])
                S.dma("sp", dst[tt * 128:(tt + 1) * 128, :], xi.t[:], reads=[xi])

        def norm_mod(l, n, cond, presq=None):
            RB = AV([128, T])
            if presq is None:
                sq = yT
                S.op("act", lambda: nc.scalar.activation(out=sq.t[:], in_=xT.t[:], func=AF.Square),
                     reads=[xT], writes=[sq] + yT.halves)
            else:
                sq = presq
            for tg in range(2):
                b = nextbank()

                def f():
                    ins = None
                    for kc in range(8):
                        ins = nc.tensor.matmul(pbank(b), onesB.t[:], sq.t[:, kc, tg * 512:(tg + 1) * 512],
                                               start=(kc == 0), stop=(kc == 7))
                    return ins
                S.op("pe", f, reads=[sq, onesB], writes=[PB[b]])
                rstd_from_ss(pbank(b), RB.t[:, tg * 512:(tg + 1) * 512], 1.0 / D, EPS, [PB[b]], [RB])
            shi = 0 if n == 0 else 3
            tmp = [AV([128, 1024]) for _ in range(2)]
            for kc in range(8):
                tm = tmp[kc % 2]
                S.op("dve", lambda: nc.vector.scalar_tensor_tensor(
                    out=tm.t[:], in0=xT.t[:, kc, :], scalar=GS[l][n].t[:, kc, cond:cond + 1], in1=RB.t[:],
                    op0=ALU.mult, op1=ALU.mult), reads=[xT, GS[l][n], RB], writes=[tm])
                S.op("act", lambda: nc.scalar.activation(out=hT.t[:, kc, :], in_=tm.t[:], func=AF.Identity,
                                                         bias=MOD[l].t[:, shi * 8 + kc, cond:cond + 1], scale=1.0),
                     reads=[tm, MOD[l]], writes=[hT])

        def zmm(slot, tt, b):
            wv = slotv(slot, 8, 512)

            def f():
                ins = None
                for kc in range(8):
                    ins = nc.tensor.matmul(pbank(b), hT.t[:, kc, tt * 128:(tt + 1) * 128], wv[:, kc, :],
                                           start=(kc == 0), stop=(kc == 7))
                return ins
            S.op("pe", f, reads=[hT, slot], writes=[PB[b]])

        def group_rstd(src_ap, ngrp, gsz, scale, bias, sqt, ss):
            src_tt, sq_tt = sqt
            S.op("dve", lambda: nc.vector.tensor_tensor(out=sq_tt.t[:, 0:ngrp * gsz], in0=src_ap, in1=src_ap,
                                                        op=ALU.mult), reads=[src_tt], writes=[sq_tt])
            S.op("dve", lambda: nc.vector.tensor_reduce(
                out=ss.t[:, 0:ngrp], in_=sq_tt.t[:, 0:ngrp * gsz].rearrange("p (a b) -> p a b", b=gsz),
                axis=AX.X, op=ALU.add), reads=[sq_tt], writes=[ss])
            rstd_from_ss(ss.t[:, 0:ngrp], ss.t[:, 0:ngrp], scale, bias, [ss], [ss])

        def transposes_to(src_tt, src_ap_fn, nblk, dst_tt, dst_ap, pool=(0, 1, 2, 3)):
            b = nextbank(pool)
            pv = pbank(b).bitcast(BF16)

            def f():
                ins = None
                for i in range(nblk):
                    ins = nc.tensor.transpose(pv[:, i * 128:(i + 1) * 128], src_ap_fn(i), identB.t[:])
                return ins
            S.op("pe", f, reads=[src_tt, identB], writes=[PB[b]])
            S.op("dve", lambda: nc.vector.tensor_copy(dst_ap, pv[:, 0:nblk * 128].rearrange("p (a b) -> p a b", b=128)),
                 reads=[PB[b]], writes=[dst_tt])

        def mixer(l, half):
            cond = half
            nseq, L = (4, 256) if half == 0 else (1, 1024)
            cpl = L // 128
            AR.reset()
            ytok = AV([128, 8, 1024], BF16)
            ar_mark = AR.off

            class _V:
                pass
            SCR = _V()
            SCR.ap = yT.t[:].bitcast(F32)
            SCR.t = SCR.ap
            SCR.res = yT.res
            slot = w_get()
            GE = AV([128, 8, 512])
            VN = AV([128, 8, 256], BF16)
            ssG = AV([128, 8])
            for tt in range(8):
                b = nextbank()
                zmm(slot, tt, b)
                S.op("act", lambda: nc.scalar.activation(out=GE.t[:, tt, :], in_=pbank(b), func=AF.Gelu_apprx_tanh),
                     reads=[PB[b]], writes=[GE])
            w_done()
            gv = GE.t[:, :, 256:512]
            sv = SCR.t[:, :, 0:256]

            def g0_stageB():
                S.op("dve", lambda: nc.vector.tensor_tensor(out=sv, in0=gv, in1=gv, op=ALU.mult), reads=[GE], writes=[SCR])
                S.op("dve", lambda: nc.vector.tensor_reduce(out=ssG.t[:], in_=sv, axis=AX.X, op=ALU.add),
                     reads=[SCR], writes=[ssG])
                rstd_from_ss(ssG.t[:], ssG.t[:], 1.0 / 256, EPS, [ssG], [ssG])
                S.op("dve", lambda: nc.vector.tensor_tensor(out=gv, in0=gv,
                                                            in1=ssG.t[:].unsqueeze(2).broadcast_to([128, 8, 256]),
                                                            op=ALU.mult), reads=[GE, ssG], writes=[GE])
                S.op("dve", lambda: nc.vector.tensor_tensor(
                    out=VN.t[:], in0=gv, in1=SGN.t[:, l, :].unsqueeze(1).broadcast_to([128, 8, 256]), op=ALU.mult),
                    reads=[GE, SGN], writes=[VN])

            def g0_stageC(tt):
                b2 = nextbank()

                def f():
                    ins = None
                    for g in range(4):
                        ins = nc.tensor.matmul(pbank(b2)[:, g * 64:(g + 1) * 64], WST.t[:, l * 4 + g, :],
                                               VN.t[:, tt, g * 64:(g + 1) * 64], start=True, stop=True)
                    return ins
                S.op("pe", f, reads=[WST, VN], writes=[PB[b2]])
                for g in range(4):
                    S.op("dve", lambda: nc.vector.scalar_tensor_tensor(
                        out=ytok.t[:, tt, g * 64:(g + 1) * 64], in0=pbank(b2)[:, g * 64:(g + 1) * 64],
                        scalar=BS.t[:, l * 4 + g:l * 4 + g + 1], in1=GE.t[:, tt, g * 64:(g + 1) * 64],
                        op0=ALU.add, op1=ALU.mult), reads=[PB[b2], BS, GE], writes=[ytok])

            ckpt(13)

            QT = AV([128, 2, 1024], BF16)
            KT = AV([128, 2, 1024], BF16)
            VB = AV([128, 8, 256], BF16)
            SG = AV([128, 8, 256], BF16)
            KVS = AV([128, 8, 2, 2, 64])
            RS = AV([128, 2, 2, 64])
            RSb = AV([128, 8, 2, 2, 64], BF16)

            class _W:
                pass
            RSd = []
            RSbd = []
            for d_ in range(2):
                v_ = _W()
                v_.ap = RS.t[:, d_, :, :]
                v_.t = v_.ap
                v_.res = Res()
                RSd.append(v_)
                w_ = _W()
                w_.res = Res()
                RSbd.append(w_)
            qkb = [AV([128, 512], BF16) for _ in range(2)]
            KF = [AV([128, 2, 256], BF16) for _ in range(2)]
            gtmp = [AV([128, 256]) for _ in range(2)]
            st_stage = [AV([128, 2, 2, 64]) for _ in range(2)]
            slot1 = w_get()
            slot2 = w_get()
            g0_stageB()
            for tt in range(8):
                if tt >= 1:
                    g0_stageC(tt - 1)
                b1 = nextbank()
                zmm(slot1, tt, b1)
                if tt == 0:
                    ckpt(1301)
                b2 = nextbank()
                zmm(slot2, tt, b2)
                if tt == 0:
                    ckpt(1302)
                qk = qkb[tt % 2]
                S.op("act", lambda: nc.scalar.copy(out=qk.t[:], in_=pbank(b1)), reads=[PB[b1]], writes=[qk])
                if tt == 0:
                    ckpt(1303)
                kf = KF[tt % 2]
                for d in range(2):
                    S.op("dve", lambda: nc.vector.tensor_tensor(
                        out=kf.t[:, d, :], in0=pbank(b1)[:, 256:512], in1=KDE.t[:, l, d, :], op=ALU.mult),
                        reads=[PB[b1], KDE], writes=[kf])
                    if tt == 0 and d == 0:
                        ckpt(1304)
                if tt == 0:
                    ckpt(131)
                bt = nextbank((6, 7), "t67")
                pv = pbank(bt).bitcast(BF16)

                def f():
                    ins = None
                    for i in range(4):
                        ins = nc.tensor.transpose(pv[:, i * 128:(i + 1) * 128], qk.t[:, i * 128:(i + 1) * 128],
                                                  identB.t[:])
                    return ins
                S.op("pe", f, reads=[qk, identB], writes=[PB[bt]])
                if tt == 0:
                    ckpt(132)
                S.op("dve", lambda: nc.vector.tensor_copy(
                    QT.t[:, :, tt * 128:(tt + 1) * 128], pv[:, 0:256].rearrange("p (a b) -> p a b", b=128)),
                    reads=[PB[bt]], writes=[QT])
                S.op("dve", lambda: nc.vector.tensor_copy(
                    KT.t[:, :, tt * 128:(tt + 1) * 128], pv[:, 256:512].rearrange("p (a b) -> p a b", b=128)),
                    reads=[PB[bt]], writes=[KT])
                if tt == 0:
                    ckpt(133)
                S.op("act", lambda: nc.scalar.copy(out=VB.t[:, tt, :], in_=pbank(b2)[:, 0:256]),
                     reads=[PB[b2]], writes=[VB])
                gt_ = gtmp[tt % 2]
                S.op("act", lambda: nc.scalar.activation(out=gt_.t[:], in_=pbank(b2)[:, 256:512], func=AF.Silu),
                     reads=[PB[b2]], writes=[gt_])
                S.op("dve", lambda: nc.vector.tensor_tensor(out=SG.t[:, tt, :], in0=gt_.t[:], in1=RN.t[:, l, :],
                                                            op=ALU.mult), reads=[gt_, RN], writes=[SG])
                if tt == 0:
                    ckpt(134)
                bk = nextbank((4, 5), "t45")

                def f():
                    ins = None
                    for d in range(2):
                        for hp in range(2):
                            ins = nc.tensor.matmul(pbank(bk)[:, (d * 2 + hp) * 128:(d * 2 + hp + 1) * 128],
                                                   kf.t[:, d, hp * 128:(hp + 1) * 128],
                                                   VB.t[:, tt, hp * 128:(hp + 1) * 128], start=True, stop=True)
                    return ins
                S.op("pe", f, reads=[kf, VB], writes=[PB[bk]])
                if tt == 0:
                    ckpt(135)
                pk = pbank(bk).rearrange("p (a b) -> p a b", b=128)
                for j in range(2):
                    ps_ = slice(j * 64, (j + 1) * 64)
                    S.op("dve", lambda: nc.vector.tensor_copy(
                        KVS.t[ps_, tt, :, :, :].rearrange("p a b c -> p (a b) c"), pk[ps_, :, j * 64:(j + 1) * 64]),
                        reads=[PB[bk]], writes=[KVS])
            g0_stageC(7)
            w_done()
            w_done()

            ckpt(14)
            for s_ in range(nseq):
                if half == 0:
                    S.op("dve", lambda: nc.vector.memset(RS.t[:], 0.0), writes=[RSd[0], RSd[1]])
                else:
                    for d, nm in ((0, "srf"), (1, "srb")):
                        for j in range(2):
                            srcs = din[nm][l].rearrange("(hp j) d e -> j d hp e", j=2)[j]
                            S.dma("sp", RS.t[j * 64:(j + 1) * 64, d, :, :], srcs, writes=[RSd[d]])
                def step(d, tt):
                    S.op("dve", lambda: nc.vector.tensor_copy(RSb.t[:, tt, d, :, :], RSd[d].t[:]),
                         reads=[RSd[d]], writes=[RSbd[d]])
                    S.op("dve", lambda: nc.vector.tensor_tensor(out=RSd[d].t[:], in0=RSd[d].t[:],
                                                                in1=CD.t[:, l, d, :, :], op=ALU.mult),
                         reads=[RSd[d], CD], writes=[RSd[d]])
                    S.op("dve", lambda: nc.vector.tensor_tensor(out=RSd[d].t[:], in0=RSd[d].t[:],
                                                                in1=KVS.t[:, tt, d, :, :], op=ALU.add),
                         reads=[RSd[d], KVS], writes=[RSd[d]])
                for c in range(cpl):
                    step(0, s_ * cpl + c)
                    step(1, s_ * cpl + (cpl - 1 - c))
                if half == 0:
                    stg = st_stage[s_ % 2]
                    S.op("dve", lambda: nc.vector.tensor_copy(stg.t[:], RS.t[:]), reads=[RSd[0], RSd[1]], writes=[stg])
                    for d, nm in ((0, "nrf"), (1, "nrb")):
                        for j in range(2):
                            dsts = dout[nm][s_, l].rearrange("(hp j) d e -> j d hp e", j=2)[j]
                            S.dma("sp", dsts, stg.t[j * 64:(j + 1) * 64, d, :, :], reads=[stg])

            ckpt(15)
            S.barrier()
            ar_keep = AR.off
            AR.off = ar_mark
            MT = [AV([128, 512], BF16) for _ in range(2)]
            QF = [AV([128, 2, 2, 128], BF16) for _ in range(2)]
            OR = AV([128, 8, 256])
            for tt in range(8):
                c = tt % cpl
                tsl = slice(tt * 128, (tt + 1) * 128)
                bia = nextbank((0, 1), "t01")
                bib = nextbank((0, 1), "t01")

                def f():
                    ins = None
                    for j, bnk in ((0, bia), (1, bib)):
                        ps_ = slice(j * 64, (j + 1) * 64)
                        for hp in range(2):
                            ins = nc.tensor.matmul(pbank(bnk)[:, hp * 128:(hp + 1) * 128], KT.t[ps_, hp, tsl],
                                                   QT.t[ps_, hp, tsl], start=True, stop=True)
                    return ins
                S.op("pe", f, reads=[KT, QT], writes=[PB[bia], PB[bib]])
                mt = MT[tt % 2]
                mt4 = mt.t[:].rearrange("p (hp j q) -> p hp j q", hp=2, j=2)
                for j, bnk in ((0, bia), (1, bib)):
                    S.op("dve", lambda: nc.vector.tensor_tensor(
                        out=mt4[:, :, j, :], in0=pbank(bnk)[:, 0:256].rearrange("p (a b) -> p a b", b=128),
                        in1=DTm.t[:, l, :, :].rearrange("p (hp j) q -> p hp j q", j=2)[:, :, j, :], op=ALU.mult),
                        reads=[PB[bnk], DTm], writes=[mt])
                qf = QF[tt % 2]
                S.op("dve", lambda: nc.vector.tensor_tensor(
                    out=qf.t[:], in0=QD.t[:, l, :, :, :],
                    in1=QT.t[:, :, tsl].unsqueeze(2).broadcast_to([128, 2, 2, 128]), op=ALU.mult),
                    reads=[QT, QD], writes=[qf])
                use_f = not (half == 0 and c == 0)
                use_b = not (half == 0 and c == cpl - 1)
                bo = nextbank((2, 3), "t23")

                def f():
                    ins = None
                    for h in range(4):
                        hp, j = h // 2, h % 2
                        ps_ = slice(j * 64, (j + 1) * 64)
                        o_ap = pbank(bo)[:, h * 64:(h + 1) * 64]
                        last = not (use_f or use_b)
                        ins = nc.tensor.matmul(o_ap, mt.t[:, h * 128:(h + 1) * 128], VB.t[:, tt, h * 64:(h + 1) * 64],
                                               start=True, stop=last)
                        if use_f:
                            ins = nc.tensor.matmul(o_ap, qf.t[ps_, hp, 0, :], RSb.t[ps_, tt, 0, hp, :],
                                                   start=False, stop=not use_b)
                        if use_b:
                            ins = nc.tensor.matmul(o_ap, qf.t[ps_, hp, 1, :], RSb.t[ps_, tt, 1, hp, :],
                                                   start=False, stop=True)
                    return ins
                S.op("pe", f, reads=[mt, VB, qf, RSbd[0], RSbd[1]], writes=[PB[bo]])
                S.op("act", lambda: nc.scalar.copy(out=OR.t[:, tt, :], in_=pbank(bo)[:, 0:256]),
                     reads=[PB[bo]], writes=[OR])
            sv = SCR.t[:, :, 0:256]
            S.op("act", lambda: nc.scalar.activation(out=sv, in_=OR.t[:], func=AF.Square), reads=[OR], writes=[SCR])
            ssR = AV([128, 8, 4])
            S.op("dve", lambda: nc.vector.tensor_reduce(out=ssR.t[:], in_=sv.rearrange("p t (h c) -> p t h c", c=64),
                                                        axis=AX.X, op=ALU.add), reads=[SCR], writes=[ssR])
            rstd_from_ss(ssR.t[:], ssR.t[:], 1.0 / 64, EPS, [ssR], [ssR])
            o4 = OR.t[:].rearrange("p t (h c) -> p t h c", c=64)
            S.op("dve", lambda: nc.vector.tensor_tensor(out=o4, in0=o4,
                                                        in1=ssR.t[:].unsqueeze(3).broadcast_to([128, 8, 4, 64]),
                                                        op=ALU.mult), reads=[OR, ssR], writes=[OR])
            S.op("dve", lambda: nc.vector.tensor_tensor(out=ytok.t[:, :, 256:512], in0=OR.t[:], in1=SG.t[:],
                                                        op=ALU.mult), reads=[OR, SG], writes=[ytok])
            S.barrier()
            ckpt(16)
            AR.off = ar_mark

            NKC = 2 if half == 0 else 10
            KOFF = 0 if half == 0 else 256
            NKEY = 1024 + KOFF
            QTa = AV([128, 4, 1024], BF16)
            KTa = AV([128, 4, NKEY], BF16)
            VA = AV([128, NKEY // 128, 4, 132], BF16)
            S.op("dve", lambda: nc.vector.memset(VA.t[:, :, :, 128:129], 1.0), writes=[VA])
            ar_att = AR.off
            stg = [AV([128, 512]) for _ in range(2)]

            class _G:
                pass
            GR = []
            for g_ in range(2):
                o_ = _G()
                o_.QA = AV([128, 4, 512])
                o_.QB = AV([128, 4, 512], BF16)
                o_.ss = AV([128, 32])
                o_.SQ = _G()
                o_.SQ.ap = SCR.t[:, g_ * 4:(g_ + 1) * 4, :]
                o_.SQ.t = o_.SQ.ap
                o_.SQ.res = Res()
                GR.append(o_)
            S.op("dve", lambda: nc.vector.memset(GR[0].ss.t[:], 0.0), reads=[yT], writes=[GR[0].ss, GR[0].SQ, GR[1].SQ])
            if half == 1:
                for kc in range(2):
                    st = stg[kc % 2]
                    S.dma("sp", st.t[:], din["ck"][l, kc * 128:(kc + 1) * 128, :], writes=[st])
                    S.op("act", lambda: nc.scalar.copy(out=GR[0].QB.t[:, kc, :], in_=st.t[:]), reads=[st],
                         writes=[GR[0].QB])
                    transposes_to(GR[0].QB, lambda i: GR[0].QB.t[:, kc, i * 128:(i + 1) * 128], 4, KTa,
                                  KTa.t[:, :, kc * 128:(kc + 1) * 128])
                for kc in range(2):
                    st = stg[kc % 2]
                    S.dma("sp", st.t[:], din["cv"][l, kc * 128:(kc + 1) * 128, :], writes=[st])
                    S.op("act", lambda: nc.scalar.copy(out=VA.t[:, kc, :, 0:128],
                                                       in_=st.t[:].rearrange("p (a b) -> p a b", b=128)),
                         reads=[st], writes=[VA])

            def stageA1(slot, g):
                G = GR[g]
                for t in range(4):
                    tt = g * 4 + t
                    b = nextbank()
                    zmm(slot, tt, b)
                    S.op("act", lambda: nc.scalar.copy(out=G.QA.t[:, t, :], in_=pbank(b)), reads=[PB[b]], writes=[G.QA])
                    S.op("act", lambda: nc.scalar.activation(out=G.SQ.t[:, t, :], in_=pbank(b), func=AF.Square),
                         reads=[PB[b]], writes=[G.SQ])

            def stageA2(g):
                G = GR[g]
                S.op("dve", lambda: nc.vector.tensor_reduce(
                    out=G.ss.t[:], in_=G.SQ.t[:].rearrange("p t (a b) -> p (t a) b", b=64),
                    axis=AX.X, op=ALU.add), reads=[G.SQ], writes=[G.ss])

            def stageB(g, gain_tt, which):
                G = GR[g]
                if which == "q":
                    rstd_from_ss(G.ss.t[:], G.ss.t[:], 1.0, 64 * EPS, [G.ss], [G.ss])
                else:
                    rstd_from_ss(G.ss.t[:], G.ss.t[:], 1.0 / 64, EPS, [G.ss], [G.ss])
                qf_ = G.QA.t[:].rearrange("p a b -> p (a b)")
                sf_ = G.SQ.t[:].rearrange("p a b -> p (a b)")
                q3 = qf_.rearrange("p (a b) -> p a b", b=64)
                S.op("dve", lambda: nc.vector.tensor_tensor(out=q3, in0=q3,
                                                            in1=G.ss.t[:].unsqueeze(2).broadcast_to([128, 32, 64]),
                                                            op=ALU.mult), reads=[G.QA, G.ss], writes=[G.QA])
                gbc = gain_tt.t[:, l, :].unsqueeze(1).broadcast_to([128, 32, 64])
                if half == 0 and which == "q":
                    S.op("dve", lambda: nc.vector.tensor_tensor(
                        out=G.QB.t[:].rearrange("p a (g c) -> p (a g) c", c=64), in0=q3, in1=gbc, op=ALU.mult),
                        reads=[G.QA, gain_tt], writes=[G.QB])
                else:
                    S.op("dve", lambda: nc.vector.tensor_tensor(out=q3, in0=q3, in1=gbc, op=ALU.mult),
                         reads=[G.QA, gain_tt], writes=[G.QA])
                    if half == 0:
                        for s2 in range(2):
                            s_ = g * 2 + s2
                            S.dma("sp", dout["nk"][s_, l].rearrange("(t p) c -> p t c", p=128),
                                  G.QA.t[:, 2 * s2:2 * s2 + 2, :], reads=[G.QA])
                        S.op("act", lambda: nc.scalar.copy(out=G.QB.t[:].rearrange("p a b -> p (a b)"), in_=qf_),
                             reads=[G.QA], writes=[G.QB])
                    else:
                        x5 = G.QA.t[:].rearrange("p t (g a s c) -> p t g a s c", a=2, s=2, c=16)
                        r5 = G.SQ.t[:].rearrange("p t (g a s c) -> p t g a s c", a=2, s=2, c=16)
                        s5 = SIN.t[:, g * 4:(g + 1) * 4, :].rearrange("p t (a s c) -> p t a s c", s=2, c=16)
                        for sidx in range(2):
                            for ax_ in range(2):
                                S.op("dve", lambda: nc.vector.tensor_tensor(
                                    out=r5[:, :, :, ax_, sidx, :], in0=x5[:, :, :, ax_, 1 - sidx, :],
                                    in1=s5[:, :, ax_, sidx, :].unsqueeze(2).broadcast_to([128, 4, 8, 16]), op=ALU.mult),
                                    reads=[G.QA, SIN], writes=[G.SQ])
                        q4 = G.QA.t[:].rearrange("p t (g c) -> p t g c", c=64)
                        S.op("dve", lambda: nc.vector.tensor_tensor(
                            out=q4, in0=q4,
                            in1=COS.t[:, g * 4:(g + 1) * 4, :].unsqueeze(2).broadcast_to([128, 4, 8, 64]), op=ALU.mult),
                            reads=[G.QA, COS], writes=[G.QA])
                        S.op("dve", lambda: nc.vector.tensor_tensor(out=G.QB.t[:].rearrange("p a b -> p (a b)"),
                                                                    in0=qf_, in1=sf_, op=ALU.add),
                             reads=[G.QA, G.SQ], writes=[G.QB])

            def stageC(g, dstT, doff):
                G = GR[g]
                for t in range(4):
                    tt = g * 4 + t
                    b = nextbank()
                    pv = pbank(b).bitcast(BF16)

                    def f():
                        ins = None
                        for i in range(4):
                            ins = nc.tensor.transpose(pv[:, i * 128:(i + 1) * 128], G.QB.t[:, t, i * 128:(i + 1) * 128],
                                                      identB.t[:])
                        return ins
                    S.op("pe", f, reads=[G.QB, identB], writes=[PB[b]])
                    S.op("act", lambda: nc.scalar.copy(out=dstT.t[:, :, doff + tt * 128:doff + (tt + 1) * 128],
                                                       in_=pv[:, 0:512].rearrange("p (a b) -> p a b", b=128)),
                         reads=[PB[b]], writes=[dstT])

            def stageV(slot, g):
                for t in range(4):
                    tt = g * 4 + t
                    b = nextbank()
                    zmm(slot, tt, b)
                    kci = KOFF // 128 + tt
                    if half == 0:
                        st = stg[tt % 2]
                        S.op("act", lambda: nc.scalar.copy(out=st.t[:], in_=pbank(b)), reads=[PB[b]], writes=[st])
                        S.dma("sp", dout["nv"][tt // 2, l, (tt % 2) * 128:(tt % 2) * 128 + 128, :], st.t[:], reads=[st])
                        S.op("act", lambda: nc.scalar.copy(out=VA.t[:, kci, :, 0:128],
                                                           in_=st.t[:].rearrange("p (a b) -> p a b", b=128)),
                             reads=[st], writes=[VA])
                    else:
                        S.op("act", lambda: nc.scalar.copy(out=VA.t[:, kci, :, 0:128],
                                                           in_=pbank(b).rearrange("p (a b) -> p a b", b=128)),
                             reads=[PB[b]], writes=[VA])

            slot3 = w_get()
            stageA1(slot3, 0)
            stageA2(0)
            stageB(0, QN, "q")
            stageA1(slot3, 1)
            stageA2(1)
            w_done()
            slot4 = w_get()
            stageC(0, QTa, 0)
            stageB(1, QN, "q")
            stageA1(slot4, 0)
            stageA2(0)
            stageC(1, QTa, 0)
            stageB(0, KN, "k")
            stageA1(slot4, 1)
            stageA2(1)
            w_done()
            slot5 = w_get()
            stageC(0, KTa, KOFF)
            stageV(slot5, 0)
            stageB(1, KN, "k")
            stageV(slot5, 1)
            stageC(1, KTa, KOFF)
            w_done()
            S.barrier()
            ckpt(17)
            AR.off = ar_att
            OALL = AV([128, 8, 4, 128])
            ETP = [AV([128, 2, 2, 256], BF16) for _ in range(2)]
            ot2 = [AV([128, 128]) for _ in range(2)]
            work = []
            for qg in range(4):
                kcs = [qg * 2, qg * 2 + 1] if half == 0 else list(range(10))
                for h in range(4):
                    for pi in range(len(kcs) // 2):
                        work.append((qg, h, pi, len(kcs) // 2, (kcs[2 * pi], kcs[2 * pi + 1])))
            st_banks = {}

            def emit_st(widx):
                qg, h, pi, npair, kc2 = work[widx]
                q0 = qg * 256
                bsA, bsB = ((0, 1), (2, 3))[widx % 2]
                st_banks[widx] = (bsA, bsB)

                def f():
                    ins = None
                    for i, bnk in ((0, bsA), (1, bsB)):
                        ps_ = slice(i * 64, (i + 1) * 64)
                        for kk in range(2):
                            kc = kc2[kk]
                            ins = nc.tensor.matmul(pbank(bnk)[:, kk * 256:(kk + 1) * 256],
                                                   KTa.t[ps_, h, kc * 128:(kc + 1) * 128],
                                                   QTa.t[ps_, h, q0:q0 + 256], start=True, stop=True)
                    return ins
                S.op("pe", f, reads=[KTa, QTa], writes=[PB[bsA], PB[bsB]])

            OSQh = AV([128, 2048])

            def norm_half(hh):
                ov = OALL.t[:, hh * 4:(hh + 1) * 4, :, :]
                of_ = ov.rearrange("p a b c -> p (a b c)")
                ss2 = AV([128, 16])
                S.op("dve", lambda: nc.vector.tensor_tensor(out=OSQh.t[:], in0=of_, in1=of_, op=ALU.mult),
                     reads=[OALLh[hh]], writes=[OSQh])
                S.op("dve", lambda: nc.vector.tensor_reduce(out=ss2.t[:], in_=OSQh.t[:].rearrange("p (a b) -> p a b", b=128),
                                                            axis=AX.X, op=ALU.add), reads=[OSQh], writes=[ss2])
                rstd_from_ss(ss2.t[:], ss2.t[:], 1.0 / 128, EPS, [ss2], [ss2])
                o3 = of_.rearrange("p (a b) -> p a b", b=128)
                S.op("dve", lambda: nc.vector.tensor_tensor(out=o3, in0=o3,
                                                            in1=ss2.t[:].unsqueeze(2).broadcast_to([128, 16, 128]),
                                                            op=ALU.mult), reads=[OALLh[hh], ss2], writes=[OALLh[hh]])
                S.op("dve", lambda: nc.vector.tensor_tensor(
                    out=ytok.t[:, hh * 4:(hh + 1) * 4, 512:1024].rearrange("p t (h c) -> p t h c", c=128), in0=ov,
                    in1=DN.t[:, l, :].unsqueeze(1).unsqueeze(1).broadcast_to([128, 4, 4, 128]), op=ALU.mult),
                    reads=[OALLh[hh], DN], writes=[ytok])

            OALLh = [Res(), Res()]
            emit_st(0)
            itn = 0
            for widx in range(len(work)):
                qg, h, pi, npair, kc2 = work[widx]
                q0 = qg * 256
                if pi == 0:
                    accs = ((4, 5), (6, 7))[itn % 2]
                    itn += 1
                if widx + 1 < len(work):
                    emit_st(widx + 1)
                bsA, bsB = st_banks.pop(widx)
                etp = ETP[widx % 2]
                S.op("act", lambda: nc.scalar.activation(out=etp.t[:].rearrange("p i a b -> p (i a b)"),
                                                         in_=PSt[bsA // 2][:, 0:1024], func=AF.Exp),
                     reads=[PB[bsA], PB[bsB]], writes=[etp])

                def f():
                    ins = None
                    for kk in range(2):
                        kc = kc2[kk]
                        for qb_ in range(2):
                            for i in range(2):
                                first = (pi == 0 and kk == 0 and i == 0)
                                last = (pi == npair - 1 and kk == 1 and i == 1)
                                ins = nc.tensor.matmul(pbank(accs[qb_])[:, i * 129:(i + 1) * 129],
                                                       etp.t[:, i, kk, qb_ * 128:(qb_ + 1) * 128],
                                                       VA.t[:, kc, h, 0:129], start=first, stop=last)
                    return ins
                S.op("pe", f, reads=[etp, VA], writes=[PB[accs[0]], PB[accs[1]]])
                if pi == npair - 1:
                    for qb_ in range(2):
                        tt = (q0 + qb_ * 128) // 128
                        ab = accs[qb_]
                        acc = pbank(ab)
                        rc = nsmall()
                        S.op("dve", lambda: nc.vector.reciprocal(
                            rc.t[:, 0:2], acc[:, 0:258].rearrange("p (a b) -> p a b", b=129)[:, :, 128]),
                            reads=[PB[ab]], writes=[rc])
                        S.op("dve", lambda: nc.vector.tensor_tensor(out=rc.t[:, 2:3], in0=rc.t[:, 1:2],
                                                                    in1=NLAM.t[:, l:l + 1], op=ALU.mult),
                             reads=[rc, NLAM], writes=[rc])
                        o1 = ot2[qb_]
                        S.op("dve", lambda: nc.vector.tensor_scalar(o1.t[:], acc[:, 129:257], rc.t[:, 2:3], None,
                                                                    ALU.mult), reads=[PB[ab], rc], writes=[o1])
                        S.op("dve", lambda: nc.vector.scalar_tensor_tensor(
                            out=OALL.t[:, tt, h, :], in0=acc[:, 0:128], scalar=rc.t[:, 0:1], in1=o1.t[:],
                            op0=ALU.mult, op1=ALU.add), reads=[PB[ab], rc, o1], writes=[OALLh[qg // 2]])
                        if qb_ == 1 and h == 3 and qg in (1, 3):
                            norm_half(qg // 2)
            ckpt(18)
            if half == HALVES[0]:
                mods_group(l, 2)
            slots_o = [w_get(), w_get()]
            for tg in range(2):
                for tt in range(tg * 4, tg * 4 + 4):
                    b = nextbank()
                    pv = pbank(b).bitcast(BF16)

                    def f():
                        ins = None
                        for i in range(8):
                            ins = nc.tensor.transpose(pv[:, i * 128:(i + 1) * 128], ytok.t[:, tt, i * 128:(i + 1) * 128],
                                                      identB.t[:])
                        return ins
                    S.op("pe", f, reads=[ytok, identB], writes=[PB[b]])
                    S.op("dve", lambda: nc.vector.tensor_copy(
                        yT.t[:, :, tt * 128:(tt + 1) * 128], pv[:, 0:1024].rearrange("p (a b) -> p a b", b=128)),
                        reads=[PB[b]], writes=[yT.halves[tg], yT])
                for cg in range(2):
                    slot = slots_o[cg]
                    wv = slotv(slot, 8, 512)
                    for m in range(4):
                        mm = cg * 4 + m
                        b = nextbank()

                        def f():
                            ins = None
                            for kc in range(8):
                                ins = nc.tensor.matmul(pbank(b), wv[:, kc, m * 128:(m + 1) * 128],
                                                       yT.t[:, kc, tg * 512:(tg + 1) * 512],
                                                       start=(kc == 0), stop=(kc == 7))
                            return ins
                        S.op("pe", f, reads=[slot, yT.halves[tg]], writes=[PB[b]])
                        xs_ = xT.t[:, mm, tg * 512:(tg + 1) * 512]
                        S.op("dve", lambda: nc.vector.scalar_tensor_tensor(
                            out=xs_, in0=pbank(b), scalar=MOD[l].t[:, 2 * 8 + mm, cond:cond + 1], in1=xs_,
                            op0=ALU.mult, op1=ALU.add), reads=[PB[b], MOD[l], xT], writes=[xT])
                        S.op("act", lambda: nc.scalar.activation(out=hT.t[:, mm, tg * 512:(tg + 1) * 512], in_=xs_,
                                                                 func=AF.Square), reads=[xT], writes=[hT])
            w_done()
            w_done()
            S.barrier()

        def ffn(l, half):
            cond = half
            nseq, L = (4, 256) if half == 0 else (1, 1024)
            AR.reset()
            gT = AV([128, 22, 1024], BF16)
            y0 = [AV([128, 1024]) for _ in range(3)]
            sa = [AV([128, 1024]) for _ in range(2)]
            upb = ((0, 1), (2, 3), (4, 5))
            ui = 0
            for t in range(11):
                slot = w_get()
                wv = slotv(slot, 8, 512)
                for sub in range(2):
                    j = 2 * t + sub
                    for ab in range(2):
                        cols = ab * 256 + sub * 128
                        jj = ab * 22 + j
                        bb = upb[ui % 3]
                        yy = y0[ui % 3]
                        ui += 1
                        for tg in range(2):
                            bnk = bb[tg]

                            def f():
                                ins = None
                                for kc in range(8):
                                    ins = nc.tensor.matmul(pbank(bnk), wv[:, kc, cols:cols + 128],
                                                           hT.t[:, kc, tg * 512:(tg + 1) * 512],
                                                           start=(kc == 0), stop=(kc == 7))
                                return ins
                            S.op("pe", f, reads=[slot, hT], writes=[PB[bnk]])
                        pfull = PSt[bb[0] // 2][:, 0:1024]
                        S.op("act", lambda: nc.scalar.activation(out=yy.t[:], in_=pfull, func=AF.Identity,
                                                                 bias=CW.t[:, l, 3, jj:jj + 1],
                                                                 scale=CW.t[:, l, 1, jj:jj + 1]),
                             reads=[PB[bb[0]], PB[bb[1]], CW], writes=[yy])
                        pv3 = pfull.rearrange("p (s t) -> p s t", t=L)
                        yv3 = yy.t[:].rearrange("p (s t) -> p s t", t=L)
                        S.op("dve", lambda: nc.vector.scalar_tensor_tensor(
                            out=yv3[:, :, 1:L], in0=pv3[:, :, 0:L - 1], scalar=CW.t[:, l, 0, jj:jj + 1],
                            in1=yv3[:, :, 1:L], op0=ALU.mult, op1=ALU.add),
                            reads=[PB[bb[0]], PB[bb[1]], CW, yy], writes=[yy])
                        S.op("dve", lambda: nc.vector.scalar_tensor_tensor(
                            out=yv3[:, :, 0:L - 1], in0=pv3[:, :, 1:L], scalar=CW.t[:, l, 2, jj:jj + 1],
                            in1=yv3[:, :, 0:L - 1], op0=ALU.mult, op1=ALU.add),
                            reads=[PB[bb[0]], PB[bb[1]], CW, yy], writes=[yy])
                        if ab == 0:
                            sa_ = sa[j % 2]
                            S.op("act", lambda: nc.scalar.activation(out=sa_.t[:], in_=yy.t[:], func=AF.Silu),
                                 reads=[yy], writes=[sa_])
                        else:
                            sa_ = sa[j % 2]
                            S.op("pool", lambda: nc.gpsimd.tensor_tensor(out=gT.t[:, j, :], in0=sa_.t[:], in1=yy.t[:],
                                                                         op=ALU.mult), reads=[sa_, yy], writes=[gT])
                w_done()
            if half == HALVES[0]:
                mods_group(l, 5)
            for m in range(8):
                slot = w_get()
                wv = slotv(slot, 22, 128)
                for tg in range(2):
                    b = nextbank((6, 7), "t67")

                    def f():
                        ins = None
                        for kc in range(22):
                            ins = nc.tensor.matmul(pbank(b), wv[:, kc, :], gT.t[:, kc, tg * 512:(tg + 1) * 512],
                                                   start=(kc == 0), stop=(kc == 21))
                        return ins
                    S.op("pe", f, reads=[slot, gT], writes=[PB[b]])
                    xs_ = xT.t[:, m, tg * 512:(tg + 1) * 512]
                    S.op("dve", lambda: nc.vector.scalar_tensor_tensor(
                        out=xs_, in0=pbank(b), scalar=MOD[l].t[:, 5 * 8 + m, cond:cond + 1], in1=xs_,
                        op0=ALU.mult, op1=ALU.add), reads=[PB[b], MOD[l], xT], writes=[xT])
                    if l < NL - 1:
                        S.op("act", lambda: nc.scalar.activation(out=yT.t[:, m, tg * 512:(tg + 1) * 512], in_=xs_,
                                                                 func=AF.Square), reads=[xT], writes=[yT] + yT.halves)
                w_done()
            S.barrier()

        for half in HALVES:
            AR.reset()
            load_x(din["xp"] if half == 0 else din["xs"])
            S.barrier()
            ckpt(11)
            for l in range(NL):
                AR.reset()
                if half == HALVES[0]:
                    mods_group(l, 0)
                    mods_group(l, 1)
                norm_mod(l, 0, half, presq=yT)
                S.barrier()
                ckpt(12)
                mixer(l, half)
                ckpt(19)
                AR.reset()
                if half == HALVES[0]:
                    mods_group(l, 3)
                    mods_group(l, 4)
                norm_mod(l, 1, half, presq=hT)
                S.barrier()
                ckpt(20)
                ffn(l, half)
                ckpt(21)
            AR.reset()
            store_x(dout["yp"] if half == 0 else dout["ys"])
            S.barrier()
        S.drain("sp")
    except _Stop:
        pass
    return nc


_CACHE = {}


def _prep_inputs(inputs):
    f32 = lambda a: np.ascontiguousarray(np.asarray(a, dtype=np.float32))
    I = {k: f32(v) for k, v in inputs.items()}
    hc = host_consts()
    shared = {
        "norm1": I["norm1"], "w_mod": I["w_mod"], "b_mod": I["b_mod"], "w_in": I["w_in"],
        "sgu_norm": I["sgu_norm"], "sgu_w": I["sgu_w"], "sgu_b": I["sgu_b"],
        "rlf": I["ret_logit_fwd"], "rlb": I["ret_logit_bwd"], "ret_norm": I["ret_norm"].reshape(2, 256),
        "q_norm": I["q_norm"], "k_norm": I["k_norm"], "diff_lam": I["diff_lam"].reshape(2, 256),
        "diff_norm": I["diff_norm"], "w_out": I["w_out"], "norm2": I["norm2"], "ffn_up": I["ffn_up"],
        "ffn_conv": I["ffn_conv"], "ffn_conv_b": I["ffn_conv_b"], "ffn_down": I["ffn_down"],
    }
    shared.update(hc)
    maps = []
    for c in range(8):
        m = dict(shared)
        m["xp"] = np.ascontiguousarray(I["x_prompt"][4 * c:4 * c + 4].reshape(1024, 1024))
        m["xs"] = np.ascontiguousarray(I["x_sample"][c])
        m["cvec"] = np.ascontiguousarray(np.stack([I["c_ctx"], I["c"][c]], axis=0))
        m["ck"] = np.ascontiguousarray(I["cache_k"][c].reshape(2, 256, 512))
        m["cv"] = np.ascontiguousarray(I["cache_v"][c].reshape(2, 256, 512))
        m["srf"] = np.ascontiguousarray(I["state_ret_fwd"][c])
        m["srb"] = np.ascontiguousarray(I["state_ret_bwd"][c])
        maps.append(m)
    return maps


def kernel(**inputs):
    maps = _prep_inputs(inputs)
    if "nc" not in _CACHE:
        _CACHE["nc"] = build()
    nc = _CACHE["nc"]
    res = run_bass_kernel_spmd(nc, maps, core_ids=list(range(8)))
    R = res.results
    yp = np.concatenate([np.asarray(R[c]["yp"]).reshape(4, 256, 1024) for c in range(8)], axis=0)
    ys = np.stack([np.asarray(R[c]["ys"]) for c in range(8)], axis=0)
    nk = np.concatenate([np.asarray(R[c]["nk"]).reshape(4, 2, 256, 4, 2, 64) for c in range(8)], axis=0)
    nv = np.concatenate([np.asarray(R[c]["nv"]).reshape(4, 2, 256, 4, 128) for c in range(8)], axis=0)
    nrf = np.concatenate([np.asarray(R[c]["nrf"]) for c in range(8)], axis=0)
    nrb = np.concatenate([np.asarray(R[c]["nrb"]) for c in range(8)], axis=0)
    return (yp.astype(np.float32), ys.astype(np.float32), nk.astype(np.float32), nv.astype(np.float32),
            nrf.astype(np.float32), nrb.astype(np.float32))
```

```python
import math
from contextlib import ExitStack
import numpy as np
import concourse.bass as bass
import concourse.mybir as mybir
from concourse.bass_utils import run_bass_kernel_spmd

F32 = mybir.dt.float32
BF16 = mybir.dt.bfloat16
AF = mybir.ActivationFunctionType
ALU = mybir.AluOpType
AX = mybir.AxisListType

D = 1024
T = 1024
DFF = 2816
EPS = 1e-6
NSLOT = 4


class Res:
    __slots__ = ("w", "r", "sem", "persist", "excl")

    def __init__(self, persist=False, excl=False):
        self.excl = excl
        self.w = None
        self.r = {}
        self.sem = None
        self.persist = persist


class TT:
    def __init__(self, t, res=None):
        self.t = t
        self.res = res if res is not None else Res()


class Sch:
    def __init__(self, nc, es):
        self.nc = nc
        self.es = es
        self.E = {"pe": nc.tensor, "act": nc.scalar, "dve": nc.vector, "pool": nc.gpsimd, "sp": nc.sync}
        self.sem = {}
        self.cnt = {}
        self.seen = {k: {} for k in self.E}
        for k in ("pe", "act", "dve", "pool"):
            self.sem[k] = es.enter_context(nc.semaphore("s_" + k))
            self.cnt[k] = 0
        self.nsem = 0
        self.dsems = []
        self.free = []
        self.live = []

    def newsem(self):
        if self.free:
            return self.free.pop()
        name = "d%d" % self.nsem
        self.nsem += 1
        self.sem[name] = self.es.enter_context(self.nc.semaphore(name))
        self.cnt[name] = 0
        self.dsems.append(name)
        return name

    def recycle(self):
        keep = []
        for r in self.live:
            if r.persist:
                keep.append(r)
            else:
                self.free.append(r.sem)
                r.sem = None
        self.live = keep

    def _wait(self, eng, raw, other):
        best = {}
        for t in raw:
            if t is None:
                continue
            k, v = t
            if k == eng and eng in ("pe", "sp"):
                continue
            if v > best.get(k, 0):
                best[k] = v
        for t in other:
            if t is None:
                continue
            k, v = t
            if k == eng and eng in ("pe", "sp"):
                continue
            if v > best.get(k, 0):
                best[k] = v
        sn = self.seen[eng]
        for k, v in best.items():
            if sn.get(k, 0) >= v:
                continue
            self.E[eng].wait_ge(self.sem[k], v)
            sn[k] = v

    def _deps(self, reads, writes):
        raw = [r.w for r in reads]
        other = []
        for w in writes:
            other.append(w.w)
            other.extend(w.r.items())
        return raw, other

    def _commit(self, tok, reads, writes):
        k, v = tok
        for r in reads:
            if r.r.get(k, 0) < v:
                r.r[k] = v
        for w in writes:
            w.w = tok
            w.r = {}

    def op(self, eng, fn, reads=(), writes=()):
        reads = [getattr(x, 'res', x) for x in reads] + [x.extra for x in reads if hasattr(x, 'extra')]
        writes = [getattr(x, 'res', x) for x in writes]
        writes = writes + [r for r in reads if r.excl and r not in writes]
        raw, other = self._deps(reads, writes)
        self._wait(eng, raw, other)
        ins = fn()
        ins.then_inc(self.sem[eng], 1)
        self.cnt[eng] += 1
        tok = (eng, self.cnt[eng])
        self._commit(tok, reads, writes)
        return tok

    def dma(self, q, out, in_, reads=(), writes=(), key=None):
        reads = [getattr(x, 'res', x) for x in reads]
        writes = [getattr(x, 'res', x) for x in writes]
        kres = key if key is not None else (writes[0] if writes else reads[0])
        kres = getattr(kres, 'res', kres)
        if kres.sem is None:
            kres.sem = self.newsem()
            self.live.append(kres)
        raw, other = self._deps(reads, writes)
        other = list(other) + [(kres.sem, self.cnt[kres.sem])]
        self._wait(q, raw, other)
        ins = self.E[q].dma_start(out=out, in_=in_)
        ins.then_inc(self.sem[kres.sem], 16)
        self.cnt[kres.sem] += 16
        tok = (kres.sem, self.cnt[kres.sem])
        self._commit(tok, reads, writes)
        return tok

    def barrier(self):
        engs = ("pe", "act", "dve", "pool")
        for e in engs + ("sp",):
            for f in engs:
                if e == f and e == "pe":
                    continue
                v = self.cnt[f]
                if v > 0 and self.seen[e].get(f, 0) < v:
                    self.E[e].wait_ge(self.sem[f], v)
                    self.seen[e][f] = v
            for r in self.live:
                if r.persist:
                    continue
                v = self.cnt[r.sem]
                if v > 0 and self.seen[e].get(r.sem, 0) < v:
                    self.E[e].wait_ge(self.sem[r.sem], v)
                    self.seen[e][r.sem] = v
        self.recycle()

    def drain(self, q="sp"):
        for k in self.dsems:
            v = self.cnt[k]
            if v > 0 and self.seen[q].get(k, 0) < v:
                self.E[q].wait_ge(self.sem[k], v)
                self.seen[q][k] = v
        for f in ("pe", "act", "dve", "pool"):
            v = self.cnt[f]
            if v > 0 and self.seen[q].get(f, 0) < v:
                self.E[q].wait_ge(self.sem[f], v)
                self.seen[q][f] = v


def host_consts():
    ROPE_PAIRS = 16
    t = np.arange(1024)
    inv = (10000.0 ** (-np.arange(ROPE_PAIRS, dtype=np.float32) / ROPE_PAIRS)).astype(np.float32)
    ar = (t // 64).astype(np.float32)[:, None] * inv
    ac = (t % 64).astype(np.float32)[:, None] * inv
    cr, sr, cc, sc_ = np.cos(ar), np.sin(ar), np.cos(ac), np.sin(ac)
    cos64 = np.concatenate([cr, cr, cc, cc], axis=1).astype(np.float32)
    sin64 = np.concatenate([-sr, sr, -sc_, sc_], axis=1).astype(np.float32)
    k = np.arange(128)[:, None]
    q = np.arange(128)[None, :]
    dm = np.zeros((128, 4, 128), np.float32)
    dm[:, 0] = np.maximum(q - k, 0)
    dm[:, 1] = np.maximum(k - q, 0)
    dm[:, 2] = (q >= k)
    dm[:, 3] = (k >= q)
    pr = np.zeros((128, 2, 128), np.float32)
    pr[:, 0, :] = np.arange(128) + 1.0
    pr[:, 1, :] = 128.0 - np.arange(128)
    kp = np.zeros((128, 2), np.float32)
    kp[:, 0] = 127.0 - np.arange(128)
    kp[:, 1] = np.arange(128)
    return dict(cos64=cos64, sin64=sin64, dmat=dm, posrow=pr, kpos=kp)


IN_SPECS = [
    ("xp", (1024, 1024)), ("xs", (1024, 1024)), ("cvec", (2, 1024)),
    ("ck", (2, 256, 512)), ("cv", (2, 256, 512)), ("srf", (2, 4, 64, 64)), ("srb", (2, 4, 64, 64)),
    ("norm1", (2, 1024)), ("w_mod", (2, 1024, 6144)), ("b_mod", (2, 6144)), ("w_in", (2, 1024, 3072)),
    ("sgu_norm", (2, 256)), ("sgu_w", (2, 4, 128, 128)), ("sgu_b", (2, 4, 128)),
    ("rlf", (2, 4)), ("rlb", (2, 4)), ("ret_norm", (2, 256)), ("q_norm", (2, 64)), ("k_norm", (2, 64)),
    ("diff_lam", (2, 256)), ("diff_norm", (2, 128)), ("w_out", (2, 1024, 1024)), ("norm2", (2, 1024)),
    ("ffn_up", (2, 1024, 5632)), ("ffn_conv", (2, 3, 5632)), ("ffn_conv_b", (2, 5632)),
    ("ffn_down", (2, 2816, 1024)),
    ("cos64", (1024, 64)), ("sin64", (1024, 64)), ("dmat", (128, 4, 128)), ("posrow", (128, 2, 128)),
    ("kpos", (128, 2)),
]
OUT_SPECS = [
    ("yp", (1024, 1024)), ("ys", (1024, 1024)), ("nk", (4, 2, 256, 512)), ("nv", (4, 2, 256, 512)),
    ("nrf", (4, 2, 4, 64, 64)), ("nrb", (4, 2, 4, 64, 64)),
]


def build(cfg=None):
    cfg = cfg or {}
    NL = cfg.get("n_layers", 2)
    HALVES = cfg.get("halves", (0, 1))
    taps = cfg.get("taps", None)
    nc = bass.Bass("TRN2", target_bir_lowering=False)
    try:
        nc.allow_low_precision("bf16 matmul operands with fp32 accumulation")
    except Exception:
        pass
    din = {n: nc.dram_tensor(n, list(s), F32, kind="ExternalInput").ap() for n, s in IN_SPECS}
    dout = {n: nc.dram_tensor(n, list(s), F32, kind="ExternalOutput").ap() for n, s in OUT_SPECS}
    es = ExitStack()
    STOP = cfg.get("stop", None)

    class _Stop(Exception):
        pass
    try:
      with es:
        S = Sch(nc, es)

        def ckpt(k):
            if STOP == k:
                S.drain("sp")
                raise _Stop()

        def sb(name, shape, dt=F32):
            return TT(es.enter_context(nc.sbuf_tensor(name, list(shape), dt)), Res(persist=True))

        def tap(name, ap, shape, reads):
            if taps is None or name not in taps:
                return
            d = nc.dram_tensor("tap_" + name, list(shape), ap.dtype, kind="ExternalOutput").ap()
            taps[name] = (list(shape), ap.dtype)
            S.dma("sp", d, ap, reads=reads)

        PSt = [es.enter_context(nc.psum_tensor("ps%d" % i, [128, 1024], F32)) for i in range(4)]
        PB = []
        for i in range(8):
            PB.append(TT(PSt[i // 2], Res(excl=True)))

        def pbank(i):
            return PSt[i // 2][:, (i % 2) * 512:(i % 2) * 512 + 512]

        rot = {"n": 0}

        def nextbank(pool=(0, 1, 2, 3), key="n"):
            i = pool[rot.setdefault(key, 0) % len(pool)]
            rot[key] += 1
            return i

        xT = sb("xT", [128, 8, T])
        hT = sb("hT", [128, 8, T], BF16)
        hT.halves = [Res(persist=True), Res(persist=True)]
        yT = sb("yT", [128, 8, T], BF16)
        yT.halves = [Res(persist=True), Res(persist=True)]
        ring = [sb("ring%d" % i, [128, 4096], BF16) for i in range(NSLOT)]
        for r_ in ring:
            r_.extra = Res(persist=True)
            r_.extra.sem = S.newsem()
            r_.res.sem = S.newsem()
        ARENA_BYTES = 74 * 1024
        arena = sb("arena", [128, ARENA_BYTES // 4], F32)

        identF = sb("identF", [128, 128])
        identB = sb("identB", [128, 128], BF16)
        onesB = sb("onesB", [128, 128], BF16)
        sTb = sb("sTb", [128, 8, 2], BF16)
        MOD = [sb("MOD%d" % l, [128, 48, 2]) for l in range(2)]
        GS = [[sb("GS%d%d" % (l, n), [128, 8, 2]) for n in range(2)] for l in range(2)]
        NT = sb("NT", [128, 32])
        CW = sb("CW", [128, 2, 4, 44])
        WST = sb("WST", [128, 8, 128], BF16)
        BS = sb("BS", [128, 8])
        SGN = sb("SGN", [128, 2, 256])
        RN = sb("RN", [128, 2, 256])
        QN = sb("QN", [128, 2, 64])
        KN = sb("KN", [128, 2, 64])
        DN = sb("DN", [128, 2, 128])
        LG = sb("LG", [128, 16])
        KP = sb("KP", [128, 2])
        C128 = sb("C128", [128, 64])
        DTm = sb("DTm", [128, 2, 4, 128])
        QD = sb("QD", [128, 2, 2, 2, 128])
        KDE = sb("KDE", [128, 2, 2, 256])
        CD = sb("CD", [128, 2, 2, 2, 64])
        NLAM = sb("NLAM", [128, 2])
        COS = sb("COS", [128, 8, 64])
        SIN = sb("SIN", [128, 8, 64])
        small = [sb("small%d" % i, [128, 16]) for i in range(8)]
        srot = {"i": 0}

        def nsmall():
            s = small[srot["i"] % len(small)]
            srot["i"] += 1
            return s

        wq = []

        def slotv(slot, a, b):
            return slot.t[:, 0:a * b].rearrange("p (a b) -> p a b", b=b)

        def wsrc(name, l, c0, c1):
            return din[name][l, :, c0:c1].rearrange("(kc p) n -> p kc n", p=128)

        def plan_weights():
            def modt(l, which):
                for t2 in range(2):
                    c0 = which * 1024 + t2 * 512
                    wq.append([(lambda s: slotv(s, 8, 512), wsrc("w_mod", l, c0, c0 + 512))])
            first = True
            for half in HALVES:
                for l in range(NL):
                    if first:
                        modt(l, 0)
                        modt(l, 1)
                    for g in range(6):
                        wq.append([(lambda s: slotv(s, 8, 512), wsrc("w_in", l, g * 512, (g + 1) * 512))])
                    if first:
                        modt(l, 2)
                    for g in range(2):
                        wq.append([(lambda s: slotv(s, 8, 512), wsrc("w_out", l, g * 512, (g + 1) * 512))])
                    if first:
                        modt(l, 3)
                        modt(l, 4)
                    for t in range(11):
                        wq.append([
                            (lambda s: slotv(s, 8, 512)[:, :, 0:256], wsrc("ffn_up", l, t * 256, (t + 1) * 256)),
                            (lambda s: slotv(s, 8, 512)[:, :, 256:512],
                             wsrc("ffn_up", l, DFF + t * 256, DFF + (t + 1) * 256)),
                        ])
                    if first:
                        modt(l, 5)
                    for m in range(8):
                        wq.append([(lambda s: slotv(s, 22, 128), wsrc("ffn_down", l, m * 128, (m + 1) * 128))])
                first = False

        plan_weights()
        wstate = {"next_load": 0, "next_use": 0}

        def w_issue():
            i = wstate["next_load"]
            if i >= len(wq):
                return
            slot = ring[i % NSLOT]
            for n_, (dstf, src) in enumerate(wq[i]):
                S.dma("pool", dstf(slot), src, writes=[slot.res if n_ == 0 else slot.extra])
            wstate["next_load"] = i + 1

        def w_get():
            i = wstate["next_use"]
            wstate["next_use"] = i + 1
            assert i < wstate["next_load"]
            return ring[i % NSLOT]

        def w_done():
            w_issue()

        class Arena:
            def __init__(self):
                self.off = 0

            def reset(self):
                self.off = 0

            def get(self, shape, dt):
                n = 1
                for s_ in shape[1:]:
                    n *= s_
                nbytes = n * (4 if dt == F32 else 2)
                nbytes = (nbytes + 63) // 64 * 64
                w0 = self.off // 4
                w1 = (self.off + nbytes) // 4
                assert self.off + nbytes <= ARENA_BYTES, ("arena overflow", self.off, nbytes)
                self.off += nbytes
                ap = arena.t[0:shape[0], w0:w1]
                if dt == BF16:
                    ap = ap.bitcast(BF16)
                nfree = n
                ap = ap[:, 0:nfree]
                if len(shape) > 2:
                    names = " ".join("d%d" % i for i in range(len(shape) - 1))
                    kw = {"d%d" % i: shape[i + 1] for i in range(len(shape) - 1)}
                    ap = ap.rearrange("p (%s) -> p %s" % (names, names), **kw)
                return ap

        AR = Arena()

        class AV:
            def __init__(self, shape, dt=F32):
                self.ap = AR.get(shape, dt)
                self.res = Res()

            @property
            def t(self):
                return self.ap

        S.op("pool", lambda: nc.gpsimd.memset(identF.t[:], 0.0), writes=[identF])
        S.op("pool", lambda: nc.gpsimd.affine_select(out=identF.t[:], in_=identF.t[:], pattern=[[-1, 128]],
                                                      compare_op=ALU.not_equal, fill=1.0, base=0,
                                                      channel_multiplier=1), reads=[identF], writes=[identF])
        S.op("pool", lambda: nc.gpsimd.memset(onesB.t[:], 1.0), writes=[onesB])
        S.op("pool", lambda: nc.gpsimd.memset(C128.t[:], 128.0), writes=[C128])
        S.op("dve", lambda: nc.vector.tensor_copy(identB.t[:], identF.t[:]), reads=[identF], writes=[identB])

        ckpt(1)
        for _ in range(NSLOT):
            w_issue()

        cres = Res()
        crow = AV([2, 1024])
        bmrow = AV([96, 128])
        nrow = AV([32, 128])
        cvrow = AV([44, 2, 4, 128])
        wsraw = AV([128, 8, 128])
        bsrow = AV([8, 128])
        BMT = sb("BMT", [128, 96])
        DL = AV([128, 2, 256])
        DM = AV([128, 4, 128])
        PR = AV([128, 2, 128])

        def cload(dst_tt, dst_ap, src_ap):
            S.dma("sp", dst_ap, src_ap, writes=[dst_tt])

        cload(crow, crow.t[:], din["cvec"])
        cload(bmrow, bmrow.t[:], din["b_mod"].rearrange("l (j p) -> (l j) p", p=128))
        cload(nrow, nrow.t[0:16, :], din["norm1"].rearrange("l (kc p) -> (l kc) p", p=128))
        cload(nrow, nrow.t[16:32, :], din["norm2"].rearrange("l (kc p) -> (l kc) p", p=128))
        for l in range(2):
            cload(cvrow, cvrow.t[:, l, 0:3, :], din["ffn_conv"][l].rearrange("j (c p) -> c j p", p=128))
            cload(cvrow, cvrow.t[:, l, 3, :], din["ffn_conv_b"][l].rearrange("(c p) -> c p", p=128))
        cload(wsraw, wsraw.t[:], din["sgu_w"].rearrange("l g p q -> p (l g) q"))
        cload(bsrow, bsrow.t[:], din["sgu_b"].rearrange("l g p -> (l g) p"))
        cload(SGN, SGN.t[:], din["sgu_norm"].partition_broadcast(128))
        cload(RN, RN.t[:], din["ret_norm"].partition_broadcast(128))
        cload(QN, QN.t[:], din["q_norm"].partition_broadcast(128))
        cload(KN, KN.t[:], din["k_norm"].partition_broadcast(128))
        cload(DN, DN.t[:], din["diff_norm"].partition_broadcast(128))
        cload(DL, DL.t[:], din["diff_lam"].partition_broadcast(128))
        cload(LG, LG.t[:, 0:8], din["rlf"].rearrange("l h -> (l h)").partition_broadcast(128))
        cload(LG, LG.t[:, 8:16], din["rlb"].rearrange("l h -> (l h)").partition_broadcast(128))
        cload(DM, DM.t[:], din["dmat"])
        cload(PR, PR.t[:], din["posrow"])
        cload(KP, KP.t[:], din["kpos"])
        cload(COS, COS.t[:], din["cos64"].rearrange("(t p) c -> p t c", p=128))
        cload(SIN, SIN.t[:], din["sin64"].rearrange("(t p) c -> p t c", p=128))

        ckpt(2)
        csil = AV([2, 1024])
        S.op("act", lambda: nc.scalar.activation(out=csil.t[:], in_=crow.t[:], func=AF.Silu),
             reads=[crow], writes=[csil])
        b = nextbank()

        def f():
            ins = None
            for kc in range(8):
                ins = nc.tensor.transpose(pbank(b)[:, kc * 2:kc * 2 + 2], csil.t[0:2, kc * 128:(kc + 1) * 128],
                                          identF.t[0:2, 0:2])
            return ins
        S.op("pe", f, reads=[csil, identF], writes=[PB[b]])
        S.op("dve", lambda: nc.vector.tensor_copy(sTb.t[:].rearrange("p a b -> p (a b)"), pbank(b)[:, 0:16]),
             reads=[PB[b]], writes=[sTb])

        ckpt(3)
        b = nextbank()
        S.op("pe", lambda: nc.tensor.transpose(pbank(b)[:, 0:32], nrow.t[0:32, :], identF.t[0:32, 0:32]),
             reads=[nrow, identF], writes=[PB[b]])
        S.op("dve", lambda: nc.vector.tensor_copy(NT.t[:], pbank(b)[:, 0:32]), reads=[PB[b]], writes=[NT])
        b = nextbank()
        S.op("pe", lambda: nc.tensor.transpose(pbank(b)[:, 0:96], bmrow.t[0:96, :], identF.t[0:96, 0:96]),
             reads=[bmrow, identF], writes=[PB[b]])
        S.op("dve", lambda: nc.vector.tensor_copy(BMT.t[:], pbank(b)[:, 0:96]), reads=[PB[b]], writes=[BMT])
        ckpt(4)
        b = nextbank()

        def f():
            ins = None
            for l in range(2):
                for j in range(4):
                    c0 = (l * 4 + j) * 44
                    ins = nc.tensor.transpose(pbank(b)[:, c0:c0 + 44], cvrow.t[0:44, l, j, :], identF.t[0:44, 0:44])
            return ins
        S.op("pe", f, reads=[cvrow, identF], writes=[PB[b]])
        S.op("dve", lambda: nc.vector.tensor_copy(CW.t[:].rearrange("p a b c -> p (a b c)"), pbank(b)[:, 0:352]),
             reads=[PB[b]], writes=[CW])
        ckpt(5)
        for hb in range(2):
            b = nextbank()

            def f():
                ins = None
                for i in range(4):
                    ins = nc.tensor.transpose(pbank(b)[:, i * 128:(i + 1) * 128], wsraw.t[:, hb * 4 + i, :], identF.t[:])
                return ins
            S.op("pe", f, reads=[wsraw, identF], writes=[PB[b]])
            S.op("dve", lambda: nc.vector.tensor_copy(
                WST.t[:, hb * 4:(hb + 1) * 4, :].rearrange("p a b -> p (a b)"), pbank(b)[:, 0:512]),
                reads=[PB[b]], writes=[WST])
        b = nextbank()
        S.op("pe", lambda: nc.tensor.transpose(pbank(b)[:, 0:8], bsrow.t[0:8, :], identF.t[0:8, 0:8]),
             reads=[bsrow, identF], writes=[PB[b]])
        S.op("dve", lambda: nc.vector.tensor_copy(BS.t[:], pbank(b)[:, 0:8]), reads=[PB[b]], writes=[BS])

        ckpt(6)
        modrow = sb("modrow", [2, 1024])

        def mods_group(l, which):
            for t2 in range(2):
                slot = w_get()
                b = nextbank()
                wv = slotv(slot, 8, 512)

                def f():
                    ins = None
                    for kc in range(8):
                        ins = nc.tensor.matmul(pbank(b)[0:2, :], sTb.t[:, kc, :], wv[:, kc, :],
                                               start=(kc == 0), stop=(kc == 7))
                    return ins
                S.op("pe", f, reads=[sTb, slot], writes=[PB[b]])
                w_done()
                S.op("dve", lambda: nc.vector.tensor_copy(modrow.t[:, t2 * 512:(t2 + 1) * 512], pbank(b)[0:2, :]),
                     reads=[PB[b]], writes=[modrow])
            b = nextbank()

            def f():
                ins = None
                for jb in range(8):
                    ins = nc.tensor.transpose(pbank(b)[:, jb * 2:jb * 2 + 2], modrow.t[0:2, jb * 128:(jb + 1) * 128],
                                              identF.t[0:2, 0:2])
                return ins
            S.op("pe", f, reads=[modrow, identF], writes=[PB[b]])
            c0 = l * 48 + which * 8
            S.op("dve", lambda: nc.vector.tensor_tensor(
                out=MOD[l].t[:, which * 8:(which + 1) * 8, :], in0=pbank(b)[:, 0:16].rearrange("p (a b) -> p a b", b=2),
                in1=BMT.t[:, c0:c0 + 8].unsqueeze(2).broadcast_to([128, 8, 2]), op=ALU.add),
                 reads=[PB[b], BMT], writes=[MOD[l]])
            if which in (1, 4):
                n = 0 if which == 1 else 1
                ntv = NT.t[:, (n * 2 + l) * 8:(n * 2 + l) * 8 + 8]
                S.op("dve", lambda: nc.vector.scalar_tensor_tensor(
                    out=GS[l][n].t[:], in0=MOD[l].t[:, which * 8:(which + 1) * 8, :], scalar=1.0,
                    in1=ntv.unsqueeze(2).broadcast_to([128, 8, 2]), op0=ALU.add, op1=ALU.mult),
                    reads=[MOD[l], NT], writes=[GS[l][n]])

        ckpt(7)
        S.op("act", lambda: nc.scalar.activation(out=LG.t[:], in_=LG.t[:], func=AF.Exp, scale=-1.0),
             reads=[LG], writes=[LG])
        S.op("act", lambda: nc.scalar.activation(out=LG.t[:], in_=LG.t[:], func=AF.Ln, bias=1.0, scale=1.0),
             reads=[LG], writes=[LG])
        S.op("dve", lambda: nc.vector.tensor_scalar(LG.t[:], LG.t[:], -1.0, None, ALU.mult), reads=[LG], writes=[LG])

        ckpt(8)

        def lgi(d, l, h):
            return d * 8 + l * 4 + h

        dtmp = [AV([128, 128]) for i in range(4)]
        for l in range(NL):
            for h in range(4):
                tf, tb = dtmp[(h % 2) * 2], dtmp[(h % 2) * 2 + 1]
                i_f, i_b = lgi(0, l, h), lgi(1, l, h)
                S.op("act", lambda: nc.scalar.activation(out=tf.t[:], in_=DM.t[:, 0, :], func=AF.Exp,
                                                         scale=LG.t[:, i_f:i_f + 1]), reads=[DM, LG], writes=[tf])
                S.op("act", lambda: nc.scalar.activation(out=tb.t[:], in_=DM.t[:, 1, :], func=AF.Exp,
                                                         scale=LG.t[:, i_b:i_b + 1]), reads=[DM, LG], writes=[tb])
                S.op("dve", lambda: nc.vector.scalar_tensor_tensor(out=tf.t[:], in0=tf.t[:], scalar=0.125,
                                                                   in1=DM.t[:, 2, :], op0=ALU.mult, op1=ALU.mult),
                     reads=[tf, DM], writes=[tf])
                S.op("dve", lambda: nc.vector.scalar_tensor_tensor(out=tb.t[:], in0=tb.t[:], scalar=0.125,
                                                                   in1=DM.t[:, 3, :], op0=ALU.mult, op1=ALU.mult),
                     reads=[tb, DM], writes=[tb])
                S.op("dve", lambda: nc.vector.tensor_tensor(out=DTm.t[:, l, h, :], in0=tf.t[:], in1=tb.t[:],
                                                            op=ALU.add), reads=[tf, tb], writes=[DTm])
            for hp in range(2):
                for d in range(2):
                    for j in range(2):
                        ii = lgi(d, l, 2 * hp + j)
                        ps_ = slice(j * 64, (j + 1) * 64)
                        S.op("act", lambda: nc.scalar.activation(out=QD.t[ps_, l, hp, d, :], in_=PR.t[ps_, d, :],
                                                                 func=AF.Exp, scale=LG.t[ps_, ii:ii + 1]),
                             reads=[PR, LG], writes=[QD])
                        S.op("act", lambda: nc.scalar.activation(out=CD.t[ps_, l, d, hp, :], in_=C128.t[ps_, :],
                                                                 func=AF.Exp, scale=LG.t[ps_, ii:ii + 1]),
                             reads=[C128, LG], writes=[CD])
            for d in range(2):
                for h in range(4):
                    ii = lgi(d, l, h)
                    S.op("act", lambda: nc.scalar.activation(
                        out=KDE.t[:, l, d, h * 64:(h + 1) * 64], in_=KP.t[:, d:d + 1].broadcast_to([128, 64]),
                        func=AF.Exp, scale=LG.t[:, ii:ii + 1], bias=math.log(0.125)),
                        reads=[KP, LG], writes=[KDE])
            lam_init = 0.8 - 0.6 * math.exp(-0.3 * l)
            pr_ = nsmall()
            dlv = DL.t[:, l, :].rearrange("p (a b c) -> p a b c", a=2, b=2)
            lt = dtmp[0]
            S.op("dve", lambda: nc.vector.tensor_tensor(out=lt.t[:].rearrange("p (a c) -> p a c", a=2),
                                                        in0=dlv[:, :, 0, :], in1=dlv[:, :, 1, :], op=ALU.mult),
                 reads=[DL], writes=[lt])
            S.op("dve", lambda: nc.vector.tensor_reduce(out=pr_.t[:, 0:2],
                                                        in_=lt.t[:].rearrange("p (a c) -> p a c", a=2),
                                                        axis=AX.X, op=ALU.add), reads=[lt], writes=[pr_])
            S.op("act", lambda: nc.scalar.activation(out=pr_.t[:, 2:4], in_=pr_.t[:, 0:2], func=AF.Exp),
                 reads=[pr_], writes=[pr_])
            S.op("dve", lambda: nc.vector.tensor_tensor(out=pr_.t[:, 4:5], in0=pr_.t[:, 3:4], in1=pr_.t[:, 2:3],
                                                        op=ALU.subtract), reads=[pr_], writes=[pr_])
            S.op("dve", lambda: nc.vector.tensor_scalar(NLAM.t[:, l:l + 1], pr_.t[:, 4:5], -lam_init, None, ALU.add),
                 reads=[pr_], writes=[NLAM])
            S.op("dve", lambda: nc.vector.tensor_scalar(DN.t[:, l, :], DN.t[:, l, :], 1.0 - lam_init, None, ALU.mult),
                 reads=[DN], writes=[DN])

        S.barrier()
        ckpt(10)

        def rstd_from_ss(ss_ap, out_ap, scale, bias, R, W):
            S.op("act", lambda: nc.scalar.activation(out=out_ap, in_=ss_ap, func=AF.Sqrt, bias=bias, scale=scale),
                 reads=R, writes=W)
            S.op("dve", lambda: nc.vector.reciprocal(out_ap, out_ap), reads=W, writes=W)

        def load_x(src):
            xin = [AV([128, 1024]) for _ in range(2)]
            for tt in range(8):
                xi = xin[tt % 2]
                S.dma("sp", xi.t[:], src[tt * 128:(tt + 1) * 128, :], writes=[xi])
                for hb in range(2):
                    b = nextbank()

                    def f():
                        ins = None
                        for i in range(4):
                            c = hb * 4 + i
                            ins = nc.tensor.transpose(pbank(b)[:, i * 128:(i + 1) * 128],
                                                      xi.t[:, c * 128:(c + 1) * 128], identF.t[:])
                        return ins
                    S.op("pe", f, reads=[xi, identF], writes=[PB[b]])
                    eng = "act" if hb == 0 else "dve"
                    dst = xT.t[:, hb * 4:(hb + 1) * 4, tt * 128:(tt + 1) * 128]
                    srcp = pbank(b).rearrange("p (a b) -> p a b", b=128)
                    if eng == "act":
                        S.op("act", lambda: nc.scalar.copy(out=dst, in_=srcp), reads=[PB[b]], writes=[xT])
                    else:
                        S.op("dve", lambda: nc.vector.tensor_copy(dst, srcp), reads=[PB[b]], writes=[xT])

        def store_x(dst):
            xo = [AV([128, 1024]) for _ in range(2)]
            for tt in range(8):
                xi = xo[tt % 2]
                for hb in range(2):
                    b = nextbank()

                    def f():
                        ins = None
                        for i in range(4):
                            c = hb * 4 + i
                            ins = nc.tensor.transpose(pbank(b)[:, i * 128:(i + 1) * 128],
                                                      xT.t[:, c, tt * 128:(tt + 1) * 128], identF.t[:])
                        return ins
                    S.op("pe", f, reads=[xT, identF], writes=[PB[b]])
                    dstp = xi.t[:, hb * 512:(hb + 1) * 512]
                    if hb == 0:
                        S.op("act", lambda: nc.scalar.copy(out=dstp, in_=pbank(b)), reads=[PB[b]], writes=[xi])
                    else:
                        S.op("dve", lambda: nc.vector.tensor_copy(dstp, pbank(b)), reads=[PB[b]], writes=[xi])
                S.dma("sp", dst[tt * 128:(tt + 1) * 128, :], xi.t[:], reads=[xi])

        def norm_mod(l, n, cond, presq=None):
            RBh = [AV([128, 512]) for _ in range(2)]
            if presq is None:
                sq = yT
                S.op("act", lambda: nc.scalar.activation(out=sq.t[:], in_=xT.t[:], func=AF.Square),
                     reads=[xT], writes=[sq] + yT.halves)
            else:
                sq = presq
            for tg in range(2):
                b = nextbank()
                sqres = hT.halves[tg] if sq is hT else sq

                def f():
                    ins = None
                    for kc in range(8):
                        ins = nc.tensor.matmul(pbank(b), onesB.t[:], sq.t[:, kc, tg * 512:(tg + 1) * 512],
                                               start=(kc == 0), stop=(kc == 7))
                    return ins
                S.op("pe", f, reads=[sqres, onesB], writes=[PB[b]])
                rstd_from_ss(pbank(b), RBh[tg].t[:], 1.0 / D, EPS, [PB[b]], [RBh[tg]])
            shi = 0 if n == 0 else 3
            tmp = [AV([128, 512]) for _ in range(4)]
            ti = 0
            for tg in range(2):
                for kc in range(8):
                    tm = tmp[ti % 4]
                    ti += 1
                    S.op("dve", lambda: nc.vector.scalar_tensor_tensor(
                        out=tm.t[:], in0=xT.t[:, kc, tg * 512:(tg + 1) * 512], scalar=GS[l][n].t[:, kc, cond:cond + 1],
                        in1=RBh[tg].t[:], op0=ALU.mult, op1=ALU.mult), reads=[xT, GS[l][n], RBh[tg]], writes=[tm])
                    S.op("act", lambda: nc.scalar.activation(out=hT.t[:, kc, tg * 512:(tg + 1) * 512], in_=tm.t[:],
                                                             func=AF.Identity,
                                                             bias=MOD[l].t[:, shi * 8 + kc, cond:cond + 1], scale=1.0),
                         reads=[tm, MOD[l]], writes=[hT.halves[tg]])

        def zmm(slot, tt, b):
            wv = slotv(slot, 8, 512)

            def f():
                ins = None
                for kc in range(8):
                    ins = nc.tensor.matmul(pbank(b), hT.t[:, kc, tt * 128:(tt + 1) * 128], wv[:, kc, :],
                                           start=(kc == 0), stop=(kc == 7))
                return ins
            S.op("pe", f, reads=[hT.halves[tt // 4], slot], writes=[PB[b]])

        def group_rstd(src_ap, ngrp, gsz, scale, bias, sqt, ss):
            src_tt, sq_tt = sqt
            S.op("dve", lambda: nc.vector.tensor_tensor(out=sq_tt.t[:, 0:ngrp * gsz], in0=src_ap, in1=src_ap,
                                                        op=ALU.mult), reads=[src_tt], writes=[sq_tt])
            S.op("dve", lambda: nc.vector.tensor_reduce(
                out=ss.t[:, 0:ngrp], in_=sq_tt.t[:, 0:ngrp * gsz].rearrange("p (a b) -> p a b", b=gsz),
                axis=AX.X, op=ALU.add), reads=[sq_tt], writes=[ss])
            rstd_from_ss(ss.t[:, 0:ngrp], ss.t[:, 0:ngrp], scale, bias, [ss], [ss])

        def transposes_to(src_tt, src_ap_fn, nblk, dst_tt, dst_ap, pool=(0, 1, 2, 3)):
            b = nextbank(pool)
            pv = pbank(b).bitcast(BF16)

            def f():
                ins = None
                for i in range(nblk):
                    ins = nc.tensor.transpose(pv[:, i * 128:(i + 1) * 128], src_ap_fn(i), identB.t[:])
                return ins
            S.op("pe", f, reads=[src_tt, identB], writes=[PB[b]])
            S.op("dve", lambda: nc.vector.tensor_copy(dst_ap, pv[:, 0:nblk * 128].rearrange("p (a b) -> p a b", b=128)),
                 reads=[PB[b]], writes=[dst_tt])

        def mixer(l, half):
            cond = half
            nseq, L = (4, 256) if half == 0 else (1, 1024)
            cpl = L // 128
            AR.reset()
            ytok = AV([128, 8, 1024], BF16)
            ar_mark = AR.off

            class _V:
                pass
            SCR = _V()
            SCR.ap = yT.t[:].bitcast(F32)
            SCR.t = SCR.ap
            SCR.res = yT.res
            slot = w_get()
            GE = AV([128, 8, 512])
            VN = AV([128, 8, 256], BF16)
            ssG = AV([128, 8])
            for tt in range(8):
                b = nextbank()
                zmm(slot, tt, b)
                S.op("act", lambda: nc.scalar.activation(out=GE.t[:, tt, :], in_=pbank(b), func=AF.Gelu_apprx_tanh),
                     reads=[PB[b]], writes=[GE])
            w_done()
            gv = GE.t[:, :, 256:512]
            sv = SCR.t[:, :, 0:256]

            def g0_stageB():
                S.op("dve", lambda: nc.vector.tensor_tensor(out=sv, in0=gv, in1=gv, op=ALU.mult), reads=[GE], writes=[SCR])
                S.op("dve", lambda: nc.vector.tensor_reduce(out=ssG.t[:], in_=sv, axis=AX.X, op=ALU.add),
                     reads=[SCR], writes=[ssG])
                rstd_from_ss(ssG.t[:], ssG.t[:], 1.0 / 256, EPS, [ssG], [ssG])
                S.op("dve", lambda: nc.vector.tensor_tensor(out=gv, in0=gv,
                                                            in1=ssG.t[:].unsqueeze(2).broadcast_to([128, 8, 256]),
                                                            op=ALU.mult), reads=[GE, ssG], writes=[GE])
                S.op("dve", lambda: nc.vector.tensor_tensor(
                    out=VN.t[:], in0=gv, in1=SGN.t[:, l, :].unsqueeze(1).broadcast_to([128, 8, 256]), op=ALU.mult),
                    reads=[GE, SGN], writes=[VN])

            def g0_stageC(tt):
                b2 = nextbank()

                def f():
                    ins = None
                    for g in range(4):
                        ins = nc.tensor.matmul(pbank(b2)[:, g * 64:(g + 1) * 64], WST.t[:, l * 4 + g, :],
                                               VN.t[:, tt, g * 64:(g + 1) * 64], start=True, stop=True)
                    return ins
                S.op("pe", f, reads=[WST, VN], writes=[PB[b2]])
                for g in range(4):
                    S.op("dve", lambda: nc.vector.scalar_tensor_tensor(
                        out=ytok.t[:, tt, g * 64:(g + 1) * 64], in0=pbank(b2)[:, g * 64:(g + 1) * 64],
                        scalar=BS.t[:, l * 4 + g:l * 4 + g + 1], in1=GE.t[:, tt, g * 64:(g + 1) * 64],
                        op0=ALU.add, op1=ALU.mult), reads=[PB[b2], BS, GE], writes=[ytok])

            ckpt(13)

            QT = AV([128, 2, 1024], BF16)
            KT = AV([128, 2, 1024], BF16)
            VB = AV([128, 8, 256], BF16)
            SG = AV([128, 8, 256], BF16)
            KVS = AV([128, 8, 2, 2, 64])
            RS = AV([128, 2, 2, 64])
            RSb = AV([128, 8, 2, 2, 64], BF16)

            class _W:
                pass
            RSd = []
            RSbd = []
            for d_ in range(2):
                v_ = _W()
                v_.ap = RS.t[:, d_, :, :]
                v_.t = v_.ap
                v_.res = Res()
                RSd.append(v_)
                w_ = _W()
                w_.res = Res()
                RSbd.append(w_)
            qkb = [AV([128, 512], BF16) for _ in range(2)]
            KF = [AV([128, 2, 256], BF16) for _ in range(2)]
            gtmp = [AV([128, 256]) for _ in range(2)]
            st_stage = [AV([128, 2, 2, 64]) for _ in range(2)]
            slot1 = w_get()
            slot2 = w_get()
            g0_stageB()
            for tt in range(8):
                if tt >= 1:
                    g0_stageC(tt - 1)
                b1 = nextbank()
                zmm(slot1, tt, b1)
                if tt == 0:
                    ckpt(1301)
                b2 = nextbank()
                zmm(slot2, tt, b2)
                if tt == 0:
                    ckpt(1302)
                qk = qkb[tt % 2]
                S.op("act", lambda: nc.scalar.copy(out=qk.t[:], in_=pbank(b1)), reads=[PB[b1]], writes=[qk])
                if tt == 0:
                    ckpt(1303)
                kf = KF[tt % 2]
                for d in range(2):
                    S.op("dve", lambda: nc.vector.tensor_tensor(
                        out=kf.t[:, d, :], in0=pbank(b1)[:, 256:512], in1=KDE.t[:, l, d, :], op=ALU.mult),
                        reads=[PB[b1], KDE], writes=[kf])
                    if tt == 0 and d == 0:
                        ckpt(1304)
                if tt == 0:
                    ckpt(131)
                bt = nextbank((6, 7), "t67")
                pv = pbank(bt).bitcast(BF16)

                def f():
                    ins = None
                    for i in range(4):
                        ins = nc.tensor.transpose(pv[:, i * 128:(i + 1) * 128], qk.t[:, i * 128:(i + 1) * 128],
                                                  identB.t[:])
                    return ins
                S.op("pe", f, reads=[qk, identB], writes=[PB[bt]])
                if tt == 0:
                    ckpt(132)
                S.op("dve", lambda: nc.vector.tensor_copy(
                    QT.t[:, :, tt * 128:(tt + 1) * 128], pv[:, 0:256].rearrange("p (a b) -> p a b", b=128)),
                    reads=[PB[bt]], writes=[QT])
                S.op("dve", lambda: nc.vector.tensor_copy(
                    KT.t[:, :, tt * 128:(tt + 1) * 128], pv[:, 256:512].rearrange("p (a b) -> p a b", b=128)),
                    reads=[PB[bt]], writes=[KT])
                if tt == 0:
                    ckpt(133)
                S.op("act", lambda: nc.scalar.copy(out=VB.t[:, tt, :], in_=pbank(b2)[:, 0:256]),
                     reads=[PB[b2]], writes=[VB])
                gt_ = gtmp[tt % 2]
                S.op("act", lambda: nc.scalar.activation(out=gt_.t[:], in_=pbank(b2)[:, 256:512], func=AF.Silu),
                     reads=[PB[b2]], writes=[gt_])
                S.op("dve", lambda: nc.vector.tensor_tensor(out=SG.t[:, tt, :], in0=gt_.t[:], in1=RN.t[:, l, :],
                                                            op=ALU.mult), reads=[gt_, RN], writes=[SG])
                if tt == 0:
                    ckpt(134)
                bk = nextbank((4, 5), "t45")

                def f():
                    ins = None
                    for d in range(2):
                        for hp in range(2):
                            ins = nc.tensor.matmul(pbank(bk)[:, (d * 2 + hp) * 128:(d * 2 + hp + 1) * 128],
                                                   kf.t[:, d, hp * 128:(hp + 1) * 128],
                                                   VB.t[:, tt, hp * 128:(hp + 1) * 128], start=True, stop=True)
                    return ins
                S.op("pe", f, reads=[kf, VB], writes=[PB[bk]])
                if tt == 0:
                    ckpt(135)
                pk = pbank(bk).rearrange("p (a b) -> p a b", b=128)
                for j in range(2):
                    ps_ = slice(j * 64, (j + 1) * 64)
                    S.op("dve", lambda: nc.vector.tensor_copy(
                        KVS.t[ps_, tt, :, :, :].rearrange("p a b c -> p (a b) c"), pk[ps_, :, j * 64:(j + 1) * 64]),
                        reads=[PB[bk]], writes=[KVS])
            g0_stageC(7)
            w_done()
            w_done()

            ckpt(14)
            for s_ in range(nseq):
                if half == 0:
                    S.op("dve", lambda: nc.vector.memset(RS.t[:], 0.0), writes=[RSd[0], RSd[1]])
                else:
                    for d, nm in ((0, "srf"), (1, "srb")):
                        for j in range(2):
                            srcs = din[nm][l].rearrange("(hp j) d e -> j d hp e", j=2)[j]
                            S.dma("sp", RS.t[j * 64:(j + 1) * 64, d, :, :], srcs, writes=[RSd[d]])
                def step(d, tt):
                    S.op("dve", lambda: nc.vector.tensor_copy(RSb.t[:, tt, d, :, :], RSd[d].t[:]),
                         reads=[RSd[d]], writes=[RSbd[d]])
                    S.op("dve", lambda: nc.vector.tensor_tensor(out=RSd[d].t[:], in0=RSd[d].t[:],
                                                                in1=CD.t[:, l, d, :, :], op=ALU.mult),
                         reads=[RSd[d], CD], writes=[RSd[d]])
                    S.op("dve", lambda: nc.vector.tensor_tensor(out=RSd[d].t[:], in0=RSd[d].t[:],
                                                                in1=KVS.t[:, tt, d, :, :], op=ALU.add),
                         reads=[RSd[d], KVS], writes=[RSd[d]])
                for c in range(cpl):
                    step(0, s_ * cpl + c)
                    step(1, s_ * cpl + (cpl - 1 - c))
                if half == 0:
                    stg = st_stage[s_ % 2]
                    S.op("dve", lambda: nc.vector.tensor_copy(stg.t[:], RS.t[:]), reads=[RSd[0], RSd[1]], writes=[stg])
                    for d, nm in ((0, "nrf"), (1, "nrb")):
                        for j in range(2):
                            dsts = dout[nm][s_, l].rearrange("(hp j) d e -> j d hp e", j=2)[j]
                            S.dma("sp", dsts, stg.t[j * 64:(j + 1) * 64, d, :, :], reads=[stg])

            ckpt(15)
            S.barrier()
            ar_keep = AR.off
            AR.off = ar_mark
            MT = [AV([128, 512], BF16) for _ in range(2)]
            QF = [AV([128, 2, 2, 128], BF16) for _ in range(2)]
            OR = AV([128, 8, 256])
            for tt in range(8):
                c = tt % cpl
                tsl = slice(tt * 128, (tt + 1) * 128)
                bia = nextbank((0, 1), "t01")
                bib = nextbank((0, 1), "t01")

                def f():
                    ins = None
                    for j, bnk in ((0, bia), (1, bib)):
                        ps_ = slice(j * 64, (j + 1) * 64)
                        for hp in range(2):
                            ins = nc.tensor.matmul(pbank(bnk)[:, hp * 128:(hp + 1) * 128], KT.t[ps_, hp, tsl],
                                                   QT.t[ps_, hp, tsl], start=True, stop=True)
                    return ins
                S.op("pe", f, reads=[KT, QT], writes=[PB[bia], PB[bib]])
                mt = MT[tt % 2]
                mt4 = mt.t[:].rearrange("p (hp j q) -> p hp j q", hp=2, j=2)
                for j, bnk in ((0, bia), (1, bib)):
                    S.op("dve", lambda: nc.vector.tensor_tensor(
                        out=mt4[:, :, j, :], in0=pbank(bnk)[:, 0:256].rearrange("p (a b) -> p a b", b=128),
                        in1=DTm.t[:, l, :, :].rearrange("p (hp j) q -> p hp j q", j=2)[:, :, j, :], op=ALU.mult),
                        reads=[PB[bnk], DTm], writes=[mt])
                qf = QF[tt % 2]
                S.op("dve", lambda: nc.vector.tensor_tensor(
                    out=qf.t[:], in0=QD.t[:, l, :, :, :],
                    in1=QT.t[:, :, tsl].unsqueeze(2).broadcast_to([128, 2, 2, 128]), op=ALU.mult),
                    reads=[QT, QD], writes=[qf])
                use_f = not (half == 0 and c == 0)
                use_b = not (half == 0 and c == cpl - 1)
                bo = nextbank((2, 3), "t23")

                def f():
                    ins = None
                    for h in range(4):
                        hp, j = h // 2, h % 2
                        ps_ = slice(j * 64, (j + 1) * 64)
                        o_ap = pbank(bo)[:, h * 64:(h + 1) * 64]
                        last = not (use_f or use_b)
                        ins = nc.tensor.matmul(o_ap, mt.t[:, h * 128:(h + 1) * 128], VB.t[:, tt, h * 64:(h + 1) * 64],
                                               start=True, stop=last)
                        if use_f:
                            ins = nc.tensor.matmul(o_ap, qf.t[ps_, hp, 0, :], RSb.t[ps_, tt, 0, hp, :],
                                                   start=False, stop=not use_b)
                        if use_b:
                            ins = nc.tensor.matmul(o_ap, qf.t[ps_, hp, 1, :], RSb.t[ps_, tt, 1, hp, :],
                                                   start=False, stop=True)
                    return ins
                S.op("pe", f, reads=[mt, VB, qf, RSbd[0], RSbd[1]], writes=[PB[bo]])
                S.op("act", lambda: nc.scalar.copy(out=OR.t[:, tt, :], in_=pbank(bo)[:, 0:256]),
                     reads=[PB[bo]], writes=[OR])
            sv = SCR.t[:, :, 0:256]
            S.op("act", lambda: nc.scalar.activation(out=sv, in_=OR.t[:], func=AF.Square), reads=[OR], writes=[SCR])
            ssR = AV([128, 8, 4])
            S.op("dve", lambda: nc.vector.tensor_reduce(out=ssR.t[:], in_=sv.rearrange("p t (h c) -> p t h c", c=64),
                                                        axis=AX.X, op=ALU.add), reads=[SCR], writes=[ssR])
            rstd_from_ss(ssR.t[:], ssR.t[:], 1.0 / 64, EPS, [ssR], [ssR])
            o4 = OR.t[:].rearrange("p t (h c) -> p t h c", c=64)
            S.op("dve", lambda: nc.vector.tensor_tensor(out=o4, in0=o4,
                                                        in1=ssR.t[:].unsqueeze(3).broadcast_to([128, 8, 4, 64]),
                                                        op=ALU.mult), reads=[OR, ssR], writes=[OR])
            S.op("dve", lambda: nc.vector.tensor_tensor(out=ytok.t[:, :, 256:512], in0=OR.t[:], in1=SG.t[:],
                                                        op=ALU.mult), reads=[OR, SG], writes=[ytok])
            S.barrier()
            ckpt(16)
            AR.off = ar_mark

            NKC = 2 if half == 0 else 10
            KOFF = 0 if half == 0 else 256
            NKEY = 1024 + KOFF
            QTa = AV([128, 4, 1024], BF16)
            KTa = AV([128, 4, NKEY], BF16)
            VA = AV([128, NKEY // 128, 4, 132], BF16)
            S.op("dve", lambda: nc.vector.memset(VA.t[:, :, :, 128:129], 1.0), writes=[VA])
            ar_att = AR.off
            stg = [AV([128, 512]) for _ in range(2)]

            class _G:
                pass
            GR = []
            for g_ in range(2):
                o_ = _G()
                o_.QA = AV([128, 4, 512])
                o_.QB = AV([128, 4, 512], BF16)
                o_.ss = AV([128, 32])
                o_.SQ = _G()
                o_.SQ.ap = SCR.t[:, g_ * 4:(g_ + 1) * 4, :]
                o_.SQ.t = o_.SQ.ap
                o_.SQ.res = Res()
                GR.append(o_)
            S.op("dve", lambda: nc.vector.memset(GR[0].ss.t[:], 0.0), reads=[yT], writes=[GR[0].ss, GR[0].SQ, GR[1].SQ])
            if half == 1:
                for kc in range(2):
                    st = stg[kc % 2]
                    S.dma("sp", st.t[:], din["ck"][l, kc * 128:(kc + 1) * 128, :], writes=[st])
                    S.op("act", lambda: nc.scalar.copy(out=GR[0].QB.t[:, kc, :], in_=st.t[:]), reads=[st],
                         writes=[GR[0].QB])
                    transposes_to(GR[0].QB, lambda i: GR[0].QB.t[:, kc, i * 128:(i + 1) * 128], 4, KTa,
                                  KTa.t[:, :, kc * 128:(kc + 1) * 128])
                for kc in range(2):
                    st = stg[kc % 2]
                    S.dma("sp", st.t[:], din["cv"][l, kc * 128:(kc + 1) * 128, :], writes=[st])
                    S.op("act", lambda: nc.scalar.copy(out=VA.t[:, kc, :, 0:128],
                                                       in_=st.t[:].rearrange("p (a b) -> p a b", b=128)),
                         reads=[st], writes=[VA])

            def stageA1(slot, g):
                G = GR[g]
                for t in range(4):
                    tt = g * 4 + t
                    b = nextbank()
                    zmm(slot, tt, b)
                    S.op("act", lambda: nc.scalar.copy(out=G.QA.t[:, t, :], in_=pbank(b)), reads=[PB[b]], writes=[G.QA])
                    S.op("act", lambda: nc.scalar.activation(out=G.SQ.t[:, t, :], in_=pbank(b), func=AF.Square),
                         reads=[PB[b]], writes=[G.SQ])

            def stageA2(g):
                G = GR[g]
                S.op("dve", lambda: nc.vector.tensor_reduce(
                    out=G.ss.t[:], in_=G.SQ.t[:].rearrange("p t (a b) -> p (t a) b", b=64),
                    axis=AX.X, op=ALU.add), reads=[G.SQ], writes=[G.ss])

            def stageB(g, gain_tt, which):
                G = GR[g]
                if which == "q":
                    rstd_from_ss(G.ss.t[:], G.ss.t[:], 1.0, 64 * EPS, [G.ss], [G.ss])
                else:
                    rstd_from_ss(G.ss.t[:], G.ss.t[:], 1.0 / 64, EPS, [G.ss], [G.ss])
                qf_ = G.QA.t[:].rearrange("p a b -> p (a b)")
                sf_ = G.SQ.t[:].rearrange("p a b -> p (a b)")
                q3 = qf_.rearrange("p (a b) -> p a b", b=64)
                S.op("dve", lambda: nc.vector.tensor_tensor(out=q3, in0=q3,
                                                            in1=G.ss.t[:].unsqueeze(2).broadcast_to([128, 32, 64]),
                                                            op=ALU.mult), reads=[G.QA, G.ss], writes=[G.QA])
                gbc = gain_tt.t[:, l, :].unsqueeze(1).broadcast_to([128, 32, 64])
                if half == 0 and which == "q":
                    S.op("dve", lambda: nc.vector.tensor_tensor(
                        out=G.QB.t[:].rearrange("p a (g c) -> p (a g) c", c=64), in0=q3, in1=gbc, op=ALU.mult),
                        reads=[G.QA, gain_tt], writes=[G.QB])
                else:
                    S.op("dve", lambda: nc.vector.tensor_tensor(out=q3, in0=q3, in1=gbc, op=ALU.mult),
                         reads=[G.QA, gain_tt], writes=[G.QA])
                    if half == 0:
                        for s2 in range(2):
                            s_ = g * 2 + s2
                            S.dma("sp", dout["nk"][s_, l].rearrange("(t p) c -> p t c", p=128),
                                  G.QA.t[:, 2 * s2:2 * s2 + 2, :], reads=[G.QA])
                        S.op("act", lambda: nc.scalar.copy(out=G.QB.t[:].rearrange("p a b -> p (a b)"), in_=qf_),
                             reads=[G.QA], writes=[G.QB])
                    else:
                        x5 = G.QA.t[:].rearrange("p t (g a s c) -> p t g a s c", a=2, s=2, c=16)
                        r5 = G.SQ.t[:].rearrange("p t (g a s c) -> p t g a s c", a=2, s=2, c=16)
                        s5 = SIN.t[:, g * 4:(g + 1) * 4, :].rearrange("p t (a s c) -> p t a s c", s=2, c=16)
                        for sidx in range(2):
                            for ax_ in range(2):
                                S.op("dve", lambda: nc.vector.tensor_tensor(
                                    out=r5[:, :, :, ax_, sidx, :], in0=x5[:, :, :, ax_, 1 - sidx, :],
                                    in1=s5[:, :, ax_, sidx, :].unsqueeze(2).broadcast_to([128, 4, 8, 16]), op=ALU.mult),
                                    reads=[G.QA, SIN], writes=[G.SQ])
                        q4 = G.QA.t[:].rearrange("p t (g c) -> p t g c", c=64)
                        S.op("dve", lambda: nc.vector.tensor_tensor(
                            out=q4, in0=q4,
                            in1=COS.t[:, g * 4:(g + 1) * 4, :].unsqueeze(2).broadcast_to([128, 4, 8, 64]), op=ALU.mult),
                            reads=[G.QA, COS], writes=[G.QA])
                        S.op("dve", lambda: nc.vector.tensor_tensor(out=G.QB.t[:].rearrange("p a b -> p (a b)"),
                                                                    in0=qf_, in1=sf_, op=ALU.add),
                             reads=[G.QA, G.SQ], writes=[G.QB])

            def stageC(g, dstT, doff):
                G = GR[g]
                for t in range(4):
                    tt = g * 4 + t
                    b = nextbank()
                    pv = pbank(b).bitcast(BF16)

                    def f():
                        ins = None
                        for i in range(4):
                            ins = nc.tensor.transpose(pv[:, i * 128:(i + 1) * 128], G.QB.t[:, t, i * 128:(i + 1) * 128],
                                                      identB.t[:])
                        return ins
                    S.op("pe", f, reads=[G.QB, identB], writes=[PB[b]])
                    S.op("act", lambda: nc.scalar.copy(out=dstT.t[:, :, doff + tt * 128:doff + (tt + 1) * 128],
                                                       in_=pv[:, 0:512].rearrange("p (a b) -> p a b", b=128)),
                         reads=[PB[b]], writes=[dstT])

            def stageV(slot, g):
                for t in range(4):
                    tt = g * 4 + t
                    b = nextbank()
                    zmm(slot, tt, b)
                    kci = KOFF // 128 + tt
                    if half == 0:
                        st = stg[tt % 2]
                        S.op("act", lambda: nc.scalar.copy(out=st.t[:], in_=pbank(b)), reads=[PB[b]], writes=[st])
                        S.dma("sp", dout["nv"][tt // 2, l, (tt % 2) * 128:(tt % 2) * 128 + 128, :], st.t[:], reads=[st])
                        S.op("act", lambda: nc.scalar.copy(out=VA.t[:, kci, :, 0:128],
                                                           in_=st.t[:].rearrange("p (a b) -> p a b", b=128)),
                             reads=[st], writes=[VA])
                    else:
                        S.op("act", lambda: nc.scalar.copy(out=VA.t[:, kci, :, 0:128],
                                                           in_=pbank(b).rearrange("p (a b) -> p a b", b=128)),
                             reads=[PB[b]], writes=[VA])

            slot3 = w_get()
            stageA1(slot3, 0)
            stageA2(0)
            stageB(0, QN, "q")
            stageA1(slot3, 1)
            stageA2(1)
            w_done()
            slot4 = w_get()
            stageC(0, QTa, 0)
            stageB(1, QN, "q")
            stageA1(slot4, 0)
            stageA2(0)
            stageC(1, QTa, 0)
            stageB(0, KN, "k")
            stageA1(slot4, 1)
            stageA2(1)
            w_done()
            slot5 = w_get()
            stageC(0, KTa, KOFF)
            stageV(slot5, 0)
            stageB(1, KN, "k")
            stageV(slot5, 1)
            stageC(1, KTa, KOFF)
            w_done()
            S.barrier()
            ckpt(17)
            AR.off = ar_att
            OALL = AV([128, 8, 4, 128])
            ETP = [AV([128, 2, 2, 256], BF16) for _ in range(2)]
            ot2 = [AV([128, 128]) for _ in range(2)]
            work = []
            for qg in range(4):
                kcs = [qg * 2, qg * 2 + 1] if half == 0 else list(range(10))
                for h in range(4):
                    for pi in range(len(kcs) // 2):
                        work.append((qg, h, pi, len(kcs) // 2, (kcs[2 * pi], kcs[2 * pi + 1])))
            st_banks = {}

            def emit_st(widx):
                qg, h, pi, npair, kc2 = work[widx]
                q0 = qg * 256
                bsA, bsB = ((0, 1), (2, 3))[widx % 2]
                st_banks[widx] = (bsA, bsB)

                def f():
                    ins = None
                    for i, bnk in ((0, bsA), (1, bsB)):
                        ps_ = slice(i * 64, (i + 1) * 64)
                        for kk in range(2):
                            kc = kc2[kk]
                            ins = nc.tensor.matmul(pbank(bnk)[:, kk * 256:(kk + 1) * 256],
                                                   KTa.t[ps_, h, kc * 128:(kc + 1) * 128],
                                                   QTa.t[ps_, h, q0:q0 + 256], start=True, stop=True)
                    return ins
                S.op("pe", f, reads=[KTa, QTa], writes=[PB[bsA], PB[bsB]])

            OSQh = AV([128, 2048])

            def norm_half(hh):
                ov = OALL.t[:, hh * 4:(hh + 1) * 4, :, :]
                of_ = ov.rearrange("p a b c -> p (a b c)")
                ss2 = AV([128, 16])
                S.op("dve", lambda: nc.vector.tensor_tensor(out=OSQh.t[:], in0=of_, in1=of_, op=ALU.mult),
                     reads=[OALLh[hh]], writes=[OSQh])
                S.op("dve", lambda: nc.vector.tensor_reduce(out=ss2.t[:], in_=OSQh.t[:].rearrange("p (a b) -> p a b", b=128),
                                                            axis=AX.X, op=ALU.add), reads=[OSQh], writes=[ss2])
                rstd_from_ss(ss2.t[:], ss2.t[:], 1.0 / 128, EPS, [ss2], [ss2])
                o3 = of_.rearrange("p (a b) -> p a b", b=128)
                S.op("dve", lambda: nc.vector.tensor_tensor(out=o3, in0=o3,
                                                            in1=ss2.t[:].unsqueeze(2).broadcast_to([128, 16, 128]),
                                                            op=ALU.mult), reads=[OALLh[hh], ss2], writes=[OALLh[hh]])
                S.op("dve", lambda: nc.vector.tensor_tensor(
                    out=ytok.t[:, hh * 4:(hh + 1) * 4, 512:1024].rearrange("p t (h c) -> p t h c", c=128), in0=ov,
                    in1=DN.t[:, l, :].unsqueeze(1).unsqueeze(1).broadcast_to([128, 4, 4, 128]), op=ALU.mult),
                    reads=[OALLh[hh], DN], writes=[ytok])

            OALLh = [Res(), Res()]
            emit_st(0)
            itn = 0
            for widx in range(len(work)):
                qg, h, pi, npair, kc2 = work[widx]
                q0 = qg * 256
                if pi == 0:
                    accs = ((4, 5), (6, 7))[itn % 2]
                    itn += 1
                if widx + 1 < len(work):
                    emit_st(widx + 1)
                bsA, bsB = st_banks.pop(widx)
                etp = ETP[widx % 2]
                S.op("act", lambda: nc.scalar.activation(out=etp.t[:].rearrange("p i a b -> p (i a b)"),
                                                         in_=PSt[bsA // 2][:, 0:1024], func=AF.Exp),
                     reads=[PB[bsA], PB[bsB]], writes=[etp])

                def f():
                    ins = None
                    for kk in range(2):
                        kc = kc2[kk]
                        for qb_ in range(2):
                            for i in range(2):
                                first = (pi == 0 and kk == 0 and i == 0)
                                last = (pi == npair - 1 and kk == 1 and i == 1)
                                ins = nc.tensor.matmul(pbank(accs[qb_])[:, i * 129:(i + 1) * 129],
                                                       etp.t[:, i, kk, qb_ * 128:(qb_ + 1) * 128],
                                                       VA.t[:, kc, h, 0:129], start=first, stop=last)
                    return ins
                S.op("pe", f, reads=[etp, VA], writes=[PB[accs[0]], PB[accs[1]]])
                if pi == npair - 1:
                    for qb_ in range(2):
                        tt = (q0 + qb_ * 128) // 128
                        ab = accs[qb_]
                        acc = pbank(ab)
                        rc = nsmall()
                        S.op("dve", lambda: nc.vector.reciprocal(
                            rc.t[:, 0:2], acc[:, 0:258].rearrange("p (a b) -> p a b", b=129)[:, :, 128]),
                            reads=[PB[ab]], writes=[rc])
                        S.op("dve", lambda: nc.vector.tensor_tensor(out=rc.t[:, 2:3], in0=rc.t[:, 1:2],
                                                                    in1=NLAM.t[:, l:l + 1], op=ALU.mult),
                             reads=[rc, NLAM], writes=[rc])
                        o1 = ot2[qb_]
                        S.op("dve", lambda: nc.vector.tensor_scalar(o1.t[:], acc[:, 129:257], rc.t[:, 2:3], None,
                                                                    ALU.mult), reads=[PB[ab], rc], writes=[o1])
                        S.op("dve", lambda: nc.vector.scalar_tensor_tensor(
                            out=OALL.t[:, tt, h, :], in0=acc[:, 0:128], scalar=rc.t[:, 0:1], in1=o1.t[:],
                            op0=ALU.mult, op1=ALU.add), reads=[PB[ab], rc, o1], writes=[OALLh[qg // 2]])
                        if qb_ == 1 and h == 3 and qg in (1, 3):
                            norm_half(qg // 2)
            ckpt(18)
            if half == HALVES[0]:
                mods_group(l, 2)
            slots_o = [w_get(), w_get()]
            for tg in range(2):
                for tt in range(tg * 4, tg * 4 + 4):
                    b = nextbank()
                    pv = pbank(b).bitcast(BF16)

                    def f():
                        ins = None
                        for i in range(8):
                            ins = nc.tensor.transpose(pv[:, i * 128:(i + 1) * 128], ytok.t[:, tt, i * 128:(i + 1) * 128],
                                                      identB.t[:])
                        return ins
                    S.op("pe", f, reads=[ytok, identB], writes=[PB[b]])
                    S.op("dve", lambda: nc.vector.tensor_copy(
                        yT.t[:, :, tt * 128:(tt + 1) * 128], pv[:, 0:1024].rearrange("p (a b) -> p a b", b=128)),
                        reads=[PB[b]], writes=[yT.halves[tg], yT])
                for cg in range(2):
                    slot = slots_o[cg]
                    wv = slotv(slot, 8, 512)
                    for m in range(4):
                        mm = cg * 4 + m
                        b = nextbank()

                        def f():
                            ins = None
                            for kc in range(8):
                                ins = nc.tensor.matmul(pbank(b), wv[:, kc, m * 128:(m + 1) * 128],
                                                       yT.t[:, kc, tg * 512:(tg + 1) * 512],
                                                       start=(kc == 0), stop=(kc == 7))
                            return ins
                        S.op("pe", f, reads=[slot, yT.halves[tg]], writes=[PB[b]])
                        xs_ = xT.t[:, mm, tg * 512:(tg + 1) * 512]
                        S.op("dve", lambda: nc.vector.scalar_tensor_tensor(
                            out=xs_, in0=pbank(b), scalar=MOD[l].t[:, 2 * 8 + mm, cond:cond + 1], in1=xs_,
                            op0=ALU.mult, op1=ALU.add), reads=[PB[b], MOD[l], xT], writes=[xT])
                        S.op("act", lambda: nc.scalar.activation(out=hT.t[:, mm, tg * 512:(tg + 1) * 512], in_=xs_,
                                                                 func=AF.Square), reads=[xT], writes=[hT.halves[tg]])
            w_done()
            w_done()
            S.barrier()

        def ffn(l, half):
            cond = half
            nseq, L = (4, 256) if half == 0 else (1, 1024)
            AR.reset()
            gT = AV([128, 22, 1024], BF16)
            y0 = [AV([128, 1024]) for _ in range(3)]
            sa = [AV([128, 1024]) for _ in range(2)]
            upb = ((0, 1), (2, 3), (4, 5))
            ui = 0
            for t in range(11):
                slot = w_get()
                wv = slotv(slot, 8, 512)
                for sub in range(2):
                    j = 2 * t + sub
                    for ab in range(2):
                        cols = ab * 256 + sub * 128
                        jj = ab * 22 + j
                        bb = upb[ui % 3]
                        yy = y0[ui % 3]
                        ui += 1
                        for tg in range(2):
                            bnk = bb[tg]

                            def f():
                                ins = None
                                for kc in range(8):
                                    ins = nc.tensor.matmul(pbank(bnk), wv[:, kc, cols:cols + 128],
                                                           hT.t[:, kc, tg * 512:(tg + 1) * 512],
                                                           start=(kc == 0), stop=(kc == 7))
                                return ins
                            S.op("pe", f, reads=[slot, hT.halves[tg]], writes=[PB[bnk]])
                        pfull = PSt[bb[0] // 2][:, 0:1024]
                        S.op("act", lambda: nc.scalar.activation(out=yy.t[:], in_=pfull, func=AF.Identity,
                                                                 bias=CW.t[:, l, 3, jj:jj + 1],
                                                                 scale=CW.t[:, l, 1, jj:jj + 1]),
                             reads=[PB[bb[0]], PB[bb[1]], CW], writes=[yy])
                        pv3 = pfull.rearrange("p (s t) -> p s t", t=L)
                        yv3 = yy.t[:].rearrange("p (s t) -> p s t", t=L)
                        S.op("dve", lambda: nc.vector.scalar_tensor_tensor(
                            out=yv3[:, :, 1:L], in0=pv3[:, :, 0:L - 1], scalar=CW.t[:, l, 0, jj:jj + 1],
                            in1=yv3[:, :, 1:L], op0=ALU.mult, op1=ALU.add),
                            reads=[PB[bb[0]], PB[bb[1]], CW, yy], writes=[yy])
                        S.op("dve", lambda: nc.vector.scalar_tensor_tensor(
                            out=yv3[:, :, 0:L - 1], in0=pv3[:, :, 1:L], scalar=CW.t[:, l, 2, jj:jj + 1],
                            in1=yv3[:, :, 0:L - 1], op0=ALU.mult, op1=ALU.add),
                            reads=[PB[bb[0]], PB[bb[1]], CW, yy], writes=[yy])
                        if ab == 0:
                            sa_ = sa[j % 2]
                            S.op("act", lambda: nc.scalar.activation(out=sa_.t[:], in_=yy.t[:], func=AF.Silu),
                                 reads=[yy], writes=[sa_])
                        else:
                            sa_ = sa[j % 2]
                            S.op("pool", lambda: nc.gpsimd.tensor_tensor(out=gT.t[:, j, :], in0=sa_.t[:], in1=yy.t[:],
                                                                         op=ALU.mult), reads=[sa_, yy], writes=[gT])
                w_done()
            if half == HALVES[0]:
                mods_group(l, 5)
            for m in range(8):
                slot = w_get()
                wv = slotv(slot, 22, 128)
                for tg in range(2):
                    b = nextbank((6, 7), "t67")

                    def f():
                        ins = None
                        for kc in range(22):
                            ins = nc.tensor.matmul(pbank(b), wv[:, kc, :], gT.t[:, kc, tg * 512:(tg + 1) * 512],
                                                   start=(kc == 0), stop=(kc == 21))
                        return ins
                    S.op("pe", f, reads=[slot, gT], writes=[PB[b]])
                    xs_ = xT.t[:, m, tg * 512:(tg + 1) * 512]
                    S.op("dve", lambda: nc.vector.scalar_tensor_tensor(
                        out=xs_, in0=pbank(b), scalar=MOD[l].t[:, 5 * 8 + m, cond:cond + 1], in1=xs_,
                        op0=ALU.mult, op1=ALU.add), reads=[PB[b], MOD[l], xT], writes=[xT])
                    if l < NL - 1:
                        S.op("act", lambda: nc.scalar.activation(out=yT.t[:, m, tg * 512:(tg + 1) * 512], in_=xs_,
                                                                 func=AF.Square), reads=[xT], writes=[yT] + yT.halves)
                w_done()
            S.barrier()

        for half in HALVES:
            AR.reset()
            load_x(din["xp"] if half == 0 else din["xs"])
            S.barrier()
            ckpt(11)
            for l in range(NL):
                AR.reset()
                if half == HALVES[0]:
                    mods_group(l, 0)
                    mods_group(l, 1)
                norm_mod(l, 0, half, presq=(yT if l > 0 else None))
                ckpt(12)
                mixer(l, half)
                ckpt(19)
                AR.reset()
                if half == HALVES[0]:
                    mods_group(l, 3)
                    mods_group(l, 4)
                norm_mod(l, 1, half, presq=hT)
                ckpt(20)
                ffn(l, half)
                ckpt(21)
            AR.reset()
            store_x(dout["yp"] if half == 0 else dout["ys"])
            S.barrier()
        S.drain("sp")
    except _Stop:
        pass
    return nc


_CACHE = {}


def _prep_inputs(inputs):
    f32 = lambda a: np.ascontiguousarray(np.asarray(a, dtype=np.float32))
    I = {k: f32(v) for k, v in inputs.items()}
    hc = host_consts()
    shared = {
        "norm1": I["norm1"], "w_mod": I["w_mod"], "b_mod": I["b_mod"], "w_in": I["w_in"],
        "sgu_norm": I["sgu_norm"], "sgu_w": I["sgu_w"], "sgu_b": I["sgu_b"],
        "rlf": I["ret_logit_fwd"], "rlb": I["ret_logit_bwd"], "ret_norm": I["ret_norm"].reshape(2, 256),
        "q_norm": I["q_norm"], "k_norm": I["k_norm"], "diff_lam": I["diff_lam"].reshape(2, 256),
        "diff_norm": I["diff_norm"], "w_out": I["w_out"], "norm2": I["norm2"], "ffn_up": I["ffn_up"],
        "ffn_conv": I["ffn_conv"], "ffn_conv_b": I["ffn_conv_b"], "ffn_down": I["ffn_down"],
    }
    shared.update(hc)
    maps = []
    for c in range(8):
        m = dict(shared)
        m["xp"] = np.ascontiguousarray(I["x_prompt"][4 * c:4 * c + 4].reshape(1024, 1024))
        m["xs"] = np.ascontiguousarray(I["x_sample"][c])
        m["cvec"] = np.ascontiguousarray(np.stack([I["c_ctx"], I["c"][c]], axis=0))
        m["ck"] = np.ascontiguousarray(I["cache_k"][c].reshape(2, 256, 512))
        m["cv"] = np.ascontiguousarray(I["cache_v"][c].reshape(2, 256, 512))
        m["srf"] = np.ascontiguousarray(I["state_ret_fwd"][c])
        m["srb"] = np.ascontiguousarray(I["state_ret_bwd"][c])
        maps.append(m)
    return maps


def kernel(**inputs):
    maps = _prep_inputs(inputs)
    if "nc" not in _CACHE:
        _CACHE["nc"] = build()
    nc = _CACHE["nc"]
    res = run_bass_kernel_spmd(nc, maps, core_ids=list(range(8)))
    R = res.results
    yp = np.concatenate([np.asarray(R[c]["yp"]).reshape(4, 256, 1024) for c in range(8)], axis=0)
    ys = np.stack([np.asarray(R[c]["ys"]) for c in range(8)], axis=0)
    nk = np.concatenate([np.asarray(R[c]["nk"]).reshape(4, 2, 256, 4, 2, 64) for c in range(8)], axis=0)
    nv = np.concatenate([np.asarray(R[c]["nv"]).reshape(4, 2, 256, 4, 128) for c in range(8)], axis=0)
    nrf = np.concatenate([np.asarray(R[c]["nrf"]) for c in range(8)], axis=0)
    nrb = np.concatenate([np.asarray(R[c]["nrb"]) for c in range(8)], axis=0)
    return (yp.astype(np.float32), ys.astype(np.float32), nk.astype(np.float32), nv.astype(np.float32),
            nrf.astype(np.float32), nrb.astype(np.float32))
```

```python
import math
from contextlib import ExitStack
import numpy as np
import concourse.bass as bass
import concourse.mybir as mybir
from concourse.bass_utils import run_bass_kernel_spmd

F32 = mybir.dt.float32
BF16 = mybir.dt.bfloat16
AF = mybir.ActivationFunctionType
ALU = mybir.AluOpType
AX = mybir.AxisListType

D = 1024
T = 1024
DFF = 2816
EPS = 1e-6
NSLOT = 4


class Res:
    __slots__ = ("w", "r", "sem", "persist", "excl")

    def __init__(self, persist=False, excl=False):
        self.excl = excl
        self.w = None
        self.r = {}
        self.sem = None
        self.persist = persist


class TT:
    def __init__(self, t, res=None):
        self.t = t
        self.res = res if res is not None else Res()


class Sch:
    def __init__(self, nc, es):
        self.nc = nc
        self.es = es
        self.E = {"pe": nc.tensor, "act": nc.scalar, "dve": nc.vector, "pool": nc.gpsimd, "sp": nc.sync}
        self.sem = {}
        self.cnt = {}
        self.seen = {k: {} for k in self.E}
        for k in ("pe", "act", "dve", "pool"):
            self.sem[k] = es.enter_context(nc.semaphore("s_" + k))
            self.cnt[k] = 0
        self.nsem = 0
        self.dsems = []
        self.free = []
        self.live = []

    def newsem(self):
        if self.free:
            return self.free.pop()
        name = "d%d" % self.nsem
        self.nsem += 1
        self.sem[name] = self.es.enter_context(self.nc.semaphore(name))
        self.cnt[name] = 0
        self.dsems.append(name)
        return name

    def recycle(self):
        keep = []
        for r in self.live:
            if r.persist:
                keep.append(r)
            else:
                self.free.append(r.sem)
                r.sem = None
        self.live = keep

    def _wait(self, eng, raw, other):
        best = {}
        for t in raw:
            if t is None:
                continue
            k, v = t
            if k == eng and eng in ("pe", "sp"):
                continue
            if v > best.get(k, 0):
                best[k] = v
        for t in other:
            if t is None:
                continue
            k, v = t
            if k == eng and eng in ("pe", "sp"):
                continue
            if v > best.get(k, 0):
                best[k] = v
        sn = self.seen[eng]
        for k, v in best.items():
            if sn.get(k, 0) >= v:
                continue
            self.E[eng].wait_ge(self.sem[k], v)
            sn[k] = v

    def _deps(self, reads, writes):
        raw = [r.w for r in reads]
        other = []
        for w in writes:
            other.append(w.w)
            other.extend(w.r.items())
        return raw, other

    def _commit(self, tok, reads, writes):
        k, v = tok
        for r in reads:
            if r.r.get(k, 0) < v:
                r.r[k] = v
        for w in writes:
            w.w = tok
            w.r = {}

    def op(self, eng, fn, reads=(), writes=()):
        reads = [getattr(x, 'res', x) for x in reads] + [x.extra for x in reads if hasattr(x, 'extra')]
        writes = [getattr(x, 'res', x) for x in writes]
        writes = writes + [r for r in reads if r.excl and r not in writes]
        raw, other = self._deps(reads, writes)
        self._wait(eng, raw, other)
        ins = fn()
        ins.then_inc(self.sem[eng], 1)
        self.cnt[eng] += 1
        tok = (eng, self.cnt[eng])
        self._commit(tok, reads, writes)
        return tok

    def dma(self, q, out, in_, reads=(), writes=(), key=None):
        reads = [getattr(x, 'res', x) for x in reads]
        writes = [getattr(x, 'res', x) for x in writes]
        kres = key if key is not None else (writes[0] if writes else reads[0])
        kres = getattr(kres, 'res', kres)
        if kres.sem is None:
            kres.sem = self.newsem()
            self.live.append(kres)
        raw, other = self._deps(reads, writes)
        other = list(other) + [(kres.sem, self.cnt[kres.sem])]
        self._wait(q, raw, other)
        ins = self.E[q].dma_start(out=out, in_=in_)
        ins.then_inc(self.sem[kres.sem], 16)
        self.cnt[kres.sem] += 16
        tok = (kres.sem, self.cnt[kres.sem])
        self._commit(tok, reads, writes)
        return tok

    def barrier(self):
        engs = ("pe", "act", "dve", "pool")
        for e in engs + ("sp",):
            for f in engs:
                if e == f and e == "pe":
                    continue
                v = self.cnt[f]
                if v > 0 and self.seen[e].get(f, 0) < v:
                    self.E[e].wait_ge(self.sem[f], v)
                    self.seen[e][f] = v
            for r in self.live:
                if r.persist:
                    continue
                v = self.cnt[r.sem]
                if v > 0 and self.seen[e].get(r.sem, 0) < v:
                    self.E[e].wait_ge(self.sem[r.sem], v)
                    self.seen[e][r.sem] = v
        self.recycle()

    def drain(self, q="sp"):
        for k in self.dsems:
            v = self.cnt[k]
            if v > 0 and self.seen[q].get(k, 0) < v:
                self.E[q].wait_ge(self.sem[k], v)
                self.seen[q][k] = v
        for f in ("pe", "act", "dve", "pool"):
            v = self.cnt[f]
            if v > 0 and self.seen[q].get(f, 0) < v:
                self.E[q].wait_ge(self.sem[f], v)
                self.seen[q][f] = v


def host_consts():
    ROPE_PAIRS = 16
    t = np.arange(1024)
    inv = (10000.0 ** (-np.arange(ROPE_PAIRS, dtype=np.float32) / ROPE_PAIRS)).astype(np.float32)
    ar = (t // 64).astype(np.float32)[:, None] * inv
    ac = (t % 64).astype(np.float32)[:, None] * inv
    cr, sr, cc, sc_ = np.cos(ar), np.sin(ar), np.cos(ac), np.sin(ac)
    cos64 = np.concatenate([cr, cr, cc, cc], axis=1).astype(np.float32)
    sin64 = np.concatenate([-sr, sr, -sc_, sc_], axis=1).astype(np.float32)
    k = np.arange(128)[:, None]
    q = np.arange(128)[None, :]
    dm = np.zeros((128, 4, 128), np.float32)
    dm[:, 0] = np.maximum(q - k, 0)
    dm[:, 1] = np.maximum(k - q, 0)
    dm[:, 2] = (q >= k)
    dm[:, 3] = (k >= q)
    pr = np.zeros((128, 2, 128), np.float32)
    pr[:, 0, :] = np.arange(128) + 1.0
    pr[:, 1, :] = 128.0 - np.arange(128)
    kp = np.zeros((128, 2), np.float32)
    kp[:, 0] = 127.0 - np.arange(128)
    kp[:, 1] = np.arange(128)
    return dict(cos64=cos64, sin64=sin64, dmat=dm, posrow=pr, kpos=kp)


IN_SPECS = [
    ("xp", (1024, 1024)), ("xs", (1024, 1024)), ("cvec", (2, 1024)),
    ("ck", (2, 256, 512)), ("cv", (2, 256, 512)), ("srf", (2, 4, 64, 64)), ("srb", (2, 4, 64, 64)),
    ("norm1", (2, 1024)), ("w_mod", (2, 1024, 6144)), ("b_mod", (2, 6144)), ("w_in", (2, 1024, 3072)),
    ("sgu_norm", (2, 256)), ("sgu_w", (2, 4, 128, 128)), ("sgu_b", (2, 4, 128)),
    ("rlf", (2, 4)), ("rlb", (2, 4)), ("ret_norm", (2, 256)), ("q_norm", (2, 64)), ("k_norm", (2, 64)),
    ("diff_lam", (2, 256)), ("diff_norm", (2, 128)), ("w_out", (2, 1024, 1024)), ("norm2", (2, 1024)),
    ("ffn_up", (2, 1024, 5632)), ("ffn_conv", (2, 3, 5632)), ("ffn_conv_b", (2, 5632)),
    ("ffn_down", (2, 2816, 1024)),
    ("cos64", (1024, 64)), ("sin64", (1024, 64)), ("dmat", (128, 4, 128)), ("posrow", (128, 2, 128)),
    ("kpos", (128, 2)),
]
OUT_SPECS = [
    ("yp", (1024, 1024)), ("ys", (1024, 1024)), ("nk", (4, 2, 256, 512)), ("nv", (4, 2, 256, 512)),
    ("nrf", (4, 2, 4, 64, 64)), ("nrb", (4, 2, 4, 64, 64)),
]


def build(cfg=None):
    cfg = cfg or {}
    NL = cfg.get("n_layers", 2)
    HALVES = cfg.get("halves", (0, 1))
    taps = cfg.get("taps", None)
    nc = bass.Bass("TRN2", target_bir_lowering=False)
    try:
        nc.allow_low_precision("bf16 matmul operands with fp32 accumulation")
    except Exception:
        pass
    din = {n: nc.dram_tensor(n, list(s), F32, kind="ExternalInput").ap() for n, s in IN_SPECS}
    dout = {n: nc.dram_tensor(n, list(s), F32, kind="ExternalOutput").ap() for n, s in OUT_SPECS}
    es = ExitStack()
    STOP = cfg.get("stop", None)

    class _Stop(Exception):
        pass
    try:
      with es:
        S = Sch(nc, es)

        def ckpt(k):
            if STOP == k:
                S.drain("sp")
                raise _Stop()

        def sb(name, shape, dt=F32):
            return TT(es.enter_context(nc.sbuf_tensor(name, list(shape), dt)), Res(persist=True))

        def tap(name, ap, shape, reads):
            if taps is None or name not in taps:
                return
            d = nc.dram_tensor("tap_" + name, list(shape), ap.dtype, kind="ExternalOutput").ap()
            taps[name] = (list(shape), ap.dtype)
            S.dma("sp", d, ap, reads=reads)

        PSt = [es.enter_context(nc.psum_tensor("ps%d" % i, [128, 1024], F32)) for i in range(4)]
        PB = []
        for i in range(8):
            PB.append(TT(PSt[i // 2], Res(excl=True)))

        def pbank(i):
            return PSt[i // 2][:, (i % 2) * 512:(i % 2) * 512 + 512]

        rot = {"n": 0}

        def nextbank(pool=(0, 1, 2, 3), key="n"):
            i = pool[rot.setdefault(key, 0) % len(pool)]
            rot[key] += 1
            return i

        xT = sb("xT", [128, 8, T])
        hT = sb("hT", [128, 8, T], BF16)
        hT.halves = [Res(persist=True), Res(persist=True)]
        yT = sb("yT", [128, 8, T], BF16)
        yT.halves = [Res(persist=True), Res(persist=True)]
        ring = [sb("ring%d" % i, [128, 4096], BF16) for i in range(NSLOT)]
        for r_ in ring:
            r_.extra = Res(persist=True)
            r_.extra.sem = S.newsem()
            r_.res.sem = S.newsem()
        ARENA_BYTES = 74 * 1024
        arena = sb("arena", [128, ARENA_BYTES // 4], F32)

        identF = sb("identF", [128, 128])
        identB = sb("identB", [128, 128], BF16)
        onesB = sb("onesB", [128, 128], BF16)
        sTb = sb("sTb", [128, 8, 2], BF16)
        MOD = [sb("MOD%d" % l, [128, 48, 2]) for l in range(2)]
        GS = [[sb("GS%d%d" % (l, n), [128, 8, 2]) for n in range(2)] for l in range(2)]
        NT = sb("NT", [128, 32])
        CW = sb("CW", [128, 2, 4, 44])
        WST = sb("WST", [128, 8, 128], BF16)
        BS = sb("BS", [128, 8])
        SGN = sb("SGN", [128, 2, 256])
        RN = sb("RN", [128, 2, 256])
        QN = sb("QN", [128, 2, 64])
        KN = sb("KN", [128, 2, 64])
        DN = sb("DN", [128, 2, 128])
        LG = sb("LG", [128, 16])
        KP = sb("KP", [128, 2])
        C128 = sb("C128", [128, 64])
        DTm = sb("DTm", [128, 2, 4, 128])
        QD = sb("QD", [128, 2, 2, 2, 128])
        KDE = sb("KDE", [128, 2, 2, 256])
        CD = sb("CD", [128, 2, 2, 2, 64])
        NLAM = sb("NLAM", [128, 2])
        COS = sb("COS", [128, 8, 64])
        SIN = sb("SIN", [128, 8, 64])
        small = [sb("small%d" % i, [128, 16]) for i in range(8)]
        srot = {"i": 0}

        def nsmall():
            s = small[srot["i"] % len(small)]
            srot["i"] += 1
            return s

        wq = []

        def slotv(slot, a, b):
            return slot.t[:, 0:a * b].rearrange("p (a b) -> p a b", b=b)

        def wsrc(name, l, c0, c1):
            return din[name][l, :, c0:c1].rearrange("(kc p) n -> p kc n", p=128)

        def plan_weights():
            def modt(l, which):
                for t2 in range(2):
                    c0 = which * 1024 + t2 * 512
                    wq.append([(lambda s: slotv(s, 8, 512), wsrc("w_mod", l, c0, c0 + 512))])
            first = True
            for half in HALVES:
                for l in range(NL):
                    if first:
                        modt(l, 0)
                        modt(l, 1)
                    for g in range(6):
                        wq.append([(lambda s: slotv(s, 8, 512), wsrc("w_in", l, g * 512, (g + 1) * 512))])
                    if first:
                        modt(l, 2)
                    for g in range(2):
                        wq.append([(lambda s: slotv(s, 8, 512), wsrc("w_out", l, g * 512, (g + 1) * 512))])
                    if first:
                        modt(l, 3)
                        modt(l, 4)
                    for t in range(11):
                        wq.append([
                            (lambda s: slotv(s, 8, 512)[:, :, 0:256], wsrc("ffn_up", l, t * 256, (t + 1) * 256)),
                            (lambda s: slotv(s, 8, 512)[:, :, 256:512],
                             wsrc("ffn_up", l, DFF + t * 256, DFF + (t + 1) * 256)),
                        ])
                    if first:
                        modt(l, 5)
                    for m in range(8):
                        wq.append([(lambda s: slotv(s, 22, 128), wsrc("ffn_down", l, m * 128, (m + 1) * 128))])
                first = False

        plan_weights()
        wstate = {"next_load": 0, "next_use": 0}

        def w_issue():
            i = wstate["next_load"]
            if i >= len(wq):
                return
            slot = ring[i % NSLOT]
            for n_, (dstf, src) in enumerate(wq[i]):
                S.dma("pool", dstf(slot), src, writes=[slot.res if n_ == 0 else slot.extra])
            wstate["next_load"] = i + 1

        def w_get():
            i = wstate["next_use"]
            wstate["next_use"] = i + 1
            assert i < wstate["next_load"]
            return ring[i % NSLOT]

        def w_done():
            w_issue()

        class Arena:
            def __init__(self):
                self.off = 0

            def reset(self):
                self.off = 0

            def get(self, shape, dt):
                n = 1
                for s_ in shape[1:]:
                    n *= s_
                nbytes = n * (4 if dt == F32 else 2)
                nbytes = (nbytes + 63) // 64 * 64
                w0 = self.off // 4
                w1 = (self.off + nbytes) // 4
                assert self.off + nbytes <= ARENA_BYTES, ("arena overflow", self.off, nbytes)
                self.off += nbytes
                ap = arena.t[0:shape[0], w0:w1]
                if dt == BF16:
                    ap = ap.bitcast(BF16)
                nfree = n
                ap = ap[:, 0:nfree]
                if len(shape) > 2:
                    names = " ".join("d%d" % i for i in range(len(shape) - 1))
                    kw = {"d%d" % i: shape[i + 1] for i in range(len(shape) - 1)}
                    ap = ap.rearrange("p (%s) -> p %s" % (names, names), **kw)
                return ap

        AR = Arena()

        class AV:
            def __init__(self, shape, dt=F32):
                self.ap = AR.get(shape, dt)
                self.res = Res()

            @property
            def t(self):
                return self.ap

        S.op("pool", lambda: nc.gpsimd.memset(identF.t[:], 0.0), writes=[identF])
        S.op("pool", lambda: nc.gpsimd.affine_select(out=identF.t[:], in_=identF.t[:], pattern=[[-1, 128]],
                                                      compare_op=ALU.not_equal, fill=1.0, base=0,
                                                      channel_multiplier=1), reads=[identF], writes=[identF])
        S.op("pool", lambda: nc.gpsimd.memset(onesB.t[:], 1.0), writes=[onesB])
        S.op("pool", lambda: nc.gpsimd.memset(C128.t[:], 128.0), writes=[C128])
        S.op("dve", lambda: nc.vector.tensor_copy(identB.t[:], identF.t[:]), reads=[identF], writes=[identB])

        ckpt(1)
        for _ in range(NSLOT):
            w_issue()

        cres = Res()
        crow = AV([2, 1024])
        bmrow = AV([96, 128])
        nrow = AV([32, 128])
        cvrow = AV([44, 2, 4, 128])
        wsraw = AV([128, 8, 128])
        bsrow = AV([8, 128])
        BMT = sb("BMT", [128, 96])
        DL = AV([128, 2, 256])
        DM = AV([128, 4, 128])
        PR = AV([128, 2, 128])

        def cload(dst_tt, dst_ap, src_ap):
            S.dma("sp", dst_ap, src_ap, writes=[dst_tt])

        cload(crow, crow.t[:], din["cvec"])
        cload(bmrow, bmrow.t[:], din["b_mod"].rearrange("l (j p) -> (l j) p", p=128))
        cload(nrow, nrow.t[0:16, :], din["norm1"].rearrange("l (kc p) -> (l kc) p", p=128))
        cload(nrow, nrow.t[16:32, :], din["norm2"].rearrange("l (kc p) -> (l kc) p", p=128))
        for l in range(2):
            cload(cvrow, cvrow.t[:, l, 0:3, :], din["ffn_conv"][l].rearrange("j (c p) -> c j p", p=128))
            cload(cvrow, cvrow.t[:, l, 3, :], din["ffn_conv_b"][l].rearrange("(c p) -> c p", p=128))
        cload(wsraw, wsraw.t[:], din["sgu_w"].rearrange("l g p q -> p (l g) q"))
        cload(bsrow, bsrow.t[:], din["sgu_b"].rearrange("l g p -> (l g) p"))
        cload(SGN, SGN.t[:], din["sgu_norm"].partition_broadcast(128))
        cload(RN, RN.t[:], din["ret_norm"].partition_broadcast(128))
        cload(QN, QN.t[:], din["q_norm"].partition_broadcast(128))
        cload(KN, KN.t[:], din["k_norm"].partition_broadcast(128))
        cload(DN, DN.t[:], din["diff_norm"].partition_broadcast(128))
        cload(DL, DL.t[:], din["diff_lam"].partition_broadcast(128))
        cload(LG, LG.t[:, 0:8], din["rlf"].rearrange("l h -> (l h)").partition_broadcast(128))
        cload(LG, LG.t[:, 8:16], din["rlb"].rearrange("l h -> (l h)").partition_broadcast(128))
        cload(DM, DM.t[:], din["dmat"])
        cload(PR, PR.t[:], din["posrow"])
        cload(KP, KP.t[:], din["kpos"])
        cload(COS, COS.t[:], din["cos64"].rearrange("(t p) c -> p t c", p=128))
        cload(SIN, SIN.t[:], din["sin64"].rearrange("(t p) c -> p t c", p=128))

        ckpt(2)
        csil = AV([2, 1024])
        S.op("act", lambda: nc.scalar.activation(out=csil.t[:], in_=crow.t[:], func=AF.Silu),
             reads=[crow], writes=[csil])
        b = nextbank()

        def f():
            ins = None
            for kc in range(8):
                ins = nc.tensor.transpose(pbank(b)[:, kc * 2:kc * 2 + 2], csil.t[0:2, kc * 128:(kc + 1) * 128],
                                          identF.t[0:2, 0:2])
            return ins
        S.op("pe", f, reads=[csil, identF], writes=[PB[b]])
        S.op("dve", lambda: nc.vector.tensor_copy(sTb.t[:].rearrange("p a b -> p (a b)"), pbank(b)[:, 0:16]),
             reads=[PB[b]], writes=[sTb])

        ckpt(3)
        b = nextbank()
        S.op("pe", lambda: nc.tensor.transpose(pbank(b)[:, 0:32], nrow.t[0:32, :], identF.t[0:32, 0:32]),
             reads=[nrow, identF], writes=[PB[b]])
        S.op("dve", lambda: nc.vector.tensor_copy(NT.t[:], pbank(b)[:, 0:32]), reads=[PB[b]], writes=[NT])
        b = nextbank()
        S.op("pe", lambda: nc.tensor.transpose(pbank(b)[:, 0:96], bmrow.t[0:96, :], identF.t[0:96, 0:96]),
             reads=[bmrow, identF], writes=[PB[b]])
        S.op("dve", lambda: nc.vector.tensor_copy(BMT.t[:], pbank(b)[:, 0:96]), reads=[PB[b]], writes=[BMT])
        ckpt(4)
        b = nextbank()

        def f():
            ins = None
            for l in range(2):
                for j in range(4):
                    c0 = (l * 4 + j) * 44
                    ins = nc.tensor.transpose(pbank(b)[:, c0:c0 + 44], cvrow.t[0:44, l, j, :], identF.t[0:44, 0:44])
            return ins
        S.op("pe", f, reads=[cvrow, identF], writes=[PB[b]])
        S.op("dve", lambda: nc.vector.tensor_copy(CW.t[:].rearrange("p a b c -> p (a b c)"), pbank(b)[:, 0:352]),
             reads=[PB[b]], writes=[CW])
        ckpt(5)
        for hb in range(2):
            b = nextbank()

            def f():
                ins = None
                for i in range(4):
                    ins = nc.tensor.transpose(pbank(b)[:, i * 128:(i + 1) * 128], wsraw.t[:, hb * 4 + i, :], identF.t[:])
                return ins
            S.op("pe", f, reads=[wsraw, identF], writes=[PB[b]])
            S.op("dve", lambda: nc.vector.tensor_copy(
                WST.t[:, hb * 4:(hb + 1) * 4, :].rearrange("p a b -> p (a b)"), pbank(b)[:, 0:512]),
                reads=[PB[b]], writes=[WST])
        b = nextbank()
        S.op("pe", lambda: nc.tensor.transpose(pbank(b)[:, 0:8], bsrow.t[0:8, :], identF.t[0:8, 0:8]),
             reads=[bsrow, identF], writes=[PB[b]])
        S.op("dve", lambda: nc.vector.tensor_copy(BS.t[:], pbank(b)[:, 0:8]), reads=[PB[b]], writes=[BS])

        ckpt(6)
        modrow = sb("modrow", [2, 1024])

        def mods_group(l, which):
            b = nextbank()
            for t2 in range(2):
                slot = w_get()
                wv = slotv(slot, 8, 512)

                def f():
                    ins = None
                    for j in range(4):
                        jb = t2 * 4 + j
                        for kc in range(8):
                            ins = nc.tensor.matmul(pbank(b)[:, jb * 2:jb * 2 + 2], wv[:, kc, j * 128:(j + 1) * 128],
                                                   sTb.t[:, kc, :], start=(t2 == 0 and j == 0 and kc == 0),
                                                   stop=(t2 == 1 and j == 3 and kc == 7))
                    return ins
                S.op("pe", f, reads=[sTb, slot], writes=[PB[b]])
                w_done()
            c0 = l * 48 + which * 8
            S.op("dve", lambda: nc.vector.tensor_tensor(
                out=MOD[l].t[:, which * 8:(which + 1) * 8, :], in0=pbank(b)[:, 0:16].rearrange("p (a b) -> p a b", b=2),
                in1=BMT.t[:, c0:c0 + 8].unsqueeze(2).broadcast_to([128, 8, 2]), op=ALU.add),
                 reads=[PB[b], BMT], writes=[MOD[l]])
            if which in (1, 4):
                n = 0 if which == 1 else 1
                ntv = NT.t[:, (n * 2 + l) * 8:(n * 2 + l) * 8 + 8]
                S.op("dve", lambda: nc.vector.scalar_tensor_tensor(
                    out=GS[l][n].t[:], in0=MOD[l].t[:, which * 8:(which + 1) * 8, :], scalar=1.0,
                    in1=ntv.unsqueeze(2).broadcast_to([128, 8, 2]), op0=ALU.add, op1=ALU.mult),
                    reads=[MOD[l], NT], writes=[GS[l][n]])

        ckpt(7)
        S.op("act", lambda: nc.scalar.activation(out=LG.t[:], in_=LG.t[:], func=AF.Exp, scale=-1.0),
             reads=[LG], writes=[LG])
        S.op("act", lambda: nc.scalar.activation(out=LG.t[:], in_=LG.t[:], func=AF.Ln, bias=1.0, scale=1.0),
             reads=[LG], writes=[LG])
        S.op("dve", lambda: nc.vector.tensor_scalar(LG.t[:], LG.t[:], -1.0, None, ALU.mult), reads=[LG], writes=[LG])

        ckpt(8)

        def lgi(d, l, h):
            return d * 8 + l * 4 + h

        dtmp = [AV([128, 128]) for i in range(4)]
        for l in range(NL):
            for h in range(4):
                tf, tb = dtmp[(h % 2) * 2], dtmp[(h % 2) * 2 + 1]
                i_f, i_b = lgi(0, l, h), lgi(1, l, h)
                S.op("act", lambda: nc.scalar.activation(out=tf.t[:], in_=DM.t[:, 0, :], func=AF.Exp,
                                                         scale=LG.t[:, i_f:i_f + 1]), reads=[DM, LG], writes=[tf])
                S.op("act", lambda: nc.scalar.activation(out=tb.t[:], in_=DM.t[:, 1, :], func=AF.Exp,
                                                         scale=LG.t[:, i_b:i_b + 1]), reads=[DM, LG], writes=[tb])
                S.op("dve", lambda: nc.vector.scalar_tensor_tensor(out=tf.t[:], in0=tf.t[:], scalar=0.125,
                                                                   in1=DM.t[:, 2, :], op0=ALU.mult, op1=ALU.mult),
                     reads=[tf, DM], writes=[tf])
                S.op("dve", lambda: nc.vector.scalar_tensor_tensor(out=tb.t[:], in0=tb.t[:], scalar=0.125,
                                                                   in1=DM.t[:, 3, :], op0=ALU.mult, op1=ALU.mult),
                     reads=[tb, DM], writes=[tb])
                S.op("dve", lambda: nc.vector.tensor_tensor(out=DTm.t[:, l, h, :], in0=tf.t[:], in1=tb.t[:],
                                                            op=ALU.add), reads=[tf, tb], writes=[DTm])
            for hp in range(2):
                for d in range(2):
                    for j in range(2):
                        ii = lgi(d, l, 2 * hp + j)
                        ps_ = slice(j * 64, (j + 1) * 64)
                        S.op("act", lambda: nc.scalar.activation(out=QD.t[ps_, l, hp, d, :], in_=PR.t[ps_, d, :],
                                                                 func=AF.Exp, scale=LG.t[ps_, ii:ii + 1]),
                             reads=[PR, LG], writes=[QD])
                        S.op("act", lambda: nc.scalar.activation(out=CD.t[ps_, l, d, hp, :], in_=C128.t[ps_, :],
                                                                 func=AF.Exp, scale=LG.t[ps_, ii:ii + 1]),
                             reads=[C128, LG], writes=[CD])
            for d in range(2):
                for h in range(4):
                    ii = lgi(d, l, h)
                    S.op("act", lambda: nc.scalar.activation(
                        out=KDE.t[:, l, d, h * 64:(h + 1) * 64], in_=KP.t[:, d:d + 1].broadcast_to([128, 64]),
                        func=AF.Exp, scale=LG.t[:, ii:ii + 1], bias=math.log(0.125)),
                        reads=[KP, LG], writes=[KDE])
            lam_init = 0.8 - 0.6 * math.exp(-0.3 * l)
            pr_ = nsmall()
            dlv = DL.t[:, l, :].rearrange("p (a b c) -> p a b c", a=2, b=2)
            lt = dtmp[0]
            S.op("dve", lambda: nc.vector.tensor_tensor(out=lt.t[:].rearrange("p (a c) -> p a c", a=2),
                                                        in0=dlv[:, :, 0, :], in1=dlv[:, :, 1, :], op=ALU.mult),
                 reads=[DL], writes=[lt])
            S.op("dve", lambda: nc.vector.tensor_reduce(out=pr_.t[:, 0:2],
                                                        in_=lt.t[:].rearrange("p (a c) -> p a c", a=2),
                                                        axis=AX.X, op=ALU.add), reads=[lt], writes=[pr_])
            S.op("act", lambda: nc.scalar.activation(out=pr_.t[:, 2:4], in_=pr_.t[:, 0:2], func=AF.Exp),
                 reads=[pr_], writes=[pr_])
            S.op("dve", lambda: nc.vector.tensor_tensor(out=pr_.t[:, 4:5], in0=pr_.t[:, 3:4], in1=pr_.t[:, 2:3],
                                                        op=ALU.subtract), reads=[pr_], writes=[pr_])
            S.op("dve", lambda: nc.vector.tensor_scalar(NLAM.t[:, l:l + 1], pr_.t[:, 4:5], -lam_init, None, ALU.add),
                 reads=[pr_], writes=[NLAM])
            S.op("dve", lambda: nc.vector.tensor_scalar(DN.t[:, l, :], DN.t[:, l, :], 1.0 - lam_init, None, ALU.mult),
                 reads=[DN], writes=[DN])

        S.barrier()
        ckpt(10)

        def rstd_from_ss(ss_ap, out_ap, scale, bias, R, W):
            S.op("act", lambda: nc.scalar.activation(out=out_ap, in_=ss_ap, func=AF.Sqrt, bias=bias, scale=scale),
                 reads=R, writes=W)
            S.op("dve", lambda: nc.vector.reciprocal(out_ap, out_ap), reads=W, writes=W)

        def load_x(src):
            xin = [AV([128, 1024]) for _ in range(2)]
            for tt in range(8):
                xi = xin[tt % 2]
                S.dma("sp", xi.t[:], src[tt * 128:(tt + 1) * 128, :], writes=[xi])
                for hb in range(2):
                    b = nextbank()

                    def f():
                        ins = None
                        for i in range(4):
                            c = hb * 4 + i
                            ins = nc.tensor.transpose(pbank(b)[:, i * 128:(i + 1) * 128],
                                                      xi.t[:, c * 128:(c + 1) * 128], identF.t[:])
                        return ins
                    S.op("pe", f, reads=[xi, identF], writes=[PB[b]])
                    eng = "act" if hb == 0 else "dve"
                    dst = xT.t[:, hb * 4:(hb + 1) * 4, tt * 128:(tt + 1) * 128]
                    srcp = pbank(b).rearrange("p (a b) -> p a b", b=128)
                    if eng == "act":
                        S.op("act", lambda: nc.scalar.copy(out=dst, in_=srcp), reads=[PB[b]], writes=[xT])
                    else:
                        S.op("dve", lambda: nc.vector.tensor_copy(dst, srcp), reads=[PB[b]], writes=[xT])

        def store_x(dst):
            xo = [AV([128, 1024]) for _ in range(2)]
            for tt in range(8):
                xi = xo[tt % 2]
                for hb in range(2):
                    b = nextbank()

                    def f():
                        ins = None
                        for i in range(4):
                            c = hb * 4 + i
                            ins = nc.tensor.transpose(pbank(b)[:, i * 128:(i + 1) * 128],
                                                      xT.t[:, c, tt * 128:(tt + 1) * 128], identF.t[:])
                        return ins
                    S.op("pe", f, reads=[xT, identF], writes=[PB[b]])
                    dstp = xi.t[:, hb * 512:(hb + 1) * 512]
                    if hb == 0:
                        S.op("act", lambda: nc.scalar.copy(out=dstp, in_=pbank(b)), reads=[PB[b]], writes=[xi])
                    else:
                        S.op("dve", lambda: nc.vector.tensor_copy(dstp, pbank(b)), reads=[PB[b]], writes=[xi])
                S.dma("sp", dst[tt * 128:(tt + 1) * 128, :], xi.t[:], reads=[xi])

        def norm_mod(l, n, cond, presq=None):
            RBh = [AV([128, 512]) for _ in range(2)]
            if presq is None:
                sq = yT
                S.op("act", lambda: nc.scalar.activation(out=sq.t[:], in_=xT.t[:], func=AF.Square),
                     reads=[xT], writes=[sq] + yT.halves)
            else:
                sq = presq
            for tg in range(2):
                b = nextbank()
                sqres = hT.halves[tg] if sq is hT else sq

                def f():
                    ins = None
                    for kc in range(8):
                        ins = nc.tensor.matmul(pbank(b), onesB.t[:], sq.t[:, kc, tg * 512:(tg + 1) * 512],
                                               start=(kc == 0), stop=(kc == 7))
                    return ins
                S.op("pe", f, reads=[sqres, onesB], writes=[PB[b]])
                rstd_from_ss(pbank(b), RBh[tg].t[:], 1.0 / D, EPS, [PB[b]], [RBh[tg]])
            shi = 0 if n == 0 else 3
            tmp = [AV([128, 512]) for _ in range(4)]
            ti = 0
            for tg in range(2):
                for kc in range(8):
                    tm = tmp[ti % 4]
                    ti += 1
                    S.op("dve", lambda: nc.vector.scalar_tensor_tensor(
                        out=tm.t[:], in0=xT.t[:, kc, tg * 512:(tg + 1) * 512], scalar=GS[l][n].t[:, kc, cond:cond + 1],
                        in1=RBh[tg].t[:], op0=ALU.mult, op1=ALU.mult), reads=[xT, GS[l][n], RBh[tg]], writes=[tm])
                    S.op("act", lambda: nc.scalar.activation(out=hT.t[:, kc, tg * 512:(tg + 1) * 512], in_=tm.t[:],
                                                             func=AF.Identity,
                                                             bias=MOD[l].t[:, shi * 8 + kc, cond:cond + 1], scale=1.0),
                         reads=[tm, MOD[l]], writes=[hT.halves[tg]])

        def zmm(slot, tt, b):
            wv = slotv(slot, 8, 512)

            def f():
                ins = None
                for kc in range(8):
                    ins = nc.tensor.matmul(pbank(b), hT.t[:, kc, tt * 128:(tt + 1) * 128], wv[:, kc, :],
                                           start=(kc == 0), stop=(kc == 7))
                return ins
            S.op("pe", f, reads=[hT.halves[tt // 4], slot], writes=[PB[b]])

        def group_rstd(src_ap, ngrp, gsz, scale, bias, sqt, ss):
            src_tt, sq_tt = sqt
            S.op("dve", lambda: nc.vector.tensor_tensor(out=sq_tt.t[:, 0:ngrp * gsz], in0=src_ap, in1=src_ap,
                                                        op=ALU.mult), reads=[src_tt], writes=[sq_tt])
            S.op("dve", lambda: nc.vector.tensor_reduce(
                out=ss.t[:, 0:ngrp], in_=sq_tt.t[:, 0:ngrp * gsz].rearrange("p (a b) -> p a b", b=gsz),
                axis=AX.X, op=ALU.add), reads=[sq_tt], writes=[ss])
            rstd_from_ss(ss.t[:, 0:ngrp], ss.t[:, 0:ngrp], scale, bias, [ss], [ss])

        def transposes_to(src_tt, src_ap_fn, nblk, dst_tt, dst_ap, pool=(0, 1, 2, 3)):
            b = nextbank(pool)
            pv = pbank(b).bitcast(BF16)

            def f():
                ins = None
                for i in range(nblk):
                    ins = nc.tensor.transpose(pv[:, i * 128:(i + 1) * 128], src_ap_fn(i), identB.t[:])
                return ins
            S.op("pe", f, reads=[src_tt, identB], writes=[PB[b]])
            S.op("dve", lambda: nc.vector.tensor_copy(dst_ap, pv[:, 0:nblk * 128].rearrange("p (a b) -> p a b", b=128)),
                 reads=[PB[b]], writes=[dst_tt])

        def mixer(l, half):
            cond = half
            nseq, L = (4, 256) if half == 0 else (1, 1024)
            cpl = L // 128
            AR.reset()
            ytok = AV([128, 8, 1024], BF16)
            ar_mark = AR.off

            class _V:
                pass
            SCR = _V()
            SCR.ap = yT.t[:].bitcast(F32)
            SCR.t = SCR.ap
            SCR.res = yT.res
            slot = w_get()
            GE = AV([128, 8, 512])
            VN = AV([128, 8, 256], BF16)
            ssG = AV([128, 8])
            for tt in range(8):
                b = nextbank()
                zmm(slot, tt, b)
                S.op("act", lambda: nc.scalar.activation(out=GE.t[:, tt, :], in_=pbank(b), func=AF.Gelu_apprx_tanh),
                     reads=[PB[b]], writes=[GE])
            w_done()
            gv = GE.t[:, :, 256:512]
            sv = SCR.t[:, :, 0:256]

            def g0_stageB():
                S.op("dve", lambda: nc.vector.tensor_tensor(out=sv, in0=gv, in1=gv, op=ALU.mult), reads=[GE], writes=[SCR])
                S.op("dve", lambda: nc.vector.tensor_reduce(out=ssG.t[:], in_=sv, axis=AX.X, op=ALU.add),
                     reads=[SCR], writes=[ssG])
                rstd_from_ss(ssG.t[:], ssG.t[:], 1.0 / 256, EPS, [ssG], [ssG])
                S.op("dve", lambda: nc.vector.tensor_tensor(out=gv, in0=gv,
                                                            in1=ssG.t[:].unsqueeze(2).broadcast_to([128, 8, 256]),
                                                            op=ALU.mult), reads=[GE, ssG], writes=[GE])
                S.op("dve", lambda: nc.vector.tensor_tensor(
                    out=VN.t[:], in0=gv, in1=SGN.t[:, l, :].unsqueeze(1).broadcast_to([128, 8, 256]), op=ALU.mult),
                    reads=[GE, SGN], writes=[VN])

            def g0_stageC(tt):
                b2 = nextbank()

                def f():
                    ins = None
                    for g in range(4):
                        ins = nc.tensor.matmul(pbank(b2)[:, g * 64:(g + 1) * 64], WST.t[:, l * 4 + g, :],
                                               VN.t[:, tt, g * 64:(g + 1) * 64], start=True, stop=True)
                    return ins
                S.op("pe", f, reads=[WST, VN], writes=[PB[b2]])
                for g in range(4):
                    S.op("dve", lambda: nc.vector.scalar_tensor_tensor(
                        out=ytok.t[:, tt, g * 64:(g + 1) * 64], in0=pbank(b2)[:, g * 64:(g + 1) * 64],
                        scalar=BS.t[:, l * 4 + g:l * 4 + g + 1], in1=GE.t[:, tt, g * 64:(g + 1) * 64],
                        op0=ALU.add, op1=ALU.mult), reads=[PB[b2], BS, GE], writes=[ytok])

            ckpt(13)

            QT = AV([128, 2, 1024], BF16)
            KT = AV([128, 2, 1024], BF16)
            VB = AV([128, 8, 256], BF16)
            SG = AV([128, 8, 256], BF16)
            KVS = AV([128, 8, 2, 2, 64])
            RS = AV([128, 2, 2, 64])
            RSb = AV([128, 8, 2, 2, 64], BF16)

            class _W:
                pass
            RSd = []
            RSbd = []
            for d_ in range(2):
                v_ = _W()
                v_.ap = RS.t[:, d_, :, :]
                v_.t = v_.ap
                v_.res = Res()
                RSd.append(v_)
                w_ = _W()
                w_.res = Res()
                RSbd.append(w_)
            qkb = [AV([128, 512], BF16) for _ in range(2)]
            KF = [AV([128, 2, 256], BF16) for _ in range(2)]
            gtmp = [AV([128, 256]) for _ in range(2)]
            st_stage = [AV([128, 2, 2, 64]) for _ in range(2)]
            slot1 = w_get()
            slot2 = w_get()
            g0_stageB()
            for tt in range(8):
                if tt >= 1:
                    g0_stageC(tt - 1)
                b1 = nextbank()
                zmm(slot1, tt, b1)
                if tt == 0:
                    ckpt(1301)
                b2 = nextbank()
                zmm(slot2, tt, b2)
                if tt == 0:
                    ckpt(1302)
                qk = qkb[tt % 2]
                S.op("act", lambda: nc.scalar.copy(out=qk.t[:], in_=pbank(b1)), reads=[PB[b1]], writes=[qk])
                if tt == 0:
                    ckpt(1303)
                kf = KF[tt % 2]
                for d in range(2):
                    S.op("dve", lambda: nc.vector.tensor_tensor(
                        out=kf.t[:, d, :], in0=pbank(b1)[:, 256:512], in1=KDE.t[:, l, d, :], op=ALU.mult),
                        reads=[PB[b1], KDE], writes=[kf])
                    if tt == 0 and d == 0:
                        ckpt(1304)
                if tt == 0:
                    ckpt(131)
                bt = nextbank((6, 7), "t67")
                pv = pbank(bt).bitcast(BF16)

                def f():
                    ins = None
                    for i in range(4):
                        ins = nc.tensor.transpose(pv[:, i * 128:(i + 1) * 128], qk.t[:, i * 128:(i + 1) * 128],
                                                  identB.t[:])
                    return ins
                S.op("pe", f, reads=[qk, identB], writes=[PB[bt]])
                if tt == 0:
                    ckpt(132)
                S.op("dve", lambda: nc.vector.tensor_copy(
                    QT.t[:, :, tt * 128:(tt + 1) * 128], pv[:, 0:256].rearrange("p (a b) -> p a b", b=128)),
                    reads=[PB[bt]], writes=[QT])
                S.op("dve", lambda: nc.vector.tensor_copy(
                    KT.t[:, :, tt * 128:(tt + 1) * 128], pv[:, 256:512].rearrange("p (a b) -> p a b", b=128)),
                    reads=[PB[bt]], writes=[KT])
                if tt == 0:
                    ckpt(133)
                S.op("act", lambda: nc.scalar.copy(out=VB.t[:, tt, :], in_=pbank(b2)[:, 0:256]),
                     reads=[PB[b2]], writes=[VB])
                gt_ = gtmp[tt % 2]
                S.op("act", lambda: nc.scalar.activation(out=gt_.t[:], in_=pbank(b2)[:, 256:512], func=AF.Silu),
                     reads=[PB[b2]], writes=[gt_])
                S.op("dve", lambda: nc.vector.tensor_tensor(out=SG.t[:, tt, :], in0=gt_.t[:], in1=RN.t[:, l, :],
                                                            op=ALU.mult), reads=[gt_, RN], writes=[SG])
                if tt == 0:
                    ckpt(134)
                bk = nextbank((4, 5), "t45")

                def f():
                    ins = None
                    for d in range(2):
                        for hp in range(2):
                            ins = nc.tensor.matmul(pbank(bk)[:, (d * 2 + hp) * 128:(d * 2 + hp + 1) * 128],
                                                   kf.t[:, d, hp * 128:(hp + 1) * 128],
                                                   VB.t[:, tt, hp * 128:(hp + 1) * 128], start=True, stop=True)
                    return ins
                S.op("pe", f, reads=[kf, VB], writes=[PB[bk]])
                if tt == 0:
                    ckpt(135)
                pk = pbank(bk).rearrange("p (a b) -> p a b", b=128)
                for j in range(2):
                    ps_ = slice(j * 64, (j + 1) * 64)
                    S.op("dve", lambda: nc.vector.tensor_copy(
                        KVS.t[ps_, tt, :, :, :].rearrange("p a b c -> p (a b) c"), pk[ps_, :, j * 64:(j + 1) * 64]),
                        reads=[PB[bk]], writes=[KVS])
            g0_stageC(7)
            w_done()
            w_done()

            ckpt(14)
            for s_ in range(nseq):
                if half == 0:
                    S.op("dve", lambda: nc.vector.memset(RS.t[:], 0.0), writes=[RSd[0], RSd[1]])
                else:
                    for d, nm in ((0, "srf"), (1, "srb")):
                        for j in range(2):
                            srcs = din[nm][l].rearrange("(hp j) d e -> j d hp e", j=2)[j]
                            S.dma("sp", RS.t[j * 64:(j + 1) * 64, d, :, :], srcs, writes=[RSd[d]])
                def step(d, tt):
                    S.op("dve", lambda: nc.vector.tensor_copy(RSb.t[:, tt, d, :, :], RSd[d].t[:]),
                         reads=[RSd[d]], writes=[RSbd[d]])
                    S.op("dve", lambda: nc.vector.tensor_tensor(out=RSd[d].t[:], in0=RSd[d].t[:],
                                                                in1=CD.t[:, l, d, :, :], op=ALU.mult),
                         reads=[RSd[d], CD], writes=[RSd[d]])
                    S.op("dve", lambda: nc.vector.tensor_tensor(out=RSd[d].t[:], in0=RSd[d].t[:],
                                                                in1=KVS.t[:, tt, d, :, :], op=ALU.add),
                         reads=[RSd[d], KVS], writes=[RSd[d]])
                for c in range(cpl):
                    step(0, s_ * cpl + c)
                    step(1, s_ * cpl + (cpl - 1 - c))
                if half == 0:
                    stg = st_stage[s_ % 2]
                    S.op("dve", lambda: nc.vector.tensor_copy(stg.t[:], RS.t[:]), reads=[RSd[0], RSd[1]], writes=[stg])
                    for d, nm in ((0, "nrf"), (1, "nrb")):
                        for j in range(2):
                            dsts = dout[nm][s_, l].rearrange("(hp j) d e -> j d hp e", j=2)[j]
                            S.dma("sp", dsts, stg.t[j * 64:(j + 1) * 64, d, :, :], reads=[stg])

            ckpt(15)
            S.barrier()
            ar_keep = AR.off
            AR.off = ar_mark
            MT = [AV([128, 512], BF16) for _ in range(2)]
            QF = [AV([128, 2, 2, 128], BF16) for _ in range(2)]
            OR = AV([128, 8, 256])
            for tt in range(8):
                c = tt % cpl
                tsl = slice(tt * 128, (tt + 1) * 128)
                bia = nextbank((0, 1), "t01")
                bib = nextbank((0, 1), "t01")

                def f():
                    ins = None
                    for j, bnk in ((0, bia), (1, bib)):
                        ps_ = slice(j * 64, (j + 1) * 64)
                        for hp in range(2):
                            ins = nc.tensor.matmul(pbank(bnk)[:, hp * 128:(hp + 1) * 128], KT.t[ps_, hp, tsl],
                                                   QT.t[ps_, hp, tsl], start=True, stop=True)
                    return ins
                S.op("pe", f, reads=[KT, QT], writes=[PB[bia], PB[bib]])
                mt = MT[tt % 2]
                mt4 = mt.t[:].rearrange("p (hp j q) -> p hp j q", hp=2, j=2)
                for j, bnk in ((0, bia), (1, bib)):
                    S.op("dve", lambda: nc.vector.tensor_tensor(
                        out=mt4[:, :, j, :], in0=pbank(bnk)[:, 0:256].rearrange("p (a b) -> p a b", b=128),
                        in1=DTm.t[:, l, :, :].rearrange("p (hp j) q -> p hp j q", j=2)[:, :, j, :], op=ALU.mult),
                        reads=[PB[bnk], DTm], writes=[mt])
                qf = QF[tt % 2]
                S.op("dve", lambda: nc.vector.tensor_tensor(
                    out=qf.t[:], in0=QD.t[:, l, :, :, :],
                    in1=QT.t[:, :, tsl].unsqueeze(2).broadcast_to([128, 2, 2, 128]), op=ALU.mult),
                    reads=[QT, QD], writes=[qf])
                use_f = not (half == 0 and c == 0)
                use_b = not (half == 0 and c == cpl - 1)
                bo = nextbank((2, 3), "t23")

                def f():
                    ins = None
                    for h in range(4):
                        hp, j = h // 2, h % 2
                        ps_ = slice(j * 64, (j + 1) * 64)
                        o_ap = pbank(bo)[:, h * 64:(h + 1) * 64]
                        last = not (use_f or use_b)
                        ins = nc.tensor.matmul(o_ap, mt.t[:, h * 128:(h + 1) * 128], VB.t[:, tt, h * 64:(h + 1) * 64],
                                               start=True, stop=last)
                        if use_f:
                            ins = nc.tensor.matmul(o_ap, qf.t[ps_, hp, 0, :], RSb.t[ps_, tt, 0, hp, :],
                                                   start=False, stop=not use_b)
                        if use_b:
                            ins = nc.tensor.matmul(o_ap, qf.t[ps_, hp, 1, :], RSb.t[ps_, tt, 1, hp, :],
                                                   start=False, stop=True)
                    return ins
                S.op("pe", f, reads=[mt, VB, qf, RSbd[0], RSbd[1]], writes=[PB[bo]])
                S.op("act", lambda: nc.scalar.copy(out=OR.t[:, tt, :], in_=pbank(bo)[:, 0:256]),
                     reads=[PB[bo]], writes=[OR])
            sv = SCR.t[:, :, 0:256]
            S.op("act", lambda: nc.scalar.activation(out=sv, in_=OR.t[:], func=AF.Square), reads=[OR], writes=[SCR])
            ssR = AV([128, 8, 4])
            S.op("dve", lambda: nc.vector.tensor_reduce(out=ssR.t[:], in_=sv.rearrange("p t (h c) -> p t h c", c=64),
                                                        axis=AX.X, op=ALU.add), reads=[SCR], writes=[ssR])
            rstd_from_ss(ssR.t[:], ssR.t[:], 1.0 / 64, EPS, [ssR], [ssR])
            o4 = OR.t[:].rearrange("p t (h c) -> p t h c", c=64)
            S.op("dve", lambda: nc.vector.tensor_tensor(out=o4, in0=o4,
                                                        in1=ssR.t[:].unsqueeze(3).broadcast_to([128, 8, 4, 64]),
                                                        op=ALU.mult), reads=[OR, ssR], writes=[OR])
            S.op("dve", lambda: nc.vector.tensor_tensor(out=ytok.t[:, :, 256:512], in0=OR.t[:], in1=SG.t[:],
                                                        op=ALU.mult), reads=[OR, SG], writes=[ytok])
            S.barrier()
            ckpt(16)
            AR.off = ar_mark

            NKC = 2 if half == 0 else 10
            KOFF = 0 if half == 0 else 256
            NKEY = 1024 + KOFF
            QTa = AV([128, 4, 1024], BF16)
            KTa = AV([128, 4, NKEY], BF16)
            VA = AV([128, NKEY // 128, 4, 132], BF16)
            S.op("dve", lambda: nc.vector.memset(VA.t[:, :, :, 128:129], 1.0), writes=[VA])
            ar_att = AR.off
            stg = [AV([128, 512]) for _ in range(2)]

            class _G:
                pass
            GR = []
            for g_ in range(2):
                o_ = _G()
                o_.QA = AV([128, 4, 512])
                o_.QB = AV([128, 4, 512], BF16)
                o_.ss = AV([128, 32])
                o_.SQ = _G()
                o_.SQ.ap = SCR.t[:, g_ * 4:(g_ + 1) * 4, :]
                o_.SQ.t = o_.SQ.ap
                o_.SQ.res = Res()
                GR.append(o_)
            S.op("dve", lambda: nc.vector.memset(GR[0].ss.t[:], 0.0), reads=[yT], writes=[GR[0].ss, GR[0].SQ, GR[1].SQ])
            if half == 1:
                for kc in range(2):
                    st = stg[kc % 2]
                    S.dma("sp", st.t[:], din["ck"][l, kc * 128:(kc + 1) * 128, :], writes=[st])
                    S.op("act", lambda: nc.scalar.copy(out=GR[0].QB.t[:, kc, :], in_=st.t[:]), reads=[st],
                         writes=[GR[0].QB])
                    transposes_to(GR[0].QB, lambda i: GR[0].QB.t[:, kc, i * 128:(i + 1) * 128], 4, KTa,
                                  KTa.t[:, :, kc * 128:(kc + 1) * 128])
                for kc in range(2):
                    st = stg[kc % 2]
                    S.dma("sp", st.t[:], din["cv"][l, kc * 128:(kc + 1) * 128, :], writes=[st])
                    S.op("act", lambda: nc.scalar.copy(out=VA.t[:, kc, :, 0:128],
                                                       in_=st.t[:].rearrange("p (a b) -> p a b", b=128)),
                         reads=[st], writes=[VA])

            def stageA1(slot, g):
                G = GR[g]
                for t in range(4):
                    tt = g * 4 + t
                    b = nextbank()
                    zmm(slot, tt, b)
                    S.op("act", lambda: nc.scalar.copy(out=G.QA.t[:, t, :], in_=pbank(b)), reads=[PB[b]], writes=[G.QA])
                    S.op("act", lambda: nc.scalar.activation(out=G.SQ.t[:, t, :], in_=pbank(b), func=AF.Square),
                         reads=[PB[b]], writes=[G.SQ])

            def stageA2(g):
                G = GR[g]
                S.op("dve", lambda: nc.vector.tensor_reduce(
                    out=G.ss.t[:], in_=G.SQ.t[:].rearrange("p t (a b) -> p (t a) b", b=64),
                    axis=AX.X, op=ALU.add), reads=[G.SQ], writes=[G.ss])

            def stageB(g, gain_tt, which):
                G = GR[g]
                if which == "q":
                    rstd_from_ss(G.ss.t[:], G.ss.t[:], 1.0, 64 * EPS, [G.ss], [G.ss])
                else:
                    rstd_from_ss(G.ss.t[:], G.ss.t[:], 1.0 / 64, EPS, [G.ss], [G.ss])
                qf_ = G.QA.t[:].rearrange("p a b -> p (a b)")
                sf_ = G.SQ.t[:].rearrange("p a b -> p (a b)")
                q3 = qf_.rearrange("p (a b) -> p a b", b=64)
                S.op("dve", lambda: nc.vector.tensor_tensor(out=q3, in0=q3,
                                                            in1=G.ss.t[:].unsqueeze(2).broadcast_to([128, 32, 64]),
                                                            op=ALU.mult), reads=[G.QA, G.ss], writes=[G.QA])
                gbc = gain_tt.t[:, l, :].unsqueeze(1).broadcast_to([128, 32, 64])
                if half == 0 and which == "q":
                    S.op("dve", lambda: nc.vector.tensor_tensor(
                        out=G.QB.t[:].rearrange("p a (g c) -> p (a g) c", c=64), in0=q3, in1=gbc, op=ALU.mult),
                        reads=[G.QA, gain_tt], writes=[G.QB])
                else:
                    S.op("dve", lambda: nc.vector.tensor_tensor(out=q3, in0=q3, in1=gbc, op=ALU.mult),
                         reads=[G.QA, gain_tt], writes=[G.QA])
                    if half == 0:
                        for s2 in range(2):
                            s_ = g * 2 + s2
                            S.dma("sp", dout["nk"][s_, l].rearrange("(t p) c -> p t c", p=128),
                                  G.QA.t[:, 2 * s2:2 * s2 + 2, :], reads=[G.QA])
                        S.op("act", lambda: nc.scalar.copy(out=G.QB.t[:].rearrange("p a b -> p (a b)"), in_=qf_),
                             reads=[G.QA], writes=[G.QB])
                    else:
                        x5 = G.QA.t[:].rearrange("p t (g a s c) -> p t g a s c", a=2, s=2, c=16)
                        r5 = G.SQ.t[:].rearrange("p t (g a s c) -> p t g a s c", a=2, s=2, c=16)
                        s5 = SIN.t[:, g * 4:(g + 1) * 4, :].rearrange("p t (a s c) -> p t a s c", s=2, c=16)
                        for sidx in range(2):
                            for ax_ in range(2):
                                S.op("dve", lambda: nc.vector.tensor_tensor(
                                    out=r5[:, :, :, ax_, sidx, :], in0=x5[:, :, :, ax_, 1 - sidx, :],
                                    in1=s5[:, :, ax_, sidx, :].unsqueeze(2).broadcast_to([128, 4, 8, 16]), op=ALU.mult),
                                    reads=[G.QA, SIN], writes=[G.SQ])
                        q4 = G.QA.t[:].rearrange("p t (g c) -> p t g c", c=64)
                        S.op("dve", lambda: nc.vector.tensor_tensor(
                            out=q4, in0=q4,
                            in1=COS.t[:, g * 4:(g + 1) * 4, :].unsqueeze(2).broadcast_to([128, 4, 8, 64]), op=ALU.mult),
                            reads=[G.QA, COS], writes=[G.QA])
                        S.op("dve", lambda: nc.vector.tensor_tensor(out=G.QB.t[:].rearrange("p a b -> p (a b)"),
                                                                    in0=qf_, in1=sf_, op=ALU.add),
                             reads=[G.QA, G.SQ], writes=[G.QB])

            def stageC(g, dstT, doff):
                G = GR[g]
                for t in range(4):
                    tt = g * 4 + t
                    b = nextbank()
                    pv = pbank(b).bitcast(BF16)

                    def f():
                        ins = None
                        for i in range(4):
                            ins = nc.tensor.transpose(pv[:, i * 128:(i + 1) * 128], G.QB.t[:, t, i * 128:(i + 1) * 128],
                                                      identB.t[:])
                        return ins
                    S.op("pe", f, reads=[G.QB, identB], writes=[PB[b]])
                    S.op("act", lambda: nc.scalar.copy(out=dstT.t[:, :, doff + tt * 128:doff + (tt + 1) * 128],
                                                       in_=pv[:, 0:512].rearrange("p (a b) -> p a b", b=128)),
                         reads=[PB[b]], writes=[dstT])

            def stageV(slot, g):
                for t in range(4):
                    tt = g * 4 + t
                    b = nextbank()
                    zmm(slot, tt, b)
                    kci = KOFF // 128 + tt
                    if half == 0:
                        st = stg[tt % 2]
                        S.op("act", lambda: nc.scalar.copy(out=st.t[:], in_=pbank(b)), reads=[PB[b]], writes=[st])
                        S.dma("sp", dout["nv"][tt // 2, l, (tt % 2) * 128:(tt % 2) * 128 + 128, :], st.t[:], reads=[st])
                        S.op("act", lambda: nc.scalar.copy(out=VA.t[:, kci, :, 0:128],
                                                           in_=st.t[:].rearrange("p (a b) -> p a b", b=128)),
                             reads=[st], writes=[VA])
                    else:
                        S.op("act", lambda: nc.scalar.copy(out=VA.t[:, kci, :, 0:128],
                                                           in_=pbank(b).rearrange("p (a b) -> p a b", b=128)),
                             reads=[PB[b]], writes=[VA])

            slot3 = w_get()
            stageA1(slot3, 0)
            stageA2(0)
            stageB(0, QN, "q")
            stageA1(slot3, 1)
            stageA2(1)
            w_done()
            slot4 = w_get()
            stageC(0, QTa, 0)
            stageB(1, QN, "q")
            stageA1(slot4, 0)
            stageA2(0)
            stageC(1, QTa, 0)
            stageB(0, KN, "k")
            stageA1(slot4, 1)
            stageA2(1)
            w_done()
            slot5 = w_get()
            stageC(0, KTa, KOFF)
            stageV(slot5, 0)
            stageB(1, KN, "k")
            stageV(slot5, 1)
            stageC(1, KTa, KOFF)
            w_done()
            S.barrier()
            ckpt(17)
            AR.off = ar_att
            OALL = AV([128, 8, 4, 128])
            ETP = [AV([128, 2, 2, 256], BF16) for _ in range(2)]
            ot2 = [AV([128, 128]) for _ in range(2)]
            work = []
            for qg in range(4):
                kcs = [qg * 2, qg * 2 + 1] if half == 0 else list(range(10))
                for h in range(4):
                    for pi in range(len(kcs) // 2):
                        work.append((qg, h, pi, len(kcs) // 2, (kcs[2 * pi], kcs[2 * pi + 1])))
            st_banks = {}

            def emit_st(widx):
                qg, h, pi, npair, kc2 = work[widx]
                q0 = qg * 256
                bsA, bsB = ((0, 1), (2, 3))[widx % 2]
                st_banks[widx] = (bsA, bsB)

                def f():
                    ins = None
                    for i, bnk in ((0, bsA), (1, bsB)):
                        ps_ = slice(i * 64, (i + 1) * 64)
                        for kk in range(2):
                            kc = kc2[kk]
                            ins = nc.tensor.matmul(pbank(bnk)[:, kk * 256:(kk + 1) * 256],
                                                   KTa.t[ps_, h, kc * 128:(kc + 1) * 128],
                                                   QTa.t[ps_, h, q0:q0 + 256], start=True, stop=True)
                    return ins
                S.op("pe", f, reads=[KTa, QTa], writes=[PB[bsA], PB[bsB]])

            OSQh = AV([128, 2048])

            def norm_half(hh):
                ov = OALL.t[:, hh * 4:(hh + 1) * 4, :, :]
                of_ = ov.rearrange("p a b c -> p (a b c)")
                ss2 = AV([128, 16])
                S.op("dve", lambda: nc.vector.tensor_tensor(out=OSQh.t[:], in0=of_, in1=of_, op=ALU.mult),
                     reads=[OALLh[hh]], writes=[OSQh])
                S.op("dve", lambda: nc.vector.tensor_reduce(out=ss2.t[:], in_=OSQh.t[:].rearrange("p (a b) -> p a b", b=128),
                                                            axis=AX.X, op=ALU.add), reads=[OSQh], writes=[ss2])
                rstd_from_ss(ss2.t[:], ss2.t[:], 1.0 / 128, EPS, [ss2], [ss2])
                o3 = of_.rearrange("p (a b) -> p a b", b=128)
                S.op("dve", lambda: nc.vector.tensor_tensor(out=o3, in0=o3,
                                                            in1=ss2.t[:].unsqueeze(2).broadcast_to([128, 16, 128]),
                                                            op=ALU.mult), reads=[OALLh[hh], ss2], writes=[OALLh[hh]])
                S.op("dve", lambda: nc.vector.tensor_tensor(
                    out=ytok.t[:, hh * 4:(hh + 1) * 4, 512:1024].rearrange("p t (h c) -> p t h c", c=128), in0=ov,
                    in1=DN.t[:, l, :].unsqueeze(1).unsqueeze(1).broadcast_to([128, 4, 4, 128]), op=ALU.mult),
                    reads=[OALLh[hh], DN], writes=[ytok])

            OALLh = [Res(), Res()]
            emit_st(0)
            itn = 0
            for widx in range(len(work)):
                qg, h, pi, npair, kc2 = work[widx]
                q0 = qg * 256
                if pi == 0:
                    accs = ((4, 5), (6, 7))[itn % 2]
                    itn += 1
                if widx + 1 < len(work):
                    emit_st(widx + 1)
                bsA, bsB = st_banks.pop(widx)
                etp = ETP[widx % 2]
                S.op("act", lambda: nc.scalar.activation(out=etp.t[:].rearrange("p i a b -> p (i a b)"),
                                                         in_=PSt[bsA // 2][:, 0:1024], func=AF.Exp),
                     reads=[PB[bsA], PB[bsB]], writes=[etp])

                def f():
                    ins = None
                    for kk in range(2):
                        kc = kc2[kk]
                        for qb_ in range(2):
                            for i in range(2):
                                first = (pi == 0 and kk == 0 and i == 0)
                                last = (pi == npair - 1 and kk == 1 and i == 1)
                                ins = nc.tensor.matmul(pbank(accs[qb_])[:, i * 129:(i + 1) * 129],
                                                       etp.t[:, i, kk, qb_ * 128:(qb_ + 1) * 128],
                                                       VA.t[:, kc, h, 0:129], start=first, stop=last)
                    return ins
                S.op("pe", f, reads=[etp, VA], writes=[PB[accs[0]], PB[accs[1]]])
                if pi == npair - 1:
                    for qb_ in range(2):
                        tt = (q0 + qb_ * 128) // 128
                        ab = accs[qb_]
                        acc = pbank(ab)
                        rc = nsmall()
                        S.op("dve", lambda: nc.vector.reciprocal(
                            rc.t[:, 0:2], acc[:, 0:258].rearrange("p (a b) -> p a b", b=129)[:, :, 128]),
                            reads=[PB[ab]], writes=[rc])
                        S.op("dve", lambda: nc.vector.tensor_tensor(out=rc.t[:, 2:3], in0=rc.t[:, 1:2],
                                                                    in1=NLAM.t[:, l:l + 1], op=ALU.mult),
                             reads=[rc, NLAM], writes=[rc])
                        o1 = ot2[qb_]
                        S.op("dve", lambda: nc.vector.tensor_scalar(o1.t[:], acc[:, 129:257], rc.t[:, 2:3], None,
                                                                    ALU.mult), reads=[PB[ab], rc], writes=[o1])
                        S.op("dve", lambda: nc.vector.scalar_tensor_tensor(
                            out=OALL.t[:, tt, h, :], in0=acc[:, 0:128], scalar=rc.t[:, 0:1], in1=o1.t[:],
                            op0=ALU.mult, op1=ALU.add), reads=[PB[ab], rc, o1], writes=[OALLh[qg // 2]])
                        if qb_ == 1 and h == 3 and qg in (1, 3):
                            norm_half(qg // 2)
            ckpt(18)
            if half == HALVES[0]:
                mods_group(l, 2)
            slots_o = [w_get(), w_get()]
            for tg in range(2):
                for tt in range(tg * 4, tg * 4 + 4):
                    b = nextbank()
                    pv = pbank(b).bitcast(BF16)

                    def f():
                        ins = None
                        for i in range(8):
                            ins = nc.tensor.transpose(pv[:, i * 128:(i + 1) * 128], ytok.t[:, tt, i * 128:(i + 1) * 128],
                                                      identB.t[:])
                        return ins
                    S.op("pe", f, reads=[ytok, identB], writes=[PB[b]])
                    S.op("dve", lambda: nc.vector.tensor_copy(
                        yT.t[:, :, tt * 128:(tt + 1) * 128], pv[:, 0:1024].rearrange("p (a b) -> p a b", b=128)),
                        reads=[PB[b]], writes=[yT.halves[tg], yT])
                for cg in range(2):
                    slot = slots_o[cg]
                    wv = slotv(slot, 8, 512)
                    for m in range(4):
                        mm = cg * 4 + m
                        b = nextbank()

                        def f():
                            ins = None
                            for kc in range(8):
                                ins = nc.tensor.matmul(pbank(b), wv[:, kc, m * 128:(m + 1) * 128],
                                                       yT.t[:, kc, tg * 512:(tg + 1) * 512],
                                                       start=(kc == 0), stop=(kc == 7))
                            return ins
                        S.op("pe", f, reads=[slot, yT.halves[tg]], writes=[PB[b]])
                        xs_ = xT.t[:, mm, tg * 512:(tg + 1) * 512]
                        S.op("dve", lambda: nc.vector.scalar_tensor_tensor(
                            out=xs_, in0=pbank(b), scalar=MOD[l].t[:, 2 * 8 + mm, cond:cond + 1], in1=xs_,
                            op0=ALU.mult, op1=ALU.add), reads=[PB[b], MOD[l], xT], writes=[xT])
                        S.op("act", lambda: nc.scalar.activation(out=hT.t[:, mm, tg * 512:(tg + 1) * 512], in_=xs_,
                                                                 func=AF.Square), reads=[xT], writes=[hT.halves[tg]])
            w_done()
            w_done()
            S.barrier()

        def ffn(l, half):
            cond = half
            nseq, L = (4, 256) if half == 0 else (1, 1024)
            AR.reset()
            gT = AV([128, 22, 1024], BF16)
            y0 = [AV([128, 1024]) for _ in range(3)]
            sa = [AV([128, 1024]) for _ in range(2)]
            upb = ((0, 1), (2, 3), (4, 5))
            ui = 0
            for t in range(11):
                slot = w_get()
                wv = slotv(slot, 8, 512)
                for sub in range(2):
                    j = 2 * t + sub
                    for ab in range(2):
                        cols = ab * 256 + sub * 128
                        jj = ab * 22 + j
                        bb = upb[ui % 3]
                        yy = y0[ui % 3]
                        ui += 1
                        for tg in range(2):
                            bnk = bb[tg]

                            def f():
                                ins = None
                                for kc in range(8):
                                    ins = nc.tensor.matmul(pbank(bnk), wv[:, kc, cols:cols + 128],
                                                           hT.t[:, kc, tg * 512:(tg + 1) * 512],
                                                           start=(kc == 0), stop=(kc == 7))
                                return ins
                            S.op("pe", f, reads=[slot, hT.halves[tg]], writes=[PB[bnk]])
                        pfull = PSt[bb[0] // 2][:, 0:1024]
                        S.op("act", lambda: nc.scalar.activation(out=yy.t[:], in_=pfull, func=AF.Identity,
                                                                 bias=CW.t[:, l, 3, jj:jj + 1],
                                                                 scale=CW.t[:, l, 1, jj:jj + 1]),
                             reads=[PB[bb[0]], PB[bb[1]], CW], writes=[yy])
                        pv3 = pfull.rearrange("p (s t) -> p s t", t=L)
                        yv3 = yy.t[:].rearrange("p (s t) -> p s t", t=L)
                        S.op("dve", lambda: nc.vector.scalar_tensor_tensor(
                            out=yv3[:, :, 1:L], in0=pv3[:, :, 0:L - 1], scalar=CW.t[:, l, 0, jj:jj + 1],
                            in1=yv3[:, :, 1:L], op0=ALU.mult, op1=ALU.add),
                            reads=[PB[bb[0]], PB[bb[1]], CW, yy], writes=[yy])
                        S.op("dve", lambda: nc.vector.scalar_tensor_tensor(
                            out=yv3[:, :, 0:L - 1], in0=pv3[:, :, 1:L], scalar=CW.t[:, l, 2, jj:jj + 1],
                            in1=yv3[:, :, 0:L - 1], op0=ALU.mult, op1=ALU.add),
                            reads=[PB[bb[0]], PB[bb[1]], CW, yy], writes=[yy])
                        if ab == 0:
                            sa_ = sa[j % 2]
                            S.op("act", lambda: nc.scalar.activation(out=sa_.t[:], in_=yy.t[:], func=AF.Silu),
                                 reads=[yy], writes=[sa_])
                        else:
                            sa_ = sa[j % 2]
                            S.op("pool", lambda: nc.gpsimd.tensor_tensor(out=gT.t[:, j, :], in0=sa_.t[:], in1=yy.t[:],
                                                                         op=ALU.mult), reads=[sa_, yy], writes=[gT])
                w_done()
            if half == HALVES[0]:
                mods_group(l, 5)
            for m in range(8):
                slot = w_get()
                wv = slotv(slot, 22, 128)
                for tg in range(2):
                    b = nextbank((6, 7), "t67")

                    def f():
                        ins = None
                        for kc in range(22):
                            ins = nc.tensor.matmul(pbank(b), wv[:, kc, :], gT.t[:, kc, tg * 512:(tg + 1) * 512],
                                                   start=(kc == 0), stop=(kc == 21))
                        return ins
                    S.op("pe", f, reads=[slot, gT], writes=[PB[b]])
                    xs_ = xT.t[:, m, tg * 512:(tg + 1) * 512]
                    S.op("dve", lambda: nc.vector.scalar_tensor_tensor(
                        out=xs_, in0=pbank(b), scalar=MOD[l].t[:, 5 * 8 + m, cond:cond + 1], in1=xs_,
                        op0=ALU.mult, op1=ALU.add), reads=[PB[b], MOD[l], xT], writes=[xT])
                    if l < NL - 1:
                        S.op("act", lambda: nc.scalar.activation(out=yT.t[:, m, tg * 512:(tg + 1) * 512], in_=xs_,
                                                                 func=AF.Square), reads=[xT], writes=[yT] + yT.halves)
                w_done()
            S.barrier()

        for half in HALVES:
            AR.reset()
            load_x(din["xp"] if half == 0 else din["xs"])
            S.barrier()
            ckpt(11)
            for l in range(NL):
                AR.reset()
                if half == HALVES[0]:
                    mods_group(l, 0)
                    mods_group(l, 1)
                norm_mod(l, 0, half, presq=(yT if l > 0 else None))
                ckpt(12)
                mixer(l, half)
                ckpt(19)
                AR.reset()
                if half == HALVES[0]:
                    mods_group(l, 3)
                    mods_group(l, 4)
                norm_mod(l, 1, half, presq=hT)
                ckpt(20)
                ffn(l, half)
                ckpt(21)
            AR.reset()
            store_x(dout["yp"] if half == 0 else dout["ys"])
            S.barrier()
        S.drain("sp")
    except _Stop:
        pass
    return nc


_CACHE = {}


def _prep_inputs(inputs):
    f32 = lambda a: np.ascontiguousarray(np.asarray(a, dtype=np.float32))
    I = {k: f32(v) for k, v in inputs.items()}
    hc = host_consts()
    shared = {
        "norm1": I["norm1"], "w_mod": I["w_mod"], "b_mod": I["b_mod"], "w_in": I["w_in"],
        "sgu_norm": I["sgu_norm"], "sgu_w": I["sgu_w"], "sgu_b": I["sgu_b"],
        "rlf": I["ret_logit_fwd"], "rlb": I["ret_logit_bwd"], "ret_norm": I["ret_norm"].reshape(2, 256),
        "q_norm": I["q_norm"], "k_norm": I["k_norm"], "diff_lam": I["diff_lam"].reshape(2, 256),
        "diff_norm": I["diff_norm"], "w_out": I["w_out"], "norm2": I["norm2"], "ffn_up": I["ffn_up"],
        "ffn_conv": I["ffn_conv"], "ffn_conv_b": I["ffn_conv_b"], "ffn_down": I["ffn_down"],
    }
    shared.update(hc)
    maps = []
    for c in range(8):
        m = dict(shared)
        m["xp"] = np.ascontiguousarray(I["x_prompt"][4 * c:4 * c + 4].reshape(1024, 1024))
        m["xs"] = np.ascontiguousarray(I["x_sample"][c])
        m["cvec"] = np.ascontiguousarray(np.stack([I["c_ctx"], I["c"][c]], axis=0))
        m["ck"] = np.ascontiguousarray(I["cache_k"][c].reshape(2, 256, 512))
        m["cv"] = np.ascontiguousarray(I["cache_v"][c].reshape(2, 256, 512))
        m["srf"] = np.ascontiguousarray(I["state_ret_fwd"][c])
        m["srb"] = np.ascontiguousarray(I["state_ret_bwd"][c])
        maps.append(m)
    return maps


def kernel(**inputs):
    maps = _prep_inputs(inputs)
    if "nc" not in _CACHE:
        _CACHE["nc"] = build()
    nc = _CACHE["nc"]
    res = run_bass_kernel_spmd(nc, maps, core_ids=list(range(8)))
    R = res.results
    yp = np.concatenate([np.asarray(R[c]["yp"]).reshape(4, 256, 1024) for c in range(8)], axis=0)
    ys = np.stack([np.asarray(R[c]["ys"]) for c in range(8)], axis=0)
    nk = np.concatenate([np.asarray(R[c]["nk"]).reshape(4, 2, 256, 4, 2, 64) for c in range(8)], axis=0)
    nv = np.concatenate([np.asarray(R[c]["nv"]).reshape(4, 2, 256, 4, 128) for c in range(8)], axis=0)
    nrf = np.concatenate([np.asarray(R[c]["nrf"]) for c in range(8)], axis=0)
    nrb = np.concatenate([np.asarray(R[c]["nrb"]) for c in range(8)], axis=0)
    return (yp.astype(np.float32), ys.astype(np.float32), nk.astype(np.float32), nv.astype(np.float32),
            nrf.astype(np.float32), nrb.astype(np.float32))
```
